# Optimizing a Trainium2 kernel written in Bass

```python
import math
import jax, jax.numpy as jnp
from jax import lax
import numpy as np

D_MODEL = 1024
BATCH = 16
SEQ = 2048
DEPTH = 2
DEC_BATCH = 8
DEC_SEQ = 4096
PAST_LEN = 128

N_MIXERS = 2
N_LAYERS_A = (DEPTH + 1) // 2
N_LAYERS_B = DEPTH // 2
MIX_WIDTH = 3 * D_MODEL // 4
GROUP_DIM = 64
N_GROUPS = MIX_WIDTH // GROUP_DIM
CHUNK = 128
N_MEM = 256
XATTN_WIDTH = D_MODEL // 4
XATTN_HEADS = 4
XATTN_HEAD_DIM = XATTN_WIDTH // XATTN_HEADS
HYENA_ORDER = 2
FILTER_EMB = 33
FILTER_BANDS = (FILTER_EMB - 1) // 2
FILTER_HIDDEN = 64
D_FF = 2816
NORM_EPS = 1e-6
LN_EPS = 1e-5
DECAY_TARGET = 1e-2
FAST_DECAY_PCT = 0.3
SLOW_DECAY_PCT = 1.5

kernel_name = "hybrid_gmlp_hyena_encoder"


def rms_norm(x, g):
    xf = x.astype(jnp.float32)
    y = xf * lax.rsqrt(jnp.mean(xf * xf, axis=-1, keepdims=True) + NORM_EPS)
    return (y * g.astype(jnp.float32)).astype(x.dtype)


def layer_norm(x, g, b):
    xf = x.astype(jnp.float32)
    mu = jnp.mean(xf, axis=-1, keepdims=True)
    var = jnp.mean(jnp.square(xf - mu), axis=-1, keepdims=True)
    y = (xf - mu) * lax.rsqrt(var + LN_EPS)
    return (y * g.astype(jnp.float32) + b.astype(jnp.float32)).astype(x.dtype)


def dwconv3(x, w, b):
    xp = jnp.pad(x, ((0, 0), (1, 1), (0, 0)))
    return xp[:, :-2] * w[0] + xp[:, 1:-1] * w[1] + xp[:, 2:] * w[2] + b


def gmlp_spatial_gating(h_u, h_v, ln_g, ln_b, w_s, b_s):
    bsz, length, _ = h_u.shape
    u = jax.nn.gelu(h_u)
    v = layer_norm(jax.nn.gelu(h_v), ln_g, ln_b)
    v = v.reshape(bsz, length // CHUNK, CHUNK, N_GROUPS, GROUP_DIM)
    mixed = jnp.einsum('gts,bcsgd->bctgd', w_s, v) + b_s.T[None, None, :, :, None]
    return u * mixed.reshape(bsz, length, MIX_WIDTH)


def hyena_filters(length, w1, b1, w2, b2, w3, freq):
    f32 = jnp.float32
    t = jnp.linspace(0.0, 1.0, length, dtype=f32)[:, None]
    w = (2.0 * math.pi / length) * jnp.arange(length, dtype=f32)[:, None]
    f = jnp.linspace(1e-4, FILTER_BANDS - 1, FILTER_BANDS, dtype=f32)[None, :]
    feat = jnp.concatenate([t, jnp.cos(f * w), -jnp.sin(f * w)], axis=-1)
    fr = freq.astype(f32)
    h = jnp.sin(fr * (feat @ w1.astype(f32) + b1.astype(f32)))
    h = jnp.sin(fr * (h @ w2.astype(f32) + b2.astype(f32)))
    h = (h @ w3.astype(f32)).reshape(length, 2, HYENA_ORDER, MIX_WIDTH)
    min_decay = math.log(DECAY_TARGET) / SLOW_DECAY_PCT
    max_decay = math.log(DECAY_TARGET) / FAST_DECAY_PCT
    deltas = jnp.abs(jnp.linspace(min_decay, max_decay, MIX_WIDTH, dtype=f32))
    decay = jnp.exp(-t * deltas[None, :])
    return h * decay[:, None, None, :]


def bidir_fftconv(z, h_fwd, h_bwd, bias):
    length, ch = h_fwd.shape
    k = jnp.concatenate([h_fwd, jnp.zeros((1, ch), jnp.float32), h_bwd[:0:-1]], axis=0)
    zf = z.astype(jnp.float32)
    zs = jnp.fft.rfft(zf, n=2 * length, axis=1)
    ks = jnp.fft.rfft(k, n=2 * length, axis=0)
    y = jnp.fft.irfft(zs * ks[None], n=2 * length, axis=1)[:, :length]
    return (y + zf * bias.astype(jnp.float32)).astype(z.dtype)


def hyena_mixer(proj, sconv_w, sconv_b, w1, b1, w2, b2, w3, freq, fbias):
    length = proj.shape[1]
    p = dwconv3(proj, sconv_w, sconv_b)
    v = p[..., :MIX_WIDTH]
    x1 = p[..., MIX_WIDTH:2 * MIX_WIDTH]
    x2 = p[..., 2 * MIX_WIDTH:]
    h = hyena_filters(length, w1, b1, w2, b2, w3, freq)
    z = x1 * bidir_fftconv(v, h[:, 0, 0], h[:, 1, 0], fbias[0])
    return x2 * bidir_fftconv(z, h[:, 0, 1], h[:, 1, 1], fbias[1])


def memory_cross_attention(q, mem, mem_g, w_kv):
    bsz, length, _ = q.shape
    q = q.reshape(bsz, length, XATTN_HEADS, XATTN_HEAD_DIM)
    kv = (rms_norm(mem, mem_g) @ w_kv).reshape(bsz, N_MEM, 2, XATTN_HEADS, XATTN_HEAD_DIM)
    k, v = kv[:, :, 0], kv[:, :, 1]
    s = jnp.einsum('bshd,bmhd->bhsm', q, k).astype(jnp.float32) * (XATTN_HEAD_DIM ** -0.5)
    p = jax.nn.softmax(s, axis=-1).astype(v.dtype)
    o = jnp.einsum('bhsm,bmhd->bshd', p, v)
    return o.reshape(bsz, length, XATTN_WIDTH)


def conv_ffn(x, w_up, conv_w, conv_b, w_down):
    h = x @ w_up
    gate = dwconv3(h[..., :D_FF], conv_w, conv_b)
    return (jax.nn.gelu(gate) * h[..., D_FF:]) @ w_down


def _trunk(x, mem, p):
    for i in range(DEPTH):
        j = i // N_MIXERS
        hn = rms_norm(x, p['norm_mix_pre'][i])
        if i % N_MIXERS == 0:
            proj = hn @ p['a_w_in'][j]
            mix = gmlp_spatial_gating(proj[..., :MIX_WIDTH], proj[..., MIX_WIDTH:2 * MIX_WIDTH],
                                      p['a_ln_g'][j], p['a_ln_b'][j], p['a_w_s'][j], p['a_b_s'][j])
            q = proj[..., 2 * MIX_WIDTH:]
        else:
            proj = hn @ p['b_w_in'][j]
            mix = hyena_mixer(proj[..., :3 * MIX_WIDTH], p['b_sconv_w'][j], p['b_sconv_b'][j],
                              p['b_filt_w1'][j], p['b_filt_b1'][j], p['b_filt_w2'][j], p['b_filt_b2'][j],
                              p['b_filt_w3'][j], p['b_filt_freq'][j], p['b_filt_bias'][j])
            q = proj[..., 3 * MIX_WIDTH:]
        att = memory_cross_attention(q, mem, p['norm_mem'][i], p['xattn_w_kv'][i])
        out = jnp.concatenate([mix, att], axis=-1) @ p['w_out'][i]
        x = x + rms_norm(out, p['norm_mix_post'][i])
        hn = rms_norm(x, p['norm_ffn_pre'][i])
        f = conv_ffn(hn, p['ffn_w_up'][i], p['ffn_conv_w'][i], p['ffn_conv_b'][i], p['ffn_w_down'][i])
        x = x + rms_norm(f, p['norm_ffn_post'][i])
    return x


def setup_inputs(seed: int = 0) -> dict:
    key = jax.random.key(seed)
    ks = jax.random.split(key, 40)
    f32 = jnp.float32

    def nrm(k, shape, scale):
        return jax.random.normal(k, shape, f32) * scale

    def gain(k, shape):
        return 1.0 + 0.02 * jax.random.normal(k, shape, f32)

    d = D_MODEL
    return {
        'x_prompt': nrm(ks[0], (BATCH, SEQ, d), 1.0),
        'x_sample': nrm(ks[1], (DEC_BATCH, DEC_SEQ, d), 1.0),
        'mem_prompt': nrm(ks[2], (BATCH, N_MEM, d), 1.0),
        'mem_sample': nrm(ks[3], (DEC_BATCH, N_MEM, d), 1.0),
        'norm_mix_pre': gain(ks[4], (DEPTH, d)),
        'norm_mix_post': gain(ks[5], (DEPTH, d)),
        'norm_ffn_pre': gain(ks[6], (DEPTH, d)),
        'norm_ffn_post': gain(ks[7], (DEPTH, d)),
        'norm_mem': gain(ks[8], (DEPTH, d)),
        'a_w_in': nrm(ks[9], (N_LAYERS_A, d, 2 * MIX_WIDTH + XATTN_WIDTH), d ** -0.5),
        'a_ln_g': gain(ks[10], (N_LAYERS_A, MIX_WIDTH)),
        'a_ln_b': nrm(ks[11], (N_LAYERS_A, MIX_WIDTH), 0.02),
        'a_w_s': nrm(ks[12], (N_LAYERS_A, N_GROUPS, CHUNK, CHUNK), CHUNK ** -0.5),
        'a_b_s': nrm(ks[13], (N_LAYERS_A, N_GROUPS, CHUNK), 0.02),
        'b_w_in': nrm(ks[14], (N_LAYERS_B, d, 3 * MIX_WIDTH + XATTN_WIDTH), d ** -0.5),
        'b_sconv_w': nrm(ks[15], (N_LAYERS_B, 3, 3 * MIX_WIDTH), 3 ** -0.5),
        'b_sconv_b': nrm(ks[16], (N_LAYERS_B, 3 * MIX_WIDTH), 0.02),
        'b_filt_w1': nrm(ks[17], (N_LAYERS_B, FILTER_EMB, FILTER_HIDDEN), FILTER_EMB ** -0.5),
        'b_filt_b1': nrm(ks[18], (N_LAYERS_B, FILTER_HIDDEN), 0.02),
        'b_filt_w2': nrm(ks[19], (N_LAYERS_B, FILTER_HIDDEN, FILTER_HIDDEN), FILTER_HIDDEN ** -0.5),
        'b_filt_b2': nrm(ks[20], (N_LAYERS_B, FILTER_HIDDEN), 0.02),
        'b_filt_w3': nrm(ks[21], (N_LAYERS_B, FILTER_HIDDEN, 2 * HYENA_ORDER * MIX_WIDTH), 0.1 * FILTER_HIDDEN ** -0.5),
        'b_filt_freq': gain(ks[22], (N_LAYERS_B, FILTER_HIDDEN)),
        'b_filt_bias': nrm(ks[23], (N_LAYERS_B, HYENA_ORDER, MIX_WIDTH), 0.5),
        'xattn_w_kv': nrm(ks[24], (DEPTH, d, 2 * XATTN_WIDTH), d ** -0.5),
        'w_out': nrm(ks[25], (DEPTH, MIX_WIDTH + XATTN_WIDTH, d), (MIX_WIDTH + XATTN_WIDTH) ** -0.5),
        'ffn_w_up': nrm(ks[26], (DEPTH, d, 2 * D_FF), d ** -0.5),
        'ffn_conv_w': nrm(ks[27], (DEPTH, 3, D_FF), 3 ** -0.5),
        'ffn_conv_b': nrm(ks[28], (DEPTH, D_FF), 0.02),
        'ffn_w_down': nrm(ks[29], (DEPTH, D_FF, d), D_FF ** -0.5),
    }


def reference(x_prompt, x_sample, mem_prompt, mem_sample,
              norm_mix_pre, norm_mix_post, norm_ffn_pre, norm_ffn_post, norm_mem,
              a_w_in, a_ln_g, a_ln_b, a_w_s, a_b_s,
              b_w_in, b_sconv_w, b_sconv_b, b_filt_w1, b_filt_b1, b_filt_w2, b_filt_b2,
              b_filt_w3, b_filt_freq, b_filt_bias,
              xattn_w_kv, w_out, ffn_w_up, ffn_conv_w, ffn_conv_b, ffn_w_down):
    params = {
        'norm_mix_pre': norm_mix_pre, 'norm_mix_post': norm_mix_post,
        'norm_ffn_pre': norm_ffn_pre, 'norm_ffn_post': norm_ffn_post, 'norm_mem': norm_mem,
        'a_w_in': a_w_in, 'a_ln_g': a_ln_g, 'a_ln_b': a_ln_b, 'a_w_s': a_w_s, 'a_b_s': a_b_s,
        'b_w_in': b_w_in, 'b_sconv_w': b_sconv_w, 'b_sconv_b': b_sconv_b,
        'b_filt_w1': b_filt_w1, 'b_filt_b1': b_filt_b1, 'b_filt_w2': b_filt_w2, 'b_filt_b2': b_filt_b2,
        'b_filt_w3': b_filt_w3, 'b_filt_freq': b_filt_freq, 'b_filt_bias': b_filt_bias,
        'xattn_w_kv': xattn_w_kv, 'w_out': w_out,
        'ffn_w_up': ffn_w_up, 'ffn_conv_w': ffn_conv_w, 'ffn_conv_b': ffn_conv_b, 'ffn_w_down': ffn_w_down,
    }
    y_prompt = _trunk(x_prompt, mem_prompt, params)
    y_sample = _trunk(x_sample, mem_sample, params)
    return (y_prompt, y_sample)
```

```python
import math
from contextlib import ExitStack

import numpy as np
import concourse.bass as bass
import concourse.mybir as mybir
from concourse.bass_utils import run_bass_kernel_spmd

F32 = mybir.dt.float32
BF16 = mybir.dt.bfloat16
AF = mybir.ActivationFunctionType
ALU = mybir.AluOpType

NT = 8192
SEQS = [(0, 2048), (2048, 2048), (4096, 4096)]
NORM_EPS = 1e-6
LN_EPS = 1e-5
D = 1024
MW = 768
DFF = 2816
NCH_FF = 22
GELU = AF.Gelu_apprx_tanh
STOP = None
LIMIT = None

CV_MIXPRE = [0, 16]
CV_FFNPRE = [8, 24]
CV_MEM = [32, 40]
CV_FFC = [48, 48 + 88]
CV_SC = 48 + 176
NCV = CV_SC + 72


class Res:
    __slots__ = ("name", "w", "r", "chan", "excl")

    def __init__(self, name, excl=False):
        self.name = name
        self.w = None
        self.r = []
        self.chan = None
        self.excl = excl


class Chan:
    __slots__ = ("sem", "cnt")

    def __init__(self, sem):
        self.sem = sem
        self.cnt = 0


class TB:
    def __init__(self, t, name):
        self.t = t
        self.r = Res(name)


class Sched:
    def __init__(self, nc, stack):
        self.nc = nc
        self.stack = stack
        self.eng = {"pe": nc.tensor, "act": nc.scalar, "dve": nc.vector,
                    "pool": nc.gpsimd, "sp": nc.sync}
        self.esem = {}
        self.ecnt = {}
        for e in ("pe", "act", "dve", "pool"):
            self.esem[e] = stack.enter_context(nc.semaphore("s_" + e))
            self.ecnt[e] = 0
        self.known = {e: {} for e in self.eng}
        self.chans = []
        self.free = []
        self.ninst = 0
        self.nwait = 0

    def _need(self, e, ev, same_ok):
        if ev is None:
            return
        sem, val = ev
        if same_ok and e in self.esem and sem is self.esem[e]:
            return
        k = self.known[e]
        if k.get(id(sem), 0) >= val:
            return
        k[id(sem)] = val
        self.eng[e].wait_ge(sem, val)
        self.nwait += 1

    def _deps(self, e, reads, writes, same_ok):
        for r in reads:
            self._need(e, r.w, same_ok)
        for r in writes:
            self._need(e, r.w, same_ok)
            for ev in r.r:
                self._need(e, ev, same_ok)

    def _post(self, ev, reads, writes):
        for r in reads:
            r.r.append(ev)
            if len(r.r) > 10:
                d = {}
                for s, v in r.r:
                    if id(s) not in d or d[id(s)][1] < v:
                        d[id(s)] = (s, v)
                r.r = list(d.values())
        for r in writes:
            r.w = ev
            r.r = []

    def op(self, e, fn, reads=(), writes=()):
        ex = [r for r in reads if r.excl]
        if ex:
            writes = list(writes) + ex
            reads = [r for r in reads if not r.excl]
        self._deps(e, reads, writes, same_ok=(e == "pe"))
        ins = fn()
        self.ecnt[e] += 1
        ins.then_inc(self.esem[e], 1)
        ev = (self.esem[e], self.ecnt[e])
        self._post(ev, reads, writes)
        self.ninst += 1
        return ev

    def dma(self, q, out, in_, chan, reads=(), writes=()):
        self._deps(q, reads, writes, same_ok=False)
        if chan.chan is None:
            if self.free:
                chan.chan = self.free.pop()
            else:
                c = Chan(self.stack.enter_context(self.nc.semaphore("d%d" % len(self.chans))))
                self.chans.append(c)
                chan.chan = c
        c = chan.chan
        ins = self.eng[q].dma_start(out=out, in_=in_)
        c.cnt += 16
        ins.then_inc(c.sem, 16)
        ev = (c.sem, c.cnt)
        self._post(ev, reads, writes)
        self.ninst += 1
        return ev

    def barrier(self):
        for e in self.eng:
            for e2 in self.esem:
                if self.ecnt[e2]:
                    self._need(e, (self.esem[e2], self.ecnt[e2]), False)
            for c in self.chans:
                if c.cnt:
                    self._need(e, (c.sem, c.cnt), False)
        self.free = list(self.chans)


def build(stages=("kv", "A", "B0", "C", "FF", "FD", "D", "B1"), dbg=(), ext=()):
    nc = bass.Bass("TRN2", target_bir_lowering=False)

    def din(name, shape, dt=F32):
        return nc.dram_tensor(name, list(shape), dt, kind="ExternalInput").ap()

    def dscr(name, shape, dt):
        kind = "ExternalOutput" if name in dbg else ("ExternalInput" if name in ext else "Internal")
        return nc.dram_tensor(name, list(shape), dt, kind=kind).ap()

    X0 = din("X", [NT, D])
    MEM = din("MEM", [768, D])
    A_W_IN = din("a_w_in", [D, 1792])
    B_W_IN = din("b_w_in", [D, 2560])
    W_KV = din("w_kv", [2, D, 512])
    W_OUT = din("w_out", [2, D, D])
    W_UP = din("w_up", [2, D, 2 * DFF])
    W_DN = din("w_dn", [2, DFF, D])
    WST = din("wsT", [128, 12, 128])
    BST = din("bsT", [128, 12])
    COLV = din("colv", [128, NCV])
    ROWV = din("rowv", [1, 4 * D + 3 * MW])
    IDENT = din("ident", [128, 128])
    F1T = din("f1t", [64, 128])
    GT = din("gt", [2, 8, 64, 8, 3, 64])
    DTB = din("dtb", [2, 64, 3, 64])
    ET = din("et", [2, 8, 64, 8, 2, 64])
    FEAT = [din("featP", [33, 2048]), din("featS", [33, 4096])]
    SU = din("su", [2, 64, 64])
    FW1 = din("fw1", [33, 64])
    FW2 = din("fw2", [64, 64])
    FW3 = din("fw3", [64, 3072])
    FVEC = din("fvec", [64, 3])
    FBIAS = din("fbias", [1, 1536])
    Y = nc.dram_tensor("Y", [NT, D], F32, kind="ExternalOutput").ap()

    X1 = dscr("X1", [NT, D], F32)
    X2 = dscr("X2", [NT, D], F32)
    X3 = dscr("X3", [NT, D], F32)
    HNT1 = dscr("HNT1", [D, NT], BF16)
    HNT2 = dscr("HNT2", [D, NT], BF16)
    HNT3 = dscr("HNT3", [D, NT], BF16)
    PT = dscr("PT", [2304, NT], BF16)
    ATT = dscr("ATT", [256, NT], BF16)
    MIXT = dscr("MIXT", [MW, NT], BF16)
    KF = dscr("KF", [2, 6, 2, 8, 64, 8, 2, 128], BF16)

    with ExitStack() as st:
        S = Sched(nc, st)

        uid = [0]

        def sb(stack, name, shape, dt):
            uid[0] += 1
            return TB(stack.enter_context(nc.sbuf_tensor("sb%d_%s" % (uid[0], name), list(shape), dt)), name)

        PSA = st.enter_context(nc.psum_tensor("PSA", [128, 7 * 512], F32))
        PBt = st.enter_context(nc.psum_tensor("PB", [128, 1024], BF16))
        BK = [Res("bank%d" % i, excl=True) for i in range(7)]
        PBr = Res("pb", excl=True)

        def ps(b, lo=0, hi=512):
            return PSA[:, b * 512 + lo: b * 512 + hi]

        ident = sb(st, "ident", [128, 128], BF16)
        ones = sb(st, "ones", [128, 128], BF16)
        colv = sb(st, "colv", [128, NCV], F32)
        kT = sb(st, "kT", [128, 6, 2, 256], BF16)
        vv = sb(st, "vv", [128, 6, 2, 256], BF16)
        S.dma("pool", ident.t[:], IDENT, ident.r, writes=[ident.r])
        S.dma("sp", colv.t[:], COLV, colv.r, writes=[colv.r])
        S.op("dve", lambda: nc.vector.memset(ones.t[:], 1.0), writes=[ones.r])

        def mm(out, lhsT, rhs, start, stop, reads, writes):
            S.op("pe", lambda: nc.tensor.matmul(out, lhsT=lhsT, rhs=rhs, start=start, stop=stop),
                 reads=reads, writes=writes)

        def norm_T(ph, xt_ap, xt_res, gbase, out_ap, out_res, wk):
            junk, ss, xn = wk["junk"], wk["ss"], wk["xn"]
            S.op("act", lambda: nc.scalar.activation(out=junk.t[:, 0:D], in_=xt_ap, func=AF.Square,
                                                     accum_out=ss.t[:, 0:1]),
                 reads=[xt_res], writes=[junk.r, ss.r])
            S.op("act", lambda: nc.scalar.activation(out=ss.t[:, 1:2], in_=ss.t[:, 0:1], func=AF.Sqrt,
                                                     scale=1.0 / D, bias=wk["eps"].t[:, 0:1]),
                 reads=[ss.r, wk["eps"].r], writes=[ss.r])
            S.op("dve", lambda: nc.vector.reciprocal(out=ss.t[:, 2:3], in_=ss.t[:, 1:2]),
                 reads=[ss.r], writes=[ss.r])
            S.op("dve", lambda: nc.vector.tensor_scalar(out=xn.t[:], in0=xt_ap, scalar1=ss.t[:, 2:3],
                                                        scalar2=None, op0=ALU.mult),
                 reads=[xt_res, ss.r], writes=[xn.r])
            for j in range(8):
                S.op("pe", lambda j=j: nc.tensor.transpose(out=PBt[:, j * 128:(j + 1) * 128],
                                                           in_=xn.t[:, j * 128:(j + 1) * 128],
                                                           identity=ident.t[:]),
                     reads=[xn.r, ident.r], writes=[PBr])
            S.op("dve", lambda: nc.vector.tensor_tensor(
                out=out_ap, in0=PBt[:, 0:1024].rearrange("p (j t) -> p j t", j=8),
                in1=colv.t[:, gbase:gbase + 8].unsqueeze(2).to_broadcast([128, 8, 128]), op=ALU.mult),
                reads=[PBr, colv.r], writes=[out_res])

        def mk_normwk(ph, tag):
            wk = {"junk": sb(ph, "junk" + tag, [128, D], BF16),
                  "ss": sb(ph, "ss" + tag, [128, 4], F32),
                  "xn": sb(ph, "xn" + tag, [128, D], BF16),
                  "eps": sb(ph, "eps" + tag, [128, 2], F32)}
            S.op("dve", lambda: nc.vector.memset(wk["eps"].t[:, 0:1], NORM_EPS), writes=[wk["eps"].r])
            S.op("dve", lambda: nc.vector.memset(wk["eps"].t[:, 1:2], LN_EPS), writes=[wk["eps"].r])
            return wk

        def hnt_view(H):
            return H.rearrange("(j p) t -> p j t", p=128)

        def epilogue(ph, banks, xres_ap, xres_res, gpost_ap, gpost_res, tok0, XOUT, gnext, HOUT, wk, ek):
            b0 = banks[0]
            pout = PSA[:, b0 * 512: b0 * 512 + 1024]
            br = [BK[b] for b in banks]
            junk, ss = wk["junk"], ek["ss"]
            S.op("act", lambda: nc.scalar.activation(out=junk.t[:, 0:D], in_=pout, func=AF.Square,
                                                     accum_out=ss.t[:, 0:1]),
                 reads=br, writes=[junk.r, ss.r])
            S.op("act", lambda: nc.scalar.activation(out=ss.t[:, 1:2], in_=ss.t[:, 0:1], func=AF.Sqrt,
                                                     scale=1.0 / D, bias=wk["eps"].t[:, 0:1]),
                 reads=[ss.r, wk["eps"].r], writes=[ss.r])
            S.op("dve", lambda: nc.vector.reciprocal(out=ss.t[:, 2:3], in_=ss.t[:, 1:2]),
                 reads=[ss.r], writes=[ss.r])
            tp = ek["tp"]
            S.op("dve", lambda: nc.vector.tensor_tensor(out=tp.t[:], in0=pout, in1=gpost_ap, op=ALU.mult),
                 reads=br + [gpost_res], writes=[tp.r])
            xnew = ek["xnew"]
            S.op("dve", lambda: nc.vector.scalar_tensor_tensor(out=xnew.t[:], in0=tp.t[:], scalar=ss.t[:, 2:3],
                                                               in1=xres_ap, op0=ALU.mult, op1=ALU.add),
                 reads=[tp.r, ss.r, xres_res], writes=[xnew.r])
            S.dma("sp", XOUT[tok0:tok0 + 128, :], xnew.t[:], ek["xst"], reads=[xnew.r])
            if HOUT is not None:
                hno = ek["hno"]
                norm_T(ph, xnew.t[:], xnew.r, gnext, hno.t[:], hno.r, wk)
                S.dma("sp", hnt_view(HOUT)[:, :, tok0:tok0 + 128], hno.t[:], ek["hst"], reads=[hno.r])

        def mk_epi(ph, tag):
            return {"ss": sb(ph, "ess" + tag, [128, 4], F32),
                    "tp": sb(ph, "tp" + tag, [128, D], F32),
                    "xnew": sb(ph, "xnew" + tag, [128, D], F32),
                    "hno": sb(ph, "hno" + tag, [128, 8, 128], BF16),
                    "xst": Res("xst" + tag), "hst": Res("hst" + tag)}

        def attention(hn_ap, hn_res, wq_ap, wq_res, ls, cat, wk, banks):
            bq, bs0, bs1, bav0, bav1 = banks
            qT, pT, rden = wk["qT"], wk["pT"], wk["rden"]
            for hc in range(2):
                for j in range(8):
                    mm(ps(bq, hc * 128, hc * 128 + 128), wq_ap(j, hc), hn_ap(j), j == 0, j == 7,
                       [wq_res, hn_res], [BK[bq]])
            S.op("act", lambda: nc.scalar.activation(out=qT.t[:].rearrange("p a t -> p (a t)"), in_=ps(bq, 0, 256),
                                                     func=AF.Copy, scale=0.125),
                 reads=[BK[bq]], writes=[qT.r])
            for h in range(4):
                hc, po = h // 2, (h % 2) * 64
                for mc in range(2):
                    idx = (h % 2) * 4 + (h // 2) * 2 + mc
                    b = bs0 if idx < 4 else bs1
                    col = (idx % 4) * 128
                    mm(PSA[:, b * 512 + col: b * 512 + col + 128],
                       kT.t[po:po + 64, ls, hc, mc * 128:(mc + 1) * 128], qT.t[po:po + 64, hc, :], True, True,
                       [kT.r, qT.r], [BK[b]])
            for half, b in ((0, bs0), (1, bs1)):
                S.op("act", lambda half=half, b=b: nc.scalar.activation(
                    out=pT.t[:, half * 4:(half + 1) * 4, :].rearrange("p a t -> p (a t)"), in_=ps(b), func=AF.Exp),
                    reads=[BK[b]], writes=[pT.r])
            for hc in range(2):
                b = bav0 if hc == 0 else bav1
                for part in range(4):
                    h = 2 * hc + (part % 2)
                    for mc in range(2):
                        lhsT = vv.t[:, ls, mc, hc * 128:(hc + 1) * 128] if part < 2 else ones.t[:]
                        mm(ps(b, part * 128, part * 128 + 128), lhsT, pT.t[:, (h % 2) * 4 + (h // 2) * 2 + mc, :], mc == 0, mc == 1,
                           [vv.r, ones.r, pT.r], [BK[b]])
                for hh in range(2):
                    lo = hh * 64
                    S.op("dve", lambda hh=hh, lo=lo, b=b, hc=hc: nc.vector.reciprocal(
                        out=rden.t[lo:lo + 64, hc, :], in_=PSA[lo:lo + 64, b * 512 + (2 + hh) * 128: b * 512 + (3 + hh) * 128]),
                        reads=[BK[b]], writes=[rden.r])
                for hh in range(2):
                    lo = hh * 64
                    S.op("dve", lambda hh=hh, lo=lo, b=b, hc=hc: nc.vector.tensor_tensor(
                        out=cat.t[lo:lo + 64, 6 + hc, :], in0=PSA[lo:lo + 64, b * 512 + hh * 128: b * 512 + (hh + 1) * 128],
                        in1=rden.t[lo:lo + 64, hc, :], op=ALU.mult),
                        reads=[BK[b], rden.r], writes=[cat.r])

        def mk_attwk(ph, tag):
            return {"qT": sb(ph, "qT" + tag, [128, 2, 128], BF16),
                    "pT": sb(ph, "pT" + tag, [128, 8, 128], BF16),
                    "rden": sb(ph, "rden" + tag, [128, 2, 128], F32)}

        def seq_of(tok):
            for si, (t0, L) in enumerate(SEQS):
                if t0 <= tok < t0 + L:
                    return si, t0, L
            raise ValueError

        def load_hnt_tile(HSRC, hb, a, T):
            si, t0, L = seq_of(a)
            lo = a - 1 if a > t0 else a
            hi = a + T + 1 if a + T < t0 + L else a + T
            if lo == a:
                S.op("dve", lambda: nc.vector.memset(hb.t[:, :, 0:1], 0.0), writes=[hb.r])
            if hi == a + T:
                S.op("dve", lambda: nc.vector.memset(hb.t[:, :, T + 1:T + 2], 0.0), writes=[hb.r])
            S.dma("sp", hb.t[:, :, lo - (a - 1): hi - (a - 1)], hnt_view(HSRC)[:, :, lo:hi], hb.r, writes=[hb.r])

        def halo_all(hb, T, w_ap, w_res, nchk, bank, hal):
            for ci in range(nchk):
                for j in range(8):
                    mm(ps(bank, 2 * ci, 2 * ci + 2), w_ap(ci, j), hb.t[:, j, 0:T + 2:T + 1], j == 0, j == 7,
                       [w_res, hb.r], [BK[bank]])
            S.op("act", lambda: nc.scalar.copy(out=hal.t[:, 0:2 * nchk], in_=ps(bank, 0, 2 * nchk)),
                 reads=[BK[bank]], writes=[hal.r])

        def conv_chunk(hb, T, w_ap, w_res, cv, nchk, ci, bmain, hal, tbuf):
            for j in range(8):
                mm(ps(bmain, 0, T), w_ap(j), hb.t[:, j, 1:T + 1], j == 0, j == 7, [w_res, hb.r], [BK[bmain]])
            c0, c1, c2, cb = (colv.t[:, cv + k * nchk + ci: cv + k * nchk + ci + 1] for k in range(4))
            S.op("act", lambda: nc.scalar.activation(out=tbuf.t[:, 0:T], in_=ps(bmain, 0, T), func=AF.Identity,
                                                     scale=c1, bias=cb),
                 reads=[BK[bmain], colv.r], writes=[tbuf.r])
            S.op("dve", lambda: nc.vector.scalar_tensor_tensor(out=tbuf.t[:, 1:T], in0=ps(bmain, 0, T - 1), scalar=c0,
                                                               in1=tbuf.t[:, 1:T], op0=ALU.mult, op1=ALU.add),
                 reads=[BK[bmain], tbuf.r, colv.r], writes=[tbuf.r])
            S.op("dve", lambda: nc.vector.scalar_tensor_tensor(out=tbuf.t[:, 0:T - 1], in0=ps(bmain, 1, T), scalar=c2,
                                                               in1=tbuf.t[:, 0:T - 1], op0=ALU.mult, op1=ALU.add),
                 reads=[BK[bmain], tbuf.r, colv.r], writes=[tbuf.r])
            S.op("dve", lambda: nc.vector.scalar_tensor_tensor(out=tbuf.t[:, 0:1], in0=hal.t[:, 2 * ci:2 * ci + 1], scalar=c0,
                                                               in1=tbuf.t[:, 0:1], op0=ALU.mult, op1=ALU.add),
                 reads=[hal.r, tbuf.r, colv.r], writes=[tbuf.r])
            S.op("dve", lambda: nc.vector.scalar_tensor_tensor(out=tbuf.t[:, T - 1:T], in0=hal.t[:, 2 * ci + 1:2 * ci + 2],
                                                               scalar=c2, in1=tbuf.t[:, T - 1:T], op0=ALU.mult, op1=ALU.add),
                 reads=[hal.r, tbuf.r, colv.r], writes=[tbuf.r])

        if "kv" in stages:
            with ExitStack() as ph:
                wkv = sb(ph, "wkv", [128, 2, 8, 512], BF16)
                for l in range(2):
                    S.dma("pool", wkv.t[:, l], W_KV[l].rearrange("(j p) n -> p j n", p=128), wkv.r, writes=[wkv.r])
                wk = mk_normwk(ph, "kv")
                xin = [sb(ph, "kvx%d" % i, [128, D], F32) for i in range(2)]
                hn = [sb(ph, "kvh%d" % i, [128, 8, 128], BF16) for i in range(2)]
                it = 0
                for l in range(2):
                    for s in range(3):
                        for mt in range(2):
                            xb, hb = xin[it % 2], hn[it % 2]
                            it += 1
                            r0 = s * 256 + mt * 128
                            S.dma("sp", xb.t[:], MEM[r0:r0 + 128, :], xb.r, writes=[xb.r])
                            norm_T(ph, xb.t[:], xb.r, CV_MEM[l], hb.t[:], hb.r, wk)
                            ls = l * 3 + s
                            for hc in range(2):
                                b = hc
                                for j in range(8):
                                    mm(ps(b, 0, 128), wkv.t[:, l, j, hc * 128:(hc + 1) * 128], hb.t[:, j, :], j == 0, j == 7,
                                       [wkv.r, hb.r], [BK[b]])
                                S.op("act", lambda b=b, hc=hc, ls=ls, mt=mt: nc.scalar.copy(
                                    out=kT.t[:, ls, hc, mt * 128:(mt + 1) * 128], in_=ps(b, 0, 128)),
                                    reads=[BK[b]], writes=[kT.r])
                            for j in range(8):
                                mm(ps(2, 0, 256), hb.t[:, j, :], wkv.t[:, l, j, 256:512], j == 0, j == 7,
                                   [wkv.r, hb.r], [BK[2]])
                            S.op("act", lambda ls=ls, mt=mt: nc.scalar.copy(out=vv.t[:, ls, mt, :], in_=ps(2, 0, 256)),
                                 reads=[BK[2]], writes=[vv.r])
                S.barrier()

        if "A" in stages:
            with ExitStack() as ph:
                win = sb(ph, "a_win", [128, 8, 1792], BF16)
                for j in range(8):
                    S.dma("pool", win.t[:, j, :], A_W_IN[j * 128:(j + 1) * 128, :], win.r, writes=[win.r])
                wout = sb(ph, "a_wout", [128, 8, D], BF16)
                S.dma("pool", wout.t[:], W_OUT[0].rearrange("(j p) n -> p j n", p=128), wout.r, writes=[wout.r])
                wst = sb(ph, "a_wst", [128, 12, 128], BF16)
                S.dma("pool", wst.t[:], WST, wst.r, writes=[wst.r])
                bst = sb(ph, "a_bst", [128, 12], F32)
                S.dma("sp", bst.t[:], BST, bst.r, writes=[bst.r])
                gpost = sb(ph, "a_gpost", [128, D], F32)
                S.dma("sp", gpost.t[:], ROWV[:, 0:D].partition_broadcast(128), gpost.r, writes=[gpost.r])
                lng = sb(ph, "a_lng", [128, MW], F32)
                lnb = sb(ph, "a_lnb", [128, MW], F32)
                S.dma("sp", lng.t[:], ROWV[:, 4 * D:4 * D + MW].partition_broadcast(128), lng.r, writes=[lng.r])
                S.dma("sp", lnb.t[:], ROWV[:, 4 * D + MW:4 * D + 2 * MW].partition_broadcast(128), lnb.r, writes=[lnb.r])
                wk = mk_normwk(ph, "a")
                ek = mk_epi(ph, "a")
                awk = mk_attwk(ph, "a")
                xin = [sb(ph, "ax%d" % i, [128, D], F32) for i in range(3)]
                hn = [sb(ph, "ah%d" % i, [128, 8, 128], BF16) for i in range(2)]
                gu = sb(ph, "a_gu", [128, MW], F32)
                gv = sb(ph, "a_gv", [128, MW], F32)
                vn = sb(ph, "a_vn", [128, MW], BF16)
                tmp = sb(ph, "a_tmp", [128, MW], F32)
                mix = sb(ph, "a_mix", [128, MW], BF16)
                st6 = sb(ph, "a_st", [128, 2, 6], F32)
                mv = sb(ph, "a_mv", [128, 4], F32)
                cat = [sb(ph, "a_cat%d" % i, [128, 8, 128], BF16) for i in range(2)]
                nch = LIMIT or NT // 128

                def load_x(c):
                    xb = xin[c % 3]
                    S.dma("sp", xb.t[:], X0[c * 128:(c + 1) * 128, :], xb.r, writes=[xb.r])

                load_x(0)
                load_x(1)
                for c in range(nch):
                    if c + 2 < nch:
                        load_x(c + 2)
                    xb, hb, cb = xin[c % 3], hn[c % 2], cat[c % 2]
                    si, _, _ = seq_of(c * 128)
                    norm_T(ph, xb.t[:], xb.r, CV_MIXPRE[0], hb.t[:], hb.r, wk)
                    if STOP == 1:
                        break
                    for nb in range(3):
                        for j in range(8):
                            mm(ps(nb), hb.t[:, j, :], win.t[:, j, nb * 512:(nb + 1) * 512], j == 0, j == 7,
                               [hb.r, win.r], [BK[nb]])
                    S.op("act", lambda: nc.scalar.activation(out=gv.t[:], in_=PSA[:, MW:2 * MW], func=GELU),
                         reads=[BK[1], BK[2]], writes=[gv.r])
                    S.op("act", lambda: nc.scalar.activation(out=gu.t[:], in_=PSA[:, 0:MW], func=GELU),
                         reads=[BK[0], BK[1]], writes=[gu.r])
                    if STOP == 2:
                        break
                    for k in range(2):
                        S.op("dve", lambda k=k: nc.vector.bn_stats(out=st6.t[:, k, :], in_=gv.t[:, k * 384:(k + 1) * 384]),
                             reads=[gv.r], writes=[st6.r])
                    S.op("dve", lambda: nc.vector.bn_aggr(out=mv.t[:, 0:2], in_=st6.t[:]), reads=[st6.r], writes=[mv.r])
                    S.op("act", lambda: nc.scalar.activation(out=mv.t[:, 2:3], in_=mv.t[:, 1:2], func=AF.Sqrt,
                                                             scale=1.0, bias=wk["eps"].t[:, 1:2]),
                         reads=[mv.r, wk["eps"].r], writes=[mv.r])
                    S.op("dve", lambda: nc.vector.reciprocal(out=mv.t[:, 3:4], in_=mv.t[:, 2:3]), reads=[mv.r], writes=[mv.r])
                    S.op("dve", lambda: nc.vector.tensor_scalar(out=gv.t[:], in0=gv.t[:], scalar1=mv.t[:, 0:1],
                                                                scalar2=mv.t[:, 3:4], op0=ALU.subtract, op1=ALU.mult),
                         reads=[gv.r, mv.r], writes=[gv.r])
                    S.op("pool", lambda: nc.gpsimd.tensor_tensor(out=gv.t[:], in0=gv.t[:], in1=lng.t[:], op=ALU.mult),
                         reads=[gv.r, lng.r], writes=[gv.r])
                    S.op("pool", lambda: nc.gpsimd.tensor_tensor(out=vn.t[:], in0=gv.t[:], in1=lnb.t[:], op=ALU.add),
                         reads=[gv.r, lnb.r], writes=[vn.r])
                    if STOP == 3:
                        break
                    for g in range(12):
                        col = 3 * 512 + g * 64
                        mm(PSA[:, col:col + 64], wst.t[:, g, :], vn.t[:, g * 64:(g + 1) * 64], True, True,
                           [wst.r, vn.r], [BK[3 + (g // 8)]])
                    S.op("dve", lambda: nc.vector.tensor_tensor(
                        out=tmp.t[:].rearrange("p (g d) -> p g d", g=12),
                        in0=PSA[:, 3 * 512:3 * 512 + MW].rearrange("p (g d) -> p g d", g=12),
                        in1=bst.t[:].unsqueeze(2).to_broadcast([128, 12, 64]), op=ALU.add),
                        reads=[BK[3], BK[4], bst.r], writes=[tmp.r])
                    S.op("pool", lambda: nc.gpsimd.tensor_tensor(out=mix.t[:], in0=tmp.t[:], in1=gu.t[:], op=ALU.mult),
                         reads=[tmp.r, gu.r], writes=[mix.r])
                    if STOP == 4:
                        break
                    for k in range(6):
                        S.op("pe", lambda k=k: nc.tensor.transpose(out=PBt[:, k * 128:(k + 1) * 128],
                                                                   in_=mix.t[:, k * 128:(k + 1) * 128], identity=ident.t[:]),
                             reads=[mix.r, ident.r], writes=[PBr])
                    S.op("act", lambda: nc.scalar.copy(out=cb.t[:, 0:6, :].rearrange("p a t -> p (a t)"), in_=PBt[:, 0:768]),
                         reads=[PBr], writes=[cb.r])
                    if STOP == 5:
                        break
                    attention(lambda j: hb.t[:, j, :], hb.r,
                              lambda j, hc: win.t[:, j, 1536 + hc * 128:1536 + (hc + 1) * 128], win.r,
                              0 * 3 + si, cb, awk, (5, 0, 1, 5, 6))
                    if STOP == 6:
                        break
                    for nb in range(2):
                        for k in range(8):
                            mm(ps(3 + nb), cb.t[:, k, :], wout.t[:, k, nb * 512:(nb + 1) * 512], k == 0, k == 7,
                               [cb.r, wout.r], [BK[3 + nb]])
                    if STOP == 7:
                        break
                    epilogue(ph, (3, 4), xb.t[:], xb.r, gpost.t[:], gpost.r, c * 128, X1, CV_FFNPRE[0], HNT1, wk, ek)
                if "KTD" in dbg:
                    KTD = dscr("KTD", [128, 6 * 2 * 256], BF16)
                    VVD = dscr("VVD", [128, 6 * 2 * 256], BF16)
                    S.dma("sp", KTD, kT.t[:].rearrange("p a b c -> p (a b c)"), Res("ktd"), reads=[kT.r])
                    S.dma("sp", VVD, vv.t[:].rearrange("p a b c -> p (a b c)"), Res("vvd"), reads=[vv.r])
                S.barrier()

        def ffn_phase(layer, XIN, HIN, XOUT, gnext, HOUT):
            T = 256
            with ExitStack() as ph:
                wup = sb(ph, "wup", [128, 8, 2 * DFF], BF16)
                for j in range(8):
                    S.dma("pool", wup.t[:, j, :], W_UP[layer, j * 128:(j + 1) * 128, :], wup.r, writes=[wup.r])
                wdn = sb(ph, "wdn", [128, NCH_FF, D], BF16)
                for ci in range(NCH_FF):
                    S.dma("pool", wdn.t[:, ci, :], W_DN[layer, ci * 128:(ci + 1) * 128, :], wdn.r, writes=[wdn.r])
                gpost = sb(ph, "f_gpost", [128, D], F32)
                S.dma("sp", gpost.t[:], ROWV[:, (2 + layer) * D:(3 + layer) * D].partition_broadcast(128), gpost.r,
                      writes=[gpost.r])
                wk = mk_normwk(ph, "f")
                ek = mk_epi(ph, "f")
                hb = sb(ph, "f_hb", [128, 8, T + 2], BF16)
                xin = [sb(ph, "fx%d" % i, [128, D], F32) for i in range(2)]
                tb = [sb(ph, "ft%d" % i, [128, T], F32) for i in range(2)]
                hact = sb(ph, "hact", [128, NCH_FF, T], BF16)
                hal = sb(ph, "f_hal", [128, 2 * NCH_FF], F32)
                cv = CV_FFC[layer]
                ntile = LIMIT or NT // T
                xcnt = 0
                for ti in range(ntile):
                    a = ti * T
                    load_hnt_tile(HIN, hb, a, T)
                    halo_all(hb, T, lambda ci, j: wup.t[:, j, ci * 128:(ci + 1) * 128], wup.r, NCH_FF, 4, hal)
                    for ci in range(NCH_FF):
                        tbuf = tb[ci % 2]
                        bm, bu = ci % 2, 2 + ci % 2
                        conv_chunk(hb, T, lambda j: wup.t[:, j, ci * 128:(ci + 1) * 128], wup.r, cv, NCH_FF, ci,
                                   bm, hal, tbuf)
                        S.op("act", lambda tbuf=tbuf: nc.scalar.activation(out=tbuf.t[:, 0:T], in_=tbuf.t[:, 0:T], func=GELU),
                             reads=[tbuf.r], writes=[tbuf.r])
                        for j in range(8):
                            mm(ps(bu, 0, T), wup.t[:, j, DFF + ci * 128:DFF + (ci + 1) * 128], hb.t[:, j, 1:T + 1],
                               j == 0, j == 7, [wup.r, hb.r], [BK[bu]])
                        S.op("dve", lambda tbuf=tbuf, bu=bu, ci=ci: nc.vector.tensor_tensor(
                            out=hact.t[:, ci, :], in0=ps(bu, 0, T), in1=tbuf.t[:, 0:T], op=ALU.mult),
                            reads=[BK[bu], tbuf.r], writes=[hact.r])
                    for tt in range(T // 128):
                        xb = xin[xcnt % 2]
                        xcnt += 1
                        tok0 = a + tt * 128
                        S.dma("sp", xb.t[:], XIN[tok0:tok0 + 128, :], xb.r, writes=[xb.r])
                        for nb in range(2):
                            for ci in range(NCH_FF):
                                mm(ps(5 + nb), hact.t[:, ci, tt * 128:(tt + 1) * 128], wdn.t[:, ci, nb * 512:(nb + 1) * 512],
                                   ci == 0, ci == NCH_FF - 1, [hact.r, wdn.r], [BK[5 + nb]])
                        epilogue(ph, (5, 6), xb.t[:], xb.r, gpost.t[:], gpost.r, tok0, XOUT, gnext, HOUT, wk, ek)
                S.barrier()

        if "B0" in stages:
            ffn_phase(0, X1, HNT1, X2, CV_MIXPRE[1], HNT2)

        if "C" in stages:
            T = 512
            with ExitStack() as ph:
                win = sb(ph, "b_win", [128, 8, 2560], BF16)
                for j in range(8):
                    S.dma("pool", win.t[:, j, :], B_W_IN[j * 128:(j + 1) * 128, :], win.r, writes=[win.r])
                awk = mk_attwk(ph, "c")
                hbs = [sb(ph, "c_hb%d" % i, [128, 8, T + 2], BF16) for i in range(2)]
                tb = [sb(ph, "c_t%d" % i, [128, T], F32) for i in range(2)]
                ob = [sb(ph, "c_o%d" % i, [128, T], BF16) for i in range(3)]
                hal = sb(ph, "c_hal", [128, 36], F32)
                cat = [sb(ph, "c_cat%d" % i, [128, 8, 128], BF16) for i in range(2)]
                ntile = LIMIT or NT // T
                oc = 0
                cc = 0
                for ti in range(ntile):
                    a = ti * T
                    hb = hbs[ti % 2]
                    si, _, _ = seq_of(a)
                    load_hnt_tile(HNT2, hb, a, T)
                    halo_all(hb, T, lambda ci, j: win.t[:, j, ci * 128:(ci + 1) * 128], win.r, 18, 4, hal)
                    for ci in range(18):
                        tbuf = tb[ci % 2]
                        conv_chunk(hb, T, lambda j: win.t[:, j, ci * 128:(ci + 1) * 128], win.r, CV_SC, 18, ci,
                                   ci % 2, hal, tbuf)
                        o = ob[oc % 3]
                        oc += 1
                        S.op("pool", lambda o=o, tbuf=tbuf: nc.gpsimd.tensor_copy(out=o.t[:], in_=tbuf.t[:]),
                             reads=[tbuf.r], writes=[o.r])
                        S.dma("sp", PT[ci * 128:(ci + 1) * 128, a:a + T], o.t[:], o.r, reads=[o.r])
                    for tt in range(T // 128):
                        cb = cat[cc % 2]
                        cc += 1
                        attention(lambda j: hb.t[:, j, 1 + tt * 128:1 + (tt + 1) * 128], hb.r,
                                  lambda j, hc: win.t[:, j, 2304 + hc * 128:2304 + (hc + 1) * 128], win.r,
                                  3 + si, cb, awk, (5, 2, 3, 5, 6))
                        S.dma("sp", ATT.rearrange("(k p) t -> p k t", p=128)[:, :, a + tt * 128:a + (tt + 1) * 128],
                              cb.t[:, 6:8, :], cb.r, reads=[cb.r])
                S.barrier()


        UNITS = [dict(tok0=0, L=2048, N2=32, nb=2), dict(tok0=4096, L=4096, N2=64, nb=1)]
        NG = 6

        def load_unit_tables(ph):
            f1 = sb(ph, "h_f1", [64, 128], BF16)
            S.dma("pool", f1.t[:], F1T, f1.r, writes=[f1.r])
            return f1

        if "FF" in stages:
            with ExitStack() as ph:
                f1 = load_unit_tables(ph)
                w3sd = sb(ph, "h_w3sd", [64, NG, 2, 2, 128], BF16)
                fvec = sb(ph, "h_fvec", [64, 8], F32)
                w1 = sb(ph, "h_w1", [33, 64], F32)
                w2 = sb(ph, "h_w2", [64, 64], F32)
                dl = sb(ph, "h_dl", [64, MW], F32)
                fb = sb(ph, "h_fb", [1, 2 * MW], F32)
                S.dma("sp", fvec.t[:, 0:3], FVEC, fvec.r, writes=[fvec.r])
                S.dma("sp", w1.t[:], FW1, w1.r, writes=[w1.r])
                S.dma("sp", w2.t[:], FW2, w2.r, writes=[w2.r])
                S.dma("sp", dl.t[:], ROWV[:, 4 * D + 2 * MW:4 * D + 3 * MW].partition_broadcast(64), dl.r, writes=[dl.r])
                S.dma("sp", fb.t[:], FBIAS, fb.r, writes=[fb.r])
                S.op("dve", lambda: nc.vector.tensor_tensor(out=fvec.t[:, 3:4], in0=fvec.t[:, 0:1], in1=fvec.t[:, 1:2], op=ALU.mult),
                     reads=[fvec.r], writes=[fvec.r])
                S.op("dve", lambda: nc.vector.tensor_tensor(out=fvec.t[:, 4:5], in0=fvec.t[:, 0:1], in1=fvec.t[:, 2:3], op=ALU.mult),
                     reads=[fvec.r], writes=[fvec.r])
                with ExitStack() as tmp:
                    w3r = sb(tmp, "h_w3r", [64, NG, 2, 2, 128], F32)
                    S.dma("sp", w3r.t[:].rearrange("p g o d c -> p (g o d c)"), FW3, w3r.r, writes=[w3r.r])
                    S.op("dve", lambda: nc.vector.tensor_tensor(out=w3sd.t[:, :, :, 0, :], in0=w3r.t[:, :, :, 0, :],
                                                                in1=w3r.t[:, :, :, 1, :], op=ALU.add),
                         reads=[w3r.r], writes=[w3sd.r])
                    S.op("dve", lambda: nc.vector.tensor_tensor(out=w3sd.t[:, :, :, 1, :], in0=w3r.t[:, :, :, 0, :],
                                                                in1=w3r.t[:, :, :, 1, :], op=ALU.subtract),
                         reads=[w3r.r], writes=[w3sd.r])
                    S.barrier()
                h2T = sb(ph, "h_h2T", [64, 4096], BF16)
                htok = sb(ph, "h_htok", [64, 2, 128, 64], BF16)
                Abuf = sb(ph, "h_A", [64, 64, 2, 128], BF16)
                kst = [sb(ph, "h_kst%d" % i, [64, 8, 2, 128], BF16) for i in range(2)]
                gts = [sb(ph, "h_gt%d" % i, [64, 8, 3, 64], BF16) for i in range(2)]
                dec = [sb(ph, "h_dec%d" % i, [64, 128], F32) for i in range(2)]
                su = sb(ph, "h_su", [64, 64], F32)
                mt_ = [sb(ph, "h_mt%d" % i, [64, 512], F32) for i in range(3)]
                h1c = sb(ph, "h_h1c", [64, 512], F32)
                ft = [sb(ph, "h_ft%d" % i, [33, 512], F32) for i in range(2)]
                t0f = sb(ph, "h_t0f", [1, 256], F32)
                MAGIC = 12582912.0
                TWO_PI = 2.0 * math.pi

                def sin_layer(psum_ap, psum_res, fcol, out_ap, out_res):
                    a, u, k = mt_
                    S.op("dve", lambda: nc.vector.tensor_scalar(out=a.t[:], in0=psum_ap, scalar1=fvec.t[:, 0:1],
                                                                scalar2=fvec.t[:, fcol:fcol + 1], op0=ALU.mult, op1=ALU.add),
                         reads=[psum_res, fvec.r], writes=[a.r])
                    S.op("dve", lambda: nc.vector.tensor_scalar(out=u.t[:], in0=a.t[:], scalar1=1.0 / TWO_PI, scalar2=MAGIC,
                                                                op0=ALU.mult, op1=ALU.add), reads=[a.r], writes=[u.r])
                    S.op("dve", lambda: nc.vector.tensor_scalar(out=k.t[:], in0=u.t[:], scalar1=MAGIC, scalar2=-TWO_PI,
                                                                op0=ALU.subtract, op1=ALU.mult), reads=[u.r], writes=[k.r])
                    S.op("dve", lambda: nc.vector.tensor_tensor(out=a.t[:], in0=a.t[:], in1=k.t[:], op=ALU.add),
                         reads=[a.r, k.r], writes=[a.r])
                    S.op("dve", lambda: nc.vector.tensor_scalar(out=a.t[:], in0=a.t[:], scalar1=3.1415925, scalar2=-3.1415925,
                                                                op0=ALU.min, op1=ALU.max), reads=[a.r], writes=[a.r])
                    S.op("act", lambda: nc.scalar.activation(out=out_ap, in_=a.t[:], func=AF.Sin), reads=[a.r], writes=[out_res])

                for ui, U in enumerate(UNITS):
                    L, N2, nbt = U["L"], U["N2"], U["nb"]
                    S.dma("sp", su.t[:], SU[ui], su.r, writes=[su.r])
                    for cc in range(L // 512):
                        fbuf = ft[cc % 2]
                        S.dma("sp", fbuf.t[:], FEAT[ui][:, cc * 512:(cc + 1) * 512], fbuf.r, writes=[fbuf.r])
                        mm(PSA[0:64, 0:512], w1.t[:], fbuf.t[:], True, True, [w1.r, fbuf.r], [BK[0]])
                        sin_layer(PSA[0:64, 0:512], BK[0], 3, h1c.t[:], h1c.r)
                        mm(PSA[0:64, 512:1024], w2.t[:], h1c.t[:], True, True, [w2.r, h1c.r], [BK[1]])
                        sin_layer(PSA[0:64, 512:1024], BK[1], 4, h2T.t[:, cc * 512:(cc + 1) * 512], h2T.r)
                    for g in range(NG):
                        for o in range(2):
                            for n2 in range(N2):
                                b = 2 + n2 % 2
                                d_ = dec[n2 % 2]
                                mm(PSA[0:64, b * 512:b * 512 + 256], h2T.t[:, n2:L:N2],
                                   w3sd.t[:, g, o].rearrange("p a c -> p (a c)"), True, True, [h2T.r, w3sd.r], [BK[b]])
                                S.op("act", lambda d_=d_, n2=n2: nc.scalar.activation(
                                    out=d_.t[:], in_=dl.t[:, g * 128:(g + 1) * 128], func=AF.Exp, scale=su.t[:, n2:n2 + 1]),
                                    reads=[dl.r, su.r], writes=[d_.r])
                                for bb in range(nbt):
                                    i = bb * N2 + n2
                                    S.op("dve", lambda b=b, d_=d_, i=i: nc.vector.tensor_tensor(
                                        out=htok.t[:, :, :, i],
                                        in0=PSA[0:64, b * 512:b * 512 + 256].rearrange("p (a c) -> p a c", a=2),
                                        in1=d_.t[:].unsqueeze(1).to_broadcast([64, 2, 128]), op=ALU.mult),
                                        reads=[BK[b], d_.r], writes=[htok.r])
                            for bb in range(nbt):
                                i0 = bb * N2
                                S.op("dve", lambda i0=i0: nc.vector.tensor_tensor(
                                    out=t0f.t[0:1, 0:128], in0=htok.t[0:1, 0, :, i0], in1=htok.t[0:1, 1, :, i0], op=ALU.add),
                                    reads=[htok.r], writes=[t0f.r])
                                S.op("dve", lambda: nc.vector.scalar_tensor_tensor(
                                    out=t0f.t[0:1, 128:256], in0=t0f.t[0:1, 0:128], scalar=0.5,
                                    in1=fb.t[0:1, o * MW + g * 128:o * MW + (g + 1) * 128], op0=ALU.mult, op1=ALU.add),
                                    reads=[t0f.r, fb.r], writes=[t0f.r])
                                for a_ in range(2):
                                    S.op("dve", lambda a_=a_, i0=i0: nc.vector.tensor_copy(
                                        out=htok.t[0:1, a_, :, i0], in_=t0f.t[0:1, 128:256]),
                                        reads=[t0f.r], writes=[htok.r])
                            for a_ in range(2):
                                for c0 in range(0, 128, 4):
                                    b = (c0 // 4) % 2
                                    for c in range(c0, c0 + 4):
                                        mm(PSA[0:64, b * 512 + (c - c0) * 128:b * 512 + (c - c0 + 1) * 128],
                                           htok.t[:, a_, c, :], f1.t[:], True, True, [htok.r, f1.r], [BK[b]])
                                    S.op("act", lambda b=b, c0=c0: nc.scalar.copy(
                                        out=Abuf.t[:, :, :, c0:c0 + 4].rearrange("p k r c -> p (k r) c"),
                                        in_=PSA[0:64, b * 512:(b + 1) * 512].rearrange("p (c x) -> p x c", c=4)),
                                        reads=[BK[b]], writes=[Abuf.r])
                                for kb in range(8):
                                    gt = gts[kb % 2]
                                    S.dma("pool", gt.t[:], GT[ui, kb], gt.r, writes=[gt.r])
                                    ks = kst[kb % 2]
                                    for kp in range(4):
                                        b = 4 + kp % 2
                                        for kk in range(2):
                                            kl = kp * 2 + kk
                                            k1 = kb * 8 + kl
                                            v0, v1 = (0, 2) if a_ == 0 else (1, 0)
                                            out = PSA[0:64, b * 512 + kk * 128:b * 512 + (kk + 1) * 128]
                                            mm(out, gt.t[:, kl, v0, :], Abuf.t[:, k1, 0, :], True, False, [gt.r, Abuf.r], [BK[b]])
                                            mm(out, gt.t[:, kl, v1, :], Abuf.t[:, k1, 1, :], False, True, [gt.r, Abuf.r], [BK[b]])
                                        S.op("act", lambda b=b, ks=ks, kp=kp, a_=a_: nc.scalar.copy(
                                            out=ks.t[:, kp * 2:kp * 2 + 2, a_, :],
                                            in_=PSA[0:64, b * 512:b * 512 + 256].rearrange("p (k c) -> p k c", k=2)),
                                            reads=[BK[b]], writes=[ks.r])
                                    S.dma("sp", KF[ui, g, o, kb, :, :, a_, :], ks.t[:, :, a_, :], ks.r, reads=[ks.r])
                S.barrier()

        if "FD" in stages:
            with ExitStack() as ph:
                f1 = load_unit_tables(ph)
                dtb = sb(ph, "h_dtb", [64, 3, 64], BF16)
                vtok = sb(ph, "h_vtok", [64, 128, 64], BF16)
                x1tok = sb(ph, "h_x1tok", [64, 128, 64], BF16)
                AC = sb(ph, "h_AC", [64, 64, 2, 128], BF16)
                Yb = sb(ph, "h_Y", [64, 2, 128, 64], BF16)
                Xs = [sb(ph, "h_xs%d" % i, [64, 8, 2, 128], BF16) for i in range(2)]
                kfs = [sb(ph, "h_kf%d" % i, [64, 8, 2, 128], BF16) for i in range(2)]
                t1 = sb(ph, "h_t1", [64, 8, 128], F32)
                t2 = sb(ph, "h_t2", [64, 8, 128], F32)
                gts = [sb(ph, "h_gtd%d" % i, [64, 8, 3, 64], BF16) for i in range(2)]
                ets = [sb(ph, "h_et%d" % i, [64, 8, 2, 64], BF16) for i in range(2)]
                x2s = sb(ph, "h_x2s", [128, 4096], BF16)
                mixs = sb(ph, "h_mixs", [128, 4096], BF16)
                units = UNITS if LIMIT is None else UNITS[:LIMIT]
                for ui, U in enumerate(units):
                    L, N2, nbt, tok0 = U["L"], U["N2"], U["nb"], U["tok0"]
                    S.dma("pool", dtb.t[:], DTB[ui], dtb.r, writes=[dtb.r])
                    for g in range(NG if STOP is None else STOP):
                        for bb in range(nbt):
                            lo = tok0 + bb * L
                            S.dma("sp", vtok.t[:, :, bb * N2:(bb + 1) * N2],
                                  PT[g * 128:(g + 1) * 128, lo:lo + L].rearrange("c (n1 n2) -> n1 c n2", n2=N2),
                                  vtok.r, writes=[vtok.r])
                            S.dma("sp", x1tok.t[:, :, bb * N2:(bb + 1) * N2],
                                  PT[MW + g * 128:MW + (g + 1) * 128, lo:lo + L].rearrange("c (n1 n2) -> n1 c n2", n2=N2),
                                  x1tok.r, writes=[x1tok.r])
                        S.dma("sp", x2s.t[:], PT[2 * MW + g * 128:2 * MW + (g + 1) * 128, tok0:tok0 + 4096], x2s.r,
                              writes=[x2s.r])
                        for o in range(2):
                            for c0 in range(0, 128, 4):
                                b = (c0 // 4) % 2
                                for c in range(c0, c0 + 4):
                                    mm(PSA[0:64, b * 512 + (c - c0) * 128:b * 512 + (c - c0 + 1) * 128],
                                       vtok.t[:, c, :], f1.t[:], True, True, [vtok.r, f1.r], [BK[b]])
                                S.op("act", lambda b=b, c0=c0: nc.scalar.copy(
                                    out=AC.t[:, :, :, c0:c0 + 4].rearrange("p k r c -> p (k r) c"),
                                    in_=PSA[0:64, b * 512:(b + 1) * 512].rearrange("p (c x) -> p x c", c=4)),
                                    reads=[BK[b]], writes=[AC.r])
                            for kb in range(8):
                                gt, kf, xs = gts[kb % 2], kfs[kb % 2], Xs[kb % 2]
                                S.dma("pool", gt.t[:], GT[ui, kb], gt.r, writes=[gt.r])
                                S.dma("sp", kf.t[:], KF[ui, g, o, kb], kf.r, writes=[kf.r])
                                for kp in range(4):
                                    b = 2 + kp % 2
                                    for kk in range(2):
                                        kl = kp * 2 + kk
                                        k1 = kb * 8 + kl
                                        ore = PSA[0:64, b * 512 + kk * 256:b * 512 + kk * 256 + 128]
                                        oim = PSA[0:64, b * 512 + kk * 256 + 128:b * 512 + kk * 256 + 256]
                                        mm(ore, gt.t[:, kl, 0, :], AC.t[:, k1, 0, :], True, False, [gt.r, AC.r], [BK[b]])
                                        mm(ore, gt.t[:, kl, 2, :], AC.t[:, k1, 1, :], False, True, [gt.r, AC.r], [BK[b]])
                                        mm(oim, gt.t[:, kl, 1, :], AC.t[:, k1, 0, :], True, False, [gt.r, AC.r], [BK[b]])
                                        mm(oim, gt.t[:, kl, 0, :], AC.t[:, k1, 1, :], False, True, [gt.r, AC.r], [BK[b]])
                                    S.op("act", lambda b=b, xs=xs, kp=kp: nc.scalar.copy(
                                        out=xs.t[:, kp * 2:kp * 2 + 2, :, :].rearrange("p k r c -> p (k r c)"),
                                        in_=PSA[0:64, b * 512:(b + 1) * 512]),
                                        reads=[BK[b]], writes=[xs.r])
                                xre, xim = xs.t[:, :, 0, :], xs.t[:, :, 1, :]
                                kre, kim = kf.t[:, :, 0, :], kf.t[:, :, 1, :]
                                yre = Yb.t[:, 0, :, kb * 8:(kb + 1) * 8].rearrange("p c k -> p k c")
                                yim = Yb.t[:, 1, :, kb * 8:(kb + 1) * 8].rearrange("p c k -> p k c")
                                S.op("dve", lambda: nc.vector.tensor_tensor(out=t1.t[:], in0=xre, in1=kre, op=ALU.mult),
                                     reads=[xs.r, kf.r], writes=[t1.r])
                                S.op("pool", lambda: nc.gpsimd.tensor_tensor(out=t2.t[:], in0=xim, in1=kim, op=ALU.mult),
                                     reads=[xs.r, kf.r], writes=[t2.r])
                                S.op("dve", lambda: nc.vector.tensor_tensor(out=yre, in0=t1.t[:], in1=t2.t[:], op=ALU.subtract),
                                     reads=[t1.r, t2.r], writes=[Yb.r])
                                S.op("dve", lambda: nc.vector.tensor_tensor(out=t1.t[:], in0=xre, in1=kim, op=ALU.mult),
                                     reads=[xs.r, kf.r], writes=[t1.r])
                                S.op("pool", lambda: nc.gpsimd.tensor_tensor(out=t2.t[:], in0=xim, in1=kre, op=ALU.mult),
                                     reads=[xs.r, kf.r], writes=[t2.r])
                                S.op("dve", lambda: nc.vector.tensor_tensor(out=yim, in0=t1.t[:], in1=t2.t[:], op=ALU.add),
                                     reads=[t1.r, t2.r], writes=[Yb.r])
                            for c0 in range(0, 128, 4):
                                b = (c0 // 4) % 2
                                for c in range(c0, c0 + 4):
                                    out = PSA[0:64, b * 512 + (c - c0) * 128:b * 512 + (c - c0 + 1) * 128]
                                    mm(out, Yb.t[:, 0, c, :], dtb.t[:, 1:3, :].rearrange("p v i -> p (v i)"), True, False,
                                       [Yb.r, dtb.r], [BK[b]])
                                    mm(out, Yb.t[:, 1, c, :], dtb.t[:, 0:2, :].rearrange("p v i -> p (v i)"), False, True,
                                       [Yb.r, dtb.r], [BK[b]])
                                S.op("act", lambda b=b, c0=c0: nc.scalar.copy(
                                    out=AC.t[:, :, :, c0:c0 + 4],
                                    in_=PSA[0:64, b * 512:(b + 1) * 512].rearrange("p (c r i) -> p i r c", c=4, r=2)),
                                    reads=[BK[b]], writes=[AC.r])
                            for ib in range(8):
                                et = ets[ib % 2]
                                S.dma("pool", et.t[:], ET[ui, ib], et.r, writes=[et.r])
                                if o == 0:
                                    for ip in range(2):
                                        b = 4 + ip
                                        for il4 in range(4):
                                            il = ip * 4 + il4
                                            i = ib * 8 + il
                                            out = PSA[0:64, b * 512 + il4 * 128:b * 512 + (il4 + 1) * 128]
                                            mm(out, et.t[:, il, 0, :], AC.t[:, i, 0, :], True, False, [et.r, AC.r], [BK[b]])
                                            mm(out, et.t[:, il, 1, :], AC.t[:, i, 1, :], False, True, [et.r, AC.r], [BK[b]])
                                        i0 = ib * 8 + ip * 4
                                        S.op("dve", lambda b=b, i0=i0: nc.vector.tensor_tensor(
                                            out=vtok.t[:, :, i0:i0 + 4].rearrange("p c i -> p i c"),
                                            in0=PSA[0:64, b * 512:(b + 1) * 512].rearrange("p (i c) -> p i c", i=4),
                                            in1=x1tok.t[:, :, i0:i0 + 4].rearrange("p c i -> p i c"), op=ALU.mult),
                                            reads=[BK[b], x1tok.r], writes=[vtok.r])
                                else:
                                    b = 4 + ib % 2
                                    for il in range(8):
                                        i = ib * 8 + il
                                        out = PSA[:, b * 512 + il * 64:b * 512 + (il + 1) * 64]
                                        mm(out, AC.t[:, i, 0, :], et.t[:, il, 0, :], True, False, [et.r, AC.r], [BK[b]])
                                        mm(out, AC.t[:, i, 1, :], et.t[:, il, 1, :], False, True, [et.r, AC.r], [BK[b]])
                                    bb = (ib * 8) // N2
                                    n20 = (ib * 8) % N2
                                    view = lambda t: t[:, bb * L:(bb + 1) * L].rearrange("p (t1 n2) -> p n2 t1", n2=N2)[:, n20:n20 + 8, :]
                                    S.op("dve", lambda b=b, view=view: nc.vector.tensor_tensor(
                                        out=view(mixs.t), in0=PSA[:, b * 512:(b + 1) * 512].rearrange("p (i t) -> p i t", i=8),
                                        in1=view(x2s.t), op=ALU.mult),
                                        reads=[BK[b], x2s.r], writes=[mixs.r])
                        S.dma("sp", MIXT[g * 128:(g + 1) * 128, tok0:tok0 + 4096], mixs.t[:], mixs.r, reads=[mixs.r])
                S.barrier()
        if "D" in stages:
            with ExitStack() as ph:
                wout = sb(ph, "d_wout", [128, 8, D], BF16)
                S.dma("pool", wout.t[:], W_OUT[1].rearrange("(j p) n -> p j n", p=128), wout.r, writes=[wout.r])
                gpost = sb(ph, "d_gpost", [128, D], F32)
                S.dma("sp", gpost.t[:], ROWV[:, D:2 * D].partition_broadcast(128), gpost.r, writes=[gpost.r])
                wk = mk_normwk(ph, "d")
                ek = mk_epi(ph, "d")
                xin = [sb(ph, "dx%d" % i, [128, D], F32) for i in range(2)]
                cat = [sb(ph, "d_cat%d" % i, [128, 8, 128], BF16) for i in range(2)]
                nch = LIMIT or NT // 128
                for c in range(nch):
                    xb, cb = xin[c % 2], cat[c % 2]
                    t0 = c * 128
                    S.dma("sp", xb.t[:], X2[t0:t0 + 128, :], xb.r, writes=[xb.r])
                    S.dma("sp", cb.t[:, 0:6, :], MIXT.rearrange("(k p) t -> p k t", p=128)[:, :, t0:t0 + 128], cb.r,
                          writes=[cb.r])
                    S.dma("sp", cb.t[:, 6:8, :], ATT.rearrange("(k p) t -> p k t", p=128)[:, :, t0:t0 + 128], cb.r,
                          writes=[cb.r])
                    for nb in range(2):
                        for k in range(8):
                            mm(ps(3 + nb), cb.t[:, k, :], wout.t[:, k, nb * 512:(nb + 1) * 512], k == 0, k == 7,
                               [cb.r, wout.r], [BK[3 + nb]])
                    epilogue(ph, (3, 4), xb.t[:], xb.r, gpost.t[:], gpost.r, t0, X3, CV_FFNPRE[1], HNT3, wk, ek)
                S.barrier()
        if "B1" in stages:
            ffn_phase(1, X3, HNT3, Y, None, None)

        S.barrier()
        print("program: %d instructions, %d waits, %d dma sems" % (S.ninst, S.nwait, len(S.chans)))
    return nc


def _hyena_tables():
    t = {}
    n1 = np.arange(64)[:, None]
    k1 = np.arange(64)[None, :]
    ang = 2 * np.pi * n1 * (k1 + 0.5) / 128.0
    f1 = np.zeros((64, 128), np.float64)
    f1[:, 0::2] = np.cos(ang)
    f1[:, 1::2] = -np.sin(ang)
    t["f1t"] = f1.astype(np.float32)
    gt = np.zeros((2, 8, 64, 8, 3, 64), np.float64)
    dtb = np.zeros((2, 64, 3, 64), np.float64)
    et = np.zeros((2, 8, 64, 8, 2, 64), np.float64)
    su = np.zeros((2, 64, 64), np.float64)
    for ui, (L, N2, nb) in enumerate(((2048, 32, 2), (4096, 64, 1))):
        N = 2 * L
        n2 = np.arange(N2)[:, None]
        k2 = np.arange(N2)[None, :]
        for k1v in range(64):
            G = np.exp(-2j * np.pi * n2 * (k1v + 128 * k2 + 0.5) / N)
            Gf = np.zeros((64, 64), np.complex128)
            for b in range(nb):
                Gf[b * N2:(b + 1) * N2, b * N2:(b + 1) * N2] = G
            kb, kl = divmod(k1v, 8)
            gt[ui, kb, :, kl, 0, :] = Gf.real
            gt[ui, kb, :, kl, 1, :] = Gf.imag
            gt[ui, kb, :, kl, 2, :] = -Gf.imag
        Dm = np.exp(2j * np.pi * np.arange(N2)[:, None] * np.arange(N2)[None, :] / N2)
        Df = np.zeros((64, 64), np.complex128)
        for b in range(nb):
            Df[b * N2:(b + 1) * N2, b * N2:(b + 1) * N2] = Dm
        dtb[ui, :, 0, :] = -Df.imag
        dtb[ui, :, 1, :] = Df.real
        dtb[ui, :, 2, :] = Df.imag
        k1c = np.arange(64)[:, None]
        t1 = np.arange(64)[None, :]
        for i in range(64):
            t2 = i % N2
            E = (2.0 / N) * np.exp(2j * np.pi * (t2 + N2 * t1) * (k1c + 0.5) / N)
            ib, il = divmod(i, 8)
            et[ui, ib, :, il, 0, :] = E.real
            et[ui, ib, :, il, 1, :] = -E.imag
        nn1 = np.arange(64)[:, None]
        nn2 = np.arange(64)[None, :]
        su[ui] = np.where(nn2 < N2, -(N2 * nn1 + nn2) / (L - 1.0), 0.0)
        tt = np.linspace(0.0, 1.0, L)[:, None]
        w = (2.0 * np.pi / L) * np.arange(L)[:, None]
        f = np.linspace(1e-4, 15.0, 16)[None, :]
        feat = np.concatenate([tt, np.cos(f * w), -np.sin(f * w)], axis=-1)
        t["featP" if ui == 0 else "featS"] = np.ascontiguousarray(feat.T).astype(np.float32)
    t["gt"] = gt.astype(np.float32)
    t["dtb"] = dtb.astype(np.float32)
    t["et"] = et.astype(np.float32)
    t["su"] = su.astype(np.float32)
    return t


def _prep_shared(inp):
    f = lambda a: np.ascontiguousarray(np.asarray(a, dtype=np.float32))
    sh = {}
    sh["a_w_in"] = f(inp["a_w_in"][0])
    sh["b_w_in"] = f(inp["b_w_in"][0])
    sh["w_kv"] = f(inp["xattn_w_kv"])
    sh["w_out"] = f(inp["w_out"])
    sh["w_up"] = f(inp["ffn_w_up"])
    sh["w_dn"] = f(inp["ffn_w_down"])
    sh["wsT"] = f(np.transpose(np.asarray(inp["a_w_s"][0]), (2, 0, 1)))
    sh["bsT"] = f(np.transpose(np.asarray(inp["a_b_s"][0]), (1, 0)))

    def cols(v):
        v = np.asarray(v, dtype=np.float32)
        return v.reshape(-1, 128).T

    cl = []
    for nm in ("norm_mix_pre", "norm_ffn_pre"):
        pass
    a = inp
    cl += [cols(a["norm_mix_pre"][0]), cols(a["norm_ffn_pre"][0]), cols(a["norm_mix_pre"][1]), cols(a["norm_ffn_pre"][1])]
    cl += [cols(a["norm_mem"][0]), cols(a["norm_mem"][1])]
    for l in range(2):
        for k in range(3):
            cl.append(cols(a["ffn_conv_w"][l][k]))
        cl.append(cols(a["ffn_conv_b"][l]))
    for k in range(3):
        cl.append(cols(a["b_sconv_w"][0][k]))
    cl.append(cols(a["b_sconv_b"][0]))
    colv = np.concatenate(cl, axis=1)
    assert colv.shape == (128, NCV), colv.shape
    sh["colv"] = f(colv)
    deltas = np.abs(np.linspace(math.log(1e-2) / 1.5, math.log(1e-2) / 0.3, MW, dtype=np.float32))
    rowv = np.concatenate([np.asarray(a["norm_mix_post"][0]), np.asarray(a["norm_mix_post"][1]),
                           np.asarray(a["norm_ffn_post"][0]), np.asarray(a["norm_ffn_post"][1]),
                           np.asarray(a["a_ln_g"][0]), np.asarray(a["a_ln_b"][0]), deltas]).astype(np.float32)
    sh["rowv"] = f(rowv[None, :])
    sh["ident"] = np.eye(128, dtype=np.float32)
    sh.update(_hyena_tables())
    sh["fw1"] = f(a["b_filt_w1"][0])
    sh["fw2"] = f(a["b_filt_w2"][0])
    w3 = np.asarray(a["b_filt_w3"][0], dtype=np.float32).reshape(64, 2, 2, 6, 128)
    sh["fw3"] = f(np.transpose(w3, (0, 3, 2, 1, 4)).reshape(64, 3072))
    sh["fvec"] = f(np.stack([np.asarray(a["b_filt_freq"][0]), np.asarray(a["b_filt_b1"][0]),
                             np.asarray(a["b_filt_b2"][0])], axis=1))
    sh["fbias"] = f(np.asarray(a["b_filt_bias"][0]).reshape(1, 1536))
    return sh


def _core_inputs(inp, i):
    xp = np.asarray(inp["x_prompt"])
    xs = np.asarray(inp["x_sample"])
    mp = np.asarray(inp["mem_prompt"])
    ms = np.asarray(inp["mem_sample"])
    X = np.concatenate([xp[2 * i], xp[2 * i + 1], xs[i]], axis=0)
    M = np.concatenate([mp[2 * i], mp[2 * i + 1], ms[i]], axis=0)
    return {"X": np.ascontiguousarray(X, dtype=np.float32), "MEM": np.ascontiguousarray(M, dtype=np.float32)}


def kernel(**inputs):
    sh = _prep_shared(inputs)
    nc = build()
    in_maps = []
    for i in range(8):
        m = dict(sh)
        m.update(_core_inputs(inputs, i))
        in_maps.append(m)
    res = run_bass_kernel_spmd(nc, in_maps, core_ids=list(range(8)))
    yp = np.zeros((16, 2048, D), np.float32)
    ys = np.zeros((8, 4096, D), np.float32)
    for i in range(8):
        y = res.results[i]["Y"]
        yp[2 * i] = y[0:2048]
        yp[2 * i + 1] = y[2048:4096]
        ys[i] = y[4096:8192]
    return (yp, ys)
```

```python
import math
from contextlib import ExitStack

import numpy as np
import concourse.bass as bass
import concourse.mybir as mybir
from concourse.bass_utils import run_bass_kernel_spmd

F32 = mybir.dt.float32
BF16 = mybir.dt.bfloat16
AF = mybir.ActivationFunctionType
ALU = mybir.AluOpType

NT = 8192
SEQS = [(0, 2048), (2048, 2048), (4096, 4096)]
NORM_EPS = 1e-6
LN_EPS = 1e-5
D = 1024
MW = 768
DFF = 2816
NCH_FF = 22
GELU = AF.Gelu_apprx_tanh
STOP = None
LIMIT = None

CV_MIXPRE = [0, 16]
CV_FFNPRE = [8, 24]
CV_MEM = [32, 40]
CV_FFC = [48, 48 + 88]
CV_SC = 48 + 176
NCV = CV_SC + 72


class Res:
    __slots__ = ("name", "w", "r", "chan", "excl")

    def __init__(self, name, excl=False):
        self.name = name
        self.w = None
        self.r = []
        self.chan = None
        self.excl = excl


class Chan:
    __slots__ = ("sem", "cnt", "q")

    def __init__(self, sem):
        self.sem = sem
        self.cnt = 0
        self.q = None


class TB:
    def __init__(self, t, name):
        self.t = t
        self.r = Res(name)


class Sched:
    def __init__(self, nc, stack):
        self.nc = nc
        self.stack = stack
        self.eng = {"pe": nc.tensor, "act": nc.scalar, "dve": nc.vector,
                    "pool": nc.gpsimd, "sp": nc.sync}
        self.esem = {}
        self.ecnt = {}
        for e in ("pe", "act", "dve", "pool"):
            self.esem[e] = stack.enter_context(nc.semaphore("s_" + e))
            self.ecnt[e] = 0
        self.known = {e: {} for e in self.eng}
        self.chans = []
        self.free = {"sp": [], "pool": []}
        self.ninst = 0
        self.nwait = 0

    def _need(self, e, ev, same_ok):
        if ev is None:
            return
        sem, val = ev
        if same_ok and e in self.esem and sem is self.esem[e]:
            return
        k = self.known[e]
        if k.get(id(sem), 0) >= val:
            return
        k[id(sem)] = val
        self.eng[e].wait_ge(sem, val)
        self.nwait += 1

    def _deps(self, e, reads, writes, same_ok):
        for r in reads:
            self._need(e, r.w, same_ok)
        for r in writes:
            self._need(e, r.w, same_ok)
            for ev in r.r:
                self._need(e, ev, same_ok)

    def _post(self, ev, reads, writes):
        for r in reads:
            r.r.append(ev)
            if len(r.r) > 10:
                d = {}
                for s, v in r.r:
                    if id(s) not in d or d[id(s)][1] < v:
                        d[id(s)] = (s, v)
                r.r = list(d.values())
        for r in writes:
            r.w = ev
            r.r = []

    def op(self, e, fn, reads=(), writes=()):
        ex = [r for r in reads if r.excl]
        if ex:
            writes = list(writes) + ex
            reads = [r for r in reads if not r.excl]
        self._deps(e, reads, writes, same_ok=(e == "pe"))
        ins = fn()
        self.ecnt[e] += 1
        ins.then_inc(self.esem[e], 1)
        ev = (self.esem[e], self.ecnt[e])
        self._post(ev, reads, writes)
        self.ninst += 1
        return ev

    def dma(self, q, out, in_, chan, reads=(), writes=()):
        skip = chan.chan.sem if chan.chan is not None else None
        for r in reads:
            self._need(q, r.w, False)
        for r in writes:
            if not (r.w is not None and r.w[0] is skip and r is chan and not r.r):
                self._need(q, r.w, False)
            for ev in r.r:
                self._need(q, ev, False)
        if chan.chan is None:
            if self.free[q]:
                chan.chan = self.free[q].pop()
            else:
                c = Chan(self.stack.enter_context(self.nc.semaphore("d%s%d" % (q, len(self.chans)))))
                c.q = q
                self.chans.append(c)
                chan.chan = c
        c = chan.chan
        assert c.q == q, "a DMA channel semaphore must stay on one queue type"
        ins = self.eng[q].dma_start(out=out, in_=in_)
        c.cnt += 16
        ins.then_inc(c.sem, 16)
        ev = (c.sem, c.cnt)
        self._post(ev, reads, writes)
        self.ninst += 1
        return ev

    def barrier(self):
        for e in self.eng:
            for e2 in self.esem:
                if self.ecnt[e2]:
                    self._need(e, (self.esem[e2], self.ecnt[e2]), False)
            for c in self.chans:
                if c.cnt:
                    self._need(e, (c.sem, c.cnt), False)
        self.free = {"sp": [c for c in self.chans if c.q == "sp"], "pool": [c for c in self.chans if c.q == "pool"]}


def build(stages=("kv", "A", "B0", "C", "FF", "FD", "D", "B1"), dbg=(), ext=()):
    nc = bass.Bass("TRN2", target_bir_lowering=False)

    def din(name, shape, dt=F32):
        return nc.dram_tensor(name, list(shape), dt, kind="ExternalInput").ap()

    def dscr(name, shape, dt):
        kind = "ExternalOutput" if name in dbg else ("ExternalInput" if name in ext else "Internal")
        return nc.dram_tensor(name, list(shape), dt, kind=kind).ap()

    X0 = din("X", [NT, D])
    MEM = din("MEM", [768, D])
    A_W_IN = din("a_w_in", [D, 1792])
    B_W_IN = din("b_w_in", [D, 2560])
    W_KV = din("w_kv", [2, D, 512])
    W_OUT = din("w_out", [2, D, D])
    W_UP = din("w_up", [2, D, 2 * DFF])
    W_DN = din("w_dn", [2, DFF, D])
    WST = din("wsT", [128, 12, 128])
    BST = din("bsT", [128, 12])
    COLV = din("colv", [128, NCV])
    ROWV = din("rowv", [1, 4 * D + 3 * MW])
    IDENT = din("ident", [128, 128])
    F1T = din("f1t", [64, 128])
    GT = din("gt", [2, 8, 64, 8, 3, 64])
    DTB = din("dtb", [2, 64, 3, 64])
    ET = din("et", [2, 8, 64, 8, 2, 64])
    FEAT = [din("featP", [33, 2048]), din("featS", [33, 4096])]
    SU = din("su", [2, 64, 64])
    FW1 = din("fw1", [33, 64])
    FW2 = din("fw2", [64, 64])
    FW3 = din("fw3", [64, 3072])
    FVEC = din("fvec", [64, 3])
    FBIAS = din("fbias", [1, 1536])
    Y = nc.dram_tensor("Y", [NT, D], F32, kind="ExternalOutput").ap()

    X1 = dscr("X1", [NT, D], F32)
    X2 = dscr("X2", [NT, D], F32)
    X3 = dscr("X3", [NT, D], F32)
    HNT1 = dscr("HNT1", [D, NT], BF16)
    HNT2 = dscr("HNT2", [D, NT], BF16)
    HNT3 = dscr("HNT3", [D, NT], BF16)
    PT = dscr("PT", [2304, NT], BF16)
    ATT = dscr("ATT", [256, NT], BF16)
    MIXT = dscr("MIXT", [MW, NT], BF16)
    KF = dscr("KF", [2, 6, 2, 8, 64, 8, 2, 128], BF16)

    with ExitStack() as st:
        S = Sched(nc, st)

        uid = [0]

        def sb(stack, name, shape, dt):
            uid[0] += 1
            return TB(stack.enter_context(nc.sbuf_tensor("sb%d_%s" % (uid[0], name), list(shape), dt)), name)

        PSA = st.enter_context(nc.psum_tensor("PSA", [128, 7 * 512], F32))
        PBt = st.enter_context(nc.psum_tensor("PB", [128, 1024], BF16))
        BK = [Res("bank%d" % i, excl=True) for i in range(7)]
        PBr = Res("pb", excl=True)

        def ps(b, lo=0, hi=512):
            return PSA[:, b * 512 + lo: b * 512 + hi]

        ident = sb(st, "ident", [128, 128], BF16)
        ones = sb(st, "ones", [128, 128], BF16)
        colv = sb(st, "colv", [128, NCV], F32)
        kT = sb(st, "kT", [128, 6, 2, 256], BF16)
        vv = sb(st, "vv", [128, 6, 2, 256], BF16)
        S.dma("pool", ident.t[:], IDENT, ident.r, writes=[ident.r])
        S.dma("sp", colv.t[:], COLV, colv.r, writes=[colv.r])
        S.op("dve", lambda: nc.vector.memset(ones.t[:], 1.0), writes=[ones.r])

        def mm(out, lhsT, rhs, start, stop, reads, writes):
            S.op("pe", lambda: nc.tensor.matmul(out, lhsT=lhsT, rhs=rhs, start=start, stop=stop),
                 reads=reads, writes=writes)

        def norm_T(ph, xt_ap, xt_res, gbase, out_ap, out_res, wk):
            junk, ss, xn = wk["junk"], wk["ss"], wk["xn"]
            S.op("act", lambda: nc.scalar.activation(out=junk.t[:, 0:D], in_=xt_ap, func=AF.Square,
                                                     accum_out=ss.t[:, 0:1]),
                 reads=[xt_res], writes=[junk.r, ss.r])
            S.op("act", lambda: nc.scalar.activation(out=ss.t[:, 1:2], in_=ss.t[:, 0:1], func=AF.Sqrt,
                                                     scale=1.0 / D, bias=wk["eps"].t[:, 0:1]),
                 reads=[ss.r, wk["eps"].r], writes=[ss.r])
            S.op("dve", lambda: nc.vector.reciprocal(out=ss.t[:, 2:3], in_=ss.t[:, 1:2]),
                 reads=[ss.r], writes=[ss.r])
            S.op("dve", lambda: nc.vector.tensor_scalar(out=xn.t[:], in0=xt_ap, scalar1=ss.t[:, 2:3],
                                                        scalar2=None, op0=ALU.mult),
                 reads=[xt_res, ss.r], writes=[xn.r])
            for j in range(8):
                S.op("pe", lambda j=j: nc.tensor.transpose(out=PBt[:, j * 128:(j + 1) * 128],
                                                           in_=xn.t[:, j * 128:(j + 1) * 128],
                                                           identity=ident.t[:]),
                     reads=[xn.r, ident.r], writes=[PBr])
            S.op("dve", lambda: nc.vector.tensor_tensor(
                out=out_ap, in0=PBt[:, 0:1024].rearrange("p (j t) -> p j t", j=8),
                in1=colv.t[:, gbase:gbase + 8].unsqueeze(2).to_broadcast([128, 8, 128]), op=ALU.mult),
                reads=[PBr, colv.r], writes=[out_res])

        def mk_normwk(ph, tag):
            wk = {"junk": sb(ph, "junk" + tag, [128, D], BF16),
                  "ss": sb(ph, "ss" + tag, [128, 4], F32),
                  "xn": sb(ph, "xn" + tag, [128, D], BF16),
                  "eps": sb(ph, "eps" + tag, [128, 2], F32)}
            S.op("dve", lambda: nc.vector.memset(wk["eps"].t[:, 0:1], NORM_EPS), writes=[wk["eps"].r])
            S.op("dve", lambda: nc.vector.memset(wk["eps"].t[:, 1:2], LN_EPS), writes=[wk["eps"].r])
            return wk

        def hnt_view(H):
            return H.rearrange("(j p) t -> p j t", p=128)

        def epilogue(ph, banks, xres_ap, xres_res, gpost_ap, gpost_res, tok0, XOUT, gnext, HOUT, wk, ek):
            b0 = banks[0]
            pout = PSA[:, b0 * 512: b0 * 512 + 1024]
            br = [BK[b] for b in banks]
            junk, ss = wk["junk"], ek["ss"]
            S.op("act", lambda: nc.scalar.activation(out=junk.t[:, 0:D], in_=pout, func=AF.Square,
                                                     accum_out=ss.t[:, 0:1]),
                 reads=br, writes=[junk.r, ss.r])
            S.op("act", lambda: nc.scalar.activation(out=ss.t[:, 1:2], in_=ss.t[:, 0:1], func=AF.Sqrt,
                                                     scale=1.0 / D, bias=wk["eps"].t[:, 0:1]),
                 reads=[ss.r, wk["eps"].r], writes=[ss.r])
            S.op("dve", lambda: nc.vector.reciprocal(out=ss.t[:, 2:3], in_=ss.t[:, 1:2]),
                 reads=[ss.r], writes=[ss.r])
            tp = ek["tp"]
            S.op("dve", lambda: nc.vector.tensor_tensor(out=tp.t[:], in0=pout, in1=gpost_ap, op=ALU.mult),
                 reads=br + [gpost_res], writes=[tp.r])
            xnew = ek["xnew"]
            S.op("dve", lambda: nc.vector.scalar_tensor_tensor(out=xnew.t[:], in0=tp.t[:], scalar=ss.t[:, 2:3],
                                                               in1=xres_ap, op0=ALU.mult, op1=ALU.add),
                 reads=[tp.r, ss.r, xres_res], writes=[xnew.r])
            S.dma("pool", XOUT[tok0:tok0 + 128, :], xnew.t[:], ek["xst"], reads=[xnew.r])
            if HOUT is not None:
                hno = ek["hno"]
                norm_T(ph, xnew.t[:], xnew.r, gnext, hno.t[:], hno.r, wk)
                S.dma("pool", hnt_view(HOUT)[:, :, tok0:tok0 + 128], hno.t[:], ek["hst"], reads=[hno.r])

        def mk_epi(ph, tag):
            return {"ss": sb(ph, "ess" + tag, [128, 4], F32),
                    "tp": sb(ph, "tp" + tag, [128, D], F32),
                    "xnew": sb(ph, "xnew" + tag, [128, D], F32),
                    "hno": sb(ph, "hno" + tag, [128, 8, 128], BF16),
                    "xst": Res("xst" + tag), "hst": Res("hst" + tag)}

        def attention(hn_ap, hn_res, wq_ap, wq_res, ls, cat, wk, banks):
            bq, bs0, bs1, bav0, bav1 = banks
            qT, pT, rden = wk["qT"], wk["pT"], wk["rden"]
            for hc in range(2):
                for j in range(8):
                    mm(ps(bq, hc * 128, hc * 128 + 128), wq_ap(j, hc), hn_ap(j), j == 0, j == 7,
                       [wq_res, hn_res], [BK[bq]])
            S.op("act", lambda: nc.scalar.activation(out=qT.t[:].rearrange("p a t -> p (a t)"), in_=ps(bq, 0, 256),
                                                     func=AF.Copy, scale=0.125),
                 reads=[BK[bq]], writes=[qT.r])
            for h in range(4):
                hc, po = h // 2, (h % 2) * 64
                for mc in range(2):
                    idx = (h % 2) * 4 + (h // 2) * 2 + mc
                    b = bs0 if idx < 4 else bs1
                    col = (idx % 4) * 128
                    mm(PSA[:, b * 512 + col: b * 512 + col + 128],
                       kT.t[po:po + 64, ls, hc, mc * 128:(mc + 1) * 128], qT.t[po:po + 64, hc, :], True, True,
                       [kT.r, qT.r], [BK[b]])
            for half, b in ((0, bs0), (1, bs1)):
                S.op("act", lambda half=half, b=b: nc.scalar.activation(
                    out=pT.t[:, half * 4:(half + 1) * 4, :].rearrange("p a t -> p (a t)"), in_=ps(b), func=AF.Exp),
                    reads=[BK[b]], writes=[pT.r])
            for hc in range(2):
                b = bav0 if hc == 0 else bav1
                for part in range(4):
                    h = 2 * hc + (part % 2)
                    for mc in range(2):
                        lhsT = vv.t[:, ls, mc, hc * 128:(hc + 1) * 128] if part < 2 else ones.t[:]
                        mm(ps(b, part * 128, part * 128 + 128), lhsT, pT.t[:, (h % 2) * 4 + (h // 2) * 2 + mc, :], mc == 0, mc == 1,
                           [vv.r, ones.r, pT.r], [BK[b]])
                for hh in range(2):
                    lo = hh * 64
                    S.op("dve", lambda hh=hh, lo=lo, b=b, hc=hc: nc.vector.reciprocal(
                        out=rden.t[lo:lo + 64, hc, :], in_=PSA[lo:lo + 64, b * 512 + (2 + hh) * 128: b * 512 + (3 + hh) * 128]),
                        reads=[BK[b]], writes=[rden.r])
                for hh in range(2):
                    lo = hh * 64
                    S.op("dve", lambda hh=hh, lo=lo, b=b, hc=hc: nc.vector.tensor_tensor(
                        out=cat.t[lo:lo + 64, 6 + hc, :], in0=PSA[lo:lo + 64, b * 512 + hh * 128: b * 512 + (hh + 1) * 128],
                        in1=rden.t[lo:lo + 64, hc, :], op=ALU.mult),
                        reads=[BK[b], rden.r], writes=[cat.r])

        def mk_attwk(ph, tag):
            return {"qT": sb(ph, "qT" + tag, [128, 2, 128], BF16),
                    "pT": sb(ph, "pT" + tag, [128, 8, 128], BF16),
                    "rden": sb(ph, "rden" + tag, [128, 2, 128], F32)}

        def seq_of(tok):
            for si, (t0, L) in enumerate(SEQS):
                if t0 <= tok < t0 + L:
                    return si, t0, L
            raise ValueError

        def load_hnt_tile(HSRC, hb, a, T):
            si, t0, L = seq_of(a)
            lo = a - 1 if a > t0 else a
            hi = a + T + 1 if a + T < t0 + L else a + T
            if lo == a:
                S.op("dve", lambda: nc.vector.memset(hb.t[:, :, 0:1], 0.0), writes=[hb.r])
            if hi == a + T:
                S.op("dve", lambda: nc.vector.memset(hb.t[:, :, T + 1:T + 2], 0.0), writes=[hb.r])
            S.dma("sp", hb.t[:, :, lo - (a - 1): hi - (a - 1)], hnt_view(HSRC)[:, :, lo:hi], hb.r, writes=[hb.r])

        def halo_all(hb, T, w_ap, w_res, nchk, bank, hal):
            for ci in range(nchk):
                for j in range(8):
                    mm(ps(bank, 2 * ci, 2 * ci + 2), w_ap(ci, j), hb.t[:, j, 0:T + 2:T + 1], j == 0, j == 7,
                       [w_res, hb.r], [BK[bank]])
            S.op("act", lambda: nc.scalar.copy(out=hal.t[:, 0:2 * nchk], in_=ps(bank, 0, 2 * nchk)),
                 reads=[BK[bank]], writes=[hal.r])

        def conv_chunk(hb, T, w_ap, w_res, cv, nchk, ci, bmain, hal, tbuf):
            for j in range(8):
                mm(ps(bmain, 0, T), w_ap(j), hb.t[:, j, 1:T + 1], j == 0, j == 7, [w_res, hb.r], [BK[bmain]])
            c0, c1, c2, cb = (colv.t[:, cv + k * nchk + ci: cv + k * nchk + ci + 1] for k in range(4))
            S.op("act", lambda: nc.scalar.activation(out=tbuf.t[:, 0:T], in_=ps(bmain, 0, T), func=AF.Identity,
                                                     scale=c1, bias=cb),
                 reads=[BK[bmain], colv.r], writes=[tbuf.r])
            S.op("dve", lambda: nc.vector.scalar_tensor_tensor(out=tbuf.t[:, 1:T], in0=ps(bmain, 0, T - 1), scalar=c0,
                                                               in1=tbuf.t[:, 1:T], op0=ALU.mult, op1=ALU.add),
                 reads=[BK[bmain], tbuf.r, colv.r], writes=[tbuf.r])
            S.op("dve", lambda: nc.vector.scalar_tensor_tensor(out=tbuf.t[:, 0:T - 1], in0=ps(bmain, 1, T), scalar=c2,
                                                               in1=tbuf.t[:, 0:T - 1], op0=ALU.mult, op1=ALU.add),
                 reads=[BK[bmain], tbuf.r, colv.r], writes=[tbuf.r])
            S.op("dve", lambda: nc.vector.scalar_tensor_tensor(out=tbuf.t[:, 0:1], in0=hal.t[:, 2 * ci:2 * ci + 1], scalar=c0,
                                                               in1=tbuf.t[:, 0:1], op0=ALU.mult, op1=ALU.add),
                 reads=[hal.r, tbuf.r, colv.r], writes=[tbuf.r])
            S.op("dve", lambda: nc.vector.scalar_tensor_tensor(out=tbuf.t[:, T - 1:T], in0=hal.t[:, 2 * ci + 1:2 * ci + 2],
                                                               scalar=c2, in1=tbuf.t[:, T - 1:T], op0=ALU.mult, op1=ALU.add),
                 reads=[hal.r, tbuf.r, colv.r], writes=[tbuf.r])

        if "kv" in stages:
            with ExitStack() as ph:
                wkv = sb(ph, "wkv", [128, 2, 8, 512], BF16)
                for l in range(2):
                    S.dma("pool", wkv.t[:, l], W_KV[l].rearrange("(j p) n -> p j n", p=128), wkv.r, writes=[wkv.r])
                wk = mk_normwk(ph, "kv")
                xin = [sb(ph, "kvx%d" % i, [128, D], F32) for i in range(2)]
                hn = [sb(ph, "kvh%d" % i, [128, 8, 128], BF16) for i in range(2)]
                it = 0
                for l in range(2):
                    for s in range(3):
                        for mt in range(2):
                            xb, hb = xin[it % 2], hn[it % 2]
                            it += 1
                            r0 = s * 256 + mt * 128
                            S.dma("sp", xb.t[:], MEM[r0:r0 + 128, :], xb.r, writes=[xb.r])
                            norm_T(ph, xb.t[:], xb.r, CV_MEM[l], hb.t[:], hb.r, wk)
                            ls = l * 3 + s
                            for hc in range(2):
                                b = hc
                                for j in range(8):
                                    mm(ps(b, 0, 128), wkv.t[:, l, j, hc * 128:(hc + 1) * 128], hb.t[:, j, :], j == 0, j == 7,
                                       [wkv.r, hb.r], [BK[b]])
                                S.op("act", lambda b=b, hc=hc, ls=ls, mt=mt: nc.scalar.copy(
                                    out=kT.t[:, ls, hc, mt * 128:(mt + 1) * 128], in_=ps(b, 0, 128)),
                                    reads=[BK[b]], writes=[kT.r])
                            for j in range(8):
                                mm(ps(2, 0, 256), hb.t[:, j, :], wkv.t[:, l, j, 256:512], j == 0, j == 7,
                                   [wkv.r, hb.r], [BK[2]])
                            S.op("act", lambda ls=ls, mt=mt: nc.scalar.copy(out=vv.t[:, ls, mt, :], in_=ps(2, 0, 256)),
                                 reads=[BK[2]], writes=[vv.r])
                S.barrier()

        if "A" in stages:
            with ExitStack() as ph:
                win = sb(ph, "a_win", [128, 8, 1792], BF16)
                for j in range(8):
                    S.dma("pool", win.t[:, j, :], A_W_IN[j * 128:(j + 1) * 128, :], win.r, writes=[win.r])
                wout = sb(ph, "a_wout", [128, 8, D], BF16)
                S.dma("pool", wout.t[:], W_OUT[0].rearrange("(j p) n -> p j n", p=128), wout.r, writes=[wout.r])
                wst = sb(ph, "a_wst", [128, 12, 128], BF16)
                S.dma("pool", wst.t[:], WST, wst.r, writes=[wst.r])
                bst = sb(ph, "a_bst", [128, 12], F32)
                S.dma("sp", bst.t[:], BST, bst.r, writes=[bst.r])
                gpost = sb(ph, "a_gpost", [128, D], F32)
                S.dma("sp", gpost.t[:], ROWV[:, 0:D].partition_broadcast(128), gpost.r, writes=[gpost.r])
                lng = sb(ph, "a_lng", [128, MW], F32)
                lnb = sb(ph, "a_lnb", [128, MW], F32)
                S.dma("sp", lng.t[:], ROWV[:, 4 * D:4 * D + MW].partition_broadcast(128), lng.r, writes=[lng.r])
                S.dma("sp", lnb.t[:], ROWV[:, 4 * D + MW:4 * D + 2 * MW].partition_broadcast(128), lnb.r, writes=[lnb.r])
                wk = mk_normwk(ph, "a")
                ek = mk_epi(ph, "a")
                awk = mk_attwk(ph, "a")
                xin = [sb(ph, "ax%d" % i, [128, D], F32) for i in range(3)]
                hn = [sb(ph, "ah%d" % i, [128, 8, 128], BF16) for i in range(2)]
                gu = sb(ph, "a_gu", [128, MW], F32)
                gv = sb(ph, "a_gv", [128, MW], F32)
                vn = sb(ph, "a_vn", [128, MW], BF16)
                tmp = sb(ph, "a_tmp", [128, MW], F32)
                mix = sb(ph, "a_mix", [128, MW], BF16)
                st6 = sb(ph, "a_st", [128, 2, 6], F32)
                mv = sb(ph, "a_mv", [128, 4], F32)
                cat = [sb(ph, "a_cat%d" % i, [128, 8, 128], BF16) for i in range(2)]
                nch = LIMIT or NT // 128

                def load_x(c):
                    xb = xin[c % 3]
                    S.dma("sp", xb.t[:], X0[c * 128:(c + 1) * 128, :], xb.r, writes=[xb.r])

                load_x(0)
                load_x(1)
                for c in range(nch):
                    if c + 2 < nch:
                        load_x(c + 2)
                    xb, hb, cb = xin[c % 3], hn[c % 2], cat[c % 2]
                    si, _, _ = seq_of(c * 128)
                    norm_T(ph, xb.t[:], xb.r, CV_MIXPRE[0], hb.t[:], hb.r, wk)
                    if STOP == 1:
                        break
                    for nb in range(3):
                        for j in range(8):
                            mm(ps(nb), hb.t[:, j, :], win.t[:, j, nb * 512:(nb + 1) * 512], j == 0, j == 7,
                               [hb.r, win.r], [BK[nb]])
                    S.op("act", lambda: nc.scalar.activation(out=gv.t[:], in_=PSA[:, MW:2 * MW], func=GELU),
                         reads=[BK[1], BK[2]], writes=[gv.r])
                    S.op("act", lambda: nc.scalar.activation(out=gu.t[:], in_=PSA[:, 0:MW], func=GELU),
                         reads=[BK[0], BK[1]], writes=[gu.r])
                    if STOP == 2:
                        break
                    for k in range(2):
                        S.op("dve", lambda k=k: nc.vector.bn_stats(out=st6.t[:, k, :], in_=gv.t[:, k * 384:(k + 1) * 384]),
                             reads=[gv.r], writes=[st6.r])
                    S.op("dve", lambda: nc.vector.bn_aggr(out=mv.t[:, 0:2], in_=st6.t[:]), reads=[st6.r], writes=[mv.r])
                    S.op("act", lambda: nc.scalar.activation(out=mv.t[:, 2:3], in_=mv.t[:, 1:2], func=AF.Sqrt,
                                                             scale=1.0, bias=wk["eps"].t[:, 1:2]),
                         reads=[mv.r, wk["eps"].r], writes=[mv.r])
                    S.op("dve", lambda: nc.vector.reciprocal(out=mv.t[:, 3:4], in_=mv.t[:, 2:3]), reads=[mv.r], writes=[mv.r])
                    S.op("dve", lambda: nc.vector.tensor_scalar(out=gv.t[:], in0=gv.t[:], scalar1=mv.t[:, 0:1],
                                                                scalar2=mv.t[:, 3:4], op0=ALU.subtract, op1=ALU.mult),
                         reads=[gv.r, mv.r], writes=[gv.r])
                    S.op("pool", lambda: nc.gpsimd.tensor_tensor(out=gv.t[:], in0=gv.t[:], in1=lng.t[:], op=ALU.mult),
                         reads=[gv.r, lng.r], writes=[gv.r])
                    S.op("pool", lambda: nc.gpsimd.tensor_tensor(out=vn.t[:], in0=gv.t[:], in1=lnb.t[:], op=ALU.add),
                         reads=[gv.r, lnb.r], writes=[vn.r])
                    if STOP == 3:
                        break
                    for g in range(12):
                        col = 3 * 512 + g * 64
                        mm(PSA[:, col:col + 64], wst.t[:, g, :], vn.t[:, g * 64:(g + 1) * 64], True, True,
                           [wst.r, vn.r], [BK[3 + (g // 8)]])
                    S.op("dve", lambda: nc.vector.tensor_tensor(
                        out=tmp.t[:].rearrange("p (g d) -> p g d", g=12),
                        in0=PSA[:, 3 * 512:3 * 512 + MW].rearrange("p (g d) -> p g d", g=12),
                        in1=bst.t[:].unsqueeze(2).to_broadcast([128, 12, 64]), op=ALU.add),
                        reads=[BK[3], BK[4], bst.r], writes=[tmp.r])
                    S.op("pool", lambda: nc.gpsimd.tensor_tensor(out=mix.t[:], in0=tmp.t[:], in1=gu.t[:], op=ALU.mult),
                         reads=[tmp.r, gu.r], writes=[mix.r])
                    if STOP == 4:
                        break
                    for k in range(6):
                        S.op("pe", lambda k=k: nc.tensor.transpose(out=PBt[:, k * 128:(k + 1) * 128],
                                                                   in_=mix.t[:, k * 128:(k + 1) * 128], identity=ident.t[:]),
                             reads=[mix.r, ident.r], writes=[PBr])
                    S.op("act", lambda: nc.scalar.copy(out=cb.t[:, 0:6, :].rearrange("p a t -> p (a t)"), in_=PBt[:, 0:768]),
                         reads=[PBr], writes=[cb.r])
                    if STOP == 5:
                        break
                    attention(lambda j: hb.t[:, j, :], hb.r,
                              lambda j, hc: win.t[:, j, 1536 + hc * 128:1536 + (hc + 1) * 128], win.r,
                              0 * 3 + si, cb, awk, (5, 0, 1, 5, 6))
                    if STOP == 6:
                        break
                    for nb in range(2):
                        for k in range(8):
                            mm(ps(3 + nb), cb.t[:, k, :], wout.t[:, k, nb * 512:(nb + 1) * 512], k == 0, k == 7,
                               [cb.r, wout.r], [BK[3 + nb]])
                    if STOP == 7:
                        break
                    epilogue(ph, (3, 4), xb.t[:], xb.r, gpost.t[:], gpost.r, c * 128, X1, CV_FFNPRE[0], HNT1, wk, ek)
                if "KTD" in dbg:
                    KTD = dscr("KTD", [128, 6 * 2 * 256], BF16)
                    VVD = dscr("VVD", [128, 6 * 2 * 256], BF16)
                    S.dma("sp", KTD, kT.t[:].rearrange("p a b c -> p (a b c)"), Res("ktd"), reads=[kT.r])
                    S.dma("sp", VVD, vv.t[:].rearrange("p a b c -> p (a b c)"), Res("vvd"), reads=[vv.r])
                S.barrier()

        def ffn_phase(layer, XIN, HIN, XOUT, gnext, HOUT):
            T = 256
            with ExitStack() as ph:
                wup = sb(ph, "wup", [128, 8, 2 * DFF], BF16)
                for j in range(8):
                    S.dma("pool", wup.t[:, j, :], W_UP[layer, j * 128:(j + 1) * 128, :], wup.r, writes=[wup.r])
                wdn = sb(ph, "wdn", [128, NCH_FF, D], BF16)
                for ci in range(NCH_FF):
                    S.dma("pool", wdn.t[:, ci, :], W_DN[layer, ci * 128:(ci + 1) * 128, :], wdn.r, writes=[wdn.r])
                gpost = sb(ph, "f_gpost", [128, D], F32)
                S.dma("sp", gpost.t[:], ROWV[:, (2 + layer) * D:(3 + layer) * D].partition_broadcast(128), gpost.r,
                      writes=[gpost.r])
                wk = mk_normwk(ph, "f")
                ek = mk_epi(ph, "f")
                hb = sb(ph, "f_hb", [128, 8, T + 2], BF16)
                xin = [sb(ph, "fx%d" % i, [128, D], F32) for i in range(2)]
                tb = [sb(ph, "ft%d" % i, [128, T], F32) for i in range(2)]
                hact = sb(ph, "hact", [128, NCH_FF, T], BF16)
                hal = sb(ph, "f_hal", [128, 2 * NCH_FF], F32)
                cv = CV_FFC[layer]
                ntile = LIMIT or NT // T
                xcnt = 0
                for ti in range(ntile):
                    a = ti * T
                    load_hnt_tile(HIN, hb, a, T)
                    halo_all(hb, T, lambda ci, j: wup.t[:, j, ci * 128:(ci + 1) * 128], wup.r, NCH_FF, 4, hal)
                    for ci in range(NCH_FF):
                        tbuf = tb[ci % 2]
                        bm, bu = ci % 2, 2 + ci % 2
                        conv_chunk(hb, T, lambda j: wup.t[:, j, ci * 128:(ci + 1) * 128], wup.r, cv, NCH_FF, ci,
                                   bm, hal, tbuf)
                        S.op("act", lambda tbuf=tbuf: nc.scalar.activation(out=tbuf.t[:, 0:T], in_=tbuf.t[:, 0:T], func=GELU),
                             reads=[tbuf.r], writes=[tbuf.r])
                        for j in range(8):
                            mm(ps(bu, 0, T), wup.t[:, j, DFF + ci * 128:DFF + (ci + 1) * 128], hb.t[:, j, 1:T + 1],
                               j == 0, j == 7, [wup.r, hb.r], [BK[bu]])
                        S.op("dve", lambda tbuf=tbuf, bu=bu, ci=ci: nc.vector.tensor_tensor(
                            out=hact.t[:, ci, :], in0=ps(bu, 0, T), in1=tbuf.t[:, 0:T], op=ALU.mult),
                            reads=[BK[bu], tbuf.r], writes=[hact.r])
                    for tt in range(T // 128):
                        xb = xin[xcnt % 2]
                        xcnt += 1
                        tok0 = a + tt * 128
                        S.dma("sp", xb.t[:], XIN[tok0:tok0 + 128, :], xb.r, writes=[xb.r])
                        for nb in range(2):
                            for ci in range(NCH_FF):
                                mm(ps(5 + nb), hact.t[:, ci, tt * 128:(tt + 1) * 128], wdn.t[:, ci, nb * 512:(nb + 1) * 512],
                                   ci == 0, ci == NCH_FF - 1, [hact.r, wdn.r], [BK[5 + nb]])
                        epilogue(ph, (5, 6), xb.t[:], xb.r, gpost.t[:], gpost.r, tok0, XOUT, gnext, HOUT, wk, ek)
                S.barrier()

        if "B0" in stages:
            ffn_phase(0, X1, HNT1, X2, CV_MIXPRE[1], HNT2)

        if "C" in stages:
            T = 512
            with ExitStack() as ph:
                win = sb(ph, "b_win", [128, 8, 2560], BF16)
                for j in range(8):
                    S.dma("pool", win.t[:, j, :], B_W_IN[j * 128:(j + 1) * 128, :], win.r, writes=[win.r])
                awk = mk_attwk(ph, "c")
                hbs = [sb(ph, "c_hb%d" % i, [128, 8, T + 2], BF16) for i in range(2)]
                tb = [sb(ph, "c_t%d" % i, [128, T], F32) for i in range(2)]
                ob = [sb(ph, "c_o%d" % i, [128, T], BF16) for i in range(3)]
                hal = sb(ph, "c_hal", [128, 36], F32)
                cat = [sb(ph, "c_cat%d" % i, [128, 8, 128], BF16) for i in range(2)]
                cst = [Res("cst0"), Res("cst1")]
                ntile = LIMIT or NT // T
                oc = 0
                cc = 0
                for ti in range(ntile):
                    a = ti * T
                    hb = hbs[ti % 2]
                    si, _, _ = seq_of(a)
                    load_hnt_tile(HNT2, hb, a, T)
                    halo_all(hb, T, lambda ci, j: win.t[:, j, ci * 128:(ci + 1) * 128], win.r, 18, 4, hal)
                    for ci in range(18):
                        tbuf = tb[ci % 2]
                        conv_chunk(hb, T, lambda j: win.t[:, j, ci * 128:(ci + 1) * 128], win.r, CV_SC, 18, ci,
                                   ci % 2, hal, tbuf)
                        o = ob[oc % 3]
                        oc += 1
                        S.op("pool", lambda o=o, tbuf=tbuf: nc.gpsimd.tensor_copy(out=o.t[:], in_=tbuf.t[:]),
                             reads=[tbuf.r], writes=[o.r])
                        S.dma("pool", PT[ci * 128:(ci + 1) * 128, a:a + T], o.t[:], o.r, reads=[o.r])
                    for tt in range(T // 128):
                        cb = cat[cc % 2]
                        cc += 1
                        attention(lambda j: hb.t[:, j, 1 + tt * 128:1 + (tt + 1) * 128], hb.r,
                                  lambda j, hc: win.t[:, j, 2304 + hc * 128:2304 + (hc + 1) * 128], win.r,
                                  3 + si, cb, awk, (5, 2, 3, 5, 6))
                        S.dma("pool", ATT.rearrange("(k p) t -> p k t", p=128)[:, :, a + tt * 128:a + (tt + 1) * 128],
                              cb.t[:, 6:8, :], cst[cc % 2], reads=[cb.r])
                S.barrier()


        UNITS = [dict(tok0=0, L=2048, N2=32, nb=2), dict(tok0=4096, L=4096, N2=64, nb=1)]
        NG = 6

        def load_unit_tables(ph):
            f1 = sb(ph, "h_f1", [64, 128], BF16)
            S.dma("pool", f1.t[:], F1T, f1.r, writes=[f1.r])
            return f1

        if "FF" in stages:
            with ExitStack() as ph:
                f1 = load_unit_tables(ph)
                w3sd = sb(ph, "h_w3sd", [64, NG, 2, 2, 128], BF16)
                fvec = sb(ph, "h_fvec", [64, 8], F32)
                w1 = sb(ph, "h_w1", [33, 64], F32)
                w2 = sb(ph, "h_w2", [64, 64], F32)
                dl = sb(ph, "h_dl", [64, MW], F32)
                fb = sb(ph, "h_fb", [1, 2 * MW], F32)
                S.dma("sp", fvec.t[:, 0:3], FVEC, fvec.r, writes=[fvec.r])
                S.dma("sp", w1.t[:], FW1, w1.r, writes=[w1.r])
                S.dma("sp", w2.t[:], FW2, w2.r, writes=[w2.r])
                S.dma("sp", dl.t[:], ROWV[:, 4 * D + 2 * MW:4 * D + 3 * MW].partition_broadcast(64), dl.r, writes=[dl.r])
                S.dma("sp", fb.t[:], FBIAS, fb.r, writes=[fb.r])
                S.op("dve", lambda: nc.vector.tensor_tensor(out=fvec.t[:, 3:4], in0=fvec.t[:, 0:1], in1=fvec.t[:, 1:2], op=ALU.mult),
                     reads=[fvec.r], writes=[fvec.r])
                S.op("dve", lambda: nc.vector.tensor_tensor(out=fvec.t[:, 4:5], in0=fvec.t[:, 0:1], in1=fvec.t[:, 2:3], op=ALU.mult),
                     reads=[fvec.r], writes=[fvec.r])
                with ExitStack() as tmp:
                    w3r = sb(tmp, "h_w3r", [64, NG, 2, 2, 128], F32)
                    S.dma("sp", w3r.t[:].rearrange("p g o d c -> p (g o d c)"), FW3, w3r.r, writes=[w3r.r])
                    S.op("dve", lambda: nc.vector.tensor_tensor(out=w3sd.t[:, :, :, 0, :], in0=w3r.t[:, :, :, 0, :],
                                                                in1=w3r.t[:, :, :, 1, :], op=ALU.add),
                         reads=[w3r.r], writes=[w3sd.r])
                    S.op("dve", lambda: nc.vector.tensor_tensor(out=w3sd.t[:, :, :, 1, :], in0=w3r.t[:, :, :, 0, :],
                                                                in1=w3r.t[:, :, :, 1, :], op=ALU.subtract),
                         reads=[w3r.r], writes=[w3sd.r])
                    S.barrier()
                h2T = sb(ph, "h_h2T", [64, 4096], BF16)
                htok = sb(ph, "h_htok", [64, 2, 128, 64], BF16)
                Abuf = sb(ph, "h_A", [64, 64, 2, 128], BF16)
                kst = [sb(ph, "h_kst%d" % i, [64, 8, 2, 128], BF16) for i in range(2)]
                gtab = sb(ph, "h_gtab", [64, 8, 8, 3, 64], BF16)
                dec = [sb(ph, "h_dec%d" % i, [64, 128], F32) for i in range(2)]
                su = sb(ph, "h_su", [64, 64], F32)
                mt_ = [sb(ph, "h_mt%d" % i, [64, 512], F32) for i in range(3)]
                h1c = sb(ph, "h_h1c", [64, 512], F32)
                ft = [sb(ph, "h_ft%d" % i, [33, 512], F32) for i in range(2)]
                t0f = sb(ph, "h_t0f", [1, 256], F32)
                MAGIC = 12582912.0
                TWO_PI = 2.0 * math.pi

                def sin_layer(psum_ap, psum_res, fcol, out_ap, out_res):
                    a, u, k = mt_
                    S.op("dve", lambda: nc.vector.tensor_scalar(out=a.t[:], in0=psum_ap, scalar1=fvec.t[:, 0:1],
                                                                scalar2=fvec.t[:, fcol:fcol + 1], op0=ALU.mult, op1=ALU.add),
                         reads=[psum_res, fvec.r], writes=[a.r])
                    S.op("dve", lambda: nc.vector.tensor_scalar(out=u.t[:], in0=a.t[:], scalar1=1.0 / TWO_PI, scalar2=MAGIC,
                                                                op0=ALU.mult, op1=ALU.add), reads=[a.r], writes=[u.r])
                    S.op("dve", lambda: nc.vector.tensor_scalar(out=k.t[:], in0=u.t[:], scalar1=MAGIC, scalar2=-TWO_PI,
                                                                op0=ALU.subtract, op1=ALU.mult), reads=[u.r], writes=[k.r])
                    S.op("dve", lambda: nc.vector.tensor_tensor(out=a.t[:], in0=a.t[:], in1=k.t[:], op=ALU.add),
                         reads=[a.r, k.r], writes=[a.r])
                    S.op("dve", lambda: nc.vector.tensor_scalar(out=a.t[:], in0=a.t[:], scalar1=3.1415925, scalar2=-3.1415925,
                                                                op0=ALU.min, op1=ALU.max), reads=[a.r], writes=[a.r])
                    S.op("act", lambda: nc.scalar.activation(out=out_ap, in_=a.t[:], func=AF.Sin), reads=[a.r], writes=[out_res])

                for ui, U in enumerate(UNITS):
                    L, N2, nbt = U["L"], U["N2"], U["nb"]
                    S.dma("sp", su.t[:], SU[ui], su.r, writes=[su.r])
                    for kb in range(8):
                        S.dma("pool", gtab.t[:, kb], GT[ui, kb], gtab.r, writes=[gtab.r])
                    for cc in range(L // 512):
                        fbuf = ft[cc % 2]
                        S.dma("sp", fbuf.t[:], FEAT[ui][:, cc * 512:(cc + 1) * 512], fbuf.r, writes=[fbuf.r])
                        mm(PSA[0:64, 0:512], w1.t[:], fbuf.t[:], True, True, [w1.r, fbuf.r], [BK[0]])
                        sin_layer(PSA[0:64, 0:512], BK[0], 3, h1c.t[:], h1c.r)
                        mm(PSA[0:64, 512:1024], w2.t[:], h1c.t[:], True, True, [w2.r, h1c.r], [BK[1]])
                        sin_layer(PSA[0:64, 512:1024], BK[1], 4, h2T.t[:, cc * 512:(cc + 1) * 512], h2T.r)
                    for g in range(NG):
                        for o in range(2):
                            for n2 in range(N2):
                                b = 2 + n2 % 2
                                d_ = dec[n2 % 2]
                                mm(PSA[0:64, b * 512:b * 512 + 256], h2T.t[:, n2:L:N2],
                                   w3sd.t[:, g, o].rearrange("p a c -> p (a c)"), True, True, [h2T.r, w3sd.r], [BK[b]])
                                S.op("act", lambda d_=d_, n2=n2: nc.scalar.activation(
                                    out=d_.t[:], in_=dl.t[:, g * 128:(g + 1) * 128], func=AF.Exp, scale=su.t[:, n2:n2 + 1]),
                                    reads=[dl.r, su.r], writes=[d_.r])
                                for bb in range(nbt):
                                    i = bb * N2 + n2
                                    S.op("dve", lambda b=b, d_=d_, i=i: nc.vector.tensor_tensor(
                                        out=htok.t[:, :, :, i],
                                        in0=PSA[0:64, b * 512:b * 512 + 256].rearrange("p (a c) -> p a c", a=2),
                                        in1=d_.t[:].unsqueeze(1).to_broadcast([64, 2, 128]), op=ALU.mult),
                                        reads=[BK[b], d_.r], writes=[htok.r])
                            for bb in range(nbt):
                                i0 = bb * N2
                                S.op("dve", lambda i0=i0: nc.vector.tensor_tensor(
                                    out=t0f.t[0:1, 0:128], in0=htok.t[0:1, 0, :, i0], in1=htok.t[0:1, 1, :, i0], op=ALU.add),
                                    reads=[htok.r], writes=[t0f.r])
                                S.op("dve", lambda: nc.vector.scalar_tensor_tensor(
                                    out=t0f.t[0:1, 128:256], in0=t0f.t[0:1, 0:128], scalar=0.5,
                                    in1=fb.t[0:1, o * MW + g * 128:o * MW + (g + 1) * 128], op0=ALU.mult, op1=ALU.add),
                                    reads=[t0f.r, fb.r], writes=[t0f.r])
                                for a_ in range(2):
                                    S.op("dve", lambda a_=a_, i0=i0: nc.vector.tensor_copy(
                                        out=htok.t[0:1, a_, :, i0], in_=t0f.t[0:1, 128:256]),
                                        reads=[t0f.r], writes=[htok.r])
                            for a_ in range(2):
                                for c0 in range(0, 128, 4):
                                    b = (c0 // 4) % 2
                                    for c in range(c0, c0 + 4):
                                        mm(PSA[0:64, b * 512 + (c - c0) * 128:b * 512 + (c - c0 + 1) * 128],
                                           htok.t[:, a_, c, :], f1.t[:], True, True, [htok.r, f1.r], [BK[b]])
                                    S.op("act", lambda b=b, c0=c0: nc.scalar.copy(
                                        out=Abuf.t[:, :, :, c0:c0 + 4].rearrange("p k r c -> p (k r) c"),
                                        in_=PSA[0:64, b * 512:(b + 1) * 512].rearrange("p (c x) -> p x c", c=4)),
                                        reads=[BK[b]], writes=[Abuf.r])
                                for kb in range(8):
                                    gt = gtab
                                    ks = kst[kb % 2]
                                    for kp in range(4):
                                        b = 4 + kp % 2
                                        for kk in range(2):
                                            kl = kp * 2 + kk
                                            k1 = kb * 8 + kl
                                            v0, v1 = (0, 2) if a_ == 0 else (1, 0)
                                            out = PSA[0:64, b * 512 + kk * 128:b * 512 + (kk + 1) * 128]
                                            mm(out, gt.t[:, kb, kl, v0, :], Abuf.t[:, k1, 0, :], True, False, [gt.r, Abuf.r], [BK[b]])
                                            mm(out, gt.t[:, kb, kl, v1, :], Abuf.t[:, k1, 1, :], False, True, [gt.r, Abuf.r], [BK[b]])
                                        S.op("act", lambda b=b, ks=ks, kp=kp, a_=a_: nc.scalar.copy(
                                            out=ks.t[:, kp * 2:kp * 2 + 2, a_, :],
                                            in_=PSA[0:64, b * 512:b * 512 + 256].rearrange("p (k c) -> p k c", k=2)),
                                            reads=[BK[b]], writes=[ks.r])
                                    S.dma("sp", KF[ui, g, o, kb, :, :, a_, :], ks.t[:, :, a_, :], ks.r, reads=[ks.r])
                S.barrier()

        if "FD" in stages:
            with ExitStack() as ph:
                f1 = load_unit_tables(ph)
                dtb = sb(ph, "h_dtb", [64, 3, 64], BF16)
                vtok = sb(ph, "h_vtok", [64, 128, 64], BF16)
                x1tok = sb(ph, "h_x1tok", [64, 128, 64], BF16)
                AC = sb(ph, "h_AC", [64, 64, 2, 128], BF16)
                Yb = sb(ph, "h_Y", [64, 2, 128, 64], BF16)
                Xs = [sb(ph, "h_xs%d" % i, [64, 8, 2, 128], BF16) for i in range(2)]
                kfs = [sb(ph, "h_kf%d" % i, [64, 8, 2, 128], BF16) for i in range(2)]
                t1 = sb(ph, "h_t1", [64, 8, 128], F32)
                t2 = sb(ph, "h_t2", [64, 8, 128], F32)
                gtab = sb(ph, "h_gtabd", [64, 8, 8, 3, 64], BF16)
                ets = [sb(ph, "h_et%d" % i, [64, 8, 2, 64], BF16) for i in range(2)]
                x2s = sb(ph, "h_x2s", [128, 4096], BF16)
                mixs = sb(ph, "h_mixs", [128, 4096], BF16)
                units = UNITS if LIMIT is None else UNITS[:LIMIT]
                for ui, U in enumerate(units):
                    L, N2, nbt, tok0 = U["L"], U["N2"], U["nb"], U["tok0"]
                    S.dma("pool", dtb.t[:], DTB[ui], dtb.r, writes=[dtb.r])
                    for kb in range(8):
                        S.dma("pool", gtab.t[:, kb], GT[ui, kb], gtab.r, writes=[gtab.r])
                    for g in range(NG if STOP is None else STOP):
                        for bb in range(nbt):
                            lo = tok0 + bb * L
                            S.dma("sp", vtok.t[:, :, bb * N2:(bb + 1) * N2],
                                  PT[g * 128:(g + 1) * 128, lo:lo + L].rearrange("c (n1 n2) -> n1 c n2", n2=N2),
                                  vtok.r, writes=[vtok.r])
                            S.dma("sp", x1tok.t[:, :, bb * N2:(bb + 1) * N2],
                                  PT[MW + g * 128:MW + (g + 1) * 128, lo:lo + L].rearrange("c (n1 n2) -> n1 c n2", n2=N2),
                                  x1tok.r, writes=[x1tok.r])
                        S.dma("sp", x2s.t[:], PT[2 * MW + g * 128:2 * MW + (g + 1) * 128, tok0:tok0 + 4096], x2s.r,
                              writes=[x2s.r])
                        for o in range(2):
                            for c0 in range(0, 128, 4):
                                b = (c0 // 4) % 2
                                for c in range(c0, c0 + 4):
                                    mm(PSA[0:64, b * 512 + (c - c0) * 128:b * 512 + (c - c0 + 1) * 128],
                                       vtok.t[:, c, :], f1.t[:], True, True, [vtok.r, f1.r], [BK[b]])
                                S.op("act", lambda b=b, c0=c0: nc.scalar.copy(
                                    out=AC.t[:, :, :, c0:c0 + 4].rearrange("p k r c -> p (k r) c"),
                                    in_=PSA[0:64, b * 512:(b + 1) * 512].rearrange("p (c x) -> p x c", c=4)),
                                    reads=[BK[b]], writes=[AC.r])
                            for kb in range(8):
                                gt, kf, xs = gtab, kfs[kb % 2], Xs[kb % 2]
                                S.dma("sp", kf.t[:], KF[ui, g, o, kb], kf.r, writes=[kf.r])
                                for kp in range(4):
                                    b = 2 + kp % 2
                                    for kk in range(2):
                                        kl = kp * 2 + kk
                                        k1 = kb * 8 + kl
                                        ore = PSA[0:64, b * 512 + kk * 256:b * 512 + kk * 256 + 128]
                                        oim = PSA[0:64, b * 512 + kk * 256 + 128:b * 512 + kk * 256 + 256]
                                        mm(ore, gt.t[:, kb, kl, 0, :], AC.t[:, k1, 0, :], True, False, [gt.r, AC.r], [BK[b]])
                                        mm(ore, gt.t[:, kb, kl, 2, :], AC.t[:, k1, 1, :], False, True, [gt.r, AC.r], [BK[b]])
                                        mm(oim, gt.t[:, kb, kl, 1, :], AC.t[:, k1, 0, :], True, False, [gt.r, AC.r], [BK[b]])
                                        mm(oim, gt.t[:, kb, kl, 0, :], AC.t[:, k1, 1, :], False, True, [gt.r, AC.r], [BK[b]])
                                    S.op("act", lambda b=b, xs=xs, kp=kp: nc.scalar.copy(
                                        out=xs.t[:, kp * 2:kp * 2 + 2, :, :].rearrange("p k r c -> p (k r c)"),
                                        in_=PSA[0:64, b * 512:(b + 1) * 512]),
                                        reads=[BK[b]], writes=[xs.r])
                                xre, xim = xs.t[:, :, 0, :], xs.t[:, :, 1, :]
                                kre, kim = kf.t[:, :, 0, :], kf.t[:, :, 1, :]
                                yre = Yb.t[:, 0, :, kb * 8:(kb + 1) * 8].rearrange("p c k -> p k c")
                                yim = Yb.t[:, 1, :, kb * 8:(kb + 1) * 8].rearrange("p c k -> p k c")
                                S.op("dve", lambda: nc.vector.tensor_tensor(out=t1.t[:], in0=xre, in1=kre, op=ALU.mult),
                                     reads=[xs.r, kf.r], writes=[t1.r])
                                S.op("pool", lambda: nc.gpsimd.tensor_tensor(out=t2.t[:], in0=xim, in1=kim, op=ALU.mult),
                                     reads=[xs.r, kf.r], writes=[t2.r])
                                S.op("dve", lambda: nc.vector.tensor_tensor(out=yre, in0=t1.t[:], in1=t2.t[:], op=ALU.subtract),
                                     reads=[t1.r, t2.r], writes=[Yb.r])
                                S.op("dve", lambda: nc.vector.tensor_tensor(out=t1.t[:], in0=xre, in1=kim, op=ALU.mult),
                                     reads=[xs.r, kf.r], writes=[t1.r])
                                S.op("pool", lambda: nc.gpsimd.tensor_tensor(out=t2.t[:], in0=xim, in1=kre, op=ALU.mult),
                                     reads=[xs.r, kf.r], writes=[t2.r])
                                S.op("dve", lambda: nc.vector.tensor_tensor(out=yim, in0=t1.t[:], in1=t2.t[:], op=ALU.add),
                                     reads=[t1.r, t2.r], writes=[Yb.r])
                            for c0 in range(0, 128, 4):
                                b = (c0 // 4) % 2
                                for c in range(c0, c0 + 4):
                                    out = PSA[0:64, b * 512 + (c - c0) * 128:b * 512 + (c - c0 + 1) * 128]
                                    mm(out, Yb.t[:, 0, c, :], dtb.t[:, 1:3, :].rearrange("p v i -> p (v i)"), True, False,
                                       [Yb.r, dtb.r], [BK[b]])
                                    mm(out, Yb.t[:, 1, c, :], dtb.t[:, 0:2, :].rearrange("p v i -> p (v i)"), False, True,
                                       [Yb.r, dtb.r], [BK[b]])
                                S.op("act", lambda b=b, c0=c0: nc.scalar.copy(
                                    out=AC.t[:, :, :, c0:c0 + 4],
                                    in_=PSA[0:64, b * 512:(b + 1) * 512].rearrange("p (c r i) -> p i r c", c=4, r=2)),
                                    reads=[BK[b]], writes=[AC.r])
                            for ib in range(8):
                                et = ets[ib % 2]
                                S.dma("pool", et.t[:], ET[ui, ib], et.r, writes=[et.r])
                                if o == 0:
                                    for ip in range(2):
                                        b = 4 + ip
                                        for il4 in range(4):
                                            il = ip * 4 + il4
                                            i = ib * 8 + il
                                            out = PSA[0:64, b * 512 + il4 * 128:b * 512 + (il4 + 1) * 128]
                                            mm(out, et.t[:, il, 0, :], AC.t[:, i, 0, :], True, False, [et.r, AC.r], [BK[b]])
                                            mm(out, et.t[:, il, 1, :], AC.t[:, i, 1, :], False, True, [et.r, AC.r], [BK[b]])
                                        i0 = ib * 8 + ip * 4
                                        S.op("dve", lambda b=b, i0=i0: nc.vector.tensor_tensor(
                                            out=vtok.t[:, :, i0:i0 + 4].rearrange("p c i -> p i c"),
                                            in0=PSA[0:64, b * 512:(b + 1) * 512].rearrange("p (i c) -> p i c", i=4),
                                            in1=x1tok.t[:, :, i0:i0 + 4].rearrange("p c i -> p i c"), op=ALU.mult),
                                            reads=[BK[b], x1tok.r], writes=[vtok.r])
                                else:
                                    b = 4 + ib % 2
                                    for il in range(8):
                                        i = ib * 8 + il
                                        out = PSA[:, b * 512 + il * 64:b * 512 + (il + 1) * 64]
                                        mm(out, AC.t[:, i, 0, :], et.t[:, il, 0, :], True, False, [et.r, AC.r], [BK[b]])
                                        mm(out, AC.t[:, i, 1, :], et.t[:, il, 1, :], False, True, [et.r, AC.r], [BK[b]])
                                    bb = (ib * 8) // N2
                                    n20 = (ib * 8) % N2
                                    view = lambda t: t[:, bb * L:(bb + 1) * L].rearrange("p (t1 n2) -> p n2 t1", n2=N2)[:, n20:n20 + 8, :]
                                    S.op("dve", lambda b=b, view=view: nc.vector.tensor_tensor(
                                        out=view(mixs.t), in0=PSA[:, b * 512:(b + 1) * 512].rearrange("p (i t) -> p i t", i=8),
                                        in1=view(x2s.t), op=ALU.mult),
                                        reads=[BK[b], x2s.r], writes=[mixs.r])
                        S.dma("sp", MIXT[g * 128:(g + 1) * 128, tok0:tok0 + 4096], mixs.t[:], mixs.r, reads=[mixs.r])
                S.barrier()
        if "D" in stages:
            with ExitStack() as ph:
                wout = sb(ph, "d_wout", [128, 8, D], BF16)
                S.dma("pool", wout.t[:], W_OUT[1].rearrange("(j p) n -> p j n", p=128), wout.r, writes=[wout.r])
                gpost = sb(ph, "d_gpost", [128, D], F32)
                S.dma("sp", gpost.t[:], ROWV[:, D:2 * D].partition_broadcast(128), gpost.r, writes=[gpost.r])
                wk = mk_normwk(ph, "d")
                ek = mk_epi(ph, "d")
                xin = [sb(ph, "dx%d" % i, [128, D], F32) for i in range(2)]
                cat = [sb(ph, "d_cat%d" % i, [128, 8, 128], BF16) for i in range(2)]
                nch = LIMIT or NT // 128
                for c in range(nch):
                    xb, cb = xin[c % 2], cat[c % 2]
                    t0 = c * 128
                    S.dma("sp", xb.t[:], X2[t0:t0 + 128, :], xb.r, writes=[xb.r])
                    S.dma("sp", cb.t[:, 0:6, :], MIXT.rearrange("(k p) t -> p k t", p=128)[:, :, t0:t0 + 128], cb.r,
                          writes=[cb.r])
                    S.dma("sp", cb.t[:, 6:8, :], ATT.rearrange("(k p) t -> p k t", p=128)[:, :, t0:t0 + 128], cb.r,
                          writes=[cb.r])
                    for nb in range(2):
                        for k in range(8):
                            mm(ps(3 + nb), cb.t[:, k, :], wout.t[:, k, nb * 512:(nb + 1) * 512], k == 0, k == 7,
                               [cb.r, wout.r], [BK[3 + nb]])
                    epilogue(ph, (3, 4), xb.t[:], xb.r, gpost.t[:], gpost.r, t0, X3, CV_FFNPRE[1], HNT3, wk, ek)
                S.barrier()
        if "B1" in stages:
            ffn_phase(1, X3, HNT3, Y, None, None)

        S.barrier()
        print("program: %d instructions, %d waits, %d dma sems" % (S.ninst, S.nwait, len(S.chans)))
    return nc


def _hyena_tables():
    t = {}
    n1 = np.arange(64)[:, None]
    k1 = np.arange(64)[None, :]
    ang = 2 * np.pi * n1 * (k1 + 0.5) / 128.0
    f1 = np.zeros((64, 128), np.float64)
    f1[:, 0::2] = np.cos(ang)
    f1[:, 1::2] = -np.sin(ang)
    t["f1t"] = f1.astype(np.float32)
    gt = np.zeros((2, 8, 64, 8, 3, 64), np.float64)
    dtb = np.zeros((2, 64, 3, 64), np.float64)
    et = np.zeros((2, 8, 64, 8, 2, 64), np.float64)
    su = np.zeros((2, 64, 64), np.float64)
    for ui, (L, N2, nb) in enumerate(((2048, 32, 2), (4096, 64, 1))):
        N = 2 * L
        n2 = np.arange(N2)[:, None]
        k2 = np.arange(N2)[None, :]
        for k1v in range(64):
            G = np.exp(-2j * np.pi * n2 * (k1v + 128 * k2 + 0.5) / N)
            Gf = np.zeros((64, 64), np.complex128)
            for b in range(nb):
                Gf[b * N2:(b + 1) * N2, b * N2:(b + 1) * N2] = G
            kb, kl = divmod(k1v, 8)
            gt[ui, kb, :, kl, 0, :] = Gf.real
            gt[ui, kb, :, kl, 1, :] = Gf.imag
            gt[ui, kb, :, kl, 2, :] = -Gf.imag
        Dm = np.exp(2j * np.pi * np.arange(N2)[:, None] * np.arange(N2)[None, :] / N2)
        Df = np.zeros((64, 64), np.complex128)
        for b in range(nb):
            Df[b * N2:(b + 1) * N2, b * N2:(b + 1) * N2] = Dm
        dtb[ui, :, 0, :] = -Df.imag
        dtb[ui, :, 1, :] = Df.real
        dtb[ui, :, 2, :] = Df.imag
        k1c = np.arange(64)[:, None]
        t1 = np.arange(64)[None, :]
        for i in range(64):
            t2 = i % N2
            E = (2.0 / N) * np.exp(2j * np.pi * (t2 + N2 * t1) * (k1c + 0.5) / N)
            ib, il = divmod(i, 8)
            et[ui, ib, :, il, 0, :] = E.real
            et[ui, ib, :, il, 1, :] = -E.imag
        nn1 = np.arange(64)[:, None]
        nn2 = np.arange(64)[None, :]
        su[ui] = np.where(nn2 < N2, -(N2 * nn1 + nn2) / (L - 1.0), 0.0)
        tt = np.linspace(0.0, 1.0, L)[:, None]
        w = (2.0 * np.pi / L) * np.arange(L)[:, None]
        f = np.linspace(1e-4, 15.0, 16)[None, :]
        feat = np.concatenate([tt, np.cos(f * w), -np.sin(f * w)], axis=-1)
        t["featP" if ui == 0 else "featS"] = np.ascontiguousarray(feat.T).astype(np.float32)
    t["gt"] = gt.astype(np.float32)
    t["dtb"] = dtb.astype(np.float32)
    t["et"] = et.astype(np.float32)
    t["su"] = su.astype(np.float32)
    return t


def _prep_shared(inp):
    f = lambda a: np.ascontiguousarray(np.asarray(a, dtype=np.float32))
    sh = {}
    sh["a_w_in"] = f(inp["a_w_in"][0])
    sh["b_w_in"] = f(inp["b_w_in"][0])
    sh["w_kv"] = f(inp["xattn_w_kv"])
    sh["w_out"] = f(inp["w_out"])
    sh["w_up"] = f(inp["ffn_w_up"])
    sh["w_dn"] = f(inp["ffn_w_down"])
    sh["wsT"] = f(np.transpose(np.asarray(inp["a_w_s"][0]), (2, 0, 1)))
    sh["bsT"] = f(np.transpose(np.asarray(inp["a_b_s"][0]), (1, 0)))

    def cols(v):
        v = np.asarray(v, dtype=np.float32)
        return v.reshape(-1, 128).T

    cl = []
    for nm in ("norm_mix_pre", "norm_ffn_pre"):
        pass
    a = inp
    cl += [cols(a["norm_mix_pre"][0]), cols(a["norm_ffn_pre"][0]), cols(a["norm_mix_pre"][1]), cols(a["norm_ffn_pre"][1])]
    cl += [cols(a["norm_mem"][0]), cols(a["norm_mem"][1])]
    for l in range(2):
        for k in range(3):
            cl.append(cols(a["ffn_conv_w"][l][k]))
        cl.append(cols(a["ffn_conv_b"][l]))
    for k in range(3):
        cl.append(cols(a["b_sconv_w"][0][k]))
    cl.append(cols(a["b_sconv_b"][0]))
    colv = np.concatenate(cl, axis=1)
    assert colv.shape == (128, NCV), colv.shape
    sh["colv"] = f(colv)
    deltas = np.abs(np.linspace(math.log(1e-2) / 1.5, math.log(1e-2) / 0.3, MW, dtype=np.float32))
    rowv = np.concatenate([np.asarray(a["norm_mix_post"][0]), np.asarray(a["norm_mix_post"][1]),
                           np.asarray(a["norm_ffn_post"][0]), np.asarray(a["norm_ffn_post"][1]),
                           np.asarray(a["a_ln_g"][0]), np.asarray(a["a_ln_b"][0]), deltas]).astype(np.float32)
    sh["rowv"] = f(rowv[None, :])
    sh["ident"] = np.eye(128, dtype=np.float32)
    sh.update(_hyena_tables())
    sh["fw1"] = f(a["b_filt_w1"][0])
    sh["fw2"] = f(a["b_filt_w2"][0])
    w3 = np.asarray(a["b_filt_w3"][0], dtype=np.float32).reshape(64, 2, 2, 6, 128)
    sh["fw3"] = f(np.transpose(w3, (0, 3, 2, 1, 4)).reshape(64, 3072))
    sh["fvec"] = f(np.stack([np.asarray(a["b_filt_freq"][0]), np.asarray(a["b_filt_b1"][0]),
                             np.asarray(a["b_filt_b2"][0])], axis=1))
    sh["fbias"] = f(np.asarray(a["b_filt_bias"][0]).reshape(1, 1536))
    return sh


def _core_inputs(inp, i):
    xp = np.asarray(inp["x_prompt"])
    xs = np.asarray(inp["x_sample"])
    mp = np.asarray(inp["mem_prompt"])
    ms = np.asarray(inp["mem_sample"])
    X = np.concatenate([xp[2 * i], xp[2 * i + 1], xs[i]], axis=0)
    M = np.concatenate([mp[2 * i], mp[2 * i + 1], ms[i]], axis=0)
    return {"X": np.ascontiguousarray(X, dtype=np.float32), "MEM": np.ascontiguousarray(M, dtype=np.float32)}


def kernel(**inputs):
    sh = _prep_shared(inputs)
    nc = build()
    in_maps = []
    for i in range(8):
        m = dict(sh)
        m.update(_core_inputs(inputs, i))
        in_maps.append(m)
    res = run_bass_kernel_spmd(nc, in_maps, core_ids=list(range(8)))
    yp = np.zeros((16, 2048, D), np.float32)
    ys = np.zeros((8, 4096, D), np.float32)
    for i in range(8):
        y = res.results[i]["Y"]
        yp[2 * i] = y[0:2048]
        yp[2 * i + 1] = y[2048:4096]
        ys[i] = y[4096:8192]
    return (yp, ys)
```

```python
import math
from contextlib import ExitStack

import numpy as np
import concourse.bass as bass
import concourse.mybir as mybir
from concourse.bass_utils import run_bass_kernel_spmd

F32 = mybir.dt.float32
BF16 = mybir.dt.bfloat16
AF = mybir.ActivationFunctionType
ALU = mybir.AluOpType

NT = 8192
SEQS = [(0, 2048), (2048, 2048), (4096, 4096)]
NORM_EPS = 1e-6
LN_EPS = 1e-5
D = 1024
MW = 768
DFF = 2816
NCH_FF = 22
GELU = AF.Gelu_apprx_tanh
STOP = None
LIMIT = None

CV_MIXPRE = [0, 16]
CV_FFNPRE = [8, 24]
CV_MEM = [32, 40]
CV_FFC = [48, 48 + 88]
CV_SC = 48 + 176
NCV = CV_SC + 72


class Res:
    __slots__ = ("name", "w", "r", "chan", "excl")

    def __init__(self, name, excl=False):
        self.name = name
        self.w = None
        self.r = []
        self.chan = None
        self.excl = excl


class Chan:
    __slots__ = ("sem", "cnt", "q")

    def __init__(self, sem):
        self.sem = sem
        self.cnt = 0
        self.q = None


class TB:
    def __init__(self, t, name):
        self.t = t
        self.r = Res(name)


class Sched:
    def __init__(self, nc, stack):
        self.nc = nc
        self.stack = stack
        self.eng = {"pe": nc.tensor, "act": nc.scalar, "dve": nc.vector,
                    "pool": nc.gpsimd, "sp": nc.sync}
        self.esem = {}
        self.ecnt = {}
        for e in ("pe", "act", "dve", "pool"):
            self.esem[e] = stack.enter_context(nc.semaphore("s_" + e))
            self.ecnt[e] = 0
        self.known = {e: {} for e in self.eng}
        self.chans = []
        self.free = {"sp": [], "pool": []}
        self.ninst = 0
        self.nwait = 0

    def _need(self, e, ev, same_ok):
        if ev is None:
            return
        sem, val = ev
        if same_ok and e in self.esem and sem is self.esem[e]:
            return
        k = self.known[e]
        if k.get(id(sem), 0) >= val:
            return
        k[id(sem)] = val
        self.eng[e].wait_ge(sem, val)
        self.nwait += 1

    def _deps(self, e, reads, writes, same_ok):
        for r in reads:
            self._need(e, r.w, same_ok)
        for r in writes:
            self._need(e, r.w, same_ok)
            for ev in r.r:
                self._need(e, ev, same_ok)

    def _post(self, ev, reads, writes):
        for r in reads:
            r.r.append(ev)
            if len(r.r) > 10:
                d = {}
                for s, v in r.r:
                    if id(s) not in d or d[id(s)][1] < v:
                        d[id(s)] = (s, v)
                r.r = list(d.values())
        for r in writes:
            r.w = ev
            r.r = []

    def op(self, e, fn, reads=(), writes=()):
        ex = [r for r in reads if r.excl]
        if ex:
            writes = list(writes) + ex
            reads = [r for r in reads if not r.excl]
        self._deps(e, reads, writes, same_ok=(e == "pe"))
        ins = fn()
        self.ecnt[e] += 1
        ins.then_inc(self.esem[e], 1)
        ev = (self.esem[e], self.ecnt[e])
        self._post(ev, reads, writes)
        self.ninst += 1
        return ev

    def dma(self, q, out, in_, chan, reads=(), writes=()):
        skip = chan.chan.sem if chan.chan is not None else None
        for r in reads:
            self._need(q, r.w, False)
        for r in writes:
            if not (r.w is not None and r.w[0] is skip and r is chan and not r.r):
                self._need(q, r.w, False)
            for ev in r.r:
                self._need(q, ev, False)
        if chan.chan is None:
            if self.free[q]:
                chan.chan = self.free[q].pop()
            else:
                c = Chan(self.stack.enter_context(self.nc.semaphore("d%s%d" % (q, len(self.chans)))))
                c.q = q
                self.chans.append(c)
                chan.chan = c
        c = chan.chan
        assert c.q == q, "a DMA channel semaphore must stay on one queue type"
        ins = self.eng[q].dma_start(out=out, in_=in_)
        c.cnt += 16
        ins.then_inc(c.sem, 16)
        ev = (c.sem, c.cnt)
        self._post(ev, reads, writes)
        self.ninst += 1
        return ev

    def barrier(self):
        for e in self.eng:
            for e2 in self.esem:
                if self.ecnt[e2]:
                    self._need(e, (self.esem[e2], self.ecnt[e2]), False)
            for c in self.chans:
                if c.cnt:
                    self._need(e, (c.sem, c.cnt), False)
        self.free = {"sp": [c for c in self.chans if c.q == "sp"], "pool": [c for c in self.chans if c.q == "pool"]}


def build(stages=("kv", "A", "B0", "C", "FF", "FD", "D", "B1"), dbg=(), ext=()):
    nc = bass.Bass("TRN2", target_bir_lowering=False)

    def din(name, shape, dt=F32):
        return nc.dram_tensor(name, list(shape), dt, kind="ExternalInput").ap()

    def dscr(name, shape, dt):
        kind = "ExternalOutput" if name in dbg else ("ExternalInput" if name in ext else "Internal")
        return nc.dram_tensor(name, list(shape), dt, kind=kind).ap()

    X0 = din("X", [NT, D])
    MEM = din("MEM", [768, D])
    A_W_IN = din("a_w_in", [D, 1792])
    B_W_IN = din("b_w_in", [D, 2560])
    W_KV = din("w_kv", [2, D, 512])
    W_OUT = din("w_out", [2, D, D])
    W_UP = din("w_up", [2, D, 2 * DFF])
    W_DN = din("w_dn", [2, DFF, D])
    WST = din("wsT", [128, 12, 128])
    BST = din("bsT", [128, 12])
    COLV = din("colv", [128, NCV])
    ROWV = din("rowv", [1, 4 * D + 3 * MW])
    IDENT = din("ident", [128, 128])
    F1T = din("f1t", [64, 128])
    GT = din("gt", [2, 8, 64, 8, 3, 64])
    DTB = din("dtb", [2, 64, 3, 64])
    ET = din("et", [2, 8, 64, 8, 2, 64])
    FEAT = [din("featP", [33, 2048]), din("featS", [33, 4096])]
    SU = din("su", [2, 64, 64])
    FW1 = din("fw1", [33, 64])
    FW2 = din("fw2", [64, 64])
    FW3 = din("fw3", [64, 3072])
    FVEC = din("fvec", [64, 3])
    FBIAS = din("fbias", [1, 1536])
    Y = nc.dram_tensor("Y", [NT, D], F32, kind="ExternalOutput").ap()

    X1 = dscr("X1", [NT, D], F32)
    X2 = dscr("X2", [NT, D], F32)
    X3 = dscr("X3", [NT, D], F32)
    HNT1 = dscr("HNT1", [D, NT], BF16)
    HNT2 = dscr("HNT2", [D, NT], BF16)
    HNT3 = dscr("HNT3", [D, NT], BF16)
    PT = dscr("PT", [2304, NT], BF16)
    ATT = dscr("ATT", [256, NT], BF16)
    MIXT = dscr("MIXT", [MW, NT], BF16)
    KF = dscr("KF", [2, 6, 2, 8, 64, 8, 2, 128], BF16)

    with ExitStack() as st:
        S = Sched(nc, st)

        uid = [0]

        def sb(stack, name, shape, dt):
            uid[0] += 1
            return TB(stack.enter_context(nc.sbuf_tensor("sb%d_%s" % (uid[0], name), list(shape), dt)), name)

        PSA = st.enter_context(nc.psum_tensor("PSA", [128, 7 * 512], F32))
        PBt = st.enter_context(nc.psum_tensor("PB", [128, 1024], BF16))
        BK = [Res("bank%d" % i, excl=True) for i in range(7)]
        PBr = Res("pb", excl=True)

        def ps(b, lo=0, hi=512):
            return PSA[:, b * 512 + lo: b * 512 + hi]

        ident = sb(st, "ident", [128, 128], BF16)
        ones = sb(st, "ones", [128, 128], BF16)
        colv = sb(st, "colv", [128, NCV], F32)
        kT = sb(st, "kT", [128, 6, 2, 256], BF16)
        vv = sb(st, "vv", [128, 6, 2, 256], BF16)
        S.dma("pool", ident.t[:], IDENT, ident.r, writes=[ident.r])
        S.dma("sp", colv.t[:], COLV, colv.r, writes=[colv.r])
        S.op("dve", lambda: nc.vector.memset(ones.t[:], 1.0), writes=[ones.r])

        def mm(out, lhsT, rhs, start, stop, reads, writes):
            S.op("pe", lambda: nc.tensor.matmul(out, lhsT=lhsT, rhs=rhs, start=start, stop=stop),
                 reads=reads, writes=writes)

        def evac(which, out_ap, in_ap, reads, writes):
            if which == 0:
                S.op("act", lambda: nc.scalar.copy(out=out_ap, in_=in_ap), reads=reads, writes=writes)
            else:
                S.op("dve", lambda: nc.vector.tensor_copy(out=out_ap, in_=in_ap), reads=reads, writes=writes)

        def norm_T(ph, xt_ap, xt_res, gbase, out_ap, out_res, wk):
            junk, ss, xn = wk["junk"], wk["ss"], wk["xn"]
            S.op("act", lambda: nc.scalar.activation(out=junk.t[:, 0:D], in_=xt_ap, func=AF.Square,
                                                     accum_out=ss.t[:, 0:1]),
                 reads=[xt_res], writes=[junk.r, ss.r])
            S.op("act", lambda: nc.scalar.activation(out=ss.t[:, 1:2], in_=ss.t[:, 0:1], func=AF.Sqrt,
                                                     scale=1.0 / D, bias=wk["eps"].t[:, 0:1]),
                 reads=[ss.r, wk["eps"].r], writes=[ss.r])
            S.op("dve", lambda: nc.vector.reciprocal(out=ss.t[:, 2:3], in_=ss.t[:, 1:2]),
                 reads=[ss.r], writes=[ss.r])
            S.op("dve", lambda: nc.vector.tensor_scalar(out=xn.t[:], in0=xt_ap, scalar1=ss.t[:, 2:3],
                                                        scalar2=None, op0=ALU.mult),
                 reads=[xt_res, ss.r], writes=[xn.r])
            for j in range(8):
                S.op("pe", lambda j=j: nc.tensor.transpose(out=PBt[:, j * 128:(j + 1) * 128],
                                                           in_=xn.t[:, j * 128:(j + 1) * 128],
                                                           identity=ident.t[:]),
                     reads=[xn.r, ident.r], writes=[PBr])
            S.op("dve", lambda: nc.vector.tensor_tensor(
                out=out_ap, in0=PBt[:, 0:1024].rearrange("p (j t) -> p j t", j=8),
                in1=colv.t[:, gbase:gbase + 8].unsqueeze(2).to_broadcast([128, 8, 128]), op=ALU.mult),
                reads=[PBr, colv.r], writes=[out_res])

        def mk_normwk(ph, tag):
            wk = {"junk": sb(ph, "junk" + tag, [128, D], BF16),
                  "ss": sb(ph, "ss" + tag, [128, 4], F32),
                  "xn": sb(ph, "xn" + tag, [128, D], BF16),
                  "eps": sb(ph, "eps" + tag, [128, 2], F32)}
            S.op("dve", lambda: nc.vector.memset(wk["eps"].t[:, 0:1], NORM_EPS), writes=[wk["eps"].r])
            S.op("dve", lambda: nc.vector.memset(wk["eps"].t[:, 1:2], LN_EPS), writes=[wk["eps"].r])
            return wk

        def hnt_view(H):
            return H.rearrange("(j p) t -> p j t", p=128)

        def epilogue(ph, banks, xres_ap, xres_res, gpost_ap, gpost_res, tok0, XOUT, gnext, HOUT, wk, ek):
            b0 = banks[0]
            pout = PSA[:, b0 * 512: b0 * 512 + 1024]
            br = [BK[b] for b in banks]
            junk, ss = wk["junk"], ek["ss"]
            S.op("act", lambda: nc.scalar.activation(out=junk.t[:, 0:D], in_=pout, func=AF.Square,
                                                     accum_out=ss.t[:, 0:1]),
                 reads=br, writes=[junk.r, ss.r])
            S.op("act", lambda: nc.scalar.activation(out=ss.t[:, 1:2], in_=ss.t[:, 0:1], func=AF.Sqrt,
                                                     scale=1.0 / D, bias=wk["eps"].t[:, 0:1]),
                 reads=[ss.r, wk["eps"].r], writes=[ss.r])
            S.op("dve", lambda: nc.vector.reciprocal(out=ss.t[:, 2:3], in_=ss.t[:, 1:2]),
                 reads=[ss.r], writes=[ss.r])
            tp = ek["tp"]
            S.op("dve", lambda: nc.vector.tensor_tensor(out=tp.t[:], in0=pout, in1=gpost_ap, op=ALU.mult),
                 reads=br + [gpost_res], writes=[tp.r])
            xnew = ek["xnew"]
            S.op("dve", lambda: nc.vector.scalar_tensor_tensor(out=xnew.t[:], in0=tp.t[:], scalar=ss.t[:, 2:3],
                                                               in1=xres_ap, op0=ALU.mult, op1=ALU.add),
                 reads=[tp.r, ss.r, xres_res], writes=[xnew.r])
            S.dma("pool", XOUT[tok0:tok0 + 128, :], xnew.t[:], ek["xst"], reads=[xnew.r])
            if HOUT is not None:
                hno = ek["hno"]
                norm_T(ph, xnew.t[:], xnew.r, gnext, hno.t[:], hno.r, wk)
                S.dma("pool", hnt_view(HOUT)[:, :, tok0:tok0 + 128], hno.t[:], ek["hst"], reads=[hno.r])

        def mk_epi(ph, tag):
            return {"ss": sb(ph, "ess" + tag, [128, 4], F32),
                    "tp": sb(ph, "tp" + tag, [128, D], F32),
                    "xnew": sb(ph, "xnew" + tag, [128, D], F32),
                    "hno": sb(ph, "hno" + tag, [128, 8, 128], BF16),
                    "xst": Res("xst" + tag), "hst": Res("hst" + tag)}

        def attention(hn_ap, hn_res, wq_ap, wq_res, ls, cat, wk, banks):
            bq, bs0, bs1, bav0, bav1 = banks
            qT, pT, rden = wk["qT"], wk["pT"], wk["rden"]
            for hc in range(2):
                for j in range(8):
                    mm(ps(bq, hc * 128, hc * 128 + 128), wq_ap(j, hc), hn_ap(j), j == 0, j == 7,
                       [wq_res, hn_res], [BK[bq]])
            S.op("act", lambda: nc.scalar.activation(out=qT.t[:].rearrange("p a t -> p (a t)"), in_=ps(bq, 0, 256),
                                                     func=AF.Copy, scale=0.125),
                 reads=[BK[bq]], writes=[qT.r])
            for h in range(4):
                hc, po = h // 2, (h % 2) * 64
                for mc in range(2):
                    idx = (h % 2) * 4 + (h // 2) * 2 + mc
                    b = bs0 if idx < 4 else bs1
                    col = (idx % 4) * 128
                    mm(PSA[:, b * 512 + col: b * 512 + col + 128],
                       kT.t[po:po + 64, ls, hc, mc * 128:(mc + 1) * 128], qT.t[po:po + 64, hc, :], True, True,
                       [kT.r, qT.r], [BK[b]])
            for half, b in ((0, bs0), (1, bs1)):
                S.op("act", lambda half=half, b=b: nc.scalar.activation(
                    out=pT.t[:, half * 4:(half + 1) * 4, :].rearrange("p a t -> p (a t)"), in_=ps(b), func=AF.Exp),
                    reads=[BK[b]], writes=[pT.r])
            for hc in range(2):
                b = bav0 if hc == 0 else bav1
                for part in range(4):
                    h = 2 * hc + (part % 2)
                    for mc in range(2):
                        lhsT = vv.t[:, ls, mc, hc * 128:(hc + 1) * 128] if part < 2 else ones.t[:]
                        mm(ps(b, part * 128, part * 128 + 128), lhsT, pT.t[:, (h % 2) * 4 + (h // 2) * 2 + mc, :], mc == 0, mc == 1,
                           [vv.r, ones.r, pT.r], [BK[b]])
                for hh in range(2):
                    lo = hh * 64
                    S.op("dve", lambda hh=hh, lo=lo, b=b, hc=hc: nc.vector.reciprocal(
                        out=rden.t[lo:lo + 64, hc, :], in_=PSA[lo:lo + 64, b * 512 + (2 + hh) * 128: b * 512 + (3 + hh) * 128]),
                        reads=[BK[b]], writes=[rden.r])
                for hh in range(2):
                    lo = hh * 64
                    S.op("dve", lambda hh=hh, lo=lo, b=b, hc=hc: nc.vector.tensor_tensor(
                        out=cat.t[lo:lo + 64, 6 + hc, :], in0=PSA[lo:lo + 64, b * 512 + hh * 128: b * 512 + (hh + 1) * 128],
                        in1=rden.t[lo:lo + 64, hc, :], op=ALU.mult),
                        reads=[BK[b], rden.r], writes=[cat.r])

        def mk_attwk(ph, tag):
            return {"qT": sb(ph, "qT" + tag, [128, 2, 128], BF16),
                    "pT": sb(ph, "pT" + tag, [128, 8, 128], BF16),
                    "rden": sb(ph, "rden" + tag, [128, 2, 128], F32)}

        def seq_of(tok):
            for si, (t0, L) in enumerate(SEQS):
                if t0 <= tok < t0 + L:
                    return si, t0, L
            raise ValueError

        def load_hnt_tile(HSRC, hb, a, T):
            si, t0, L = seq_of(a)
            lo = a - 1 if a > t0 else a
            hi = a + T + 1 if a + T < t0 + L else a + T
            if lo == a:
                S.op("dve", lambda: nc.vector.memset(hb.t[:, :, 0:1], 0.0), writes=[hb.r])
            if hi == a + T:
                S.op("dve", lambda: nc.vector.memset(hb.t[:, :, T + 1:T + 2], 0.0), writes=[hb.r])
            S.dma("sp", hb.t[:, :, lo - (a - 1): hi - (a - 1)], hnt_view(HSRC)[:, :, lo:hi], hb.r, writes=[hb.r])

        def conv_chunk(hb, T, w_ap, w_res, cv, nchk, ci, bmain, tbuf, fin=None):
            for j in range(8):
                mm(ps(bmain, 0, T + 2), w_ap(j), hb.t[:, j, 0:T + 2], j == 0, j == 7, [w_res, hb.r], [BK[bmain]])
            c0, c1, c2, cb = (colv.t[:, cv + k * nchk + ci: cv + k * nchk + ci + 1] for k in range(4))
            S.op("act", lambda: nc.scalar.activation(out=tbuf.t[:, 0:T], in_=ps(bmain, 1, T + 1), func=AF.Identity,
                                                     scale=c1, bias=cb),
                 reads=[BK[bmain], colv.r], writes=[tbuf.r])
            S.op("dve", lambda: nc.vector.scalar_tensor_tensor(out=tbuf.t[:, 0:T], in0=ps(bmain, 0, T), scalar=c0,
                                                               in1=tbuf.t[:, 0:T], op0=ALU.mult, op1=ALU.add),
                 reads=[BK[bmain], tbuf.r, colv.r], writes=[tbuf.r])
            fo = tbuf if fin is None else fin
            S.op("dve", lambda: nc.vector.scalar_tensor_tensor(out=fo.t[:, 0:T], in0=ps(bmain, 2, T + 2), scalar=c2,
                                                               in1=tbuf.t[:, 0:T], op0=ALU.mult, op1=ALU.add),
                 reads=[BK[bmain], tbuf.r, colv.r], writes=[fo.r])

        if "kv" in stages:
            with ExitStack() as ph:
                wkv = sb(ph, "wkv", [128, 2, 8, 512], BF16)
                for l in range(2):
                    S.dma("pool", wkv.t[:, l], W_KV[l].rearrange("(j p) n -> p j n", p=128), wkv.r, writes=[wkv.r])
                wk = mk_normwk(ph, "kv")
                xin = [sb(ph, "kvx%d" % i, [128, D], F32) for i in range(2)]
                hn = [sb(ph, "kvh%d" % i, [128, 8, 128], BF16) for i in range(2)]
                it = 0
                for l in range(2):
                    for s in range(3):
                        for mt in range(2):
                            xb, hb = xin[it % 2], hn[it % 2]
                            it += 1
                            r0 = s * 256 + mt * 128
                            S.dma("sp", xb.t[:], MEM[r0:r0 + 128, :], xb.r, writes=[xb.r])
                            norm_T(ph, xb.t[:], xb.r, CV_MEM[l], hb.t[:], hb.r, wk)
                            ls = l * 3 + s
                            for hc in range(2):
                                b = hc
                                for j in range(8):
                                    mm(ps(b, 0, 128), wkv.t[:, l, j, hc * 128:(hc + 1) * 128], hb.t[:, j, :], j == 0, j == 7,
                                       [wkv.r, hb.r], [BK[b]])
                                S.op("act", lambda b=b, hc=hc, ls=ls, mt=mt: nc.scalar.copy(
                                    out=kT.t[:, ls, hc, mt * 128:(mt + 1) * 128], in_=ps(b, 0, 128)),
                                    reads=[BK[b]], writes=[kT.r])
                            for j in range(8):
                                mm(ps(2, 0, 256), hb.t[:, j, :], wkv.t[:, l, j, 256:512], j == 0, j == 7,
                                   [wkv.r, hb.r], [BK[2]])
                            S.op("act", lambda ls=ls, mt=mt: nc.scalar.copy(out=vv.t[:, ls, mt, :], in_=ps(2, 0, 256)),
                                 reads=[BK[2]], writes=[vv.r])
                S.barrier()

        if "A" in stages:
            with ExitStack() as ph:
                win = sb(ph, "a_win", [128, 8, 1792], BF16)
                for j in range(8):
                    S.dma("pool", win.t[:, j, :], A_W_IN[j * 128:(j + 1) * 128, :], win.r, writes=[win.r])
                wout = sb(ph, "a_wout", [128, 8, D], BF16)
                S.dma("pool", wout.t[:], W_OUT[0].rearrange("(j p) n -> p j n", p=128), wout.r, writes=[wout.r])
                wst = sb(ph, "a_wst", [128, 12, 128], BF16)
                S.dma("pool", wst.t[:], WST, wst.r, writes=[wst.r])
                bst = sb(ph, "a_bst", [128, 12], F32)
                S.dma("sp", bst.t[:], BST, bst.r, writes=[bst.r])
                gpost = sb(ph, "a_gpost", [128, D], F32)
                S.dma("sp", gpost.t[:], ROWV[:, 0:D].partition_broadcast(128), gpost.r, writes=[gpost.r])
                lng = sb(ph, "a_lng", [128, MW], F32)
                lnb = sb(ph, "a_lnb", [128, MW], F32)
                S.dma("sp", lng.t[:], ROWV[:, 4 * D:4 * D + MW].partition_broadcast(128), lng.r, writes=[lng.r])
                S.dma("sp", lnb.t[:], ROWV[:, 4 * D + MW:4 * D + 2 * MW].partition_broadcast(128), lnb.r, writes=[lnb.r])
                wk = mk_normwk(ph, "a")
                ek = mk_epi(ph, "a")
                awk = mk_attwk(ph, "a")
                xin = [sb(ph, "ax%d" % i, [128, D], F32) for i in range(3)]
                hn = [sb(ph, "ah%d" % i, [128, 8, 128], BF16) for i in range(2)]
                gu = sb(ph, "a_gu", [128, MW], F32)
                gv = sb(ph, "a_gv", [128, MW], F32)
                vn = sb(ph, "a_vn", [128, MW], BF16)
                tmp = sb(ph, "a_tmp", [128, MW], F32)
                mix = sb(ph, "a_mix", [128, MW], BF16)
                st6 = sb(ph, "a_st", [128, 2, 6], F32)
                mv = sb(ph, "a_mv", [128, 4], F32)
                cat = [sb(ph, "a_cat%d" % i, [128, 8, 128], BF16) for i in range(2)]
                nch = LIMIT or NT // 128

                def load_x(c):
                    xb = xin[c % 3]
                    S.dma("sp", xb.t[:], X0[c * 128:(c + 1) * 128, :], xb.r, writes=[xb.r])

                load_x(0)
                load_x(1)
                for c in range(nch):
                    if c + 2 < nch:
                        load_x(c + 2)
                    xb, hb, cb = xin[c % 3], hn[c % 2], cat[c % 2]
                    si, _, _ = seq_of(c * 128)
                    norm_T(ph, xb.t[:], xb.r, CV_MIXPRE[0], hb.t[:], hb.r, wk)
                    if STOP == 1:
                        break
                    for nb in range(3):
                        for j in range(8):
                            mm(ps(nb), hb.t[:, j, :], win.t[:, j, nb * 512:(nb + 1) * 512], j == 0, j == 7,
                               [hb.r, win.r], [BK[nb]])
                    S.op("act", lambda: nc.scalar.activation(out=gv.t[:], in_=PSA[:, MW:2 * MW], func=GELU),
                         reads=[BK[1], BK[2]], writes=[gv.r])
                    S.op("act", lambda: nc.scalar.activation(out=gu.t[:], in_=PSA[:, 0:MW], func=GELU),
                         reads=[BK[0], BK[1]], writes=[gu.r])
                    if STOP == 2:
                        break
                    for k in range(2):
                        S.op("dve", lambda k=k: nc.vector.bn_stats(out=st6.t[:, k, :], in_=gv.t[:, k * 384:(k + 1) * 384]),
                             reads=[gv.r], writes=[st6.r])
                    S.op("dve", lambda: nc.vector.bn_aggr(out=mv.t[:, 0:2], in_=st6.t[:]), reads=[st6.r], writes=[mv.r])
                    S.op("act", lambda: nc.scalar.activation(out=mv.t[:, 2:3], in_=mv.t[:, 1:2], func=AF.Sqrt,
                                                             scale=1.0, bias=wk["eps"].t[:, 1:2]),
                         reads=[mv.r, wk["eps"].r], writes=[mv.r])
                    S.op("dve", lambda: nc.vector.reciprocal(out=mv.t[:, 3:4], in_=mv.t[:, 2:3]), reads=[mv.r], writes=[mv.r])
                    S.op("dve", lambda: nc.vector.tensor_scalar(out=gv.t[:], in0=gv.t[:], scalar1=mv.t[:, 0:1],
                                                                scalar2=mv.t[:, 3:4], op0=ALU.subtract, op1=ALU.mult),
                         reads=[gv.r, mv.r], writes=[gv.r])
                    S.op("pool", lambda: nc.gpsimd.tensor_tensor(out=gv.t[:], in0=gv.t[:], in1=lng.t[:], op=ALU.mult),
                         reads=[gv.r, lng.r], writes=[gv.r])
                    S.op("pool", lambda: nc.gpsimd.tensor_tensor(out=vn.t[:], in0=gv.t[:], in1=lnb.t[:], op=ALU.add),
                         reads=[gv.r, lnb.r], writes=[vn.r])
                    if STOP == 3:
                        break
                    for g in range(12):
                        col = 3 * 512 + g * 64
                        mm(PSA[:, col:col + 64], wst.t[:, g, :], vn.t[:, g * 64:(g + 1) * 64], True, True,
                           [wst.r, vn.r], [BK[3 + (g // 8)]])
                    S.op("dve", lambda: nc.vector.tensor_tensor(
                        out=tmp.t[:].rearrange("p (g d) -> p g d", g=12),
                        in0=PSA[:, 3 * 512:3 * 512 + MW].rearrange("p (g d) -> p g d", g=12),
                        in1=bst.t[:].unsqueeze(2).to_broadcast([128, 12, 64]), op=ALU.add),
                        reads=[BK[3], BK[4], bst.r], writes=[tmp.r])
                    S.op("pool", lambda: nc.gpsimd.tensor_tensor(out=mix.t[:], in0=tmp.t[:], in1=gu.t[:], op=ALU.mult),
                         reads=[tmp.r, gu.r], writes=[mix.r])
                    if STOP == 4:
                        break
                    for k in range(6):
                        S.op("pe", lambda k=k: nc.tensor.transpose(out=PBt[:, k * 128:(k + 1) * 128],
                                                                   in_=mix.t[:, k * 128:(k + 1) * 128], identity=ident.t[:]),
                             reads=[mix.r, ident.r], writes=[PBr])
                    S.op("act", lambda: nc.scalar.copy(out=cb.t[:, 0:6, :].rearrange("p a t -> p (a t)"), in_=PBt[:, 0:768]),
                         reads=[PBr], writes=[cb.r])
                    if STOP == 5:
                        break
                    attention(lambda j: hb.t[:, j, :], hb.r,
                              lambda j, hc: win.t[:, j, 1536 + hc * 128:1536 + (hc + 1) * 128], win.r,
                              0 * 3 + si, cb, awk, (5, 0, 1, 5, 6))
                    if STOP == 6:
                        break
                    for nb in range(2):
                        for k in range(8):
                            mm(ps(3 + nb), cb.t[:, k, :], wout.t[:, k, nb * 512:(nb + 1) * 512], k == 0, k == 7,
                               [cb.r, wout.r], [BK[3 + nb]])
                    if STOP == 7:
                        break
                    epilogue(ph, (3, 4), xb.t[:], xb.r, gpost.t[:], gpost.r, c * 128, X1, CV_FFNPRE[0], HNT1, wk, ek)
                if "KTD" in dbg:
                    KTD = dscr("KTD", [128, 6 * 2 * 256], BF16)
                    VVD = dscr("VVD", [128, 6 * 2 * 256], BF16)
                    S.dma("sp", KTD, kT.t[:].rearrange("p a b c -> p (a b c)"), Res("ktd"), reads=[kT.r])
                    S.dma("sp", VVD, vv.t[:].rearrange("p a b c -> p (a b c)"), Res("vvd"), reads=[vv.r])
                S.barrier()

        def ffn_phase(layer, XIN, HIN, XOUT, gnext, HOUT):
            T = 256
            with ExitStack() as ph:
                wup = sb(ph, "wup", [128, 8, 2 * DFF], BF16)
                for j in range(8):
                    S.dma("pool", wup.t[:, j, :], W_UP[layer, j * 128:(j + 1) * 128, :], wup.r, writes=[wup.r])
                wdn = sb(ph, "wdn", [128, NCH_FF, D], BF16)
                for ci in range(NCH_FF):
                    S.dma("pool", wdn.t[:, ci, :], W_DN[layer, ci * 128:(ci + 1) * 128, :], wdn.r, writes=[wdn.r])
                gpost = sb(ph, "f_gpost", [128, D], F32)
                S.dma("sp", gpost.t[:], ROWV[:, (2 + layer) * D:(3 + layer) * D].partition_broadcast(128), gpost.r,
                      writes=[gpost.r])
                wk = mk_normwk(ph, "f")
                ek = mk_epi(ph, "f")
                hb = sb(ph, "f_hb", [128, 8, T + 2], BF16)
                xin = [sb(ph, "fx%d" % i, [128, D], F32) for i in range(2)]
                tb = [sb(ph, "ft%d" % i, [128, T], F32) for i in range(2)]
                hact = sb(ph, "hact", [128, NCH_FF, T], BF16)
                cv = CV_FFC[layer]
                ntile = LIMIT or NT // T
                xcnt = 0
                for ti in range(ntile):
                    a = ti * T
                    load_hnt_tile(HIN, hb, a, T)
                    for ci in range(NCH_FF):
                        tbuf = tb[ci % 2]
                        bm, bu = ci % 2, 2 + ci % 2
                        conv_chunk(hb, T, lambda j: wup.t[:, j, ci * 128:(ci + 1) * 128], wup.r, cv, NCH_FF, ci,
                                   bm, tbuf)
                        S.op("act", lambda tbuf=tbuf: nc.scalar.activation(out=tbuf.t[:, 0:T], in_=tbuf.t[:, 0:T], func=GELU),
                             reads=[tbuf.r], writes=[tbuf.r])
                        for j in range(8):
                            mm(ps(bu, 0, T), wup.t[:, j, DFF + ci * 128:DFF + (ci + 1) * 128], hb.t[:, j, 1:T + 1],
                               j == 0, j == 7, [wup.r, hb.r], [BK[bu]])
                        S.op("dve", lambda tbuf=tbuf, bu=bu, ci=ci: nc.vector.tensor_tensor(
                            out=hact.t[:, ci, :], in0=ps(bu, 0, T), in1=tbuf.t[:, 0:T], op=ALU.mult),
                            reads=[BK[bu], tbuf.r], writes=[hact.r])
                    for tt in range(T // 128):
                        xb = xin[xcnt % 2]
                        xcnt += 1
                        tok0 = a + tt * 128
                        S.dma("sp", xb.t[:], XIN[tok0:tok0 + 128, :], xb.r, writes=[xb.r])
                        for nb in range(2):
                            for ci in range(NCH_FF):
                                mm(ps(5 + nb), hact.t[:, ci, tt * 128:(tt + 1) * 128], wdn.t[:, ci, nb * 512:(nb + 1) * 512],
                                   ci == 0, ci == NCH_FF - 1, [hact.r, wdn.r], [BK[5 + nb]])
                        epilogue(ph, (5, 6), xb.t[:], xb.r, gpost.t[:], gpost.r, tok0, XOUT, gnext, HOUT, wk, ek)
                S.barrier()

        if "B0" in stages:
            ffn_phase(0, X1, HNT1, X2, CV_MIXPRE[1], HNT2)

        if "C" in stages:
            T = 256
            with ExitStack() as ph:
                win = sb(ph, "b_win", [128, 8, 2560], BF16)
                for j in range(8):
                    S.dma("pool", win.t[:, j, :], B_W_IN[j * 128:(j + 1) * 128, :], win.r, writes=[win.r])
                awk = mk_attwk(ph, "c")
                hbs = [sb(ph, "c_hb%d" % i, [128, 8, T + 2], BF16) for i in range(2)]
                tb = [sb(ph, "c_t%d" % i, [128, T], F32) for i in range(2)]
                ob = [sb(ph, "c_o%d" % i, [128, T], BF16) for i in range(3)]
                cat = [sb(ph, "c_cat%d" % i, [128, 8, 128], BF16) for i in range(2)]
                cst = [Res("cst0"), Res("cst1")]
                ntile = LIMIT or NT // T
                oc = 0
                cc = 0
                for ti in range(ntile):
                    a = ti * T
                    hb = hbs[ti % 2]
                    si, _, _ = seq_of(a)
                    load_hnt_tile(HNT2, hb, a, T)
                    for ci in range(18):
                        tbuf = tb[ci % 2]
                        o = ob[oc % 3]
                        oc += 1
                        conv_chunk(hb, T, lambda j: win.t[:, j, ci * 128:(ci + 1) * 128], win.r, CV_SC, 18, ci,
                                   ci % 2, tbuf, fin=o)
                        S.dma("pool", PT[ci * 128:(ci + 1) * 128, a:a + T], o.t[:], o.r, reads=[o.r])
                    for tt in range(T // 128):
                        cb = cat[cc % 2]
                        cc += 1
                        attention(lambda j: hb.t[:, j, 1 + tt * 128:1 + (tt + 1) * 128], hb.r,
                                  lambda j, hc: win.t[:, j, 2304 + hc * 128:2304 + (hc + 1) * 128], win.r,
                                  3 + si, cb, awk, (5, 2, 3, 5, 6))
                        S.dma("pool", ATT.rearrange("(k p) t -> p k t", p=128)[:, :, a + tt * 128:a + (tt + 1) * 128],
                              cb.t[:, 6:8, :], cst[cc % 2], reads=[cb.r])
                S.barrier()


        UNITS = [dict(tok0=0, L=2048, N2=32, nb=2), dict(tok0=4096, L=4096, N2=64, nb=1)]
        NG = 6

        def load_unit_tables(ph):
            f1 = sb(ph, "h_f1", [64, 128], BF16)
            S.dma("pool", f1.t[:], F1T, f1.r, writes=[f1.r])
            return f1

        if "FF" in stages:
            with ExitStack() as ph:
                f1 = load_unit_tables(ph)
                w3sd = sb(ph, "h_w3sd", [64, NG, 2, 2, 128], BF16)
                fvec = sb(ph, "h_fvec", [64, 8], F32)
                w1 = sb(ph, "h_w1", [33, 64], F32)
                w2 = sb(ph, "h_w2", [64, 64], F32)
                dl = sb(ph, "h_dl", [64, MW], F32)
                fb = sb(ph, "h_fb", [1, 2 * MW], F32)
                S.dma("sp", fvec.t[:, 0:3], FVEC, fvec.r, writes=[fvec.r])
                S.dma("sp", w1.t[:], FW1, w1.r, writes=[w1.r])
                S.dma("sp", w2.t[:], FW2, w2.r, writes=[w2.r])
                S.dma("sp", dl.t[:], ROWV[:, 4 * D + 2 * MW:4 * D + 3 * MW].partition_broadcast(64), dl.r, writes=[dl.r])
                S.dma("sp", fb.t[:], FBIAS, fb.r, writes=[fb.r])
                S.op("dve", lambda: nc.vector.tensor_tensor(out=fvec.t[:, 3:4], in0=fvec.t[:, 0:1], in1=fvec.t[:, 1:2], op=ALU.mult),
                     reads=[fvec.r], writes=[fvec.r])
                S.op("dve", lambda: nc.vector.tensor_tensor(out=fvec.t[:, 4:5], in0=fvec.t[:, 0:1], in1=fvec.t[:, 2:3], op=ALU.mult),
                     reads=[fvec.r], writes=[fvec.r])
                with ExitStack() as tmp:
                    w3r = sb(tmp, "h_w3r", [64, NG, 2, 2, 128], F32)
                    S.dma("sp", w3r.t[:].rearrange("p g o d c -> p (g o d c)"), FW3, w3r.r, writes=[w3r.r])
                    S.op("dve", lambda: nc.vector.tensor_tensor(out=w3sd.t[:, :, :, 0, :], in0=w3r.t[:, :, :, 0, :],
                                                                in1=w3r.t[:, :, :, 1, :], op=ALU.add),
                         reads=[w3r.r], writes=[w3sd.r])
                    S.op("dve", lambda: nc.vector.tensor_tensor(out=w3sd.t[:, :, :, 1, :], in0=w3r.t[:, :, :, 0, :],
                                                                in1=w3r.t[:, :, :, 1, :], op=ALU.subtract),
                         reads=[w3r.r], writes=[w3sd.r])
                    S.barrier()
                h2T = sb(ph, "h_h2T", [64, 4096], BF16)
                htok = sb(ph, "h_htok", [64, 2, 64, 128], BF16)
                Abuf = sb(ph, "h_A", [64, 64, 2, 128], BF16)
                kst = [sb(ph, "h_kst%d" % i, [64, 8, 2, 128], BF16) for i in range(2)]
                gtab = sb(ph, "h_gtab", [64, 8, 8, 3, 64], BF16)
                dec = [sb(ph, "h_dec%d" % i, [64, 128], F32) for i in range(2)]
                su = sb(ph, "h_su", [64, 64], F32)
                mt_ = [sb(ph, "h_mt%d" % i, [64, 512], F32) for i in range(3)]
                h1c = sb(ph, "h_h1c", [64, 512], F32)
                ft = [sb(ph, "h_ft%d" % i, [33, 512], F32) for i in range(2)]
                t0f = sb(ph, "h_t0f", [1, 256], F32)
                MAGIC = 12582912.0
                TWO_PI = 2.0 * math.pi

                def sin_layer(psum_ap, psum_res, fcol, out_ap, out_res):
                    a, u, k = mt_
                    S.op("dve", lambda: nc.vector.tensor_scalar(out=a.t[:], in0=psum_ap, scalar1=fvec.t[:, 0:1],
                                                                scalar2=fvec.t[:, fcol:fcol + 1], op0=ALU.mult, op1=ALU.add),
                         reads=[psum_res, fvec.r], writes=[a.r])
                    S.op("dve", lambda: nc.vector.tensor_scalar(out=u.t[:], in0=a.t[:], scalar1=1.0 / TWO_PI, scalar2=MAGIC,
                                                                op0=ALU.mult, op1=ALU.add), reads=[a.r], writes=[u.r])
                    S.op("dve", lambda: nc.vector.tensor_scalar(out=k.t[:], in0=u.t[:], scalar1=MAGIC, scalar2=-TWO_PI,
                                                                op0=ALU.subtract, op1=ALU.mult), reads=[u.r], writes=[k.r])
                    S.op("dve", lambda: nc.vector.tensor_tensor(out=a.t[:], in0=a.t[:], in1=k.t[:], op=ALU.add),
                         reads=[a.r, k.r], writes=[a.r])
                    S.op("dve", lambda: nc.vector.tensor_scalar(out=a.t[:], in0=a.t[:], scalar1=3.1415925, scalar2=-3.1415925,
                                                                op0=ALU.min, op1=ALU.max), reads=[a.r], writes=[a.r])
                    S.op("act", lambda: nc.scalar.activation(out=out_ap, in_=a.t[:], func=AF.Sin), reads=[a.r], writes=[out_res])

                for ui, U in enumerate(UNITS):
                    L, N2, nbt = U["L"], U["N2"], U["nb"]
                    S.dma("sp", su.t[:], SU[ui], su.r, writes=[su.r])
                    for kb in range(8):
                        S.dma("pool", gtab.t[:, kb], GT[ui, kb], gtab.r, writes=[gtab.r])
                    for cc in range(L // 512):
                        fbuf = ft[cc % 2]
                        S.dma("sp", fbuf.t[:], FEAT[ui][:, cc * 512:(cc + 1) * 512], fbuf.r, writes=[fbuf.r])
                        mm(PSA[0:64, 0:512], w1.t[:], fbuf.t[:], True, True, [w1.r, fbuf.r], [BK[0]])
                        sin_layer(PSA[0:64, 0:512], BK[0], 3, h1c.t[:], h1c.r)
                        mm(PSA[0:64, 512:1024], w2.t[:], h1c.t[:], True, True, [w2.r, h1c.r], [BK[1]])
                        sin_layer(PSA[0:64, 512:1024], BK[1], 4, h2T.t[:, cc * 512:(cc + 1) * 512], h2T.r)
                    for g in range(NG):
                        for o in range(2):
                            for n2 in range(N2):
                                b = 2 + n2 % 2
                                d_ = dec[n2 % 2]
                                mm(PSA[0:64, b * 512:b * 512 + 256], h2T.t[:, n2:L:N2],
                                   w3sd.t[:, g, o].rearrange("p a c -> p (a c)"), True, True, [h2T.r, w3sd.r], [BK[b]])
                                S.op("act", lambda d_=d_, n2=n2: nc.scalar.activation(
                                    out=d_.t[:], in_=dl.t[:, g * 128:(g + 1) * 128], func=AF.Exp, scale=su.t[:, n2:n2 + 1]),
                                    reads=[dl.r, su.r], writes=[d_.r])
                                for bb in range(nbt):
                                    i = bb * N2 + n2
                                    S.op("dve", lambda b=b, d_=d_, i=i: nc.vector.tensor_tensor(
                                        out=htok.t[:, :, i, :],
                                        in0=PSA[0:64, b * 512:b * 512 + 256].rearrange("p (a c) -> p a c", a=2),
                                        in1=d_.t[:].unsqueeze(1).to_broadcast([64, 2, 128]), op=ALU.mult),
                                        reads=[BK[b], d_.r], writes=[htok.r])
                            for bb in range(nbt):
                                i0 = bb * N2
                                S.op("dve", lambda i0=i0: nc.vector.tensor_tensor(
                                    out=t0f.t[0:1, 0:128], in0=htok.t[0:1, 0, i0, :], in1=htok.t[0:1, 1, i0, :], op=ALU.add),
                                    reads=[htok.r], writes=[t0f.r])
                                S.op("dve", lambda: nc.vector.scalar_tensor_tensor(
                                    out=t0f.t[0:1, 128:256], in0=t0f.t[0:1, 0:128], scalar=0.5,
                                    in1=fb.t[0:1, o * MW + g * 128:o * MW + (g + 1) * 128], op0=ALU.mult, op1=ALU.add),
                                    reads=[t0f.r, fb.r], writes=[t0f.r])
                                for a_ in range(2):
                                    S.op("dve", lambda a_=a_, i0=i0: nc.vector.tensor_copy(
                                        out=htok.t[0:1, a_, i0, :], in_=t0f.t[0:1, 128:256]),
                                        reads=[t0f.r], writes=[htok.r])
                            for a_ in range(2):
                                for c0 in range(0, 128, 4):
                                    b = (c0 // 4) % 4
                                    for c in range(c0, c0 + 4):
                                        mm(PSA[0:64, b * 512 + (c - c0) * 128:b * 512 + (c - c0 + 1) * 128],
                                           htok.t[:, a_, :, c], f1.t[:], True, True, [htok.r, f1.r], [BK[b]])
                                    evac((c0 // 4) % 2, Abuf.t[:, :, :, c0:c0 + 4].rearrange("p k r c -> p (k r) c"),
                                         PSA[0:64, b * 512:(b + 1) * 512].rearrange("p (c x) -> p x c", c=4),
                                         [BK[b]], [Abuf.r])
                                for kb in range(8):
                                    gt = gtab
                                    ks = kst[kb % 2]
                                    for kp in range(4):
                                        b = 4 + (kb * 4 + kp) % 3
                                        for kk in range(2):
                                            kl = kp * 2 + kk
                                            k1 = kb * 8 + kl
                                            v0, v1 = (0, 2) if a_ == 0 else (1, 0)
                                            out = PSA[0:64, b * 512 + kk * 128:b * 512 + (kk + 1) * 128]
                                            mm(out, gt.t[:, kb, kl, v0, :], Abuf.t[:, k1, 0, :], True, False, [gt.r, Abuf.r], [BK[b]])
                                            mm(out, gt.t[:, kb, kl, v1, :], Abuf.t[:, k1, 1, :], False, True, [gt.r, Abuf.r], [BK[b]])
                                        evac(kp % 2, ks.t[:, kp * 2:kp * 2 + 2, a_, :],
                                             PSA[0:64, b * 512:b * 512 + 256].rearrange("p (k c) -> p k c", k=2),
                                             [BK[b]], [ks.r])
                                    S.dma("sp", KF[ui, g, o, kb, :, :, a_, :], ks.t[:, :, a_, :], ks.r, reads=[ks.r])
                S.barrier()

        if "FD" in stages:
            with ExitStack() as ph:
                f1 = load_unit_tables(ph)
                dtb = sb(ph, "h_dtb", [64, 3, 64], BF16)
                vtok = sb(ph, "h_vtok", [64, 128, 64], BF16)
                x1tok = sb(ph, "h_x1tok", [64, 128, 64], BF16)
                zview = vtok.t[:].rearrange("p c i -> p (c i)").rearrange("p (i c) -> p i c", c=128)
                AC = sb(ph, "h_AC", [64, 64, 2, 128], BF16)
                Yb = sb(ph, "h_Y", [64, 2, 64, 128], BF16)
                Xs = [sb(ph, "h_xs%d" % i, [64, 8, 2, 128], BF16) for i in range(2)]
                kfs = [sb(ph, "h_kf%d" % i, [64, 8, 2, 128], BF16) for i in range(2)]
                t1 = sb(ph, "h_t1", [64, 8, 128], F32)
                t2 = sb(ph, "h_t2", [64, 8, 128], F32)
                gtab = sb(ph, "h_gtabd", [64, 8, 8, 3, 64], BF16)
                ets = [sb(ph, "h_et%d" % i, [64, 8, 2, 64], BF16) for i in range(2)]
                x2s = sb(ph, "h_x2s", [128, 4096], BF16)
                mixs = sb(ph, "h_mixs", [128, 4096], BF16)
                units = UNITS if LIMIT is None else UNITS[:LIMIT]
                for ui, U in enumerate(units):
                    L, N2, nbt, tok0 = U["L"], U["N2"], U["nb"], U["tok0"]
                    S.dma("pool", dtb.t[:], DTB[ui], dtb.r, writes=[dtb.r])
                    for kb in range(8):
                        S.dma("pool", gtab.t[:, kb], GT[ui, kb], gtab.r, writes=[gtab.r])
                    for g in range(NG if STOP is None else STOP):
                        for bb in range(nbt):
                            lo = tok0 + bb * L
                            S.dma("sp", vtok.t[:, :, bb * N2:(bb + 1) * N2],
                                  PT[g * 128:(g + 1) * 128, lo:lo + L].rearrange("c (n1 n2) -> n1 c n2", n2=N2),
                                  vtok.r, writes=[vtok.r])
                            S.dma("sp", x1tok.t[:, :, bb * N2:(bb + 1) * N2],
                                  PT[MW + g * 128:MW + (g + 1) * 128, lo:lo + L].rearrange("c (n1 n2) -> n1 c n2", n2=N2),
                                  x1tok.r, writes=[x1tok.r])
                        S.dma("sp", x2s.t[:], PT[2 * MW + g * 128:2 * MW + (g + 1) * 128, tok0:tok0 + 4096], x2s.r,
                              writes=[x2s.r])
                        for o in range(2):
                            for c0 in range(0, 128, 4):
                                b = (c0 // 4) % 4
                                for c in range(c0, c0 + 4):
                                    mm(PSA[0:64, b * 512 + (c - c0) * 128:b * 512 + (c - c0 + 1) * 128],
                                       vtok.t[:, c, :] if o == 0 else zview[:, :, c], f1.t[:], True, True,
                                       [vtok.r, f1.r], [BK[b]])
                                evac(0, AC.t[:, :, :, c0:c0 + 4].rearrange("p k r c -> p (k r) c"),
                                     PSA[0:64, b * 512:(b + 1) * 512].rearrange("p (c x) -> p x c", c=4),
                                     [BK[b]], [AC.r])
                            for kb in range(8):
                                gt, kf, xs = gtab, kfs[kb % 2], Xs[kb % 2]
                                S.dma("sp", kf.t[:], KF[ui, g, o, kb], kf.r, writes=[kf.r])
                                for kp in range(4):
                                    b = 4 + (kb * 4 + kp) % 3
                                    for kk in range(2):
                                        kl = kp * 2 + kk
                                        k1 = kb * 8 + kl
                                        ore = PSA[0:64, b * 512 + kk * 256:b * 512 + kk * 256 + 128]
                                        oim = PSA[0:64, b * 512 + kk * 256 + 128:b * 512 + kk * 256 + 256]
                                        mm(ore, gt.t[:, kb, kl, 0, :], AC.t[:, k1, 0, :], True, False, [gt.r, AC.r], [BK[b]])
                                        mm(ore, gt.t[:, kb, kl, 2, :], AC.t[:, k1, 1, :], False, True, [gt.r, AC.r], [BK[b]])
                                        mm(oim, gt.t[:, kb, kl, 1, :], AC.t[:, k1, 0, :], True, False, [gt.r, AC.r], [BK[b]])
                                        mm(oim, gt.t[:, kb, kl, 0, :], AC.t[:, k1, 1, :], False, True, [gt.r, AC.r], [BK[b]])
                                    evac(0, xs.t[:, kp * 2:kp * 2 + 2, :, :].rearrange("p k r c -> p (k r c)"),
                                         PSA[0:64, b * 512:(b + 1) * 512], [BK[b]], [xs.r])
                                xre, xim = xs.t[:, :, 0, :], xs.t[:, :, 1, :]
                                kre, kim = kf.t[:, :, 0, :], kf.t[:, :, 1, :]
                                yre = Yb.t[:, 0, kb * 8:(kb + 1) * 8, :]
                                yim = Yb.t[:, 1, kb * 8:(kb + 1) * 8, :]
                                S.op("dve", lambda: nc.vector.tensor_tensor(out=t1.t[:], in0=xre, in1=kre, op=ALU.mult),
                                     reads=[xs.r, kf.r], writes=[t1.r])
                                S.op("dve", lambda: nc.vector.tensor_tensor(out=t2.t[:], in0=xim, in1=kim, op=ALU.mult),
                                     reads=[xs.r, kf.r], writes=[t2.r])
                                S.op("dve", lambda: nc.vector.tensor_tensor(out=yre, in0=t1.t[:], in1=t2.t[:], op=ALU.subtract),
                                     reads=[t1.r, t2.r], writes=[Yb.r])
                                S.op("dve", lambda: nc.vector.tensor_tensor(out=t1.t[:], in0=xre, in1=kim, op=ALU.mult),
                                     reads=[xs.r, kf.r], writes=[t1.r])
                                S.op("dve", lambda: nc.vector.tensor_tensor(out=t2.t[:], in0=xim, in1=kre, op=ALU.mult),
                                     reads=[xs.r, kf.r], writes=[t2.r])
                                S.op("dve", lambda: nc.vector.tensor_tensor(out=yim, in0=t1.t[:], in1=t2.t[:], op=ALU.add),
                                     reads=[t1.r, t2.r], writes=[Yb.r])
                            for c0 in range(0, 128, 4):
                                b = (c0 // 4) % 4
                                for c in range(c0, c0 + 4):
                                    out = PSA[0:64, b * 512 + (c - c0) * 128:b * 512 + (c - c0 + 1) * 128]
                                    mm(out, Yb.t[:, 0, :, c], dtb.t[:, 1:3, :].rearrange("p v i -> p (v i)"), True, False,
                                       [Yb.r, dtb.r], [BK[b]])
                                    mm(out, Yb.t[:, 1, :, c], dtb.t[:, 0:2, :].rearrange("p v i -> p (v i)"), False, True,
                                       [Yb.r, dtb.r], [BK[b]])
                                evac((c0 // 4) % 2, AC.t[:, :, :, c0:c0 + 4],
                                     PSA[0:64, b * 512:(b + 1) * 512].rearrange("p (c r i) -> p i r c", c=4, r=2),
                                     [BK[b]], [AC.r])
                            for ib in range(8):
                                et = ets[ib % 2]
                                S.dma("pool", et.t[:], ET[ui, ib], et.r, writes=[et.r])
                                if o == 0:
                                    for ip in range(2):
                                        b = 4 + (ib * 2 + ip) % 3
                                        for il4 in range(4):
                                            il = ip * 4 + il4
                                            i = ib * 8 + il
                                            out = PSA[0:64, b * 512 + il4 * 128:b * 512 + (il4 + 1) * 128]
                                            mm(out, et.t[:, il, 0, :], AC.t[:, i, 0, :], True, False, [et.r, AC.r], [BK[b]])
                                            mm(out, et.t[:, il, 1, :], AC.t[:, i, 1, :], False, True, [et.r, AC.r], [BK[b]])
                                        i0 = ib * 8 + ip * 4
                                        S.op("dve", lambda b=b, i0=i0: nc.vector.tensor_tensor(
                                            out=zview[:, i0:i0 + 4, :],
                                            in0=PSA[0:64, b * 512:(b + 1) * 512].rearrange("p (i c) -> p i c", i=4),
                                            in1=x1tok.t[:, :, i0:i0 + 4].rearrange("p c i -> p i c"), op=ALU.mult),
                                            reads=[BK[b], x1tok.r], writes=[vtok.r])
                                else:
                                    b = 4 + ib % 3
                                    for il in range(8):
                                        i = ib * 8 + il
                                        out = PSA[:, b * 512 + il * 64:b * 512 + (il + 1) * 64]
                                        mm(out, AC.t[:, i, 0, :], et.t[:, il, 0, :], True, False, [et.r, AC.r], [BK[b]])
                                        mm(out, AC.t[:, i, 1, :], et.t[:, il, 1, :], False, True, [et.r, AC.r], [BK[b]])
                                    bb = (ib * 8) // N2
                                    n20 = (ib * 8) % N2
                                    view = lambda t: t[:, bb * L:(bb + 1) * L].rearrange("p (t1 n2) -> p n2 t1", n2=N2)[:, n20:n20 + 8, :]
                                    S.op("dve", lambda b=b, view=view: nc.vector.tensor_tensor(
                                        out=view(mixs.t), in0=PSA[:, b * 512:(b + 1) * 512].rearrange("p (i t) -> p i t", i=8),
                                        in1=view(x2s.t), op=ALU.mult),
                                        reads=[BK[b], x2s.r], writes=[mixs.r])
                        S.dma("sp", MIXT[g * 128:(g + 1) * 128, tok0:tok0 + 4096], mixs.t[:], mixs.r, reads=[mixs.r])
                S.barrier()
        if "D" in stages:
            with ExitStack() as ph:
                wout = sb(ph, "d_wout", [128, 8, D], BF16)
                S.dma("pool", wout.t[:], W_OUT[1].rearrange("(j p) n -> p j n", p=128), wout.r, writes=[wout.r])
                gpost = sb(ph, "d_gpost", [128, D], F32)
                S.dma("sp", gpost.t[:], ROWV[:, D:2 * D].partition_broadcast(128), gpost.r, writes=[gpost.r])
                wk = mk_normwk(ph, "d")
                ek = mk_epi(ph, "d")
                xin = [sb(ph, "dx%d" % i, [128, D], F32) for i in range(2)]
                cat = [sb(ph, "d_cat%d" % i, [128, 8, 128], BF16) for i in range(2)]
                nch = LIMIT or NT // 128
                for c in range(nch):
                    xb, cb = xin[c % 2], cat[c % 2]
                    t0 = c * 128
                    S.dma("sp", xb.t[:], X2[t0:t0 + 128, :], xb.r, writes=[xb.r])
                    S.dma("sp", cb.t[:, 0:6, :], MIXT.rearrange("(k p) t -> p k t", p=128)[:, :, t0:t0 + 128], cb.r,
                          writes=[cb.r])
                    S.dma("sp", cb.t[:, 6:8, :], ATT.rearrange("(k p) t -> p k t", p=128)[:, :, t0:t0 + 128], cb.r,
                          writes=[cb.r])
                    for nb in range(2):
                        for k in range(8):
                            mm(ps(3 + nb), cb.t[:, k, :], wout.t[:, k, nb * 512:(nb + 1) * 512], k == 0, k == 7,
                               [cb.r, wout.r], [BK[3 + nb]])
                    epilogue(ph, (3, 4), xb.t[:], xb.r, gpost.t[:], gpost.r, t0, X3, CV_FFNPRE[1], HNT3, wk, ek)
                S.barrier()
        if "B1" in stages:
            ffn_phase(1, X3, HNT3, Y, None, None)

        S.barrier()
        print("program: %d instructions, %d waits, %d dma sems" % (S.ninst, S.nwait, len(S.chans)))
    return nc


def _hyena_tables():
    t = {}
    n1 = np.arange(64)[:, None]
    k1 = np.arange(64)[None, :]
    ang = 2 * np.pi * n1 * (k1 + 0.5) / 128.0
    f1 = np.zeros((64, 128), np.float64)
    f1[:, 0::2] = np.cos(ang)
    f1[:, 1::2] = -np.sin(ang)
    t["f1t"] = f1.astype(np.float32)
    gt = np.zeros((2, 8, 64, 8, 3, 64), np.float64)
    dtb = np.zeros((2, 64, 3, 64), np.float64)
    et = np.zeros((2, 8, 64, 8, 2, 64), np.float64)
    su = np.zeros((2, 64, 64), np.float64)
    for ui, (L, N2, nb) in enumerate(((2048, 32, 2), (4096, 64, 1))):
        N = 2 * L
        n2 = np.arange(N2)[:, None]
        k2 = np.arange(N2)[None, :]
        for k1v in range(64):
            G = np.exp(-2j * np.pi * n2 * (k1v + 128 * k2 + 0.5) / N)
            Gf = np.zeros((64, 64), np.complex128)
            for b in range(nb):
                Gf[b * N2:(b + 1) * N2, b * N2:(b + 1) * N2] = G
            kb, kl = divmod(k1v, 8)
            gt[ui, kb, :, kl, 0, :] = Gf.real
            gt[ui, kb, :, kl, 1, :] = Gf.imag
            gt[ui, kb, :, kl, 2, :] = -Gf.imag
        Dm = np.exp(2j * np.pi * np.arange(N2)[:, None] * np.arange(N2)[None, :] / N2)
        Df = np.zeros((64, 64), np.complex128)
        for b in range(nb):
            Df[b * N2:(b + 1) * N2, b * N2:(b + 1) * N2] = Dm
        dtb[ui, :, 0, :] = -Df.imag
        dtb[ui, :, 1, :] = Df.real
        dtb[ui, :, 2, :] = Df.imag
        k1c = np.arange(64)[:, None]
        t1 = np.arange(64)[None, :]
        for i in range(64):
            t2 = i % N2
            E = (2.0 / N) * np.exp(2j * np.pi * (t2 + N2 * t1) * (k1c + 0.5) / N)
            ib, il = divmod(i, 8)
            et[ui, ib, :, il, 0, :] = E.real
            et[ui, ib, :, il, 1, :] = -E.imag
        nn1 = np.arange(64)[:, None]
        nn2 = np.arange(64)[None, :]
        su[ui] = np.where(nn2 < N2, -(N2 * nn1 + nn2) / (L - 1.0), 0.0)
        tt = np.linspace(0.0, 1.0, L)[:, None]
        w = (2.0 * np.pi / L) * np.arange(L)[:, None]
        f = np.linspace(1e-4, 15.0, 16)[None, :]
        feat = np.concatenate([tt, np.cos(f * w), -np.sin(f * w)], axis=-1)
        t["featP" if ui == 0 else "featS"] = np.ascontiguousarray(feat.T).astype(np.float32)
    t["gt"] = gt.astype(np.float32)
    t["dtb"] = dtb.astype(np.float32)
    t["et"] = et.astype(np.float32)
    t["su"] = su.astype(np.float32)
    return t


def _prep_shared(inp):
    f = lambda a: np.ascontiguousarray(np.asarray(a, dtype=np.float32))
    sh = {}
    sh["a_w_in"] = f(inp["a_w_in"][0])
    sh["b_w_in"] = f(inp["b_w_in"][0])
    sh["w_kv"] = f(inp["xattn_w_kv"])
    sh["w_out"] = f(inp["w_out"])
    sh["w_up"] = f(inp["ffn_w_up"])
    sh["w_dn"] = f(inp["ffn_w_down"])
    sh["wsT"] = f(np.transpose(np.asarray(inp["a_w_s"][0]), (2, 0, 1)))
    sh["bsT"] = f(np.transpose(np.asarray(inp["a_b_s"][0]), (1, 0)))

    def cols(v):
        v = np.asarray(v, dtype=np.float32)
        return v.reshape(-1, 128).T

    cl = []
    for nm in ("norm_mix_pre", "norm_ffn_pre"):
        pass
    a = inp
    cl += [cols(a["norm_mix_pre"][0]), cols(a["norm_ffn_pre"][0]), cols(a["norm_mix_pre"][1]), cols(a["norm_ffn_pre"][1])]
    cl += [cols(a["norm_mem"][0]), cols(a["norm_mem"][1])]
    for l in range(2):
        for k in range(3):
            cl.append(cols(a["ffn_conv_w"][l][k]))
        cl.append(cols(a["ffn_conv_b"][l]))
    for k in range(3):
        cl.append(cols(a["b_sconv_w"][0][k]))
    cl.append(cols(a["b_sconv_b"][0]))
    colv = np.concatenate(cl, axis=1)
    assert colv.shape == (128, NCV), colv.shape
    sh["colv"] = f(colv)
    deltas = np.abs(np.linspace(math.log(1e-2) / 1.5, math.log(1e-2) / 0.3, MW, dtype=np.float32))
    rowv = np.concatenate([np.asarray(a["norm_mix_post"][0]), np.asarray(a["norm_mix_post"][1]),
                           np.asarray(a["norm_ffn_post"][0]), np.asarray(a["norm_ffn_post"][1]),
                           np.asarray(a["a_ln_g"][0]), np.asarray(a["a_ln_b"][0]), deltas]).astype(np.float32)
    sh["rowv"] = f(rowv[None, :])
    sh["ident"] = np.eye(128, dtype=np.float32)
    sh.update(_hyena_tables())
    sh["fw1"] = f(a["b_filt_w1"][0])
    sh["fw2"] = f(a["b_filt_w2"][0])
    w3 = np.asarray(a["b_filt_w3"][0], dtype=np.float32).reshape(64, 2, 2, 6, 128)
    sh["fw3"] = f(np.transpose(w3, (0, 3, 2, 1, 4)).reshape(64, 3072))
    sh["fvec"] = f(np.stack([np.asarray(a["b_filt_freq"][0]), np.asarray(a["b_filt_b1"][0]),
                             np.asarray(a["b_filt_b2"][0])], axis=1))
    sh["fbias"] = f(np.asarray(a["b_filt_bias"][0]).reshape(1, 1536))
    return sh


def _core_inputs(inp, i):
    xp = np.asarray(inp["x_prompt"])
    xs = np.asarray(inp["x_sample"])
    mp = np.asarray(inp["mem_prompt"])
    ms = np.asarray(inp["mem_sample"])
    X = np.concatenate([xp[2 * i], xp[2 * i + 1], xs[i]], axis=0)
    M = np.concatenate([mp[2 * i], mp[2 * i + 1], ms[i]], axis=0)
    return {"X": np.ascontiguousarray(X, dtype=np.float32), "MEM": np.ascontiguousarray(M, dtype=np.float32)}


def kernel(**inputs):
    sh = _prep_shared(inputs)
    nc = build()
    in_maps = []
    for i in range(8):
        m = dict(sh)
        m.update(_core_inputs(inputs, i))
        in_maps.append(m)
    res = run_bass_kernel_spmd(nc, in_maps, core_ids=list(range(8)))
    yp = np.zeros((16, 2048, D), np.float32)
    ys = np.zeros((8, 4096, D), np.float32)
    for i in range(8):
        y = res.results[i]["Y"]
        yp[2 * i] = y[0:2048]
        yp[2 * i + 1] = y[2048:4096]
        ys[i] = y[4096:8192]
    return (yp, ys)
```

```python
import math
from contextlib import ExitStack

import numpy as np
import concourse.bass as bass
import concourse.mybir as mybir
from concourse.bass_utils import run_bass_kernel_spmd

F32 = mybir.dt.float32
BF16 = mybir.dt.bfloat16
AF = mybir.ActivationFunctionType
ALU = mybir.AluOpType

NT = 8192
SEQS = [(0, 2048), (2048, 2048), (4096, 4096)]
NORM_EPS = 1e-6
LN_EPS = 1e-5
D = 1024
MW = 768
DFF = 2816
NCH_FF = 22
GELU = AF.Gelu_apprx_tanh
STOP = None
LIMIT = None

CV_MIXPRE = [0, 16]
CV_FFNPRE = [8, 24]
CV_MEM = [32, 40]
CV_FFC = [48, 48 + 88]
CV_SC = 48 + 176
NCV = CV_SC + 72


class Res:
    __slots__ = ("name", "w", "r", "chan", "excl")

    def __init__(self, name, excl=False):
        self.name = name
        self.w = None
        self.r = []
        self.chan = None
        self.excl = excl


class Chan:
    __slots__ = ("sem", "cnt", "q")

    def __init__(self, sem):
        self.sem = sem
        self.cnt = 0
        self.q = None


class TB:
    def __init__(self, t, name):
        self.t = t
        self.r = Res(name)


class Sched:
    def __init__(self, nc, stack):
        self.nc = nc
        self.stack = stack
        self.eng = {"pe": nc.tensor, "act": nc.scalar, "dve": nc.vector,
                    "pool": nc.gpsimd, "sp": nc.sync}
        self.esem = {}
        self.ecnt = {}
        for e in ("pe", "act", "dve", "pool"):
            self.esem[e] = stack.enter_context(nc.semaphore("s_" + e))
            self.ecnt[e] = 0
        self.known = {e: {} for e in self.eng}
        self.chans = []
        self.free = {"sp": [], "pool": []}
        self.ninst = 0
        self.nwait = 0

    def _need(self, e, ev, same_ok):
        if ev is None:
            return
        sem, val = ev
        if same_ok and e in self.esem and sem is self.esem[e]:
            return
        k = self.known[e]
        if k.get(id(sem), 0) >= val:
            return
        k[id(sem)] = val
        self.eng[e].wait_ge(sem, val)
        self.nwait += 1

    def _deps(self, e, reads, writes, same_ok):
        for r in reads:
            self._need(e, r.w, same_ok)
        for r in writes:
            self._need(e, r.w, same_ok)
            for ev in r.r:
                self._need(e, ev, same_ok)

    def _post(self, ev, reads, writes):
        for r in reads:
            r.r.append(ev)
            if len(r.r) > 10:
                d = {}
                for s, v in r.r:
                    if id(s) not in d or d[id(s)][1] < v:
                        d[id(s)] = (s, v)
                r.r = list(d.values())
        for r in writes:
            r.w = ev
            r.r = []

    def op(self, e, fn, reads=(), writes=()):
        ex = [r for r in reads if r.excl]
        if ex:
            writes = list(writes) + ex
            reads = [r for r in reads if not r.excl]
        self._deps(e, reads, writes, same_ok=(e == "pe"))
        ins = fn()
        self.ecnt[e] += 1
        ins.then_inc(self.esem[e], 1)
        ev = (self.esem[e], self.ecnt[e])
        self._post(ev, reads, writes)
        self.ninst += 1
        return ev

    def dma(self, q, out, in_, chan, reads=(), writes=()):
        skip = chan.chan.sem if chan.chan is not None else None
        for r in reads:
            self._need(q, r.w, False)
        for r in writes:
            if not (r.w is not None and r.w[0] is skip and r is chan and not r.r):
                self._need(q, r.w, False)
            for ev in r.r:
                self._need(q, ev, False)
        if chan.chan is None:
            if self.free[q]:
                chan.chan = self.free[q].pop()
            else:
                c = Chan(self.stack.enter_context(self.nc.semaphore("d%s%d" % (q, len(self.chans)))))
                c.q = q
                self.chans.append(c)
                chan.chan = c
        c = chan.chan
        assert c.q == q, "a DMA channel semaphore must stay on one queue type"
        ins = self.eng[q].dma_start(out=out, in_=in_)
        c.cnt += 16
        ins.then_inc(c.sem, 16)
        ev = (c.sem, c.cnt)
        self._post(ev, reads, writes)
        self.ninst += 1
        return ev

    def barrier(self):
        for e in self.eng:
            for e2 in self.esem:
                if self.ecnt[e2]:
                    self._need(e, (self.esem[e2], self.ecnt[e2]), False)
            for c in self.chans:
                if c.cnt:
                    self._need(e, (c.sem, c.cnt), False)
        self.free = {"sp": [c for c in self.chans if c.q == "sp"], "pool": [c for c in self.chans if c.q == "pool"]}


def build(stages=("kv", "A", "B0", "C", "FF", "FD", "D", "B1"), dbg=(), ext=()):
    nc = bass.Bass("TRN2", target_bir_lowering=False)

    def din(name, shape, dt=F32):
        return nc.dram_tensor(name, list(shape), dt, kind="ExternalInput").ap()

    def dscr(name, shape, dt):
        kind = "ExternalOutput" if name in dbg else ("ExternalInput" if name in ext else "Internal")
        return nc.dram_tensor(name, list(shape), dt, kind=kind).ap()

    X0 = din("X", [NT, D])
    MEM = din("MEM", [768, D])
    A_W_IN = din("a_w_in", [D, 1792])
    B_W_IN = din("b_w_in", [D, 2560])
    W_KV = din("w_kv", [2, D, 512])
    W_OUT = din("w_out", [2, D, D])
    W_UP = din("w_up", [2, D, 2 * DFF])
    W_DN = din("w_dn", [2, DFF, D])
    WST = din("wsT", [128, 12, 128])
    BST = din("bsT", [128, 12])
    COLV = din("colv", [128, NCV])
    ROWV = din("rowv", [1, 4 * D + 3 * MW])
    IDENT = din("ident", [128, 128])
    F1T = din("f1t", [64, 128])
    GT = din("gt", [2, 8, 64, 8, 3, 64])
    DTB = din("dtb", [2, 64, 3, 64])
    ET = din("et", [2, 8, 64, 8, 2, 64])
    FEAT = [din("featP", [33, 2048]), din("featS", [33, 4096])]
    SU = din("su", [2, 64, 64])
    FW1 = din("fw1", [33, 64])
    FW2 = din("fw2", [64, 64])
    FW3 = din("fw3", [64, 3072])
    FVEC = din("fvec", [64, 3])
    FBIAS = din("fbias", [1, 1536])
    Y = nc.dram_tensor("Y", [NT, D], F32, kind="ExternalOutput").ap()

    X1 = dscr("X1", [NT, D], F32)
    X2 = dscr("X2", [NT, D], F32)
    X3 = dscr("X3", [NT, D], F32)
    HNT1 = dscr("HNT1", [D, NT], BF16)
    HNT2 = dscr("HNT2", [D, NT], BF16)
    HNT3 = dscr("HNT3", [D, NT], BF16)
    PT = dscr("PT", [2304, NT], BF16)
    ATT = dscr("ATT", [256, NT], BF16)
    MIXT = dscr("MIXT", [MW, NT], BF16)
    KF = dscr("KF", [2, 6, 2, 8, 64, 8, 2, 128], BF16)

    with ExitStack() as st:
        S = Sched(nc, st)

        uid = [0]

        def sb(stack, name, shape, dt):
            uid[0] += 1
            return TB(stack.enter_context(nc.sbuf_tensor("sb%d_%s" % (uid[0], name), list(shape), dt)), name)

        PSA = st.enter_context(nc.psum_tensor("PSA", [128, 7 * 512], F32))
        PBt = st.enter_context(nc.psum_tensor("PB", [128, 1024], BF16))
        BK = [Res("bank%d" % i, excl=True) for i in range(7)]
        PBr = Res("pb", excl=True)

        def ps(b, lo=0, hi=512):
            return PSA[:, b * 512 + lo: b * 512 + hi]

        ident = sb(st, "ident", [128, 128], BF16)
        ones = sb(st, "ones", [128, 128], BF16)
        colv = sb(st, "colv", [128, NCV], F32)
        kT = sb(st, "kT", [128, 6, 2, 256], BF16)
        vv = sb(st, "vv", [128, 6, 2, 256], BF16)
        S.dma("pool", ident.t[:], IDENT, ident.r, writes=[ident.r])
        S.dma("sp", colv.t[:], COLV, colv.r, writes=[colv.r])
        S.op("dve", lambda: nc.vector.memset(ones.t[:], 1.0), writes=[ones.r])

        def mm(out, lhsT, rhs, start, stop, reads, writes):
            S.op("pe", lambda: nc.tensor.matmul(out, lhsT=lhsT, rhs=rhs, start=start, stop=stop),
                 reads=reads, writes=writes)

        def evac(which, out_ap, in_ap, reads, writes):
            if which == 0:
                S.op("act", lambda: nc.scalar.copy(out=out_ap, in_=in_ap), reads=reads, writes=writes)
            else:
                S.op("dve", lambda: nc.vector.tensor_copy(out=out_ap, in_=in_ap), reads=reads, writes=writes)

        def norm_T(ph, xt_ap, xt_res, gbase, out_ap, out_res, wk):
            junk, ss, xn = wk["junk"], wk["ss"], wk["xn"]
            S.op("act", lambda: nc.scalar.activation(out=junk.t[:, 0:D], in_=xt_ap, func=AF.Square,
                                                     accum_out=ss.t[:, 0:1]),
                 reads=[xt_res], writes=[junk.r, ss.r])
            S.op("act", lambda: nc.scalar.activation(out=ss.t[:, 1:2], in_=ss.t[:, 0:1], func=AF.Sqrt,
                                                     scale=1.0 / D, bias=wk["eps"].t[:, 0:1]),
                 reads=[ss.r, wk["eps"].r], writes=[ss.r])
            S.op("dve", lambda: nc.vector.reciprocal(out=ss.t[:, 2:3], in_=ss.t[:, 1:2]),
                 reads=[ss.r], writes=[ss.r])
            S.op("dve", lambda: nc.vector.tensor_scalar(out=xn.t[:], in0=xt_ap, scalar1=ss.t[:, 2:3],
                                                        scalar2=None, op0=ALU.mult),
                 reads=[xt_res, ss.r], writes=[xn.r])
            for j in range(8):
                S.op("pe", lambda j=j: nc.tensor.transpose(out=PBt[:, j * 128:(j + 1) * 128],
                                                           in_=xn.t[:, j * 128:(j + 1) * 128],
                                                           identity=ident.t[:]),
                     reads=[xn.r, ident.r], writes=[PBr])
            S.op("dve", lambda: nc.vector.tensor_tensor(
                out=out_ap, in0=PBt[:, 0:1024].rearrange("p (j t) -> p j t", j=8),
                in1=colv.t[:, gbase:gbase + 8].unsqueeze(2).to_broadcast([128, 8, 128]), op=ALU.mult),
                reads=[PBr, colv.r], writes=[out_res])

        def mk_normwk(ph, tag):
            wk = {"junk": sb(ph, "junk" + tag, [128, D], BF16),
                  "ss": sb(ph, "ss" + tag, [128, 4], F32),
                  "xn": sb(ph, "xn" + tag, [128, D], BF16),
                  "eps": sb(ph, "eps" + tag, [128, 2], F32)}
            S.op("dve", lambda: nc.vector.memset(wk["eps"].t[:, 0:1], NORM_EPS), writes=[wk["eps"].r])
            S.op("dve", lambda: nc.vector.memset(wk["eps"].t[:, 1:2], LN_EPS), writes=[wk["eps"].r])
            return wk

        def hnt_view(H):
            return H.rearrange("(j p) t -> p j t", p=128)

        def epilogue(ph, banks, xres_ap, xres_res, gpost_ap, gpost_res, tok0, XOUT, gnext, HOUT, wk, ek):
            b0 = banks[0]
            pout = PSA[:, b0 * 512: b0 * 512 + 1024]
            br = [BK[b] for b in banks]
            junk, ss = wk["junk"], ek["ss"]
            S.op("act", lambda: nc.scalar.activation(out=junk.t[:, 0:D], in_=pout, func=AF.Square,
                                                     accum_out=ss.t[:, 0:1]),
                 reads=br, writes=[junk.r, ss.r])
            S.op("act", lambda: nc.scalar.activation(out=ss.t[:, 1:2], in_=ss.t[:, 0:1], func=AF.Sqrt,
                                                     scale=1.0 / D, bias=wk["eps"].t[:, 0:1]),
                 reads=[ss.r, wk["eps"].r], writes=[ss.r])
            S.op("dve", lambda: nc.vector.reciprocal(out=ss.t[:, 2:3], in_=ss.t[:, 1:2]),
                 reads=[ss.r], writes=[ss.r])
            tp = ek["tp"]
            S.op("dve", lambda: nc.vector.tensor_tensor(out=tp.t[:], in0=pout, in1=gpost_ap, op=ALU.mult),
                 reads=br + [gpost_res], writes=[tp.r])
            xnew = ek["xnew"]
            S.op("dve", lambda: nc.vector.scalar_tensor_tensor(out=xnew.t[:], in0=tp.t[:], scalar=ss.t[:, 2:3],
                                                               in1=xres_ap, op0=ALU.mult, op1=ALU.add),
                 reads=[tp.r, ss.r, xres_res], writes=[xnew.r])
            S.dma("pool", XOUT[tok0:tok0 + 128, :], xnew.t[:], ek["xst"], reads=[xnew.r])
            if HOUT is not None:
                hno = ek["hno"]
                norm_T(ph, xnew.t[:], xnew.r, gnext, hno.t[:], hno.r, wk)
                S.dma("pool", hnt_view(HOUT)[:, :, tok0:tok0 + 128], hno.t[:], ek["hst"], reads=[hno.r])

        def mk_epi(ph, tag):
            return {"ss": sb(ph, "ess" + tag, [128, 4], F32),
                    "tp": sb(ph, "tp" + tag, [128, D], F32),
                    "xnew": sb(ph, "xnew" + tag, [128, D], F32),
                    "hno": sb(ph, "hno" + tag, [128, 8, 128], BF16),
                    "xst": Res("xst" + tag), "hst": Res("hst" + tag)}

        def attention(hn_ap, hn_res, wq_ap, wq_res, ls, cat, wk, banks):
            bq, bs0, bs1, bav0, bav1 = banks
            qT, pT, rden = wk["qT"], wk["pT"], wk["rden"]
            for hc in range(2):
                for j in range(8):
                    mm(ps(bq, hc * 128, hc * 128 + 128), wq_ap(j, hc), hn_ap(j), j == 0, j == 7,
                       [wq_res, hn_res], [BK[bq]])
            S.op("act", lambda: nc.scalar.activation(out=qT.t[:].rearrange("p a t -> p (a t)"), in_=ps(bq, 0, 256),
                                                     func=AF.Copy, scale=0.125),
                 reads=[BK[bq]], writes=[qT.r])
            for h in range(4):
                hc, po = h // 2, (h % 2) * 64
                for mc in range(2):
                    idx = (h % 2) * 4 + (h // 2) * 2 + mc
                    b = bs0 if idx < 4 else bs1
                    col = (idx % 4) * 128
                    mm(PSA[:, b * 512 + col: b * 512 + col + 128],
                       kT.t[po:po + 64, ls, hc, mc * 128:(mc + 1) * 128], qT.t[po:po + 64, hc, :], True, True,
                       [kT.r, qT.r], [BK[b]])
            for half, b in ((0, bs0), (1, bs1)):
                S.op("act", lambda half=half, b=b: nc.scalar.activation(
                    out=pT.t[:, half * 4:(half + 1) * 4, :].rearrange("p a t -> p (a t)"), in_=ps(b), func=AF.Exp),
                    reads=[BK[b]], writes=[pT.r])
            for hc in range(2):
                b = bav0 if hc == 0 else bav1
                for part in range(4):
                    h = 2 * hc + (part % 2)
                    for mc in range(2):
                        lhsT = vv.t[:, ls, mc, hc * 128:(hc + 1) * 128] if part < 2 else ones.t[:]
                        mm(ps(b, part * 128, part * 128 + 128), lhsT, pT.t[:, (h % 2) * 4 + (h // 2) * 2 + mc, :], mc == 0, mc == 1,
                           [vv.r, ones.r, pT.r], [BK[b]])
                for hh in range(2):
                    lo = hh * 64
                    S.op("dve", lambda hh=hh, lo=lo, b=b, hc=hc: nc.vector.reciprocal(
                        out=rden.t[lo:lo + 64, hc, :], in_=PSA[lo:lo + 64, b * 512 + (2 + hh) * 128: b * 512 + (3 + hh) * 128]),
                        reads=[BK[b]], writes=[rden.r])
                for hh in range(2):
                    lo = hh * 64
                    S.op("dve", lambda hh=hh, lo=lo, b=b, hc=hc: nc.vector.tensor_tensor(
                        out=cat.t[lo:lo + 64, 6 + hc, :], in0=PSA[lo:lo + 64, b * 512 + hh * 128: b * 512 + (hh + 1) * 128],
                        in1=rden.t[lo:lo + 64, hc, :], op=ALU.mult),
                        reads=[BK[b], rden.r], writes=[cat.r])

        def mk_attwk(ph, tag):
            return {"qT": sb(ph, "qT" + tag, [128, 2, 128], BF16),
                    "pT": sb(ph, "pT" + tag, [128, 8, 128], BF16),
                    "rden": sb(ph, "rden" + tag, [128, 2, 128], F32)}

        def seq_of(tok):
            for si, (t0, L) in enumerate(SEQS):
                if t0 <= tok < t0 + L:
                    return si, t0, L
            raise ValueError

        def load_hnt_tile(HSRC, hb, a, T):
            si, t0, L = seq_of(a)
            lo = a - 1 if a > t0 else a
            hi = a + T + 1 if a + T < t0 + L else a + T
            if lo == a:
                S.op("dve", lambda: nc.vector.memset(hb.t[:, :, 0:1], 0.0), writes=[hb.r])
            if hi == a + T:
                S.op("dve", lambda: nc.vector.memset(hb.t[:, :, T + 1:T + 2], 0.0), writes=[hb.r])
            S.dma("sp", hb.t[:, :, lo - (a - 1): hi - (a - 1)], hnt_view(HSRC)[:, :, lo:hi], hb.r, writes=[hb.r])

        def conv_chunk(hb, T, w_ap, w_res, cv, nchk, ci, bmain, tbuf, fin=None):
            for j in range(8):
                mm(ps(bmain, 0, T + 2), w_ap(j), hb.t[:, j, 0:T + 2], j == 0, j == 7, [w_res, hb.r], [BK[bmain]])
            c0, c1, c2, cb = (colv.t[:, cv + k * nchk + ci: cv + k * nchk + ci + 1] for k in range(4))
            S.op("act", lambda: nc.scalar.activation(out=tbuf.t[:, 0:T], in_=ps(bmain, 1, T + 1), func=AF.Identity,
                                                     scale=c1, bias=cb),
                 reads=[BK[bmain], colv.r], writes=[tbuf.r])
            S.op("dve", lambda: nc.vector.scalar_tensor_tensor(out=tbuf.t[:, 0:T], in0=ps(bmain, 0, T), scalar=c0,
                                                               in1=tbuf.t[:, 0:T], op0=ALU.mult, op1=ALU.add),
                 reads=[BK[bmain], tbuf.r, colv.r], writes=[tbuf.r])
            fo = tbuf if fin is None else fin
            S.op("dve", lambda: nc.vector.scalar_tensor_tensor(out=fo.t[:, 0:T], in0=ps(bmain, 2, T + 2), scalar=c2,
                                                               in1=tbuf.t[:, 0:T], op0=ALU.mult, op1=ALU.add),
                 reads=[BK[bmain], tbuf.r, colv.r], writes=[fo.r])

        if "kv" in stages:
            with ExitStack() as ph:
                wkv = sb(ph, "wkv", [128, 2, 8, 512], BF16)
                for l in range(2):
                    S.dma("pool", wkv.t[:, l], W_KV[l].rearrange("(j p) n -> p j n", p=128), wkv.r, writes=[wkv.r])
                wk = mk_normwk(ph, "kv")
                xin = [sb(ph, "kvx%d" % i, [128, D], F32) for i in range(2)]
                hn = [sb(ph, "kvh%d" % i, [128, 8, 128], BF16) for i in range(2)]
                it = 0
                for l in range(2):
                    for s in range(3):
                        for mt in range(2):
                            xb, hb = xin[it % 2], hn[it % 2]
                            it += 1
                            r0 = s * 256 + mt * 128
                            S.dma("sp", xb.t[:], MEM[r0:r0 + 128, :], xb.r, writes=[xb.r])
                            norm_T(ph, xb.t[:], xb.r, CV_MEM[l], hb.t[:], hb.r, wk)
                            ls = l * 3 + s
                            for hc in range(2):
                                b = hc
                                for j in range(8):
                                    mm(ps(b, 0, 128), wkv.t[:, l, j, hc * 128:(hc + 1) * 128], hb.t[:, j, :], j == 0, j == 7,
                                       [wkv.r, hb.r], [BK[b]])
                                S.op("act", lambda b=b, hc=hc, ls=ls, mt=mt: nc.scalar.copy(
                                    out=kT.t[:, ls, hc, mt * 128:(mt + 1) * 128], in_=ps(b, 0, 128)),
                                    reads=[BK[b]], writes=[kT.r])
                            for j in range(8):
                                mm(ps(2, 0, 256), hb.t[:, j, :], wkv.t[:, l, j, 256:512], j == 0, j == 7,
                                   [wkv.r, hb.r], [BK[2]])
                            S.op("act", lambda ls=ls, mt=mt: nc.scalar.copy(out=vv.t[:, ls, mt, :], in_=ps(2, 0, 256)),
                                 reads=[BK[2]], writes=[vv.r])
                S.barrier()

        if "A" in stages:
            with ExitStack() as ph:
                win = sb(ph, "a_win", [128, 8, 1792], BF16)
                for j in range(8):
                    S.dma("pool", win.t[:, j, :], A_W_IN[j * 128:(j + 1) * 128, :], win.r, writes=[win.r])
                wout = sb(ph, "a_wout", [128, 8, D], BF16)
                S.dma("pool", wout.t[:], W_OUT[0].rearrange("(j p) n -> p j n", p=128), wout.r, writes=[wout.r])
                wst = sb(ph, "a_wst", [128, 12, 128], BF16)
                S.dma("pool", wst.t[:], WST, wst.r, writes=[wst.r])
                bst = sb(ph, "a_bst", [128, 12], F32)
                S.dma("sp", bst.t[:], BST, bst.r, writes=[bst.r])
                gpost = sb(ph, "a_gpost", [128, D], F32)
                S.dma("sp", gpost.t[:], ROWV[:, 0:D].partition_broadcast(128), gpost.r, writes=[gpost.r])
                lng = sb(ph, "a_lng", [128, MW], F32)
                lnb = sb(ph, "a_lnb", [128, MW], F32)
                S.dma("sp", lng.t[:], ROWV[:, 4 * D:4 * D + MW].partition_broadcast(128), lng.r, writes=[lng.r])
                S.dma("sp", lnb.t[:], ROWV[:, 4 * D + MW:4 * D + 2 * MW].partition_broadcast(128), lnb.r, writes=[lnb.r])
                wks = [mk_normwk(ph, "a%d" % i) for i in range(2)]
                eks = [mk_epi(ph, "a%d" % i) for i in range(2)]
                awks = [mk_attwk(ph, "a%d" % i) for i in range(2)]
                xin = [sb(ph, "ax%d" % i, [128, D], F32) for i in range(5)]
                hn = [sb(ph, "ah%d" % i, [128, 8, 128], BF16) for i in range(2)]
                gus = [sb(ph, "a_gu%d" % i, [128, MW], F32) for i in range(2)]
                gvs = [sb(ph, "a_gv%d" % i, [128, MW], F32) for i in range(2)]
                vns = [sb(ph, "a_vn%d" % i, [128, MW], BF16) for i in range(2)]
                tmps = [sb(ph, "a_tmp%d" % i, [128, MW], F32) for i in range(2)]
                mixs_ = [sb(ph, "a_mix%d" % i, [128, MW], BF16) for i in range(2)]
                st6s = [sb(ph, "a_st%d" % i, [128, 2, 6], F32) for i in range(2)]
                mvs = [sb(ph, "a_mv%d" % i, [128, 4], F32) for i in range(2)]
                cat = [sb(ph, "a_cat%d" % i, [128, 8, 128], BF16) for i in range(2)]
                nch = LIMIT or NT // 128

                def load_x(c):
                    xb = xin[c % 5]
                    S.dma("sp", xb.t[:], X0[c * 128:(c + 1) * 128, :], xb.r, writes=[xb.r])

                def chunk_gen(c):
                    if c + 3 < nch:
                        load_x(c + 3)
                    p = c % 2
                    xb, hb, cb = xin[c % 5], hn[p], cat[p]
                    wk, ek, awk = wks[p], eks[p], awks[p]
                    gu, gv, vn, tmp, mix, st6, mv = gus[p], gvs[p], vns[p], tmps[p], mixs_[p], st6s[p], mvs[p]
                    si, _, _ = seq_of(c * 128)
                    norm_T(ph, xb.t[:], xb.r, CV_MIXPRE[0], hb.t[:], hb.r, wk)
                    yield
                    for nb in range(3):
                        for j in range(8):
                            mm(ps(nb), hb.t[:, j, :], win.t[:, j, nb * 512:(nb + 1) * 512], j == 0, j == 7,
                               [hb.r, win.r], [BK[nb]])
                    S.op("act", lambda: nc.scalar.activation(out=gv.t[:], in_=PSA[:, MW:2 * MW], func=GELU),
                         reads=[BK[1], BK[2]], writes=[gv.r])
                    S.op("act", lambda: nc.scalar.activation(out=gu.t[:], in_=PSA[:, 0:MW], func=GELU),
                         reads=[BK[0], BK[1]], writes=[gu.r])
                    for k in range(2):
                        S.op("dve", lambda k=k: nc.vector.bn_stats(out=st6.t[:, k, :], in_=gv.t[:, k * 384:(k + 1) * 384]),
                             reads=[gv.r], writes=[st6.r])
                    S.op("dve", lambda: nc.vector.bn_aggr(out=mv.t[:, 0:2], in_=st6.t[:]), reads=[st6.r], writes=[mv.r])
                    S.op("act", lambda: nc.scalar.activation(out=mv.t[:, 2:3], in_=mv.t[:, 1:2], func=AF.Sqrt,
                                                             scale=1.0, bias=wk["eps"].t[:, 1:2]),
                         reads=[mv.r, wk["eps"].r], writes=[mv.r])
                    S.op("dve", lambda: nc.vector.reciprocal(out=mv.t[:, 3:4], in_=mv.t[:, 2:3]), reads=[mv.r], writes=[mv.r])
                    S.op("dve", lambda: nc.vector.tensor_scalar(out=gv.t[:], in0=gv.t[:], scalar1=mv.t[:, 0:1],
                                                                scalar2=mv.t[:, 3:4], op0=ALU.subtract, op1=ALU.mult),
                         reads=[gv.r, mv.r], writes=[gv.r])
                    S.op("pool", lambda: nc.gpsimd.tensor_tensor(out=gv.t[:], in0=gv.t[:], in1=lng.t[:], op=ALU.mult),
                         reads=[gv.r, lng.r], writes=[gv.r])
                    S.op("pool", lambda: nc.gpsimd.tensor_tensor(out=vn.t[:], in0=gv.t[:], in1=lnb.t[:], op=ALU.add),
                         reads=[gv.r, lnb.r], writes=[vn.r])
                    yield
                    for g in range(12):
                        col = 3 * 512 + g * 64
                        mm(PSA[:, col:col + 64], wst.t[:, g, :], vn.t[:, g * 64:(g + 1) * 64], True, True,
                           [wst.r, vn.r], [BK[3 + (g // 8)]])
                    S.op("dve", lambda: nc.vector.tensor_tensor(
                        out=tmp.t[:].rearrange("p (g d) -> p g d", g=12),
                        in0=PSA[:, 3 * 512:3 * 512 + MW].rearrange("p (g d) -> p g d", g=12),
                        in1=bst.t[:].unsqueeze(2).to_broadcast([128, 12, 64]), op=ALU.add),
                        reads=[BK[3], BK[4], bst.r], writes=[tmp.r])
                    S.op("pool", lambda: nc.gpsimd.tensor_tensor(out=mix.t[:], in0=tmp.t[:], in1=gu.t[:], op=ALU.mult),
                         reads=[tmp.r, gu.r], writes=[mix.r])
                    yield
                    for k in range(6):
                        S.op("pe", lambda k=k: nc.tensor.transpose(out=PBt[:, k * 128:(k + 1) * 128],
                                                                   in_=mix.t[:, k * 128:(k + 1) * 128], identity=ident.t[:]),
                             reads=[mix.r, ident.r], writes=[PBr])
                    S.op("act", lambda: nc.scalar.copy(out=cb.t[:, 0:6, :].rearrange("p a t -> p (a t)"), in_=PBt[:, 0:768]),
                         reads=[PBr], writes=[cb.r])
                    yield
                    attention(lambda j: hb.t[:, j, :], hb.r,
                              lambda j, hc: win.t[:, j, 1536 + hc * 128:1536 + (hc + 1) * 128], win.r,
                              0 * 3 + si, cb, awk, (5, 0, 1, 5, 6))
                    yield
                    for nb in range(2):
                        for k in range(8):
                            mm(ps(3 + nb), cb.t[:, k, :], wout.t[:, k, nb * 512:(nb + 1) * 512], k == 0, k == 7,
                               [cb.r, wout.r], [BK[3 + nb]])
                    epilogue(ph, (3, 4), xb.t[:], xb.r, gpost.t[:], gpost.r, c * 128, X1, CV_FFNPRE[0], HNT1, wk, ek)

                for c in range(min(3, nch)):
                    load_x(c)
                active = []
                nxt = 0
                while active or nxt < nch:
                    if nxt < nch and len(active) < 2:
                        active.append(chunk_gen(nxt))
                        nxt += 1
                    for gen in list(active):
                        try:
                            next(gen)
                        except StopIteration:
                            active.remove(gen)
                if "KTD" in dbg:
                    KTD = dscr("KTD", [128, 6 * 2 * 256], BF16)
                    VVD = dscr("VVD", [128, 6 * 2 * 256], BF16)
                    S.dma("sp", KTD, kT.t[:].rearrange("p a b c -> p (a b c)"), Res("ktd"), reads=[kT.r])
                    S.dma("sp", VVD, vv.t[:].rearrange("p a b c -> p (a b c)"), Res("vvd"), reads=[vv.r])
                S.barrier()

        def ffn_phase(layer, XIN, HIN, XOUT, gnext, HOUT):
            T = 256
            with ExitStack() as ph:
                wup = sb(ph, "wup", [128, 8, 2 * DFF], BF16)
                for j in range(8):
                    S.dma("pool", wup.t[:, j, :], W_UP[layer, j * 128:(j + 1) * 128, :], wup.r, writes=[wup.r])
                wdn = sb(ph, "wdn", [128, NCH_FF, D], BF16)
                for ci in range(NCH_FF):
                    S.dma("pool", wdn.t[:, ci, :], W_DN[layer, ci * 128:(ci + 1) * 128, :], wdn.r, writes=[wdn.r])
                gpost = sb(ph, "f_gpost", [128, D], F32)
                S.dma("sp", gpost.t[:], ROWV[:, (2 + layer) * D:(3 + layer) * D].partition_broadcast(128), gpost.r,
                      writes=[gpost.r])
                wk = mk_normwk(ph, "f")
                ek = mk_epi(ph, "f")
                hb = sb(ph, "f_hb", [128, 8, T + 2], BF16)
                xin = [sb(ph, "fx%d" % i, [128, D], F32) for i in range(2)]
                tb = [sb(ph, "ft%d" % i, [128, T], F32) for i in range(2)]
                hact = sb(ph, "hact", [128, NCH_FF, T], BF16)
                cv = CV_FFC[layer]
                ntile = LIMIT or NT // T
                xcnt = 0
                for ti in range(ntile):
                    a = ti * T
                    load_hnt_tile(HIN, hb, a, T)
                    for ci in range(NCH_FF):
                        tbuf = tb[ci % 2]
                        bm, bu = ci % 2, 2 + ci % 2
                        conv_chunk(hb, T, lambda j: wup.t[:, j, ci * 128:(ci + 1) * 128], wup.r, cv, NCH_FF, ci,
                                   bm, tbuf)
                        S.op("act", lambda tbuf=tbuf: nc.scalar.activation(out=tbuf.t[:, 0:T], in_=tbuf.t[:, 0:T], func=GELU),
                             reads=[tbuf.r], writes=[tbuf.r])
                        for j in range(8):
                            mm(ps(bu, 0, T), wup.t[:, j, DFF + ci * 128:DFF + (ci + 1) * 128], hb.t[:, j, 1:T + 1],
                               j == 0, j == 7, [wup.r, hb.r], [BK[bu]])
                        S.op("dve", lambda tbuf=tbuf, bu=bu, ci=ci: nc.vector.tensor_tensor(
                            out=hact.t[:, ci, :], in0=ps(bu, 0, T), in1=tbuf.t[:, 0:T], op=ALU.mult),
                            reads=[BK[bu], tbuf.r], writes=[hact.r])
                    for tt in range(T // 128):
                        xb = xin[xcnt % 2]
                        xcnt += 1
                        tok0 = a + tt * 128
                        S.dma("sp", xb.t[:], XIN[tok0:tok0 + 128, :], xb.r, writes=[xb.r])
                        for nb in range(2):
                            for ci in range(NCH_FF):
                                mm(ps(5 + nb), hact.t[:, ci, tt * 128:(tt + 1) * 128], wdn.t[:, ci, nb * 512:(nb + 1) * 512],
                                   ci == 0, ci == NCH_FF - 1, [hact.r, wdn.r], [BK[5 + nb]])
                        epilogue(ph, (5, 6), xb.t[:], xb.r, gpost.t[:], gpost.r, tok0, XOUT, gnext, HOUT, wk, ek)
                S.barrier()

        if "B0" in stages:
            ffn_phase(0, X1, HNT1, X2, CV_MIXPRE[1], HNT2)

        if "C" in stages:
            T = 256
            with ExitStack() as ph:
                win = sb(ph, "b_win", [128, 8, 2560], BF16)
                for j in range(8):
                    S.dma("pool", win.t[:, j, :], B_W_IN[j * 128:(j + 1) * 128, :], win.r, writes=[win.r])
                awk = mk_attwk(ph, "c")
                hbs = [sb(ph, "c_hb%d" % i, [128, 8, T + 2], BF16) for i in range(2)]
                tb = [sb(ph, "c_t%d" % i, [128, T], F32) for i in range(2)]
                ob = [sb(ph, "c_o%d" % i, [128, T], BF16) for i in range(3)]
                cat = [sb(ph, "c_cat%d" % i, [128, 8, 128], BF16) for i in range(2)]
                cst = [Res("cst0"), Res("cst1")]
                ntile = LIMIT or NT // T
                oc = 0
                cc = 0
                for ti in range(ntile):
                    a = ti * T
                    hb = hbs[ti % 2]
                    si, _, _ = seq_of(a)
                    load_hnt_tile(HNT2, hb, a, T)
                    for ci in range(18):
                        tbuf = tb[ci % 2]
                        o = ob[oc % 3]
                        oc += 1
                        conv_chunk(hb, T, lambda j: win.t[:, j, ci * 128:(ci + 1) * 128], win.r, CV_SC, 18, ci,
                                   ci % 2, tbuf, fin=o)
                        S.dma("pool", PT[ci * 128:(ci + 1) * 128, a:a + T], o.t[:], o.r, reads=[o.r])
                    for tt in range(T // 128):
                        cb = cat[cc % 2]
                        cc += 1
                        attention(lambda j: hb.t[:, j, 1 + tt * 128:1 + (tt + 1) * 128], hb.r,
                                  lambda j, hc: win.t[:, j, 2304 + hc * 128:2304 + (hc + 1) * 128], win.r,
                                  3 + si, cb, awk, (5, 2, 3, 5, 6))
                        S.dma("pool", ATT.rearrange("(k p) t -> p k t", p=128)[:, :, a + tt * 128:a + (tt + 1) * 128],
                              cb.t[:, 6:8, :], cst[cc % 2], reads=[cb.r])
                S.barrier()


        UNITS = [dict(tok0=0, L=2048, N2=32, nb=2), dict(tok0=4096, L=4096, N2=64, nb=1)]
        NG = 6

        def load_unit_tables(ph):
            f1 = sb(ph, "h_f1", [64, 128], BF16)
            S.dma("pool", f1.t[:], F1T, f1.r, writes=[f1.r])
            return f1

        if "FF" in stages:
            with ExitStack() as ph:
                f1 = load_unit_tables(ph)
                w3sd = sb(ph, "h_w3sd", [64, NG, 2, 2, 128], BF16)
                fvec = sb(ph, "h_fvec", [64, 8], F32)
                w1 = sb(ph, "h_w1", [33, 64], F32)
                w2 = sb(ph, "h_w2", [64, 64], F32)
                dl = sb(ph, "h_dl", [64, MW], F32)
                fb = sb(ph, "h_fb", [1, 2 * MW], F32)
                S.dma("sp", fvec.t[:, 0:3], FVEC, fvec.r, writes=[fvec.r])
                S.dma("sp", w1.t[:], FW1, w1.r, writes=[w1.r])
                S.dma("sp", w2.t[:], FW2, w2.r, writes=[w2.r])
                S.dma("sp", dl.t[:], ROWV[:, 4 * D + 2 * MW:4 * D + 3 * MW].partition_broadcast(64), dl.r, writes=[dl.r])
                S.dma("sp", fb.t[:], FBIAS, fb.r, writes=[fb.r])
                S.op("dve", lambda: nc.vector.tensor_tensor(out=fvec.t[:, 3:4], in0=fvec.t[:, 0:1], in1=fvec.t[:, 1:2], op=ALU.mult),
                     reads=[fvec.r], writes=[fvec.r])
                S.op("dve", lambda: nc.vector.tensor_tensor(out=fvec.t[:, 4:5], in0=fvec.t[:, 0:1], in1=fvec.t[:, 2:3], op=ALU.mult),
                     reads=[fvec.r], writes=[fvec.r])
                with ExitStack() as tmp:
                    w3r = sb(tmp, "h_w3r", [64, NG, 2, 2, 128], F32)
                    S.dma("sp", w3r.t[:].rearrange("p g o d c -> p (g o d c)"), FW3, w3r.r, writes=[w3r.r])
                    S.op("dve", lambda: nc.vector.tensor_tensor(out=w3sd.t[:, :, :, 0, :], in0=w3r.t[:, :, :, 0, :],
                                                                in1=w3r.t[:, :, :, 1, :], op=ALU.add),
                         reads=[w3r.r], writes=[w3sd.r])
                    S.op("dve", lambda: nc.vector.tensor_tensor(out=w3sd.t[:, :, :, 1, :], in0=w3r.t[:, :, :, 0, :],
                                                                in1=w3r.t[:, :, :, 1, :], op=ALU.subtract),
                         reads=[w3r.r], writes=[w3sd.r])
                    S.barrier()
                h2T = sb(ph, "h_h2T", [64, 4096], BF16)
                htok = sb(ph, "h_htok", [64, 2, 64, 128], BF16)
                Abuf = sb(ph, "h_A", [64, 64, 2, 128], BF16)
                kst = [sb(ph, "h_kst%d" % i, [64, 8, 2, 128], BF16) for i in range(2)]
                gtab = sb(ph, "h_gtab", [64, 8, 8, 3, 64], BF16)
                dec = [sb(ph, "h_dec%d" % i, [64, 128], F32) for i in range(2)]
                su = sb(ph, "h_su", [64, 64], F32)
                mt_ = [sb(ph, "h_mt%d" % i, [64, 512], F32) for i in range(3)]
                h1c = sb(ph, "h_h1c", [64, 512], F32)
                ft = [sb(ph, "h_ft%d" % i, [33, 512], F32) for i in range(2)]
                t0f = sb(ph, "h_t0f", [1, 256], F32)
                MAGIC = 12582912.0
                TWO_PI = 2.0 * math.pi

                def sin_layer(psum_ap, psum_res, fcol, out_ap, out_res):
                    a, u, k = mt_
                    S.op("dve", lambda: nc.vector.tensor_scalar(out=a.t[:], in0=psum_ap, scalar1=fvec.t[:, 0:1],
                                                                scalar2=fvec.t[:, fcol:fcol + 1], op0=ALU.mult, op1=ALU.add),
                         reads=[psum_res, fvec.r], writes=[a.r])
                    S.op("dve", lambda: nc.vector.tensor_scalar(out=u.t[:], in0=a.t[:], scalar1=1.0 / TWO_PI, scalar2=MAGIC,
                                                                op0=ALU.mult, op1=ALU.add), reads=[a.r], writes=[u.r])
                    S.op("dve", lambda: nc.vector.tensor_scalar(out=k.t[:], in0=u.t[:], scalar1=MAGIC, scalar2=-TWO_PI,
                                                                op0=ALU.subtract, op1=ALU.mult), reads=[u.r], writes=[k.r])
                    S.op("dve", lambda: nc.vector.tensor_tensor(out=a.t[:], in0=a.t[:], in1=k.t[:], op=ALU.add),
                         reads=[a.r, k.r], writes=[a.r])
                    S.op("dve", lambda: nc.vector.tensor_scalar(out=a.t[:], in0=a.t[:], scalar1=3.1415925, scalar2=-3.1415925,
                                                                op0=ALU.min, op1=ALU.max), reads=[a.r], writes=[a.r])
                    S.op("act", lambda: nc.scalar.activation(out=out_ap, in_=a.t[:], func=AF.Sin), reads=[a.r], writes=[out_res])

                for ui, U in enumerate(UNITS):
                    L, N2, nbt = U["L"], U["N2"], U["nb"]
                    S.dma("sp", su.t[:], SU[ui], su.r, writes=[su.r])
                    for kb in range(8):
                        S.dma("pool", gtab.t[:, kb], GT[ui, kb], gtab.r, writes=[gtab.r])
                    for cc in range(L // 512):
                        fbuf = ft[cc % 2]
                        S.dma("sp", fbuf.t[:], FEAT[ui][:, cc * 512:(cc + 1) * 512], fbuf.r, writes=[fbuf.r])
                        mm(PSA[0:64, 0:512], w1.t[:], fbuf.t[:], True, True, [w1.r, fbuf.r], [BK[0]])
                        sin_layer(PSA[0:64, 0:512], BK[0], 3, h1c.t[:], h1c.r)
                        mm(PSA[0:64, 512:1024], w2.t[:], h1c.t[:], True, True, [w2.r, h1c.r], [BK[1]])
                        sin_layer(PSA[0:64, 512:1024], BK[1], 4, h2T.t[:, cc * 512:(cc + 1) * 512], h2T.r)
                    for g in range(NG):
                        for o in range(2):
                            for n2 in range(N2):
                                b = 2 + n2 % 2
                                d_ = dec[n2 % 2]
                                mm(PSA[0:64, b * 512:b * 512 + 256], h2T.t[:, n2:L:N2],
                                   w3sd.t[:, g, o].rearrange("p a c -> p (a c)"), True, True, [h2T.r, w3sd.r], [BK[b]])
                                S.op("act", lambda d_=d_, n2=n2: nc.scalar.activation(
                                    out=d_.t[:], in_=dl.t[:, g * 128:(g + 1) * 128], func=AF.Exp, scale=su.t[:, n2:n2 + 1]),
                                    reads=[dl.r, su.r], writes=[d_.r])
                                for bb in range(nbt):
                                    i = bb * N2 + n2
                                    S.op("dve", lambda b=b, d_=d_, i=i: nc.vector.tensor_tensor(
                                        out=htok.t[:, :, i, :],
                                        in0=PSA[0:64, b * 512:b * 512 + 256].rearrange("p (a c) -> p a c", a=2),
                                        in1=d_.t[:].unsqueeze(1).to_broadcast([64, 2, 128]), op=ALU.mult),
                                        reads=[BK[b], d_.r], writes=[htok.r])
                            for bb in range(nbt):
                                i0 = bb * N2
                                S.op("dve", lambda i0=i0: nc.vector.tensor_tensor(
                                    out=t0f.t[0:1, 0:128], in0=htok.t[0:1, 0, i0, :], in1=htok.t[0:1, 1, i0, :], op=ALU.add),
                                    reads=[htok.r], writes=[t0f.r])
                                S.op("dve", lambda: nc.vector.scalar_tensor_tensor(
                                    out=t0f.t[0:1, 128:256], in0=t0f.t[0:1, 0:128], scalar=0.5,
                                    in1=fb.t[0:1, o * MW + g * 128:o * MW + (g + 1) * 128], op0=ALU.mult, op1=ALU.add),
                                    reads=[t0f.r, fb.r], writes=[t0f.r])
                                for a_ in range(2):
                                    S.op("dve", lambda a_=a_, i0=i0: nc.vector.tensor_copy(
                                        out=htok.t[0:1, a_, i0, :], in_=t0f.t[0:1, 128:256]),
                                        reads=[t0f.r], writes=[htok.r])
                            for a_ in range(2):
                                for c0 in range(0, 128, 4):
                                    b = (c0 // 4) % 4
                                    for c in range(c0, c0 + 4):
                                        mm(PSA[0:64, b * 512 + (c - c0) * 128:b * 512 + (c - c0 + 1) * 128],
                                           htok.t[:, a_, :, c], f1.t[:], True, True, [htok.r, f1.r], [BK[b]])
                                    evac((c0 // 4) % 2, Abuf.t[:, :, :, c0:c0 + 4].rearrange("p k r c -> p (k r) c"),
                                         PSA[0:64, b * 512:(b + 1) * 512].rearrange("p (c x) -> p x c", c=4),
                                         [BK[b]], [Abuf.r])
                                for kb in range(8):
                                    gt = gtab
                                    ks = kst[kb % 2]
                                    for kp in range(4):
                                        b = 4 + (kb * 4 + kp) % 3
                                        for kk in range(2):
                                            kl = kp * 2 + kk
                                            k1 = kb * 8 + kl
                                            v0, v1 = (0, 2) if a_ == 0 else (1, 0)
                                            out = PSA[0:64, b * 512 + kk * 128:b * 512 + (kk + 1) * 128]
                                            mm(out, gt.t[:, kb, kl, v0, :], Abuf.t[:, k1, 0, :], True, False, [gt.r, Abuf.r], [BK[b]])
                                            mm(out, gt.t[:, kb, kl, v1, :], Abuf.t[:, k1, 1, :], False, True, [gt.r, Abuf.r], [BK[b]])
                                        evac(kp % 2, ks.t[:, kp * 2:kp * 2 + 2, a_, :],
                                             PSA[0:64, b * 512:b * 512 + 256].rearrange("p (k c) -> p k c", k=2),
                                             [BK[b]], [ks.r])
                                    S.dma("sp", KF[ui, g, o, kb, :, :, a_, :], ks.t[:, :, a_, :], ks.r, reads=[ks.r])
                S.barrier()

        if "FD" in stages:
            with ExitStack() as ph:
                f1 = load_unit_tables(ph)
                dtb = sb(ph, "h_dtb", [64, 3, 64], BF16)
                vtok = sb(ph, "h_vtok", [64, 128, 64], BF16)
                x1tok = sb(ph, "h_x1tok", [64, 128, 64], BF16)
                zview = vtok.t[:].rearrange("p c i -> p (c i)").rearrange("p (i c) -> p i c", c=128)
                AC = sb(ph, "h_AC", [64, 64, 2, 128], BF16)
                Yb = sb(ph, "h_Y", [64, 2, 64, 128], BF16)
                Xs = [sb(ph, "h_xs%d" % i, [64, 8, 2, 128], BF16) for i in range(2)]
                kfs = [sb(ph, "h_kf%d" % i, [64, 8, 2, 128], BF16) for i in range(2)]
                t1 = sb(ph, "h_t1", [64, 8, 128], F32)
                t2 = sb(ph, "h_t2", [64, 8, 128], F32)
                gtab = sb(ph, "h_gtabd", [64, 8, 8, 3, 64], BF16)
                ets = [sb(ph, "h_et%d" % i, [64, 8, 2, 64], BF16) for i in range(2)]
                x2s = sb(ph, "h_x2s", [128, 4096], BF16)
                mixs = sb(ph, "h_mixs", [128, 4096], BF16)
                units = UNITS if LIMIT is None else UNITS[:LIMIT]
                for ui, U in enumerate(units):
                    L, N2, nbt, tok0 = U["L"], U["N2"], U["nb"], U["tok0"]
                    S.dma("pool", dtb.t[:], DTB[ui], dtb.r, writes=[dtb.r])
                    for kb in range(8):
                        S.dma("pool", gtab.t[:, kb], GT[ui, kb], gtab.r, writes=[gtab.r])
                    for g in range(NG if STOP is None else STOP):
                        for bb in range(nbt):
                            lo = tok0 + bb * L
                            S.dma("sp", vtok.t[:, :, bb * N2:(bb + 1) * N2],
                                  PT[g * 128:(g + 1) * 128, lo:lo + L].rearrange("c (n1 n2) -> n1 c n2", n2=N2),
                                  vtok.r, writes=[vtok.r])
                            S.dma("sp", x1tok.t[:, :, bb * N2:(bb + 1) * N2],
                                  PT[MW + g * 128:MW + (g + 1) * 128, lo:lo + L].rearrange("c (n1 n2) -> n1 c n2", n2=N2),
                                  x1tok.r, writes=[x1tok.r])
                        S.dma("sp", x2s.t[:], PT[2 * MW + g * 128:2 * MW + (g + 1) * 128, tok0:tok0 + 4096], x2s.r,
                              writes=[x2s.r])
                        for o in range(2):
                            for c0 in range(0, 128, 4):
                                b = (c0 // 4) % 4
                                for c in range(c0, c0 + 4):
                                    mm(PSA[0:64, b * 512 + (c - c0) * 128:b * 512 + (c - c0 + 1) * 128],
                                       vtok.t[:, c, :] if o == 0 else zview[:, :, c], f1.t[:], True, True,
                                       [vtok.r, f1.r], [BK[b]])
                                evac(0, AC.t[:, :, :, c0:c0 + 4].rearrange("p k r c -> p (k r) c"),
                                     PSA[0:64, b * 512:(b + 1) * 512].rearrange("p (c x) -> p x c", c=4),
                                     [BK[b]], [AC.r])
                            for kb in range(8):
                                gt, kf, xs = gtab, kfs[kb % 2], Xs[kb % 2]
                                S.dma("sp", kf.t[:], KF[ui, g, o, kb], kf.r, writes=[kf.r])
                                for kp in range(4):
                                    b = 4 + (kb * 4 + kp) % 3
                                    for kk in range(2):
                                        kl = kp * 2 + kk
                                        k1 = kb * 8 + kl
                                        ore = PSA[0:64, b * 512 + kk * 256:b * 512 + kk * 256 + 128]
                                        oim = PSA[0:64, b * 512 + kk * 256 + 128:b * 512 + kk * 256 + 256]
                                        mm(ore, gt.t[:, kb, kl, 0, :], AC.t[:, k1, 0, :], True, False, [gt.r, AC.r], [BK[b]])
                                        mm(ore, gt.t[:, kb, kl, 2, :], AC.t[:, k1, 1, :], False, True, [gt.r, AC.r], [BK[b]])
                                        mm(oim, gt.t[:, kb, kl, 1, :], AC.t[:, k1, 0, :], True, False, [gt.r, AC.r], [BK[b]])
                                        mm(oim, gt.t[:, kb, kl, 0, :], AC.t[:, k1, 1, :], False, True, [gt.r, AC.r], [BK[b]])
                                    evac(0, xs.t[:, kp * 2:kp * 2 + 2, :, :].rearrange("p k r c -> p (k r c)"),
                                         PSA[0:64, b * 512:(b + 1) * 512], [BK[b]], [xs.r])
                                xre, xim = xs.t[:, :, 0, :], xs.t[:, :, 1, :]
                                kre, kim = kf.t[:, :, 0, :], kf.t[:, :, 1, :]
                                yre = Yb.t[:, 0, kb * 8:(kb + 1) * 8, :]
                                yim = Yb.t[:, 1, kb * 8:(kb + 1) * 8, :]
                                S.op("dve", lambda: nc.vector.tensor_tensor(out=t1.t[:], in0=xre, in1=kre, op=ALU.mult),
                                     reads=[xs.r, kf.r], writes=[t1.r])
                                S.op("dve", lambda: nc.vector.tensor_tensor(out=t2.t[:], in0=xim, in1=kim, op=ALU.mult),
                                     reads=[xs.r, kf.r], writes=[t2.r])
                                S.op("dve", lambda: nc.vector.tensor_tensor(out=yre, in0=t1.t[:], in1=t2.t[:], op=ALU.subtract),
                                     reads=[t1.r, t2.r], writes=[Yb.r])
                                S.op("dve", lambda: nc.vector.tensor_tensor(out=t1.t[:], in0=xre, in1=kim, op=ALU.mult),
                                     reads=[xs.r, kf.r], writes=[t1.r])
                                S.op("dve", lambda: nc.vector.tensor_tensor(out=t2.t[:], in0=xim, in1=kre, op=ALU.mult),
                                     reads=[xs.r, kf.r], writes=[t2.r])
                                S.op("dve", lambda: nc.vector.tensor_tensor(out=yim, in0=t1.t[:], in1=t2.t[:], op=ALU.add),
                                     reads=[t1.r, t2.r], writes=[Yb.r])
                            for c0 in range(0, 128, 4):
                                b = (c0 // 4) % 4
                                for c in range(c0, c0 + 4):
                                    out = PSA[0:64, b * 512 + (c - c0) * 128:b * 512 + (c - c0 + 1) * 128]
                                    mm(out, Yb.t[:, 0, :, c], dtb.t[:, 1:3, :].rearrange("p v i -> p (v i)"), True, False,
                                       [Yb.r, dtb.r], [BK[b]])
                                    mm(out, Yb.t[:, 1, :, c], dtb.t[:, 0:2, :].rearrange("p v i -> p (v i)"), False, True,
                                       [Yb.r, dtb.r], [BK[b]])
                                evac((c0 // 4) % 2, AC.t[:, :, :, c0:c0 + 4],
                                     PSA[0:64, b * 512:(b + 1) * 512].rearrange("p (c r i) -> p i r c", c=4, r=2),
                                     [BK[b]], [AC.r])
                            for ib in range(8):
                                et = ets[ib % 2]
                                S.dma("pool", et.t[:], ET[ui, ib], et.r, writes=[et.r])
                                if o == 0:
                                    for ip in range(2):
                                        b = 4 + (ib * 2 + ip) % 3
                                        for il4 in range(4):
                                            il = ip * 4 + il4
                                            i = ib * 8 + il
                                            out = PSA[0:64, b * 512 + il4 * 128:b * 512 + (il4 + 1) * 128]
                                            mm(out, et.t[:, il, 0, :], AC.t[:, i, 0, :], True, False, [et.r, AC.r], [BK[b]])
                                            mm(out, et.t[:, il, 1, :], AC.t[:, i, 1, :], False, True, [et.r, AC.r], [BK[b]])
                                        i0 = ib * 8 + ip * 4
                                        S.op("dve", lambda b=b, i0=i0: nc.vector.tensor_tensor(
                                            out=zview[:, i0:i0 + 4, :],
                                            in0=PSA[0:64, b * 512:(b + 1) * 512].rearrange("p (i c) -> p i c", i=4),
                                            in1=x1tok.t[:, :, i0:i0 + 4].rearrange("p c i -> p i c"), op=ALU.mult),
                                            reads=[BK[b], x1tok.r], writes=[vtok.r])
                                else:
                                    b = 4 + ib % 3
                                    for il in range(8):
                                        i = ib * 8 + il
                                        out = PSA[:, b * 512 + il * 64:b * 512 + (il + 1) * 64]
                                        mm(out, AC.t[:, i, 0, :], et.t[:, il, 0, :], True, False, [et.r, AC.r], [BK[b]])
                                        mm(out, AC.t[:, i, 1, :], et.t[:, il, 1, :], False, True, [et.r, AC.r], [BK[b]])
                                    bb = (ib * 8) // N2
                                    n20 = (ib * 8) % N2
                                    view = lambda t: t[:, bb * L:(bb + 1) * L].rearrange("p (t1 n2) -> p n2 t1", n2=N2)[:, n20:n20 + 8, :]
                                    S.op("dve", lambda b=b, view=view: nc.vector.tensor_tensor(
                                        out=view(mixs.t), in0=PSA[:, b * 512:(b + 1) * 512].rearrange("p (i t) -> p i t", i=8),
                                        in1=view(x2s.t), op=ALU.mult),
                                        reads=[BK[b], x2s.r], writes=[mixs.r])
                        S.dma("sp", MIXT[g * 128:(g + 1) * 128, tok0:tok0 + 4096], mixs.t[:], mixs.r, reads=[mixs.r])
                S.barrier()
        if "D" in stages:
            with ExitStack() as ph:
                wout = sb(ph, "d_wout", [128, 8, D], BF16)
                S.dma("pool", wout.t[:], W_OUT[1].rearrange("(j p) n -> p j n", p=128), wout.r, writes=[wout.r])
                gpost = sb(ph, "d_gpost", [128, D], F32)
                S.dma("sp", gpost.t[:], ROWV[:, D:2 * D].partition_broadcast(128), gpost.r, writes=[gpost.r])
                wks = [mk_normwk(ph, "d%d" % i) for i in range(2)]
                eks = [mk_epi(ph, "d%d" % i) for i in range(2)]
                xin = [sb(ph, "dx%d" % i, [128, D], F32) for i in range(3)]
                cat = [sb(ph, "d_cat%d" % i, [128, 8, 128], BF16) for i in range(3)]
                nch = LIMIT or NT // 128

                def chunk_gen(c):
                    xb, cb = xin[c % 3], cat[c % 3]
                    t0 = c * 128
                    S.dma("sp", xb.t[:], X2[t0:t0 + 128, :], xb.r, writes=[xb.r])
                    S.dma("sp", cb.t[:, 0:6, :], MIXT.rearrange("(k p) t -> p k t", p=128)[:, :, t0:t0 + 128], cb.r,
                          writes=[cb.r])
                    S.dma("sp", cb.t[:, 6:8, :], ATT.rearrange("(k p) t -> p k t", p=128)[:, :, t0:t0 + 128], cb.r,
                          writes=[cb.r])
                    yield
                    bk = (3, 4) if c % 2 == 0 else (5, 6)
                    for nb in range(2):
                        for k in range(8):
                            mm(ps(bk[nb]), cb.t[:, k, :], wout.t[:, k, nb * 512:(nb + 1) * 512], k == 0, k == 7,
                               [cb.r, wout.r], [BK[bk[nb]]])
                    yield
                    epilogue(ph, bk, xb.t[:], xb.r, gpost.t[:], gpost.r, t0, X3, CV_FFNPRE[1], HNT3, wks[c % 2], eks[c % 2])

                active = []
                nxt = 0
                while active or nxt < nch:
                    if nxt < nch and len(active) < 3:
                        active.append(chunk_gen(nxt))
                        nxt += 1
                    for gen in list(active):
                        try:
                            next(gen)
                        except StopIteration:
                            active.remove(gen)
                S.barrier()

        if "B1" in stages:
            ffn_phase(1, X3, HNT3, Y, None, None)

        S.barrier()
        print("program: %d instructions, %d waits, %d dma sems" % (S.ninst, S.nwait, len(S.chans)))
    return nc


def _hyena_tables():
    t = {}
    n1 = np.arange(64)[:, None]
    k1 = np.arange(64)[None, :]
    ang = 2 * np.pi * n1 * (k1 + 0.5) / 128.0
    f1 = np.zeros((64, 128), np.float64)
    f1[:, 0::2] = np.cos(ang)
    f1[:, 1::2] = -np.sin(ang)
    t["f1t"] = f1.astype(np.float32)
    gt = np.zeros((2, 8, 64, 8, 3, 64), np.float64)
    dtb = np.zeros((2, 64, 3, 64), np.float64)
    et = np.zeros((2, 8, 64, 8, 2, 64), np.float64)
    su = np.zeros((2, 64, 64), np.float64)
    for ui, (L, N2, nb) in enumerate(((2048, 32, 2), (4096, 64, 1))):
        N = 2 * L
        n2 = np.arange(N2)[:, None]
        k2 = np.arange(N2)[None, :]
        for k1v in range(64):
            G = np.exp(-2j * np.pi * n2 * (k1v + 128 * k2 + 0.5) / N)
            Gf = np.zeros((64, 64), np.complex128)
            for b in range(nb):
                Gf[b * N2:(b + 1) * N2, b * N2:(b + 1) * N2] = G
            kb, kl = divmod(k1v, 8)
            gt[ui, kb, :, kl, 0, :] = Gf.real
            gt[ui, kb, :, kl, 1, :] = Gf.imag
            gt[ui, kb, :, kl, 2, :] = -Gf.imag
        Dm = np.exp(2j * np.pi * np.arange(N2)[:, None] * np.arange(N2)[None, :] / N2)
        Df = np.zeros((64, 64), np.complex128)
        for b in range(nb):
            Df[b * N2:(b + 1) * N2, b * N2:(b + 1) * N2] = Dm
        dtb[ui, :, 0, :] = -Df.imag
        dtb[ui, :, 1, :] = Df.real
        dtb[ui, :, 2, :] = Df.imag
        k1c = np.arange(64)[:, None]
        t1 = np.arange(64)[None, :]
        for i in range(64):
            t2 = i % N2
            E = (2.0 / N) * np.exp(2j * np.pi * (t2 + N2 * t1) * (k1c + 0.5) / N)
            ib, il = divmod(i, 8)
            et[ui, ib, :, il, 0, :] = E.real
            et[ui, ib, :, il, 1, :] = -E.imag
        nn1 = np.arange(64)[:, None]
        nn2 = np.arange(64)[None, :]
        su[ui] = np.where(nn2 < N2, -(N2 * nn1 + nn2) / (L - 1.0), 0.0)
        tt = np.linspace(0.0, 1.0, L)[:, None]
        w = (2.0 * np.pi / L) * np.arange(L)[:, None]
        f = np.linspace(1e-4, 15.0, 16)[None, :]
        feat = np.concatenate([tt, np.cos(f * w), -np.sin(f * w)], axis=-1)
        t["featP" if ui == 0 else "featS"] = np.ascontiguousarray(feat.T).astype(np.float32)
    t["gt"] = gt.astype(np.float32)
    t["dtb"] = dtb.astype(np.float32)
    t["et"] = et.astype(np.float32)
    t["su"] = su.astype(np.float32)
    return t


def _prep_shared(inp):
    f = lambda a: np.ascontiguousarray(np.asarray(a, dtype=np.float32))
    sh = {}
    sh["a_w_in"] = f(inp["a_w_in"][0])
    sh["b_w_in"] = f(inp["b_w_in"][0])
    sh["w_kv"] = f(inp["xattn_w_kv"])
    sh["w_out"] = f(inp["w_out"])
    sh["w_up"] = f(inp["ffn_w_up"])
    sh["w_dn"] = f(inp["ffn_w_down"])
    sh["wsT"] = f(np.transpose(np.asarray(inp["a_w_s"][0]), (2, 0, 1)))
    sh["bsT"] = f(np.transpose(np.asarray(inp["a_b_s"][0]), (1, 0)))

    def cols(v):
        v = np.asarray(v, dtype=np.float32)
        return v.reshape(-1, 128).T

    cl = []
    for nm in ("norm_mix_pre", "norm_ffn_pre"):
        pass
    a = inp
    cl += [cols(a["norm_mix_pre"][0]), cols(a["norm_ffn_pre"][0]), cols(a["norm_mix_pre"][1]), cols(a["norm_ffn_pre"][1])]
    cl += [cols(a["norm_mem"][0]), cols(a["norm_mem"][1])]
    for l in range(2):
        for k in range(3):
            cl.append(cols(a["ffn_conv_w"][l][k]))
        cl.append(cols(a["ffn_conv_b"][l]))
    for k in range(3):
        cl.append(cols(a["b_sconv_w"][0][k]))
    cl.append(cols(a["b_sconv_b"][0]))
    colv = np.concatenate(cl, axis=1)
    assert colv.shape == (128, NCV), colv.shape
    sh["colv"] = f(colv)
    deltas = np.abs(np.linspace(math.log(1e-2) / 1.5, math.log(1e-2) / 0.3, MW, dtype=np.float32))
    rowv = np.concatenate([np.asarray(a["norm_mix_post"][0]), np.asarray(a["norm_mix_post"][1]),
                           np.asarray(a["norm_ffn_post"][0]), np.asarray(a["norm_ffn_post"][1]),
                           np.asarray(a["a_ln_g"][0]), np.asarray(a["a_ln_b"][0]), deltas]).astype(np.float32)
    sh["rowv"] = f(rowv[None, :])
    sh["ident"] = np.eye(128, dtype=np.float32)
    sh.update(_hyena_tables())
    sh["fw1"] = f(a["b_filt_w1"][0])
    sh["fw2"] = f(a["b_filt_w2"][0])
    w3 = np.asarray(a["b_filt_w3"][0], dtype=np.float32).reshape(64, 2, 2, 6, 128)
    sh["fw3"] = f(np.transpose(w3, (0, 3, 2, 1, 4)).reshape(64, 3072))
    sh["fvec"] = f(np.stack([np.asarray(a["b_filt_freq"][0]), np.asarray(a["b_filt_b1"][0]),
                             np.asarray(a["b_filt_b2"][0])], axis=1))
    sh["fbias"] = f(np.asarray(a["b_filt_bias"][0]).reshape(1, 1536))
    return sh


def _core_inputs(inp, i):
    xp = np.asarray(inp["x_prompt"])
    xs = np.asarray(inp["x_sample"])
    mp = np.asarray(inp["mem_prompt"])
    ms = np.asarray(inp["mem_sample"])
    X = np.concatenate([xp[2 * i], xp[2 * i + 1], xs[i]], axis=0)
    M = np.concatenate([mp[2 * i], mp[2 * i + 1], ms[i]], axis=0)
    return {"X": np.ascontiguousarray(X, dtype=np.float32), "MEM": np.ascontiguousarray(M, dtype=np.float32)}


def kernel(**inputs):
    sh = _prep_shared(inputs)
    nc = build()
    in_maps = []
    for i in range(8):
        m = dict(sh)
        m.update(_core_inputs(inputs, i))
        in_maps.append(m)
    res = run_bass_kernel_spmd(nc, in_maps, core_ids=list(range(8)))
    yp = np.zeros((16, 2048, D), np.float32)
    ys = np.zeros((8, 4096, D), np.float32)
    for i in range(8):
        y = res.results[i]["Y"]
        yp[2 * i] = y[0:2048]
        yp[2 * i + 1] = y[2048:4096]
        ys[i] = y[4096:8192]
    return (yp, ys)
```

```python
import math
from contextlib import ExitStack

import numpy as np
import concourse.bass as bass
import concourse.mybir as mybir
from concourse.bass_utils import run_bass_kernel_spmd

F32 = mybir.dt.float32
BF16 = mybir.dt.bfloat16
AF = mybir.ActivationFunctionType
ALU = mybir.AluOpType

NT = 8192
SEQS = [(0, 2048), (2048, 2048), (4096, 4096)]
NORM_EPS = 1e-6
LN_EPS = 1e-5
D = 1024
MW = 768
DFF = 2816
NCH_FF = 22
GELU = AF.Gelu_apprx_tanh
STOP = None
NGLIM = None
LIMIT = None

CV_MIXPRE = [0, 16]
CV_FFNPRE = [8, 24]
CV_MEM = [32, 40]
CV_FFC = [48, 48 + 88]
CV_SC = 48 + 176
NCV = CV_SC + 72


class Res:
    __slots__ = ("name", "w", "r", "chan", "excl")

    def __init__(self, name, excl=False):
        self.name = name
        self.w = None
        self.r = []
        self.chan = None
        self.excl = excl


class Chan:
    __slots__ = ("sem", "cnt", "q")

    def __init__(self, sem):
        self.sem = sem
        self.cnt = 0
        self.q = None


class TB:
    def __init__(self, t, name):
        self.t = t
        self.r = Res(name)


class Sched:
    def __init__(self, nc, stack):
        self.nc = nc
        self.stack = stack
        self.eng = {"pe": nc.tensor, "act": nc.scalar, "dve": nc.vector,
                    "pool": nc.gpsimd, "sp": nc.sync}
        self.esem = {}
        self.ecnt = {}
        for e in ("pe", "act", "dve", "pool"):
            self.esem[e] = stack.enter_context(nc.semaphore("s_" + e))
            self.ecnt[e] = 0
        self.known = {e: {} for e in self.eng}
        self.chans = []
        self.free = {"sp": [], "pool": []}
        self.ninst = 0
        self.nwait = 0

    def _need(self, e, ev, same_ok):
        if ev is None:
            return
        sem, val = ev
        if same_ok and e in self.esem and sem is self.esem[e]:
            return
        k = self.known[e]
        if k.get(id(sem), 0) >= val:
            return
        k[id(sem)] = val
        self.eng[e].wait_ge(sem, val)
        self.nwait += 1

    def _deps(self, e, reads, writes, same_ok):
        for r in reads:
            self._need(e, r.w, same_ok)
        for r in writes:
            self._need(e, r.w, same_ok)
            for ev in r.r:
                self._need(e, ev, same_ok)

    def _post(self, ev, reads, writes):
        for r in reads:
            r.r.append(ev)
            if len(r.r) > 10:
                d = {}
                for s, v in r.r:
                    if id(s) not in d or d[id(s)][1] < v:
                        d[id(s)] = (s, v)
                r.r = list(d.values())
        for r in writes:
            r.w = ev
            r.r = []

    def op(self, e, fn, reads=(), writes=()):
        ex = [r for r in reads if r.excl]
        if ex:
            writes = list(writes) + ex
            reads = [r for r in reads if not r.excl]
        self._deps(e, reads, writes, same_ok=(e == "pe"))
        ins = fn()
        self.ecnt[e] += 1
        ins.then_inc(self.esem[e], 1)
        ev = (self.esem[e], self.ecnt[e])
        self._post(ev, reads, writes)
        self.ninst += 1
        return ev

    def dma(self, q, out, in_, chan, reads=(), writes=()):
        skip = chan.chan.sem if chan.chan is not None else None
        for r in reads:
            self._need(q, r.w, False)
        for r in writes:
            if not (r.w is not None and r.w[0] is skip and r is chan and not r.r):
                self._need(q, r.w, False)
            for ev in r.r:
                self._need(q, ev, False)
        if chan.chan is None:
            if self.free[q]:
                chan.chan = self.free[q].pop()
            else:
                c = Chan(self.stack.enter_context(self.nc.semaphore("d%s%d" % (q, len(self.chans)))))
                c.q = q
                self.chans.append(c)
                chan.chan = c
        c = chan.chan
        assert c.q == q, "a DMA channel semaphore must stay on one queue type"
        ins = self.eng[q].dma_start(out=out, in_=in_)
        c.cnt += 16
        ins.then_inc(c.sem, 16)
        ev = (c.sem, c.cnt)
        self._post(ev, reads, writes)
        self.ninst += 1
        return ev

    def barrier(self):
        for e in self.eng:
            for e2 in self.esem:
                if self.ecnt[e2]:
                    self._need(e, (self.esem[e2], self.ecnt[e2]), False)
            for c in self.chans:
                if c.cnt:
                    self._need(e, (c.sem, c.cnt), False)
        self.free = {"sp": [c for c in self.chans if c.q == "sp"], "pool": [c for c in self.chans if c.q == "pool"]}


def build(stages=("kv", "A", "B0", "C", "FF", "FD", "D", "B1"), dbg=(), ext=()):
    nc = bass.Bass("TRN2", target_bir_lowering=False)

    def din(name, shape, dt=F32):
        return nc.dram_tensor(name, list(shape), dt, kind="ExternalInput").ap()

    def dscr(name, shape, dt):
        kind = "ExternalOutput" if name in dbg else ("ExternalInput" if name in ext else "Internal")
        return nc.dram_tensor(name, list(shape), dt, kind=kind).ap()

    X0 = din("X", [NT, D])
    MEM = din("MEM", [768, D])
    A_W_IN = din("a_w_in", [D, 1792])
    B_W_IN = din("b_w_in", [D, 2560])
    W_KV = din("w_kv", [2, D, 512])
    W_OUT = din("w_out", [2, D, D])
    W_UP = din("w_up", [2, D, 2 * DFF])
    W_DN = din("w_dn", [2, DFF, D])
    WST = din("wsT", [128, 12, 128])
    BST = din("bsT", [128, 12])
    COLV = din("colv", [128, NCV])
    ROWV = din("rowv", [1, 4 * D + 3 * MW])
    IDENT = din("ident", [128, 128])
    F1T = din("f1t", [64, 128])
    GT = din("gt", [2, 8, 64, 8, 3, 64])
    DTB = din("dtb", [2, 64, 3, 64])
    ET = din("et", [2, 8, 64, 8, 2, 64])
    FEAT = [din("featP", [33, 2048]), din("featS", [33, 4096])]
    SU = din("su", [2, 64, 64])
    FW1 = din("fw1", [33, 64])
    FW2 = din("fw2", [64, 64])
    FW3 = din("fw3", [64, 3072])
    FVEC = din("fvec", [64, 3])
    FBIAS = din("fbias", [1, 1536])
    Y = nc.dram_tensor("Y", [NT, D], F32, kind="ExternalOutput").ap()

    X1 = dscr("X1", [NT, D], F32)
    X2 = dscr("X2", [NT, D], F32)
    X3 = dscr("X3", [NT, D], F32)
    HNT1 = dscr("HNT1", [D, NT], BF16)
    HNT2 = dscr("HNT2", [D, NT], BF16)
    HNT3 = dscr("HNT3", [D, NT], BF16)
    PT = dscr("PT", [2304, NT], BF16)
    ATT = dscr("ATT", [256, NT], BF16)
    MIXT = dscr("MIXT", [MW, NT], BF16)
    KF = dscr("KF", [2, 6, 2, 8, 64, 8, 2, 128], BF16)

    with ExitStack() as st:
        S = Sched(nc, st)

        uid = [0]

        def sb(stack, name, shape, dt):
            uid[0] += 1
            return TB(stack.enter_context(nc.sbuf_tensor("sb%d_%s" % (uid[0], name), list(shape), dt)), name)

        PSA = st.enter_context(nc.psum_tensor("PSA", [128, 7 * 512], F32))
        PBt = st.enter_context(nc.psum_tensor("PB", [128, 1024], BF16))
        BK = [Res("bank%d" % i, excl=True) for i in range(7)]
        PBr = Res("pb", excl=True)

        def ps(b, lo=0, hi=512):
            return PSA[:, b * 512 + lo: b * 512 + hi]

        ident = sb(st, "ident", [128, 128], BF16)
        ones = sb(st, "ones", [128, 128], BF16)
        colv = sb(st, "colv", [128, NCV], F32)
        kT = sb(st, "kT", [128, 6, 2, 256], BF16)
        vv = sb(st, "vv", [128, 6, 2, 256], BF16)
        S.dma("pool", ident.t[:], IDENT, ident.r, writes=[ident.r])
        S.dma("sp", colv.t[:], COLV, colv.r, writes=[colv.r])
        S.op("dve", lambda: nc.vector.memset(ones.t[:], 1.0), writes=[ones.r])

        def mm(out, lhsT, rhs, start, stop, reads, writes):
            S.op("pe", lambda: nc.tensor.matmul(out, lhsT=lhsT, rhs=rhs, start=start, stop=stop),
                 reads=reads, writes=writes)

        def evac(which, out_ap, in_ap, reads, writes):
            if which == 0:
                S.op("act", lambda: nc.scalar.copy(out=out_ap, in_=in_ap), reads=reads, writes=writes)
            else:
                S.op("dve", lambda: nc.vector.tensor_copy(out=out_ap, in_=in_ap), reads=reads, writes=writes)

        def norm_T(ph, xt_ap, xt_res, gbase, out_ap, out_res, wk):
            junk, ss, xn = wk["junk"], wk["ss"], wk["xn"]
            S.op("act", lambda: nc.scalar.activation(out=junk.t[:, 0:D], in_=xt_ap, func=AF.Square,
                                                     accum_out=ss.t[:, 0:1]),
                 reads=[xt_res], writes=[junk.r, ss.r])
            S.op("act", lambda: nc.scalar.activation(out=ss.t[:, 1:2], in_=ss.t[:, 0:1], func=AF.Sqrt,
                                                     scale=1.0 / D, bias=wk["eps"].t[:, 0:1]),
                 reads=[ss.r, wk["eps"].r], writes=[ss.r])
            S.op("dve", lambda: nc.vector.reciprocal(out=ss.t[:, 2:3], in_=ss.t[:, 1:2]),
                 reads=[ss.r], writes=[ss.r])
            S.op("dve", lambda: nc.vector.tensor_scalar(out=xn.t[:], in0=xt_ap, scalar1=ss.t[:, 2:3],
                                                        scalar2=None, op0=ALU.mult),
                 reads=[xt_res, ss.r], writes=[xn.r])
            for j in range(8):
                S.op("pe", lambda j=j: nc.tensor.transpose(out=PBt[:, j * 128:(j + 1) * 128],
                                                           in_=xn.t[:, j * 128:(j + 1) * 128],
                                                           identity=ident.t[:]),
                     reads=[xn.r, ident.r], writes=[PBr])
            S.op("dve", lambda: nc.vector.tensor_tensor(
                out=out_ap, in0=PBt[:, 0:1024].rearrange("p (j t) -> p j t", j=8),
                in1=colv.t[:, gbase:gbase + 8].unsqueeze(2).to_broadcast([128, 8, 128]), op=ALU.mult),
                reads=[PBr, colv.r], writes=[out_res])

        def mk_normwk(ph, tag):
            wk = {"junk": sb(ph, "junk" + tag, [128, D], BF16),
                  "ss": sb(ph, "ss" + tag, [128, 4], F32),
                  "xn": sb(ph, "xn" + tag, [128, D], BF16),
                  "eps": sb(ph, "eps" + tag, [128, 2], F32)}
            S.op("dve", lambda: nc.vector.memset(wk["eps"].t[:, 0:1], NORM_EPS), writes=[wk["eps"].r])
            S.op("dve", lambda: nc.vector.memset(wk["eps"].t[:, 1:2], LN_EPS), writes=[wk["eps"].r])
            return wk

        def hnt_view(H):
            return H.rearrange("(j p) t -> p j t", p=128)

        def epilogue(ph, banks, xres_ap, xres_res, gpost_ap, gpost_res, tok0, XOUT, gnext, HOUT, wk, ek):
            b0 = banks[0]
            pout = PSA[:, b0 * 512: b0 * 512 + 1024]
            br = [BK[b] for b in banks]
            junk, ss = wk["junk"], ek["ss"]
            S.op("act", lambda: nc.scalar.activation(out=junk.t[:, 0:D], in_=pout, func=AF.Square,
                                                     accum_out=ss.t[:, 0:1]),
                 reads=br, writes=[junk.r, ss.r])
            S.op("act", lambda: nc.scalar.activation(out=ss.t[:, 1:2], in_=ss.t[:, 0:1], func=AF.Sqrt,
                                                     scale=1.0 / D, bias=wk["eps"].t[:, 0:1]),
                 reads=[ss.r, wk["eps"].r], writes=[ss.r])
            S.op("dve", lambda: nc.vector.reciprocal(out=ss.t[:, 2:3], in_=ss.t[:, 1:2]),
                 reads=[ss.r], writes=[ss.r])
            tp = ek["tp"]
            S.op("dve", lambda: nc.vector.tensor_tensor(out=tp.t[:], in0=pout, in1=gpost_ap, op=ALU.mult),
                 reads=br + [gpost_res], writes=[tp.r])
            xnew = ek["xnew"]
            S.op("dve", lambda: nc.vector.scalar_tensor_tensor(out=xnew.t[:], in0=tp.t[:], scalar=ss.t[:, 2:3],
                                                               in1=xres_ap, op0=ALU.mult, op1=ALU.add),
                 reads=[tp.r, ss.r, xres_res], writes=[xnew.r])
            S.dma("pool", XOUT[tok0:tok0 + 128, :], xnew.t[:], ek["xst"], reads=[xnew.r])
            if HOUT is not None:
                hno = ek["hno"]
                norm_T(ph, xnew.t[:], xnew.r, gnext, hno.t[:], hno.r, wk)
                S.dma("pool", hnt_view(HOUT)[:, :, tok0:tok0 + 128], hno.t[:], ek["hst"], reads=[hno.r])

        def mk_epi(ph, tag):
            return {"ss": sb(ph, "ess" + tag, [128, 4], F32),
                    "tp": sb(ph, "tp" + tag, [128, D], F32),
                    "xnew": sb(ph, "xnew" + tag, [128, D], F32),
                    "hno": sb(ph, "hno" + tag, [128, 8, 128], BF16),
                    "xst": Res("xst" + tag), "hst": Res("hst" + tag)}

        def attention(hn_ap, hn_res, wq_ap, wq_res, ls, cat, wk, banks):
            for _ in attention_gen(hn_ap, hn_res, wq_ap, wq_res, ls, cat, wk, banks):
                pass

        def attention_gen(hn_ap, hn_res, wq_ap, wq_res, ls, cat, wk, banks):
            bq, bs0, bs1, bav0, bav1 = banks
            qT, pT, rden = wk["qT"], wk["pT"], wk["rden"]
            for hc in range(2):
                for j in range(8):
                    mm(ps(bq, hc * 128, hc * 128 + 128), wq_ap(j, hc), hn_ap(j), j == 0, j == 7,
                       [wq_res, hn_res], [BK[bq]])
            yield
            S.op("act", lambda: nc.scalar.activation(out=qT.t[:].rearrange("p a t -> p (a t)"), in_=ps(bq, 0, 256),
                                                     func=AF.Copy, scale=0.125),
                 reads=[BK[bq]], writes=[qT.r])
            yield
            for h in range(4):
                hc, po = h // 2, (h % 2) * 64
                for mc in range(2):
                    idx = (h % 2) * 4 + (h // 2) * 2 + mc
                    b = bs0 if idx < 4 else bs1
                    col = (idx % 4) * 128
                    mm(PSA[:, b * 512 + col: b * 512 + col + 128],
                       kT.t[po:po + 64, ls, hc, mc * 128:(mc + 1) * 128], qT.t[po:po + 64, hc, :], True, True,
                       [kT.r, qT.r], [BK[b]])
            yield
            for half, b in ((0, bs0), (1, bs1)):
                S.op("act", lambda half=half, b=b: nc.scalar.activation(
                    out=pT.t[:, half * 4:(half + 1) * 4, :].rearrange("p a t -> p (a t)"), in_=ps(b), func=AF.Exp),
                    reads=[BK[b]], writes=[pT.r])
            yield
            for hc in range(2):
                b = bav0 if hc == 0 else bav1
                for part in range(4):
                    h = 2 * hc + (part % 2)
                    for mc in range(2):
                        lhsT = vv.t[:, ls, mc, hc * 128:(hc + 1) * 128] if part < 2 else ones.t[:]
                        mm(ps(b, part * 128, part * 128 + 128), lhsT, pT.t[:, (h % 2) * 4 + (h // 2) * 2 + mc, :], mc == 0, mc == 1,
                           [vv.r, ones.r, pT.r], [BK[b]])
                for hh in range(2):
                    lo = hh * 64
                    S.op("dve", lambda hh=hh, lo=lo, b=b, hc=hc: nc.vector.reciprocal(
                        out=rden.t[lo:lo + 64, hc, :], in_=PSA[lo:lo + 64, b * 512 + (2 + hh) * 128: b * 512 + (3 + hh) * 128]),
                        reads=[BK[b]], writes=[rden.r])
                for hh in range(2):
                    lo = hh * 64
                    S.op("dve", lambda hh=hh, lo=lo, b=b, hc=hc: nc.vector.tensor_tensor(
                        out=cat.t[lo:lo + 64, 6 + hc, :], in0=PSA[lo:lo + 64, b * 512 + hh * 128: b * 512 + (hh + 1) * 128],
                        in1=rden.t[lo:lo + 64, hc, :], op=ALU.mult),
                        reads=[BK[b], rden.r], writes=[cat.r])

        def mk_attwk(ph, tag):
            return {"qT": sb(ph, "qT" + tag, [128, 2, 128], BF16),
                    "pT": sb(ph, "pT" + tag, [128, 8, 128], BF16),
                    "rden": sb(ph, "rden" + tag, [128, 2, 128], F32)}

        def seq_of(tok):
            for si, (t0, L) in enumerate(SEQS):
                if t0 <= tok < t0 + L:
                    return si, t0, L
            raise ValueError

        def load_hnt_tile(HSRC, hb, a, T):
            si, t0, L = seq_of(a)
            lo = a - 1 if a > t0 else a
            hi = a + T + 1 if a + T < t0 + L else a + T
            if lo == a:
                S.op("dve", lambda: nc.vector.memset(hb.t[:, :, 0:1], 0.0), writes=[hb.r])
            if hi == a + T:
                S.op("dve", lambda: nc.vector.memset(hb.t[:, :, T + 1:T + 2], 0.0), writes=[hb.r])
            S.dma("sp", hb.t[:, :, lo - (a - 1): hi - (a - 1)], hnt_view(HSRC)[:, :, lo:hi], hb.r, writes=[hb.r])

        def conv_chunk(hb, T, w_ap, w_res, cv, nchk, ci, bmain, tbuf, fin=None):
            for j in range(8):
                mm(ps(bmain, 0, T + 2), w_ap(j), hb.t[:, j, 0:T + 2], j == 0, j == 7, [w_res, hb.r], [BK[bmain]])
            c0, c1, c2, cb = (colv.t[:, cv + k * nchk + ci: cv + k * nchk + ci + 1] for k in range(4))
            S.op("act", lambda: nc.scalar.activation(out=tbuf.t[:, 0:T], in_=ps(bmain, 1, T + 1), func=AF.Identity,
                                                     scale=c1, bias=cb),
                 reads=[BK[bmain], colv.r], writes=[tbuf.r])
            S.op("dve", lambda: nc.vector.scalar_tensor_tensor(out=tbuf.t[:, 0:T], in0=ps(bmain, 0, T), scalar=c0,
                                                               in1=tbuf.t[:, 0:T], op0=ALU.mult, op1=ALU.add),
                 reads=[BK[bmain], tbuf.r, colv.r], writes=[tbuf.r])
            fo = tbuf if fin is None else fin
            S.op("dve", lambda: nc.vector.scalar_tensor_tensor(out=fo.t[:, 0:T], in0=ps(bmain, 2, T + 2), scalar=c2,
                                                               in1=tbuf.t[:, 0:T], op0=ALU.mult, op1=ALU.add),
                 reads=[BK[bmain], tbuf.r, colv.r], writes=[fo.r])

        if "kv" in stages:
            with ExitStack() as ph:
                wkv = sb(ph, "wkv", [128, 2, 8, 512], BF16)
                for l in range(2):
                    S.dma("pool", wkv.t[:, l], W_KV[l].rearrange("(j p) n -> p j n", p=128), wkv.r, writes=[wkv.r])
                wk = mk_normwk(ph, "kv")
                xin = [sb(ph, "kvx%d" % i, [128, D], F32) for i in range(2)]
                hn = [sb(ph, "kvh%d" % i, [128, 8, 128], BF16) for i in range(2)]
                it = 0
                for l in range(2):
                    for s in range(3):
                        for mt in range(2):
                            xb, hb = xin[it % 2], hn[it % 2]
                            it += 1
                            r0 = s * 256 + mt * 128
                            S.dma("sp", xb.t[:], MEM[r0:r0 + 128, :], xb.r, writes=[xb.r])
                            norm_T(ph, xb.t[:], xb.r, CV_MEM[l], hb.t[:], hb.r, wk)
                            ls = l * 3 + s
                            for hc in range(2):
                                b = hc
                                for j in range(8):
                                    mm(ps(b, 0, 128), wkv.t[:, l, j, hc * 128:(hc + 1) * 128], hb.t[:, j, :], j == 0, j == 7,
                                       [wkv.r, hb.r], [BK[b]])
                                S.op("act", lambda b=b, hc=hc, ls=ls, mt=mt: nc.scalar.copy(
                                    out=kT.t[:, ls, hc, mt * 128:(mt + 1) * 128], in_=ps(b, 0, 128)),
                                    reads=[BK[b]], writes=[kT.r])
                            for j in range(8):
                                mm(ps(2, 0, 256), hb.t[:, j, :], wkv.t[:, l, j, 256:512], j == 0, j == 7,
                                   [wkv.r, hb.r], [BK[2]])
                            S.op("act", lambda ls=ls, mt=mt: nc.scalar.copy(out=vv.t[:, ls, mt, :], in_=ps(2, 0, 256)),
                                 reads=[BK[2]], writes=[vv.r])
                S.barrier()

        if "A" in stages:
            with ExitStack() as ph:
                win = sb(ph, "a_win", [128, 8, 1792], BF16)
                for j in range(8):
                    S.dma("pool", win.t[:, j, :], A_W_IN[j * 128:(j + 1) * 128, :], win.r, writes=[win.r])
                wout = sb(ph, "a_wout", [128, 8, D], BF16)
                S.dma("pool", wout.t[:], W_OUT[0].rearrange("(j p) n -> p j n", p=128), wout.r, writes=[wout.r])
                wst = sb(ph, "a_wst", [128, 12, 128], BF16)
                S.dma("pool", wst.t[:], WST, wst.r, writes=[wst.r])
                bst = sb(ph, "a_bst", [128, 12], F32)
                S.dma("sp", bst.t[:], BST, bst.r, writes=[bst.r])
                gpost = sb(ph, "a_gpost", [128, D], F32)
                S.dma("sp", gpost.t[:], ROWV[:, 0:D].partition_broadcast(128), gpost.r, writes=[gpost.r])
                lng = sb(ph, "a_lng", [128, MW], F32)
                lnb = sb(ph, "a_lnb", [128, MW], F32)
                S.dma("sp", lng.t[:], ROWV[:, 4 * D:4 * D + MW].partition_broadcast(128), lng.r, writes=[lng.r])
                S.dma("sp", lnb.t[:], ROWV[:, 4 * D + MW:4 * D + 2 * MW].partition_broadcast(128), lnb.r, writes=[lnb.r])
                wks = [mk_normwk(ph, "a%d" % i) for i in range(2)]
                eks = [mk_epi(ph, "a%d" % i) for i in range(2)]
                awks = [mk_attwk(ph, "a%d" % i) for i in range(2)]
                xin = [sb(ph, "ax%d" % i, [128, D], F32) for i in range(5)]
                hn = [sb(ph, "ah%d" % i, [128, 8, 128], BF16) for i in range(2)]
                gus = [sb(ph, "a_gu%d" % i, [128, MW], F32) for i in range(2)]
                gvs = [sb(ph, "a_gv%d" % i, [128, MW], F32) for i in range(2)]
                vns = [sb(ph, "a_vn%d" % i, [128, MW], BF16) for i in range(2)]
                tmps = [sb(ph, "a_tmp%d" % i, [128, MW], F32) for i in range(2)]
                mixs_ = [sb(ph, "a_mix%d" % i, [128, MW], BF16) for i in range(2)]
                st6s = [sb(ph, "a_st%d" % i, [128, 2, 6], F32) for i in range(2)]
                mvs = [sb(ph, "a_mv%d" % i, [128, 4], F32) for i in range(2)]
                cat = [sb(ph, "a_cat%d" % i, [128, 8, 128], BF16) for i in range(2)]
                nch = LIMIT or NT // 128

                def load_x(c):
                    xb = xin[c % 5]
                    S.dma("sp", xb.t[:], X0[c * 128:(c + 1) * 128, :], xb.r, writes=[xb.r])

                def chunk_gen(c):
                    if c + 3 < nch:
                        load_x(c + 3)
                    p = c % 2
                    xb, hb, cb = xin[c % 5], hn[p], cat[p]
                    wk, ek, awk = wks[p], eks[p], awks[p]
                    gu, gv, vn, tmp, mix, st6, mv = gus[p], gvs[p], vns[p], tmps[p], mixs_[p], st6s[p], mvs[p]
                    si, _, _ = seq_of(c * 128)
                    norm_T(ph, xb.t[:], xb.r, CV_MIXPRE[0], hb.t[:], hb.r, wk)
                    yield
                    for nb in range(3):
                        for j in range(8):
                            mm(ps(nb), hb.t[:, j, :], win.t[:, j, nb * 512:(nb + 1) * 512], j == 0, j == 7,
                               [hb.r, win.r], [BK[nb]])
                    yield
                    S.op("act", lambda: nc.scalar.activation(out=gv.t[:], in_=PSA[:, MW:2 * MW], func=GELU),
                         reads=[BK[1], BK[2]], writes=[gv.r])
                    S.op("act", lambda: nc.scalar.activation(out=gu.t[:], in_=PSA[:, 0:MW], func=GELU),
                         reads=[BK[0], BK[1]], writes=[gu.r])
                    for k in range(2):
                        S.op("dve", lambda k=k: nc.vector.bn_stats(out=st6.t[:, k, :], in_=gv.t[:, k * 384:(k + 1) * 384]),
                             reads=[gv.r], writes=[st6.r])
                    S.op("dve", lambda: nc.vector.bn_aggr(out=mv.t[:, 0:2], in_=st6.t[:]), reads=[st6.r], writes=[mv.r])
                    S.op("act", lambda: nc.scalar.activation(out=mv.t[:, 2:3], in_=mv.t[:, 1:2], func=AF.Sqrt,
                                                             scale=1.0, bias=wk["eps"].t[:, 1:2]),
                         reads=[mv.r, wk["eps"].r], writes=[mv.r])
                    S.op("dve", lambda: nc.vector.reciprocal(out=mv.t[:, 3:4], in_=mv.t[:, 2:3]), reads=[mv.r], writes=[mv.r])
                    S.op("dve", lambda: nc.vector.tensor_scalar(out=gv.t[:], in0=gv.t[:], scalar1=mv.t[:, 0:1],
                                                                scalar2=mv.t[:, 3:4], op0=ALU.subtract, op1=ALU.mult),
                         reads=[gv.r, mv.r], writes=[gv.r])
                    S.op("pool", lambda: nc.gpsimd.tensor_tensor(out=gv.t[:], in0=gv.t[:], in1=lng.t[:], op=ALU.mult),
                         reads=[gv.r, lng.r], writes=[gv.r])
                    S.op("pool", lambda: nc.gpsimd.tensor_tensor(out=vn.t[:], in0=gv.t[:], in1=lnb.t[:], op=ALU.add),
                         reads=[gv.r, lnb.r], writes=[vn.r])
                    yield
                    for g in range(12):
                        col = 3 * 512 + g * 64
                        mm(PSA[:, col:col + 64], wst.t[:, g, :], vn.t[:, g * 64:(g + 1) * 64], True, True,
                           [wst.r, vn.r], [BK[3 + (g // 8)]])
                    yield
                    S.op("dve", lambda: nc.vector.tensor_tensor(
                        out=tmp.t[:].rearrange("p (g d) -> p g d", g=12),
                        in0=PSA[:, 3 * 512:3 * 512 + MW].rearrange("p (g d) -> p g d", g=12),
                        in1=bst.t[:].unsqueeze(2).to_broadcast([128, 12, 64]), op=ALU.add),
                        reads=[BK[3], BK[4], bst.r], writes=[tmp.r])
                    S.op("pool", lambda: nc.gpsimd.tensor_tensor(out=mix.t[:], in0=tmp.t[:], in1=gu.t[:], op=ALU.mult),
                         reads=[tmp.r, gu.r], writes=[mix.r])
                    yield
                    for k in range(6):
                        S.op("pe", lambda k=k: nc.tensor.transpose(out=PBt[:, k * 128:(k + 1) * 128],
                                                                   in_=mix.t[:, k * 128:(k + 1) * 128], identity=ident.t[:]),
                             reads=[mix.r, ident.r], writes=[PBr])
                    S.op("act", lambda: nc.scalar.copy(out=cb.t[:, 0:6, :].rearrange("p a t -> p (a t)"), in_=PBt[:, 0:768]),
                         reads=[PBr], writes=[cb.r])
                    yield
                    yield from attention_gen(lambda j: hb.t[:, j, :], hb.r,
                                             lambda j, hc: win.t[:, j, 1536 + hc * 128:1536 + (hc + 1) * 128], win.r,
                                             0 * 3 + si, cb, awk, (5, 0, 1, 5, 6))
                    yield
                    for nb in range(2):
                        for k in range(8):
                            mm(ps(3 + nb), cb.t[:, k, :], wout.t[:, k, nb * 512:(nb + 1) * 512], k == 0, k == 7,
                               [cb.r, wout.r], [BK[3 + nb]])
                    yield
                    epilogue(ph, (3, 4), xb.t[:], xb.r, gpost.t[:], gpost.r, c * 128, X1, CV_FFNPRE[0], HNT1, wk, ek)

                for c in range(min(3, nch)):
                    load_x(c)
                active = []
                nxt = 0
                while active or nxt < nch:
                    if nxt < nch and len(active) < 2:
                        active.append(chunk_gen(nxt))
                        nxt += 1
                    for gen in list(active):
                        try:
                            next(gen)
                        except StopIteration:
                            active.remove(gen)
                if "KTD" in dbg:
                    KTD = dscr("KTD", [128, 6 * 2 * 256], BF16)
                    VVD = dscr("VVD", [128, 6 * 2 * 256], BF16)
                    S.dma("sp", KTD, kT.t[:].rearrange("p a b c -> p (a b c)"), Res("ktd"), reads=[kT.r])
                    S.dma("sp", VVD, vv.t[:].rearrange("p a b c -> p (a b c)"), Res("vvd"), reads=[vv.r])
                S.barrier()

        def ffn_phase(layer, XIN, HIN, XOUT, gnext, HOUT):
            T = 256
            with ExitStack() as ph:
                wup = sb(ph, "wup", [128, 8, 2 * DFF], BF16)
                for j in range(8):
                    S.dma("pool", wup.t[:, j, :], W_UP[layer, j * 128:(j + 1) * 128, :], wup.r, writes=[wup.r])
                wdn = sb(ph, "wdn", [128, NCH_FF, D], BF16)
                for ci in range(NCH_FF):
                    S.dma("pool", wdn.t[:, ci, :], W_DN[layer, ci * 128:(ci + 1) * 128, :], wdn.r, writes=[wdn.r])
                gpost = sb(ph, "f_gpost", [128, D], F32)
                S.dma("sp", gpost.t[:], ROWV[:, (2 + layer) * D:(3 + layer) * D].partition_broadcast(128), gpost.r,
                      writes=[gpost.r])
                wk = mk_normwk(ph, "f")
                ek = mk_epi(ph, "f")
                hb = sb(ph, "f_hb", [128, 8, T + 2], BF16)
                xin = [sb(ph, "fx%d" % i, [128, D], F32) for i in range(2)]
                tb = [sb(ph, "ft%d" % i, [128, T], F32) for i in range(2)]
                hact = sb(ph, "hact", [128, NCH_FF, T], BF16)
                cv = CV_FFC[layer]
                ntile = LIMIT or NT // T
                xcnt = 0
                for ti in range(ntile):
                    a = ti * T
                    load_hnt_tile(HIN, hb, a, T)
                    for ci in range(NCH_FF):
                        tbuf = tb[ci % 2]
                        bm, bu = ci % 2, 2 + ci % 2
                        conv_chunk(hb, T, lambda j: wup.t[:, j, ci * 128:(ci + 1) * 128], wup.r, cv, NCH_FF, ci,
                                   bm, tbuf)
                        S.op("act", lambda tbuf=tbuf: nc.scalar.activation(out=tbuf.t[:, 0:T], in_=tbuf.t[:, 0:T], func=GELU),
                             reads=[tbuf.r], writes=[tbuf.r])
                        for j in range(8):
                            mm(ps(bu, 0, T), wup.t[:, j, DFF + ci * 128:DFF + (ci + 1) * 128], hb.t[:, j, 1:T + 1],
                               j == 0, j == 7, [wup.r, hb.r], [BK[bu]])
                        S.op("dve", lambda tbuf=tbuf, bu=bu, ci=ci: nc.vector.tensor_tensor(
                            out=hact.t[:, ci, :], in0=ps(bu, 0, T), in1=tbuf.t[:, 0:T], op=ALU.mult),
                            reads=[BK[bu], tbuf.r], writes=[hact.r])
                    for tt in range(T // 128):
                        xb = xin[xcnt % 2]
                        xcnt += 1
                        tok0 = a + tt * 128
                        S.dma("sp", xb.t[:], XIN[tok0:tok0 + 128, :], xb.r, writes=[xb.r])
                        for nb in range(2):
                            for ci in range(NCH_FF):
                                mm(ps(5 + nb), hact.t[:, ci, tt * 128:(tt + 1) * 128], wdn.t[:, ci, nb * 512:(nb + 1) * 512],
                                   ci == 0, ci == NCH_FF - 1, [hact.r, wdn.r], [BK[5 + nb]])
                        epilogue(ph, (5, 6), xb.t[:], xb.r, gpost.t[:], gpost.r, tok0, XOUT, gnext, HOUT, wk, ek)
                S.barrier()

        if "B0" in stages:
            ffn_phase(0, X1, HNT1, X2, CV_MIXPRE[1], HNT2)

        if "C" in stages:
            T = 256
            with ExitStack() as ph:
                win = sb(ph, "b_win", [128, 8, 2560], BF16)
                for j in range(8):
                    S.dma("pool", win.t[:, j, :], B_W_IN[j * 128:(j + 1) * 128, :], win.r, writes=[win.r])
                awks = [mk_attwk(ph, "c%d" % i) for i in range(2)]
                hbs = [sb(ph, "c_hb%d" % i, [128, 8, T + 2], BF16) for i in range(2)]
                tb = [sb(ph, "c_t%d" % i, [128, T], F32) for i in range(2)]
                ob = [sb(ph, "c_o%d" % i, [128, T], BF16) for i in range(3)]
                cat = [sb(ph, "c_cat%d" % i, [128, 8, 128], BF16) for i in range(2)]
                cst = [Res("cst0"), Res("cst1")]
                ntile = LIMIT or NT // T
                oc = 0
                cc = 0
                for ti in range(ntile):
                    a = ti * T
                    hb = hbs[ti % 2]
                    si, _, _ = seq_of(a)
                    load_hnt_tile(HNT2, hb, a, T)
                    gens = []
                    cbs = []
                    for tt in range(T // 128):
                        cb = cat[cc % 2]
                        cc += 1
                        cbs.append((cb, cst[cc % 2], tt))
                        gens.append(attention_gen(lambda j, tt=tt: hb.t[:, j, 1 + tt * 128:1 + (tt + 1) * 128], hb.r,
                                                  lambda j, hc: win.t[:, j, 2304 + hc * 128:2304 + (hc + 1) * 128], win.r,
                                                  3 + si, cb, awks[tt % 2], (5, 2, 3, 5, 6)))
                    for ci in range(18):
                        tbuf = tb[ci % 2]
                        o = ob[oc % 3]
                        oc += 1
                        conv_chunk(hb, T, lambda j: win.t[:, j, ci * 128:(ci + 1) * 128], win.r, CV_SC, 18, ci,
                                   ci % 2, tbuf, fin=o)
                        S.dma("pool", PT[ci * 128:(ci + 1) * 128, a:a + T], o.t[:], o.r, reads=[o.r])
                        next(gens[0] if ci < 9 else gens[1], None)
                        if ci == 8:
                            for _ in gens[0]:
                                pass
                    for gen in gens:
                        for _ in gen:
                            pass
                    for cb, cres, tt in cbs:
                        S.dma("pool", ATT.rearrange("(k p) t -> p k t", p=128)[:, :, a + tt * 128:a + (tt + 1) * 128],
                              cb.t[:, 6:8, :], cres, reads=[cb.r])
                S.barrier()


        UNITS = [dict(tok0=0, L=2048, N2=32, nb=2), dict(tok0=4096, L=4096, N2=64, nb=1)]
        NG = 6

        def load_unit_tables(ph):
            f1 = sb(ph, "h_f1", [64, 128], BF16)
            S.dma("pool", f1.t[:], F1T, f1.r, writes=[f1.r])
            return f1

        if "FF" in stages:
            with ExitStack() as ph:
                f1 = load_unit_tables(ph)
                w3sd = sb(ph, "h_w3sd", [64, NG, 2, 2, 128], BF16)
                fvec = sb(ph, "h_fvec", [64, 8], F32)
                w1 = sb(ph, "h_w1", [33, 64], F32)
                w2 = sb(ph, "h_w2", [64, 64], F32)
                dl = sb(ph, "h_dl", [64, MW], F32)
                fb = sb(ph, "h_fb", [1, 2 * MW], F32)
                S.dma("sp", fvec.t[:, 0:3], FVEC, fvec.r, writes=[fvec.r])
                S.dma("sp", w1.t[:], FW1, w1.r, writes=[w1.r])
                S.dma("sp", w2.t[:], FW2, w2.r, writes=[w2.r])
                S.dma("sp", dl.t[:], ROWV[:, 4 * D + 2 * MW:4 * D + 3 * MW].partition_broadcast(64), dl.r, writes=[dl.r])
                S.dma("sp", fb.t[:], FBIAS, fb.r, writes=[fb.r])
                S.op("dve", lambda: nc.vector.tensor_tensor(out=fvec.t[:, 3:4], in0=fvec.t[:, 0:1], in1=fvec.t[:, 1:2], op=ALU.mult),
                     reads=[fvec.r], writes=[fvec.r])
                S.op("dve", lambda: nc.vector.tensor_tensor(out=fvec.t[:, 4:5], in0=fvec.t[:, 0:1], in1=fvec.t[:, 2:3], op=ALU.mult),
                     reads=[fvec.r], writes=[fvec.r])
                with ExitStack() as tmp:
                    w3r = sb(tmp, "h_w3r", [64, NG, 2, 2, 128], F32)
                    S.dma("sp", w3r.t[:].rearrange("p g o d c -> p (g o d c)"), FW3, w3r.r, writes=[w3r.r])
                    S.op("dve", lambda: nc.vector.tensor_tensor(out=w3sd.t[:, :, :, 0, :], in0=w3r.t[:, :, :, 0, :],
                                                                in1=w3r.t[:, :, :, 1, :], op=ALU.add),
                         reads=[w3r.r], writes=[w3sd.r])
                    S.op("dve", lambda: nc.vector.tensor_tensor(out=w3sd.t[:, :, :, 1, :], in0=w3r.t[:, :, :, 0, :],
                                                                in1=w3r.t[:, :, :, 1, :], op=ALU.subtract),
                         reads=[w3r.r], writes=[w3sd.r])
                    S.barrier()
                h2T = sb(ph, "h_h2T", [64, 4096], BF16)
                htok = sb(ph, "h_htok", [64, 2, 64, 128], BF16)
                Abuf = sb(ph, "h_A", [128, 64, 2, 128], BF16)
                kst = [sb(ph, "h_kst%d" % i, [64, 8, 2, 128], BF16) for i in range(2)]
                gtab = sb(ph, "h_gtab", [128, 8, 8, 3, 64], BF16)
                dec = [sb(ph, "h_dec%d" % i, [64, 128], F32) for i in range(2)]
                su = sb(ph, "h_su", [64, 64], F32)
                mt_ = [sb(ph, "h_mt%d" % i, [64, 512], F32) for i in range(3)]
                h1c = sb(ph, "h_h1c", [64, 512], F32)
                ft = [sb(ph, "h_ft%d" % i, [33, 512], F32) for i in range(2)]
                t0f = sb(ph, "h_t0f", [1, 256], F32)
                MAGIC = 12582912.0
                TWO_PI = 2.0 * math.pi

                def sin_layer(psum_ap, psum_res, fcol, out_ap, out_res):
                    a, u, k = mt_
                    S.op("dve", lambda: nc.vector.tensor_scalar(out=a.t[:], in0=psum_ap, scalar1=fvec.t[:, 0:1],
                                                                scalar2=fvec.t[:, fcol:fcol + 1], op0=ALU.mult, op1=ALU.add),
                         reads=[psum_res, fvec.r], writes=[a.r])
                    S.op("dve", lambda: nc.vector.tensor_scalar(out=u.t[:], in0=a.t[:], scalar1=1.0 / TWO_PI, scalar2=MAGIC,
                                                                op0=ALU.mult, op1=ALU.add), reads=[a.r], writes=[u.r])
                    S.op("dve", lambda: nc.vector.tensor_scalar(out=k.t[:], in0=u.t[:], scalar1=MAGIC, scalar2=-TWO_PI,
                                                                op0=ALU.subtract, op1=ALU.mult), reads=[u.r], writes=[k.r])
                    S.op("dve", lambda: nc.vector.tensor_tensor(out=a.t[:], in0=a.t[:], in1=k.t[:], op=ALU.add),
                         reads=[a.r, k.r], writes=[a.r])
                    S.op("dve", lambda: nc.vector.tensor_scalar(out=a.t[:], in0=a.t[:], scalar1=3.1415925, scalar2=-3.1415925,
                                                                op0=ALU.min, op1=ALU.max), reads=[a.r], writes=[a.r])
                    S.op("act", lambda: nc.scalar.activation(out=out_ap, in_=a.t[:], func=AF.Sin), reads=[a.r], writes=[out_res])

                for ui, U in enumerate(UNITS):
                    L, N2, nbt = U["L"], U["N2"], U["nb"]
                    S.dma("sp", su.t[:], SU[ui], su.r, writes=[su.r])
                    for kb in range(8):
                        S.dma("pool", gtab.t[0:64, kb], GT[ui, kb], gtab.r, writes=[gtab.r])
                        S.dma("pool", gtab.t[64:128, kb], GT[ui, kb], gtab.r, writes=[gtab.r])
                    for cc in range(L // 512):
                        fbuf = ft[cc % 2]
                        S.dma("sp", fbuf.t[:], FEAT[ui][:, cc * 512:(cc + 1) * 512], fbuf.r, writes=[fbuf.r])
                        mm(PSA[0:64, 0:512], w1.t[:], fbuf.t[:], True, True, [w1.r, fbuf.r], [BK[0]])
                        sin_layer(PSA[0:64, 0:512], BK[0], 3, h1c.t[:], h1c.r)
                        mm(PSA[0:64, 512:1024], w2.t[:], h1c.t[:], True, True, [w2.r, h1c.r], [BK[1]])
                        sin_layer(PSA[0:64, 512:1024], BK[1], 4, h2T.t[:, cc * 512:(cc + 1) * 512], h2T.r)
                    for g in range(NGLIM or NG):
                        for o in range(2):
                            for n2 in range(N2):
                                b = 2 + n2 % 2
                                d_ = dec[n2 % 2]
                                mm(PSA[0:64, b * 512:b * 512 + 256], h2T.t[:, n2:L:N2],
                                   w3sd.t[:, g, o].rearrange("p a c -> p (a c)"), True, True, [h2T.r, w3sd.r], [BK[b]])
                                S.op("act", lambda d_=d_, n2=n2: nc.scalar.activation(
                                    out=d_.t[:], in_=dl.t[:, g * 128:(g + 1) * 128], func=AF.Exp, scale=su.t[:, n2:n2 + 1]),
                                    reads=[dl.r, su.r], writes=[d_.r])
                                for bb in range(nbt):
                                    i = bb * N2 + n2
                                    S.op("dve", lambda b=b, d_=d_, i=i: nc.vector.tensor_tensor(
                                        out=htok.t[:, :, i, :],
                                        in0=PSA[0:64, b * 512:b * 512 + 256].rearrange("p (a c) -> p a c", a=2),
                                        in1=d_.t[:].unsqueeze(1).to_broadcast([64, 2, 128]), op=ALU.mult),
                                        reads=[BK[b], d_.r], writes=[htok.r])
                            for bb in range(nbt):
                                i0 = bb * N2
                                S.op("dve", lambda i0=i0: nc.vector.tensor_tensor(
                                    out=t0f.t[0:1, 0:128], in0=htok.t[0:1, 0, i0, :], in1=htok.t[0:1, 1, i0, :], op=ALU.add),
                                    reads=[htok.r], writes=[t0f.r])
                                S.op("dve", lambda: nc.vector.scalar_tensor_tensor(
                                    out=t0f.t[0:1, 128:256], in0=t0f.t[0:1, 0:128], scalar=0.5,
                                    in1=fb.t[0:1, o * MW + g * 128:o * MW + (g + 1) * 128], op0=ALU.mult, op1=ALU.add),
                                    reads=[t0f.r, fb.r], writes=[t0f.r])
                                for a_ in range(2):
                                    S.op("dve", lambda a_=a_, i0=i0: nc.vector.tensor_copy(
                                        out=htok.t[0:1, a_, i0, :], in_=t0f.t[0:1, 128:256]),
                                        reads=[t0f.r], writes=[htok.r])
                            for c0 in range(0, 128, 4):
                                b = (c0 // 4) % 4
                                for c in range(c0, c0 + 4):
                                    mm(PSA[:, b * 512 + (c - c0) * 128:b * 512 + (c - c0 + 1) * 128],
                                       htok.t[:, :, :, c], f1.t[:], True, True, [htok.r, f1.r], [BK[b]])
                                evac((c0 // 4) % 2, Abuf.t[:, :, :, c0:c0 + 4].rearrange("p k r c -> p (k r) c"),
                                     PSA[:, b * 512:(b + 1) * 512].rearrange("p (c x) -> p x c", c=4),
                                     [BK[b]], [Abuf.r])
                            for a_ in range(2):
                                lo = a_ * 64
                                for kb in range(8):
                                    gt = gtab
                                    ks = kst[kb % 2]
                                    for kp in range(4):
                                        b = 4 + (a_ * 32 + kb * 4 + kp) % 3
                                        for kk in range(2):
                                            kl = kp * 2 + kk
                                            k1 = kb * 8 + kl
                                            v0, v1 = (0, 2) if a_ == 0 else (1, 0)
                                            out = PSA[0:64, b * 512 + kk * 128:b * 512 + (kk + 1) * 128]
                                            mm(out, gt.t[lo:lo + 64, kb, kl, v0, :], Abuf.t[lo:lo + 64, k1, 0, :], True, False,
                                               [gt.r, Abuf.r], [BK[b]])
                                            mm(out, gt.t[lo:lo + 64, kb, kl, v1, :], Abuf.t[lo:lo + 64, k1, 1, :], False, True,
                                               [gt.r, Abuf.r], [BK[b]])
                                        evac(kp % 2, ks.t[:, kp * 2:kp * 2 + 2, a_, :],
                                             PSA[0:64, b * 512:b * 512 + 256].rearrange("p (k c) -> p k c", k=2),
                                             [BK[b]], [ks.r])
                                    S.dma("sp", KF[ui, g, o, kb, :, :, a_, :], ks.t[:, :, a_, :], ks.r, reads=[ks.r])
                S.barrier()

        if "FD" in stages:
            with ExitStack() as ph:
                f1 = load_unit_tables(ph)
                dtb = sb(ph, "h_dtb", [64, 3, 64], BF16)
                vtok = sb(ph, "h_vtok", [64, 128, 64], BF16)
                x1tok = sb(ph, "h_x1tok", [64, 128, 64], BF16)
                zview = vtok.t[:].rearrange("p c i -> p (c i)").rearrange("p (i c) -> p i c", c=128)
                AC = sb(ph, "h_AC", [64, 64, 2, 128], BF16)
                Yb = sb(ph, "h_Y", [64, 2, 64, 128], BF16)
                Xs = [sb(ph, "h_xs%d" % i, [64, 8, 2, 128], BF16) for i in range(2)]
                kfs = [sb(ph, "h_kf%d" % i, [64, 8, 2, 128], BF16) for i in range(2)]
                t1 = sb(ph, "h_t1", [64, 8, 128], F32)
                t2 = sb(ph, "h_t2", [64, 8, 128], F32)
                gtab = sb(ph, "h_gtabd", [64, 8, 8, 3, 64], BF16)
                ets = [sb(ph, "h_et%d" % i, [64, 8, 2, 64], BF16) for i in range(2)]
                x2s = sb(ph, "h_x2s", [128, 4096], BF16)
                mixs = sb(ph, "h_mixs", [128, 4096], BF16)
                units = UNITS if LIMIT is None else UNITS[:LIMIT]
                for ui, U in enumerate(units):
                    L, N2, nbt, tok0 = U["L"], U["N2"], U["nb"], U["tok0"]
                    S.dma("pool", dtb.t[:], DTB[ui], dtb.r, writes=[dtb.r])
                    for kb in range(8):
                        S.dma("pool", gtab.t[:, kb], GT[ui, kb], gtab.r, writes=[gtab.r])
                    for g in range(NG if STOP is None else STOP):
                        for bb in range(nbt):
                            lo = tok0 + bb * L
                            S.dma("sp", vtok.t[:, :, bb * N2:(bb + 1) * N2],
                                  PT[g * 128:(g + 1) * 128, lo:lo + L].rearrange("c (n1 n2) -> n1 c n2", n2=N2),
                                  vtok.r, writes=[vtok.r])
                            S.dma("sp", x1tok.t[:, :, bb * N2:(bb + 1) * N2],
                                  PT[MW + g * 128:MW + (g + 1) * 128, lo:lo + L].rearrange("c (n1 n2) -> n1 c n2", n2=N2),
                                  x1tok.r, writes=[x1tok.r])
                        S.dma("sp", x2s.t[:], PT[2 * MW + g * 128:2 * MW + (g + 1) * 128, tok0:tok0 + 4096], x2s.r,
                              writes=[x2s.r])
                        for o in range(2):
                            for c0 in range(0, 128, 4):
                                b = (c0 // 4) % 4
                                for c in range(c0, c0 + 4):
                                    mm(PSA[0:64, b * 512 + (c - c0) * 128:b * 512 + (c - c0 + 1) * 128],
                                       vtok.t[:, c, :] if o == 0 else zview[:, :, c], f1.t[:], True, True,
                                       [vtok.r, f1.r], [BK[b]])
                                evac(0, AC.t[:, :, :, c0:c0 + 4].rearrange("p k r c -> p (k r) c"),
                                     PSA[0:64, b * 512:(b + 1) * 512].rearrange("p (c x) -> p x c", c=4),
                                     [BK[b]], [AC.r])
                            for kb in range(8):
                                gt, kf, xs = gtab, kfs[kb % 2], Xs[kb % 2]
                                S.dma("sp", kf.t[:], KF[ui, g, o, kb], kf.r, writes=[kf.r])
                                for kp in range(4):
                                    b = 4 + (kb * 4 + kp) % 3
                                    for kk in range(2):
                                        kl = kp * 2 + kk
                                        k1 = kb * 8 + kl
                                        ore = PSA[0:64, b * 512 + kk * 256:b * 512 + kk * 256 + 128]
                                        oim = PSA[0:64, b * 512 + kk * 256 + 128:b * 512 + kk * 256 + 256]
                                        mm(ore, gt.t[:, kb, kl, 0, :], AC.t[:, k1, 0, :], True, False, [gt.r, AC.r], [BK[b]])
                                        mm(ore, gt.t[:, kb, kl, 2, :], AC.t[:, k1, 1, :], False, True, [gt.r, AC.r], [BK[b]])
                                        mm(oim, gt.t[:, kb, kl, 1, :], AC.t[:, k1, 0, :], True, False, [gt.r, AC.r], [BK[b]])
                                        mm(oim, gt.t[:, kb, kl, 0, :], AC.t[:, k1, 1, :], False, True, [gt.r, AC.r], [BK[b]])
                                    evac(0, xs.t[:, kp * 2:kp * 2 + 2, :, :].rearrange("p k r c -> p (k r c)"),
                                         PSA[0:64, b * 512:(b + 1) * 512], [BK[b]], [xs.r])
                                xre, xim = xs.t[:, :, 0, :], xs.t[:, :, 1, :]
                                kre, kim = kf.t[:, :, 0, :], kf.t[:, :, 1, :]
                                yre = Yb.t[:, 0, kb * 8:(kb + 1) * 8, :]
                                yim = Yb.t[:, 1, kb * 8:(kb + 1) * 8, :]
                                S.op("dve", lambda: nc.vector.tensor_tensor(out=t1.t[:], in0=xre, in1=kre, op=ALU.mult),
                                     reads=[xs.r, kf.r], writes=[t1.r])
                                S.op("dve", lambda: nc.vector.tensor_tensor(out=t2.t[:], in0=xim, in1=kim, op=ALU.mult),
                                     reads=[xs.r, kf.r], writes=[t2.r])
                                S.op("dve", lambda: nc.vector.tensor_tensor(out=yre, in0=t1.t[:], in1=t2.t[:], op=ALU.subtract),
                                     reads=[t1.r, t2.r], writes=[Yb.r])
                                S.op("dve", lambda: nc.vector.tensor_tensor(out=t1.t[:], in0=xre, in1=kim, op=ALU.mult),
                                     reads=[xs.r, kf.r], writes=[t1.r])
                                S.op("dve", lambda: nc.vector.tensor_tensor(out=t2.t[:], in0=xim, in1=kre, op=ALU.mult),
                                     reads=[xs.r, kf.r], writes=[t2.r])
                                S.op("dve", lambda: nc.vector.tensor_tensor(out=yim, in0=t1.t[:], in1=t2.t[:], op=ALU.add),
                                     reads=[t1.r, t2.r], writes=[Yb.r])
                            for c0 in range(0, 128, 4):
                                b = (c0 // 4) % 4
                                for c in range(c0, c0 + 4):
                                    out = PSA[0:64, b * 512 + (c - c0) * 128:b * 512 + (c - c0 + 1) * 128]
                                    mm(out, Yb.t[:, 0, :, c], dtb.t[:, 1:3, :].rearrange("p v i -> p (v i)"), True, False,
                                       [Yb.r, dtb.r], [BK[b]])
                                    mm(out, Yb.t[:, 1, :, c], dtb.t[:, 0:2, :].rearrange("p v i -> p (v i)"), False, True,
                                       [Yb.r, dtb.r], [BK[b]])
                                evac((c0 // 4) % 2, AC.t[:, :, :, c0:c0 + 4],
                                     PSA[0:64, b * 512:(b + 1) * 512].rearrange("p (c r i) -> p i r c", c=4, r=2),
                                     [BK[b]], [AC.r])
                            for ib in range(8):
                                et = ets[ib % 2]
                                S.dma("pool", et.t[:], ET[ui, ib], et.r, writes=[et.r])
                                if o == 0:
                                    for ip in range(2):
                                        b = 4 + (ib * 2 + ip) % 3
                                        for il4 in range(4):
                                            il = ip * 4 + il4
                                            i = ib * 8 + il
                                            out = PSA[0:64, b * 512 + il4 * 128:b * 512 + (il4 + 1) * 128]
                                            mm(out, et.t[:, il, 0, :], AC.t[:, i, 0, :], True, False, [et.r, AC.r], [BK[b]])
                                            mm(out, et.t[:, il, 1, :], AC.t[:, i, 1, :], False, True, [et.r, AC.r], [BK[b]])
                                        i0 = ib * 8 + ip * 4
                                        S.op("dve", lambda b=b, i0=i0: nc.vector.tensor_tensor(
                                            out=zview[:, i0:i0 + 4, :],
                                            in0=PSA[0:64, b * 512:(b + 1) * 512].rearrange("p (i c) -> p i c", i=4),
                                            in1=x1tok.t[:, :, i0:i0 + 4].rearrange("p c i -> p i c"), op=ALU.mult),
                                            reads=[BK[b], x1tok.r], writes=[vtok.r])
                                else:
                                    b = 4 + ib % 3
                                    for il in range(8):
                                        i = ib * 8 + il
                                        out = PSA[:, b * 512 + il * 64:b * 512 + (il + 1) * 64]
                                        mm(out, AC.t[:, i, 0, :], et.t[:, il, 0, :], True, False, [et.r, AC.r], [BK[b]])
                                        mm(out, AC.t[:, i, 1, :], et.t[:, il, 1, :], False, True, [et.r, AC.r], [BK[b]])
                                    bb = (ib * 8) // N2
                                    n20 = (ib * 8) % N2
                                    view = lambda t: t[:, bb * L:(bb + 1) * L].rearrange("p (t1 n2) -> p n2 t1", n2=N2)[:, n20:n20 + 8, :]
                                    S.op("dve", lambda b=b, view=view: nc.vector.tensor_tensor(
                                        out=view(mixs.t), in0=PSA[:, b * 512:(b + 1) * 512].rearrange("p (i t) -> p i t", i=8),
                                        in1=view(x2s.t), op=ALU.mult),
                                        reads=[BK[b], x2s.r], writes=[mixs.r])
                        S.dma("sp", MIXT[g * 128:(g + 1) * 128, tok0:tok0 + 4096], mixs.t[:], mixs.r, reads=[mixs.r])
                S.barrier()
        if "D" in stages:
            with ExitStack() as ph:
                wout = sb(ph, "d_wout", [128, 8, D], BF16)
                S.dma("pool", wout.t[:], W_OUT[1].rearrange("(j p) n -> p j n", p=128), wout.r, writes=[wout.r])
                gpost = sb(ph, "d_gpost", [128, D], F32)
                S.dma("sp", gpost.t[:], ROWV[:, D:2 * D].partition_broadcast(128), gpost.r, writes=[gpost.r])
                wks = [mk_normwk(ph, "d%d" % i) for i in range(2)]
                eks = [mk_epi(ph, "d%d" % i) for i in range(2)]
                xin = [sb(ph, "dx%d" % i, [128, D], F32) for i in range(3)]
                cat = [sb(ph, "d_cat%d" % i, [128, 8, 128], BF16) for i in range(3)]
                nch = LIMIT or NT // 128

                def chunk_gen(c):
                    xb, cb = xin[c % 3], cat[c % 3]
                    t0 = c * 128
                    S.dma("sp", xb.t[:], X2[t0:t0 + 128, :], xb.r, writes=[xb.r])
                    S.dma("sp", cb.t[:, 0:6, :], MIXT.rearrange("(k p) t -> p k t", p=128)[:, :, t0:t0 + 128], cb.r,
                          writes=[cb.r])
                    S.dma("sp", cb.t[:, 6:8, :], ATT.rearrange("(k p) t -> p k t", p=128)[:, :, t0:t0 + 128], cb.r,
                          writes=[cb.r])
                    yield
                    bk = (3, 4) if c % 2 == 0 else (5, 6)
                    for nb in range(2):
                        for k in range(8):
                            mm(ps(bk[nb]), cb.t[:, k, :], wout.t[:, k, nb * 512:(nb + 1) * 512], k == 0, k == 7,
                               [cb.r, wout.r], [BK[bk[nb]]])
                    yield
                    epilogue(ph, bk, xb.t[:], xb.r, gpost.t[:], gpost.r, t0, X3, CV_FFNPRE[1], HNT3, wks[c % 2], eks[c % 2])

                active = []
                nxt = 0
                while active or nxt < nch:
                    if nxt < nch and len(active) < 3:
                        active.append(chunk_gen(nxt))
                        nxt += 1
                    for gen in list(active):
                        try:
                            next(gen)
                        except StopIteration:
                            active.remove(gen)
                S.barrier()

        if "B1" in stages:
            ffn_phase(1, X3, HNT3, Y, None, None)

        S.barrier()
        print("program: %d instructions, %d waits, %d dma sems" % (S.ninst, S.nwait, len(S.chans)))
    return nc


def _hyena_tables():
    t = {}
    n1 = np.arange(64)[:, None]
    k1 = np.arange(64)[None, :]
    ang = 2 * np.pi * n1 * (k1 + 0.5) / 128.0
    f1 = np.zeros((64, 128), np.float64)
    f1[:, 0::2] = np.cos(ang)
    f1[:, 1::2] = -np.sin(ang)
    t["f1t"] = f1.astype(np.float32)
    gt = np.zeros((2, 8, 64, 8, 3, 64), np.float64)
    dtb = np.zeros((2, 64, 3, 64), np.float64)
    et = np.zeros((2, 8, 64, 8, 2, 64), np.float64)
    su = np.zeros((2, 64, 64), np.float64)
    for ui, (L, N2, nb) in enumerate(((2048, 32, 2), (4096, 64, 1))):
        N = 2 * L
        n2 = np.arange(N2)[:, None]
        k2 = np.arange(N2)[None, :]
        for k1v in range(64):
            G = np.exp(-2j * np.pi * n2 * (k1v + 128 * k2 + 0.5) / N)
            Gf = np.zeros((64, 64), np.complex128)
            for b in range(nb):
                Gf[b * N2:(b + 1) * N2, b * N2:(b + 1) * N2] = G
            kb, kl = divmod(k1v, 8)
            gt[ui, kb, :, kl, 0, :] = Gf.real
            gt[ui, kb, :, kl, 1, :] = Gf.imag
            gt[ui, kb, :, kl, 2, :] = -Gf.imag
        Dm = np.exp(2j * np.pi * np.arange(N2)[:, None] * np.arange(N2)[None, :] / N2)
        Df = np.zeros((64, 64), np.complex128)
        for b in range(nb):
            Df[b * N2:(b + 1) * N2, b * N2:(b + 1) * N2] = Dm
        dtb[ui, :, 0, :] = -Df.imag
        dtb[ui, :, 1, :] = Df.real
        dtb[ui, :, 2, :] = Df.imag
        k1c = np.arange(64)[:, None]
        t1 = np.arange(64)[None, :]
        for i in range(64):
            t2 = i % N2
            E = (2.0 / N) * np.exp(2j * np.pi * (t2 + N2 * t1) * (k1c + 0.5) / N)
            ib, il = divmod(i, 8)
            et[ui, ib, :, il, 0, :] = E.real
            et[ui, ib, :, il, 1, :] = -E.imag
        nn1 = np.arange(64)[:, None]
        nn2 = np.arange(64)[None, :]
        su[ui] = np.where(nn2 < N2, -(N2 * nn1 + nn2) / (L - 1.0), 0.0)
        tt = np.linspace(0.0, 1.0, L)[:, None]
        w = (2.0 * np.pi / L) * np.arange(L)[:, None]
        f = np.linspace(1e-4, 15.0, 16)[None, :]
        feat = np.concatenate([tt, np.cos(f * w), -np.sin(f * w)], axis=-1)
        t["featP" if ui == 0 else "featS"] = np.ascontiguousarray(feat.T).astype(np.float32)
    t["gt"] = gt.astype(np.float32)
    t["dtb"] = dtb.astype(np.float32)
    t["et"] = et.astype(np.float32)
    t["su"] = su.astype(np.float32)
    return t


def _prep_shared(inp):
    f = lambda a: np.ascontiguousarray(np.asarray(a, dtype=np.float32))
    sh = {}
    sh["a_w_in"] = f(inp["a_w_in"][0])
    sh["b_w_in"] = f(inp["b_w_in"][0])
    sh["w_kv"] = f(inp["xattn_w_kv"])
    sh["w_out"] = f(inp["w_out"])
    sh["w_up"] = f(inp["ffn_w_up"])
    sh["w_dn"] = f(inp["ffn_w_down"])
    sh["wsT"] = f(np.transpose(np.asarray(inp["a_w_s"][0]), (2, 0, 1)))
    sh["bsT"] = f(np.transpose(np.asarray(inp["a_b_s"][0]), (1, 0)))

    def cols(v):
        v = np.asarray(v, dtype=np.float32)
        return v.reshape(-1, 128).T

    cl = []
    for nm in ("norm_mix_pre", "norm_ffn_pre"):
        pass
    a = inp
    cl += [cols(a["norm_mix_pre"][0]), cols(a["norm_ffn_pre"][0]), cols(a["norm_mix_pre"][1]), cols(a["norm_ffn_pre"][1])]
    cl += [cols(a["norm_mem"][0]), cols(a["norm_mem"][1])]
    for l in range(2):
        for k in range(3):
            cl.append(cols(a["ffn_conv_w"][l][k]))
        cl.append(cols(a["ffn_conv_b"][l]))
    for k in range(3):
        cl.append(cols(a["b_sconv_w"][0][k]))
    cl.append(cols(a["b_sconv_b"][0]))
    colv = np.concatenate(cl, axis=1)
    assert colv.shape == (128, NCV), colv.shape
    sh["colv"] = f(colv)
    deltas = np.abs(np.linspace(math.log(1e-2) / 1.5, math.log(1e-2) / 0.3, MW, dtype=np.float32))
    rowv = np.concatenate([np.asarray(a["norm_mix_post"][0]), np.asarray(a["norm_mix_post"][1]),
                           np.asarray(a["norm_ffn_post"][0]), np.asarray(a["norm_ffn_post"][1]),
                           np.asarray(a["a_ln_g"][0]), np.asarray(a["a_ln_b"][0]), deltas]).astype(np.float32)
    sh["rowv"] = f(rowv[None, :])
    sh["ident"] = np.eye(128, dtype=np.float32)
    sh.update(_hyena_tables())
    sh["fw1"] = f(a["b_filt_w1"][0])
    sh["fw2"] = f(a["b_filt_w2"][0])
    w3 = np.asarray(a["b_filt_w3"][0], dtype=np.float32).reshape(64, 2, 2, 6, 128)
    sh["fw3"] = f(np.transpose(w3, (0, 3, 2, 1, 4)).reshape(64, 3072))
    sh["fvec"] = f(np.stack([np.asarray(a["b_filt_freq"][0]), np.asarray(a["b_filt_b1"][0]),
                             np.asarray(a["b_filt_b2"][0])], axis=1))
    sh["fbias"] = f(np.asarray(a["b_filt_bias"][0]).reshape(1, 1536))
    return sh


def _core_inputs(inp, i):
    xp = np.asarray(inp["x_prompt"])
    xs = np.asarray(inp["x_sample"])
    mp = np.asarray(inp["mem_prompt"])
    ms = np.asarray(inp["mem_sample"])
    X = np.concatenate([xp[2 * i], xp[2 * i + 1], xs[i]], axis=0)
    M = np.concatenate([mp[2 * i], mp[2 * i + 1], ms[i]], axis=0)
    return {"X": np.ascontiguousarray(X, dtype=np.float32), "MEM": np.ascontiguousarray(M, dtype=np.float32)}


def kernel(**inputs):
    sh = _prep_shared(inputs)
    nc = build()
    in_maps = []
    for i in range(8):
        m = dict(sh)
        m.update(_core_inputs(inputs, i))
        in_maps.append(m)
    res = run_bass_kernel_spmd(nc, in_maps, core_ids=list(range(8)))
    yp = np.zeros((16, 2048, D), np.float32)
    ys = np.zeros((8, 4096, D), np.float32)
    for i in range(8):
        y = res.results[i]["Y"]
        yp[2 * i] = y[0:2048]
        yp[2 * i + 1] = y[2048:4096]
        ys[i] = y[4096:8192]
    return (yp, ys)
```

```python
import math
from contextlib import ExitStack

import numpy as np
import concourse.bass as bass
import concourse.mybir as mybir
from concourse.bass_utils import run_bass_kernel_spmd

F32 = mybir.dt.float32
BF16 = mybir.dt.bfloat16
AF = mybir.ActivationFunctionType
ALU = mybir.AluOpType

NT = 8192
SEQS = [(0, 2048), (2048, 2048), (4096, 4096)]
NORM_EPS = 1e-6
LN_EPS = 1e-5
D = 1024
MW = 768
DFF = 2816
NCH_FF = 22
GELU = AF.Gelu_apprx_tanh
STOP = None
NGLIM = None
LIMIT = None

CV_MIXPRE = [0, 16]
CV_FFNPRE = [8, 24]
CV_MEM = [32, 40]
CV_FFC = [48, 48 + 88]
CV_SC = 48 + 176
NCV = CV_SC + 72


class Res:
    __slots__ = ("name", "w", "r", "chan", "excl")

    def __init__(self, name, excl=False):
        self.name = name
        self.w = None
        self.r = []
        self.chan = None
        self.excl = excl


class Chan:
    __slots__ = ("sem", "cnt", "q")

    def __init__(self, sem):
        self.sem = sem
        self.cnt = 0
        self.q = None


class TB:
    def __init__(self, t, name):
        self.t = t
        self.r = Res(name)


class Sched:
    def __init__(self, nc, stack):
        self.nc = nc
        self.stack = stack
        self.eng = {"pe": nc.tensor, "act": nc.scalar, "dve": nc.vector,
                    "pool": nc.gpsimd, "sp": nc.sync}
        self.esem = {}
        self.ecnt = {}
        for e in ("pe", "act", "dve", "pool"):
            self.esem[e] = stack.enter_context(nc.semaphore("s_" + e))
            self.ecnt[e] = 0
        self.known = {e: {} for e in self.eng}
        self.chans = []
        self.free = {"sp": [], "pool": []}
        self.ninst = 0
        self.nwait = 0

    def _need(self, e, ev, same_ok):
        if ev is None:
            return
        sem, val = ev
        if same_ok and e in self.esem and sem is self.esem[e]:
            return
        k = self.known[e]
        if k.get(id(sem), 0) >= val:
            return
        k[id(sem)] = val
        self.eng[e].wait_ge(sem, val)
        self.nwait += 1

    def _deps(self, e, reads, writes, same_ok):
        for r in reads:
            self._need(e, r.w, same_ok)
        for r in writes:
            self._need(e, r.w, same_ok)
            for ev in r.r:
                self._need(e, ev, same_ok)

    def _post(self, ev, reads, writes):
        for r in reads:
            r.r.append(ev)
            if len(r.r) > 10:
                d = {}
                for s, v in r.r:
                    if id(s) not in d or d[id(s)][1] < v:
                        d[id(s)] = (s, v)
                r.r = list(d.values())
        for r in writes:
            r.w = ev
            r.r = []

    def op(self, e, fn, reads=(), writes=()):
        ex = [r for r in reads if r.excl]
        if ex:
            writes = list(writes) + ex
            reads = [r for r in reads if not r.excl]
        self._deps(e, reads, writes, same_ok=(e == "pe"))
        ins = fn()
        self.ecnt[e] += 1
        ins.then_inc(self.esem[e], 1)
        ev = (self.esem[e], self.ecnt[e])
        self._post(ev, reads, writes)
        self.ninst += 1
        return ev

    def dma(self, q, out, in_, chan, reads=(), writes=()):
        skip = chan.chan.sem if chan.chan is not None else None
        for r in reads:
            self._need(q, r.w, False)
        for r in writes:
            if not (r.w is not None and r.w[0] is skip and r is chan and not r.r):
                self._need(q, r.w, False)
            for ev in r.r:
                self._need(q, ev, False)
        if chan.chan is None:
            if self.free[q]:
                chan.chan = self.free[q].pop()
            else:
                c = Chan(self.stack.enter_context(self.nc.semaphore("d%s%d" % (q, len(self.chans)))))
                c.q = q
                self.chans.append(c)
                chan.chan = c
        c = chan.chan
        assert c.q == q, "a DMA channel semaphore must stay on one queue type"
        ins = self.eng[q].dma_start(out=out, in_=in_)
        c.cnt += 16
        ins.then_inc(c.sem, 16)
        ev = (c.sem, c.cnt)
        self._post(ev, reads, writes)
        self.ninst += 1
        return ev

    def barrier(self):
        for e in self.eng:
            for e2 in self.esem:
                if self.ecnt[e2]:
                    self._need(e, (self.esem[e2], self.ecnt[e2]), False)
            for c in self.chans:
                if c.cnt:
                    self._need(e, (c.sem, c.cnt), False)
        self.free = {"sp": [c for c in self.chans if c.q == "sp"], "pool": [c for c in self.chans if c.q == "pool"]}


def build(stages=("kv", "A", "B0", "C", "FF", "FD", "D", "B1"), dbg=(), ext=()):
    nc = bass.Bass("TRN2", target_bir_lowering=False)

    def din(name, shape, dt=F32):
        return nc.dram_tensor(name, list(shape), dt, kind="ExternalInput").ap()

    def dscr(name, shape, dt):
        kind = "ExternalOutput" if name in dbg else ("ExternalInput" if name in ext else "Internal")
        return nc.dram_tensor(name, list(shape), dt, kind=kind).ap()

    X0 = din("X", [NT, D])
    MEM = din("MEM", [768, D])
    A_W_IN = din("a_w_in", [D, 1792])
    B_W_IN = din("b_w_in", [D, 2560])
    W_KV = din("w_kv", [2, D, 512])
    W_OUT = din("w_out", [2, D, D])
    W_UP = din("w_up", [2, D, 2 * DFF])
    W_DN = din("w_dn", [2, DFF, D])
    WST = din("wsT", [128, 12, 128])
    BST = din("bsT", [128, 12])
    COLV = din("colv", [128, NCV])
    ROWV = din("rowv", [1, 4 * D + 3 * MW])
    IDENT = din("ident", [128, 128])
    F1T = din("f1t", [64, 128])
    GT = din("gt", [2, 8, 64, 8, 3, 64])
    DTB = din("dtb", [2, 64, 3, 64])
    ET = din("et", [2, 8, 64, 8, 2, 64])
    FEAT = [din("featP", [33, 2048]), din("featS", [33, 4096])]
    SU = din("su", [2, 64, 64])
    FW1 = din("fw1", [33, 64])
    FW2 = din("fw2", [64, 64])
    FW3 = din("fw3", [64, 3072])
    FVEC = din("fvec", [64, 3])
    FBIAS = din("fbias", [1, 1536])
    Y = nc.dram_tensor("Y", [NT, D], F32, kind="ExternalOutput").ap()

    X1 = dscr("X1", [NT, D], F32)
    X2 = dscr("X2", [NT, D], F32)
    X3 = dscr("X3", [NT, D], F32)
    HNT1 = dscr("HNT1", [D, NT], BF16)
    HNT2 = dscr("HNT2", [D, NT], BF16)
    HNT3 = dscr("HNT3", [D, NT], BF16)
    PT = dscr("PT", [2304, NT], BF16)
    ATT = dscr("ATT", [256, NT], BF16)
    MIXT = dscr("MIXT", [MW, NT], BF16)
    KF = dscr("KF", [2, 6, 2, 8, 64, 8, 2, 128], BF16)

    with ExitStack() as st:
        S = Sched(nc, st)

        uid = [0]

        def sb(stack, name, shape, dt):
            uid[0] += 1
            return TB(stack.enter_context(nc.sbuf_tensor("sb%d_%s" % (uid[0], name), list(shape), dt)), name)

        PSA = st.enter_context(nc.psum_tensor("PSA", [128, 7 * 512], F32))
        PBt = st.enter_context(nc.psum_tensor("PB", [128, 1024], BF16))
        BK = [Res("bank%d" % i, excl=True) for i in range(7)]
        PBr = Res("pb", excl=True)

        def ps(b, lo=0, hi=512):
            return PSA[:, b * 512 + lo: b * 512 + hi]

        ident = sb(st, "ident", [128, 128], BF16)
        ones = sb(st, "ones", [128, 128], BF16)
        colv = sb(st, "colv", [128, NCV], F32)
        kT = sb(st, "kT", [128, 6, 2, 256], BF16)
        vv = sb(st, "vv", [128, 6, 2, 256], BF16)
        S.dma("pool", ident.t[:], IDENT, ident.r, writes=[ident.r])
        S.dma("sp", colv.t[:], COLV, colv.r, writes=[colv.r])
        S.op("dve", lambda: nc.vector.memset(ones.t[:], 1.0), writes=[ones.r])

        def mm(out, lhsT, rhs, start, stop, reads, writes):
            S.op("pe", lambda: nc.tensor.matmul(out, lhsT=lhsT, rhs=rhs, start=start, stop=stop),
                 reads=reads, writes=writes)

        def evac(which, out_ap, in_ap, reads, writes):
            if which == 0:
                S.op("act", lambda: nc.scalar.copy(out=out_ap, in_=in_ap), reads=reads, writes=writes)
            else:
                S.op("dve", lambda: nc.vector.tensor_copy(out=out_ap, in_=in_ap), reads=reads, writes=writes)

        def norm_T(ph, xt_ap, xt_res, gbase, out_ap, out_res, wk):
            junk, ss, xn = wk["junk"], wk["ss"], wk["xn"]
            S.op("act", lambda: nc.scalar.activation(out=junk.t[:, 0:D], in_=xt_ap, func=AF.Square,
                                                     accum_out=ss.t[:, 0:1]),
                 reads=[xt_res], writes=[junk.r, ss.r])
            S.op("act", lambda: nc.scalar.activation(out=ss.t[:, 1:2], in_=ss.t[:, 0:1], func=AF.Sqrt,
                                                     scale=1.0 / D, bias=wk["eps"].t[:, 0:1]),
                 reads=[ss.r, wk["eps"].r], writes=[ss.r])
            S.op("dve", lambda: nc.vector.reciprocal(out=ss.t[:, 2:3], in_=ss.t[:, 1:2]),
                 reads=[ss.r], writes=[ss.r])
            S.op("dve", lambda: nc.vector.tensor_scalar(out=xn.t[:], in0=xt_ap, scalar1=ss.t[:, 2:3],
                                                        scalar2=None, op0=ALU.mult),
                 reads=[xt_res, ss.r], writes=[xn.r])
            for j in range(8):
                S.op("pe", lambda j=j: nc.tensor.transpose(out=PBt[:, j * 128:(j + 1) * 128],
                                                           in_=xn.t[:, j * 128:(j + 1) * 128],
                                                           identity=ident.t[:]),
                     reads=[xn.r, ident.r], writes=[PBr])
            S.op("dve", lambda: nc.vector.tensor_tensor(
                out=out_ap, in0=PBt[:, 0:1024].rearrange("p (j t) -> p j t", j=8),
                in1=colv.t[:, gbase:gbase + 8].unsqueeze(2).to_broadcast([128, 8, 128]), op=ALU.mult),
                reads=[PBr, colv.r], writes=[out_res])

        def mk_normwk(ph, tag):
            wk = {"junk": sb(ph, "junk" + tag, [128, D], BF16),
                  "ss": sb(ph, "ss" + tag, [128, 4], F32),
                  "xn": sb(ph, "xn" + tag, [128, D], BF16),
                  "eps": sb(ph, "eps" + tag, [128, 2], F32)}
            S.op("dve", lambda: nc.vector.memset(wk["eps"].t[:, 0:1], NORM_EPS), writes=[wk["eps"].r])
            S.op("dve", lambda: nc.vector.memset(wk["eps"].t[:, 1:2], LN_EPS), writes=[wk["eps"].r])
            return wk

        def hnt_view(H):
            return H.rearrange("(j p) t -> p j t", p=128)

        def epilogue(ph, banks, xres_ap, xres_res, gpost_ap, gpost_res, tok0, XOUT, gnext, HOUT, wk, ek):
            b0 = banks[0]
            pout = PSA[:, b0 * 512: b0 * 512 + 1024]
            br = [BK[b] for b in banks]
            junk, ss = wk["junk"], ek["ss"]
            S.op("act", lambda: nc.scalar.activation(out=junk.t[:, 0:D], in_=pout, func=AF.Square,
                                                     accum_out=ss.t[:, 0:1]),
                 reads=br, writes=[junk.r, ss.r])
            S.op("act", lambda: nc.scalar.activation(out=ss.t[:, 1:2], in_=ss.t[:, 0:1], func=AF.Sqrt,
                                                     scale=1.0 / D, bias=wk["eps"].t[:, 0:1]),
                 reads=[ss.r, wk["eps"].r], writes=[ss.r])
            S.op("dve", lambda: nc.vector.reciprocal(out=ss.t[:, 2:3], in_=ss.t[:, 1:2]),
                 reads=[ss.r], writes=[ss.r])
            tp = ek["tp"]
            S.op("dve", lambda: nc.vector.tensor_tensor(out=tp.t[:], in0=pout, in1=gpost_ap, op=ALU.mult),
                 reads=br + [gpost_res], writes=[tp.r])
            xnew = ek["xnew"]
            S.op("dve", lambda: nc.vector.scalar_tensor_tensor(out=xnew.t[:], in0=tp.t[:], scalar=ss.t[:, 2:3],
                                                               in1=xres_ap, op0=ALU.mult, op1=ALU.add),
                 reads=[tp.r, ss.r, xres_res], writes=[xnew.r])
            S.dma("pool", XOUT[tok0:tok0 + 128, :], xnew.t[:], ek["xst"], reads=[xnew.r])
            if HOUT is not None:
                hno = ek["hno"]
                norm_T(ph, xnew.t[:], xnew.r, gnext, hno.t[:], hno.r, wk)
                S.dma("pool", hnt_view(HOUT)[:, :, tok0:tok0 + 128], hno.t[:], ek["hst"], reads=[hno.r])

        def mk_epi(ph, tag):
            return {"ss": sb(ph, "ess" + tag, [128, 4], F32),
                    "tp": sb(ph, "tp" + tag, [128, D], F32),
                    "xnew": sb(ph, "xnew" + tag, [128, D], F32),
                    "hno": sb(ph, "hno" + tag, [128, 8, 128], BF16),
                    "xst": Res("xst" + tag), "hst": Res("hst" + tag)}

        def attention(hn_ap, hn_res, wq_ap, wq_res, ls, cat, wk, banks):
            for _ in attention_gen(hn_ap, hn_res, wq_ap, wq_res, ls, cat, wk, banks):
                pass

        def attention_gen(hn_ap, hn_res, wq_ap, wq_res, ls, cat, wk, banks):
            bq, bs0, bs1, bav0, bav1 = banks
            qT, pT, rden = wk["qT"], wk["pT"], wk["rden"]
            for hc in range(2):
                for j in range(8):
                    mm(ps(bq, hc * 128, hc * 128 + 128), wq_ap(j, hc), hn_ap(j), j == 0, j == 7,
                       [wq_res, hn_res], [BK[bq]])
            yield
            S.op("act", lambda: nc.scalar.activation(out=qT.t[:].rearrange("p a t -> p (a t)"), in_=ps(bq, 0, 256),
                                                     func=AF.Copy, scale=0.125),
                 reads=[BK[bq]], writes=[qT.r])
            yield
            for h in range(4):
                hc, po = h // 2, (h % 2) * 64
                for mc in range(2):
                    idx = (h % 2) * 4 + (h // 2) * 2 + mc
                    b = bs0 if idx < 4 else bs1
                    col = (idx % 4) * 128
                    mm(PSA[:, b * 512 + col: b * 512 + col + 128],
                       kT.t[po:po + 64, ls, hc, mc * 128:(mc + 1) * 128], qT.t[po:po + 64, hc, :], True, True,
                       [kT.r, qT.r], [BK[b]])
            yield
            for half, b in ((0, bs0), (1, bs1)):
                S.op("act", lambda half=half, b=b: nc.scalar.activation(
                    out=pT.t[:, half * 4:(half + 1) * 4, :].rearrange("p a t -> p (a t)"), in_=ps(b), func=AF.Exp),
                    reads=[BK[b]], writes=[pT.r])
            yield
            for hc in range(2):
                b = bav0 if hc == 0 else bav1
                for part in range(4):
                    h = 2 * hc + (part % 2)
                    for mc in range(2):
                        lhsT = vv.t[:, ls, mc, hc * 128:(hc + 1) * 128] if part < 2 else ones.t[:]
                        mm(ps(b, part * 128, part * 128 + 128), lhsT, pT.t[:, (h % 2) * 4 + (h // 2) * 2 + mc, :], mc == 0, mc == 1,
                           [vv.r, ones.r, pT.r], [BK[b]])
                for hh in range(2):
                    lo = hh * 64
                    S.op("dve", lambda hh=hh, lo=lo, b=b, hc=hc: nc.vector.reciprocal(
                        out=rden.t[lo:lo + 64, hc, :], in_=PSA[lo:lo + 64, b * 512 + (2 + hh) * 128: b * 512 + (3 + hh) * 128]),
                        reads=[BK[b]], writes=[rden.r])
                for hh in range(2):
                    lo = hh * 64
                    S.op("dve", lambda hh=hh, lo=lo, b=b, hc=hc: nc.vector.tensor_tensor(
                        out=cat.t[lo:lo + 64, 6 + hc, :], in0=PSA[lo:lo + 64, b * 512 + hh * 128: b * 512 + (hh + 1) * 128],
                        in1=rden.t[lo:lo + 64, hc, :], op=ALU.mult),
                        reads=[BK[b], rden.r], writes=[cat.r])

        def mk_attwk(ph, tag):
            return {"qT": sb(ph, "qT" + tag, [128, 2, 128], BF16),
                    "pT": sb(ph, "pT" + tag, [128, 8, 128], BF16),
                    "rden": sb(ph, "rden" + tag, [128, 2, 128], F32)}

        def seq_of(tok):
            for si, (t0, L) in enumerate(SEQS):
                if t0 <= tok < t0 + L:
                    return si, t0, L
            raise ValueError

        def load_hnt_tile(HSRC, hb, a, T):
            si, t0, L = seq_of(a)
            lo = a - 1 if a > t0 else a
            hi = a + T + 1 if a + T < t0 + L else a + T
            if lo == a:
                S.op("dve", lambda: nc.vector.memset(hb.t[:, :, 0:1], 0.0), writes=[hb.r])
            if hi == a + T:
                S.op("dve", lambda: nc.vector.memset(hb.t[:, :, T + 1:T + 2], 0.0), writes=[hb.r])
            S.dma("sp", hb.t[:, :, lo - (a - 1): hi - (a - 1)], hnt_view(HSRC)[:, :, lo:hi], hb.r, writes=[hb.r])

        def conv_chunk(hb, T, w_ap, w_res, cv, nchk, ci, bmain, tbuf, fin=None):
            for j in range(8):
                mm(ps(bmain, 0, T + 2), w_ap(j), hb.t[:, j, 0:T + 2], j == 0, j == 7, [w_res, hb.r], [BK[bmain]])
            c0, c1, c2, cb = (colv.t[:, cv + k * nchk + ci: cv + k * nchk + ci + 1] for k in range(4))
            S.op("act", lambda: nc.scalar.activation(out=tbuf.t[:, 0:T], in_=ps(bmain, 1, T + 1), func=AF.Identity,
                                                     scale=c1, bias=cb),
                 reads=[BK[bmain], colv.r], writes=[tbuf.r])
            S.op("dve", lambda: nc.vector.scalar_tensor_tensor(out=tbuf.t[:, 0:T], in0=ps(bmain, 0, T), scalar=c0,
                                                               in1=tbuf.t[:, 0:T], op0=ALU.mult, op1=ALU.add),
                 reads=[BK[bmain], tbuf.r, colv.r], writes=[tbuf.r])
            fo = tbuf if fin is None else fin
            S.op("dve", lambda: nc.vector.scalar_tensor_tensor(out=fo.t[:, 0:T], in0=ps(bmain, 2, T + 2), scalar=c2,
                                                               in1=tbuf.t[:, 0:T], op0=ALU.mult, op1=ALU.add),
                 reads=[BK[bmain], tbuf.r, colv.r], writes=[fo.r])

        if "kv" in stages:
            with ExitStack() as ph:
                wkv = sb(ph, "wkv", [128, 2, 8, 512], BF16)
                for l in range(2):
                    S.dma("pool", wkv.t[:, l], W_KV[l].rearrange("(j p) n -> p j n", p=128), wkv.r, writes=[wkv.r])
                wk = mk_normwk(ph, "kv")
                xin = [sb(ph, "kvx%d" % i, [128, D], F32) for i in range(2)]
                hn = [sb(ph, "kvh%d" % i, [128, 8, 128], BF16) for i in range(2)]
                it = 0
                for l in range(2):
                    for s in range(3):
                        for mt in range(2):
                            xb, hb = xin[it % 2], hn[it % 2]
                            it += 1
                            r0 = s * 256 + mt * 128
                            S.dma("sp", xb.t[:], MEM[r0:r0 + 128, :], xb.r, writes=[xb.r])
                            norm_T(ph, xb.t[:], xb.r, CV_MEM[l], hb.t[:], hb.r, wk)
                            ls = l * 3 + s
                            for hc in range(2):
                                b = hc
                                for j in range(8):
                                    mm(ps(b, 0, 128), wkv.t[:, l, j, hc * 128:(hc + 1) * 128], hb.t[:, j, :], j == 0, j == 7,
                                       [wkv.r, hb.r], [BK[b]])
                                S.op("act", lambda b=b, hc=hc, ls=ls, mt=mt: nc.scalar.copy(
                                    out=kT.t[:, ls, hc, mt * 128:(mt + 1) * 128], in_=ps(b, 0, 128)),
                                    reads=[BK[b]], writes=[kT.r])
                            for j in range(8):
                                mm(ps(2, 0, 256), hb.t[:, j, :], wkv.t[:, l, j, 256:512], j == 0, j == 7,
                                   [wkv.r, hb.r], [BK[2]])
                            S.op("act", lambda ls=ls, mt=mt: nc.scalar.copy(out=vv.t[:, ls, mt, :], in_=ps(2, 0, 256)),
                                 reads=[BK[2]], writes=[vv.r])
                S.barrier()

        if "A" in stages:
            with ExitStack() as ph:
                win = sb(ph, "a_win", [128, 8, 1792], BF16)
                for j in range(8):
                    S.dma("pool", win.t[:, j, :], A_W_IN[j * 128:(j + 1) * 128, :], win.r, writes=[win.r])
                wout = sb(ph, "a_wout", [128, 8, D], BF16)
                S.dma("pool", wout.t[:], W_OUT[0].rearrange("(j p) n -> p j n", p=128), wout.r, writes=[wout.r])
                wst = sb(ph, "a_wst", [128, 12, 128], BF16)
                S.dma("pool", wst.t[:], WST, wst.r, writes=[wst.r])
                bst = sb(ph, "a_bst", [128, 12], F32)
                S.dma("sp", bst.t[:], BST, bst.r, writes=[bst.r])
                gpost = sb(ph, "a_gpost", [128, D], F32)
                S.dma("sp", gpost.t[:], ROWV[:, 0:D].partition_broadcast(128), gpost.r, writes=[gpost.r])
                lng = sb(ph, "a_lng", [128, MW], F32)
                lnb = sb(ph, "a_lnb", [128, MW], F32)
                S.dma("sp", lng.t[:], ROWV[:, 4 * D:4 * D + MW].partition_broadcast(128), lng.r, writes=[lng.r])
                S.dma("sp", lnb.t[:], ROWV[:, 4 * D + MW:4 * D + 2 * MW].partition_broadcast(128), lnb.r, writes=[lnb.r])
                wks = [mk_normwk(ph, "a%d" % i) for i in range(2)]
                eks = [mk_epi(ph, "a%d" % i) for i in range(2)]
                awks = [mk_attwk(ph, "a%d" % i) for i in range(2)]
                xin = [sb(ph, "ax%d" % i, [128, D], F32) for i in range(5)]
                hn = [sb(ph, "ah%d" % i, [128, 8, 128], BF16) for i in range(2)]
                gus = [sb(ph, "a_gu%d" % i, [128, MW], F32) for i in range(2)]
                gvs = [sb(ph, "a_gv%d" % i, [128, MW], F32) for i in range(2)]
                vns = [sb(ph, "a_vn%d" % i, [128, MW], BF16) for i in range(2)]
                tmps = [sb(ph, "a_tmp%d" % i, [128, MW], F32) for i in range(2)]
                mixs_ = [sb(ph, "a_mix%d" % i, [128, MW], BF16) for i in range(2)]
                st6s = [sb(ph, "a_st%d" % i, [128, 2, 6], F32) for i in range(2)]
                mvs = [sb(ph, "a_mv%d" % i, [128, 4], F32) for i in range(2)]
                cat = [sb(ph, "a_cat%d" % i, [128, 8, 128], BF16) for i in range(2)]
                nch = LIMIT or NT // 128

                def load_x(c):
                    xb = xin[c % 5]
                    S.dma("sp", xb.t[:], X0[c * 128:(c + 1) * 128, :], xb.r, writes=[xb.r])

                def chunk_gen(c):
                    if c + 3 < nch:
                        load_x(c + 3)
                    p = c % 2
                    xb, hb, cb = xin[c % 5], hn[p], cat[p]
                    wk, ek, awk = wks[p], eks[p], awks[p]
                    gu, gv, vn, tmp, mix, st6, mv = gus[p], gvs[p], vns[p], tmps[p], mixs_[p], st6s[p], mvs[p]
                    si, _, _ = seq_of(c * 128)
                    norm_T(ph, xb.t[:], xb.r, CV_MIXPRE[0], hb.t[:], hb.r, wk)
                    yield
                    for nb in range(3):
                        for j in range(8):
                            mm(ps(nb), hb.t[:, j, :], win.t[:, j, nb * 512:(nb + 1) * 512], j == 0, j == 7,
                               [hb.r, win.r], [BK[nb]])
                    yield
                    S.op("act", lambda: nc.scalar.activation(out=gv.t[:], in_=PSA[:, MW:2 * MW], func=GELU),
                         reads=[BK[1], BK[2]], writes=[gv.r])
                    S.op("act", lambda: nc.scalar.activation(out=gu.t[:], in_=PSA[:, 0:MW], func=GELU),
                         reads=[BK[0], BK[1]], writes=[gu.r])
                    for k in range(2):
                        S.op("dve", lambda k=k: nc.vector.bn_stats(out=st6.t[:, k, :], in_=gv.t[:, k * 384:(k + 1) * 384]),
                             reads=[gv.r], writes=[st6.r])
                    S.op("dve", lambda: nc.vector.bn_aggr(out=mv.t[:, 0:2], in_=st6.t[:]), reads=[st6.r], writes=[mv.r])
                    S.op("act", lambda: nc.scalar.activation(out=mv.t[:, 2:3], in_=mv.t[:, 1:2], func=AF.Sqrt,
                                                             scale=1.0, bias=wk["eps"].t[:, 1:2]),
                         reads=[mv.r, wk["eps"].r], writes=[mv.r])
                    S.op("dve", lambda: nc.vector.reciprocal(out=mv.t[:, 3:4], in_=mv.t[:, 2:3]), reads=[mv.r], writes=[mv.r])
                    S.op("dve", lambda: nc.vector.tensor_scalar(out=gv.t[:], in0=gv.t[:], scalar1=mv.t[:, 0:1],
                                                                scalar2=mv.t[:, 3:4], op0=ALU.subtract, op1=ALU.mult),
                         reads=[gv.r, mv.r], writes=[gv.r])
                    S.op("pool", lambda: nc.gpsimd.tensor_tensor(out=gv.t[:], in0=gv.t[:], in1=lng.t[:], op=ALU.mult),
                         reads=[gv.r, lng.r], writes=[gv.r])
                    S.op("pool", lambda: nc.gpsimd.tensor_tensor(out=vn.t[:], in0=gv.t[:], in1=lnb.t[:], op=ALU.add),
                         reads=[gv.r, lnb.r], writes=[vn.r])
                    yield
                    for g in range(12):
                        col = 3 * 512 + g * 64
                        mm(PSA[:, col:col + 64], wst.t[:, g, :], vn.t[:, g * 64:(g + 1) * 64], True, True,
                           [wst.r, vn.r], [BK[3 + (g // 8)]])
                    yield
                    S.op("dve", lambda: nc.vector.tensor_tensor(
                        out=tmp.t[:].rearrange("p (g d) -> p g d", g=12),
                        in0=PSA[:, 3 * 512:3 * 512 + MW].rearrange("p (g d) -> p g d", g=12),
                        in1=bst.t[:].unsqueeze(2).to_broadcast([128, 12, 64]), op=ALU.add),
                        reads=[BK[3], BK[4], bst.r], writes=[tmp.r])
                    S.op("pool", lambda: nc.gpsimd.tensor_tensor(out=mix.t[:], in0=tmp.t[:], in1=gu.t[:], op=ALU.mult),
                         reads=[tmp.r, gu.r], writes=[mix.r])
                    yield
                    for k in range(6):
                        S.op("pe", lambda k=k: nc.tensor.transpose(out=PBt[:, k * 128:(k + 1) * 128],
                                                                   in_=mix.t[:, k * 128:(k + 1) * 128], identity=ident.t[:]),
                             reads=[mix.r, ident.r], writes=[PBr])
                    S.op("act", lambda: nc.scalar.copy(out=cb.t[:, 0:6, :].rearrange("p a t -> p (a t)"), in_=PBt[:, 0:768]),
                         reads=[PBr], writes=[cb.r])
                    yield
                    yield from attention_gen(lambda j: hb.t[:, j, :], hb.r,
                                             lambda j, hc: win.t[:, j, 1536 + hc * 128:1536 + (hc + 1) * 128], win.r,
                                             0 * 3 + si, cb, awk, (5, 0, 1, 5, 6))
                    yield
                    for nb in range(2):
                        for k in range(8):
                            mm(ps(3 + nb), cb.t[:, k, :], wout.t[:, k, nb * 512:(nb + 1) * 512], k == 0, k == 7,
                               [cb.r, wout.r], [BK[3 + nb]])
                    yield
                    epilogue(ph, (3, 4), xb.t[:], xb.r, gpost.t[:], gpost.r, c * 128, X1, CV_FFNPRE[0], HNT1, wk, ek)

                for c in range(min(3, nch)):
                    load_x(c)
                active = []
                nxt = 0
                while active or nxt < nch:
                    if nxt < nch and len(active) < 2:
                        active.append(chunk_gen(nxt))
                        nxt += 1
                    for gen in list(active):
                        try:
                            next(gen)
                        except StopIteration:
                            active.remove(gen)
                if "KTD" in dbg:
                    KTD = dscr("KTD", [128, 6 * 2 * 256], BF16)
                    VVD = dscr("VVD", [128, 6 * 2 * 256], BF16)
                    S.dma("sp", KTD, kT.t[:].rearrange("p a b c -> p (a b c)"), Res("ktd"), reads=[kT.r])
                    S.dma("sp", VVD, vv.t[:].rearrange("p a b c -> p (a b c)"), Res("vvd"), reads=[vv.r])
                S.barrier()

        def ffn_phase(layer, XIN, HIN, XOUT, gnext, HOUT):
            T = 256
            with ExitStack() as ph:
                wup = sb(ph, "wup", [128, 8, 2 * DFF], BF16)
                for j in range(8):
                    S.dma("pool", wup.t[:, j, :], W_UP[layer, j * 128:(j + 1) * 128, :], wup.r, writes=[wup.r])
                wdn = sb(ph, "wdn", [128, NCH_FF, D], BF16)
                for ci in range(NCH_FF):
                    S.dma("pool", wdn.t[:, ci, :], W_DN[layer, ci * 128:(ci + 1) * 128, :], wdn.r, writes=[wdn.r])
                gpost = sb(ph, "f_gpost", [128, D], F32)
                S.dma("sp", gpost.t[:], ROWV[:, (2 + layer) * D:(3 + layer) * D].partition_broadcast(128), gpost.r,
                      writes=[gpost.r])
                wk = mk_normwk(ph, "f")
                ek = mk_epi(ph, "f")
                hb = sb(ph, "f_hb", [128, 8, T + 2], BF16)
                xin = [sb(ph, "fx%d" % i, [128, D], F32) for i in range(2)]
                tb = [sb(ph, "ft%d" % i, [128, T], F32) for i in range(2)]
                hact = sb(ph, "hact", [128, NCH_FF, T], BF16)
                cv = CV_FFC[layer]
                ntile = LIMIT or NT // T
                xcnt = 0
                for ti in range(ntile):
                    a = ti * T
                    load_hnt_tile(HIN, hb, a, T)
                    for ci in range(NCH_FF):
                        tbuf = tb[ci % 2]
                        bm, bu = ci % 2, 2 + ci % 2
                        conv_chunk(hb, T, lambda j: wup.t[:, j, ci * 128:(ci + 1) * 128], wup.r, cv, NCH_FF, ci,
                                   bm, tbuf)
                        S.op("act", lambda tbuf=tbuf: nc.scalar.activation(out=tbuf.t[:, 0:T], in_=tbuf.t[:, 0:T], func=GELU),
                             reads=[tbuf.r], writes=[tbuf.r])
                        for j in range(8):
                            mm(ps(bu, 0, T), wup.t[:, j, DFF + ci * 128:DFF + (ci + 1) * 128], hb.t[:, j, 1:T + 1],
                               j == 0, j == 7, [wup.r, hb.r], [BK[bu]])
                        S.op("dve", lambda tbuf=tbuf, bu=bu, ci=ci: nc.vector.tensor_tensor(
                            out=hact.t[:, ci, :], in0=ps(bu, 0, T), in1=tbuf.t[:, 0:T], op=ALU.mult),
                            reads=[BK[bu], tbuf.r], writes=[hact.r])
                    for tt in range(T // 128):
                        xb = xin[xcnt % 2]
                        xcnt += 1
                        tok0 = a + tt * 128
                        S.dma("sp", xb.t[:], XIN[tok0:tok0 + 128, :], xb.r, writes=[xb.r])
                        for nb in range(2):
                            for ci in range(NCH_FF):
                                mm(ps(5 + nb), hact.t[:, ci, tt * 128:(tt + 1) * 128], wdn.t[:, ci, nb * 512:(nb + 1) * 512],
                                   ci == 0, ci == NCH_FF - 1, [hact.r, wdn.r], [BK[5 + nb]])
                        epilogue(ph, (5, 6), xb.t[:], xb.r, gpost.t[:], gpost.r, tok0, XOUT, gnext, HOUT, wk, ek)
                S.barrier()

        if "B0" in stages:
            ffn_phase(0, X1, HNT1, X2, CV_MIXPRE[1], HNT2)

        if "C" in stages:
            T = 256
            with ExitStack() as ph:
                win = sb(ph, "b_win", [128, 8, 2560], BF16)
                for j in range(8):
                    S.dma("pool", win.t[:, j, :], B_W_IN[j * 128:(j + 1) * 128, :], win.r, writes=[win.r])
                awks = [mk_attwk(ph, "c%d" % i) for i in range(2)]
                hbs = [sb(ph, "c_hb%d" % i, [128, 8, T + 2], BF16) for i in range(2)]
                tb = [sb(ph, "c_t%d" % i, [128, T], F32) for i in range(2)]
                ob = [sb(ph, "c_o%d" % i, [128, T], BF16) for i in range(3)]
                cat = [sb(ph, "c_cat%d" % i, [128, 8, 128], BF16) for i in range(2)]
                cst = [Res("cst0"), Res("cst1")]
                ntile = LIMIT or NT // T
                oc = 0
                cc = 0
                for ti in range(ntile):
                    a = ti * T
                    hb = hbs[ti % 2]
                    si, _, _ = seq_of(a)
                    load_hnt_tile(HNT2, hb, a, T)
                    gens = []
                    cbs = []
                    for tt in range(T // 128):
                        cb = cat[cc % 2]
                        cc += 1
                        cbs.append((cb, cst[cc % 2], tt))
                        gens.append(attention_gen(lambda j, tt=tt: hb.t[:, j, 1 + tt * 128:1 + (tt + 1) * 128], hb.r,
                                                  lambda j, hc: win.t[:, j, 2304 + hc * 128:2304 + (hc + 1) * 128], win.r,
                                                  3 + si, cb, awks[tt % 2], (5, 2, 3, 5, 6)))
                    for ci in range(18):
                        tbuf = tb[ci % 2]
                        o = ob[oc % 3]
                        oc += 1
                        conv_chunk(hb, T, lambda j: win.t[:, j, ci * 128:(ci + 1) * 128], win.r, CV_SC, 18, ci,
                                   ci % 2, tbuf, fin=o)
                        S.dma("pool", PT[ci * 128:(ci + 1) * 128, a:a + T], o.t[:], o.r, reads=[o.r])
                        next(gens[0] if ci < 9 else gens[1], None)
                        if ci == 8:
                            for _ in gens[0]:
                                pass
                    for gen in gens:
                        for _ in gen:
                            pass
                    for cb, cres, tt in cbs:
                        S.dma("pool", ATT.rearrange("(k p) t -> p k t", p=128)[:, :, a + tt * 128:a + (tt + 1) * 128],
                              cb.t[:, 6:8, :], cres, reads=[cb.r])
                S.barrier()


        UNITS = [dict(tok0=0, L=2048, N2=32, nb=2), dict(tok0=4096, L=4096, N2=64, nb=1)]
        NG = 6

        def load_unit_tables(ph):
            f1 = sb(ph, "h_f1", [64, 128], BF16)
            S.dma("pool", f1.t[:], F1T, f1.r, writes=[f1.r])
            return f1

        if "FF" in stages:
            with ExitStack() as ph:
                f1 = load_unit_tables(ph)
                w3sd = sb(ph, "h_w3sd", [64, NG, 2, 2, 128], BF16)
                fvec = sb(ph, "h_fvec", [64, 8], F32)
                w1 = sb(ph, "h_w1", [33, 64], F32)
                w2 = sb(ph, "h_w2", [64, 64], F32)
                dl = sb(ph, "h_dl", [64, MW], F32)
                fb = sb(ph, "h_fb", [1, 2 * MW], F32)
                S.dma("sp", fvec.t[:, 0:3], FVEC, fvec.r, writes=[fvec.r])
                S.dma("sp", w1.t[:], FW1, w1.r, writes=[w1.r])
                S.dma("sp", w2.t[:], FW2, w2.r, writes=[w2.r])
                S.dma("sp", dl.t[:], ROWV[:, 4 * D + 2 * MW:4 * D + 3 * MW].partition_broadcast(64), dl.r, writes=[dl.r])
                S.dma("sp", fb.t[:], FBIAS, fb.r, writes=[fb.r])
                S.op("dve", lambda: nc.vector.tensor_tensor(out=fvec.t[:, 3:4], in0=fvec.t[:, 0:1], in1=fvec.t[:, 1:2], op=ALU.mult),
                     reads=[fvec.r], writes=[fvec.r])
                S.op("dve", lambda: nc.vector.tensor_tensor(out=fvec.t[:, 4:5], in0=fvec.t[:, 0:1], in1=fvec.t[:, 2:3], op=ALU.mult),
                     reads=[fvec.r], writes=[fvec.r])
                with ExitStack() as tmp:
                    w3r = sb(tmp, "h_w3r", [64, NG, 2, 2, 128], F32)
                    S.dma("sp", w3r.t[:].rearrange("p g o d c -> p (g o d c)"), FW3, w3r.r, writes=[w3r.r])
                    S.op("dve", lambda: nc.vector.tensor_tensor(out=w3sd.t[:, :, :, 0, :], in0=w3r.t[:, :, :, 0, :],
                                                                in1=w3r.t[:, :, :, 1, :], op=ALU.add),
                         reads=[w3r.r], writes=[w3sd.r])
                    S.op("dve", lambda: nc.vector.tensor_tensor(out=w3sd.t[:, :, :, 1, :], in0=w3r.t[:, :, :, 0, :],
                                                                in1=w3r.t[:, :, :, 1, :], op=ALU.subtract),
                         reads=[w3r.r], writes=[w3sd.r])
                    S.barrier()
                h2T = sb(ph, "h_h2T", [64, 4096], BF16)
                htok = sb(ph, "h_htok", [64, 2, 64, 128], BF16)
                Abuf = sb(ph, "h_A", [128, 64, 2, 128], BF16)
                kst = [sb(ph, "h_kst%d" % i, [64, 8, 2, 128], BF16) for i in range(2)]
                gtab = sb(ph, "h_gtab", [128, 8, 8, 3, 64], BF16)
                dec = [sb(ph, "h_dec%d" % i, [64, 128], F32) for i in range(2)]
                su = sb(ph, "h_su", [64, 64], F32)
                mt_ = [sb(ph, "h_mt%d" % i, [64, 512], F32) for i in range(3)]
                h1c = sb(ph, "h_h1c", [64, 512], F32)
                ft = [sb(ph, "h_ft%d" % i, [33, 512], F32) for i in range(2)]
                t0f = sb(ph, "h_t0f", [1, 256], F32)
                MAGIC = 12582912.0
                TWO_PI = 2.0 * math.pi

                def sin_layer(psum_ap, psum_res, fcol, out_ap, out_res):
                    a, u, k = mt_
                    S.op("dve", lambda: nc.vector.tensor_scalar(out=a.t[:], in0=psum_ap, scalar1=fvec.t[:, 0:1],
                                                                scalar2=fvec.t[:, fcol:fcol + 1], op0=ALU.mult, op1=ALU.add),
                         reads=[psum_res, fvec.r], writes=[a.r])
                    S.op("dve", lambda: nc.vector.tensor_scalar(out=u.t[:], in0=a.t[:], scalar1=1.0 / TWO_PI, scalar2=MAGIC,
                                                                op0=ALU.mult, op1=ALU.add), reads=[a.r], writes=[u.r])
                    S.op("dve", lambda: nc.vector.tensor_scalar(out=k.t[:], in0=u.t[:], scalar1=MAGIC, scalar2=-TWO_PI,
                                                                op0=ALU.subtract, op1=ALU.mult), reads=[u.r], writes=[k.r])
                    S.op("dve", lambda: nc.vector.tensor_tensor(out=a.t[:], in0=a.t[:], in1=k.t[:], op=ALU.add),
                         reads=[a.r, k.r], writes=[a.r])
                    S.op("dve", lambda: nc.vector.tensor_scalar(out=a.t[:], in0=a.t[:], scalar1=3.1415925, scalar2=-3.1415925,
                                                                op0=ALU.min, op1=ALU.max), reads=[a.r], writes=[a.r])
                    S.op("act", lambda: nc.scalar.activation(out=out_ap, in_=a.t[:], func=AF.Sin), reads=[a.r], writes=[out_res])

                for ui, U in enumerate(UNITS):
                    L, N2, nbt = U["L"], U["N2"], U["nb"]
                    S.dma("sp", su.t[:], SU[ui], su.r, writes=[su.r])
                    for kb in range(8):
                        S.dma("pool", gtab.t[0:64, kb], GT[ui, kb], gtab.r, writes=[gtab.r])
                        S.dma("pool", gtab.t[64:128, kb], GT[ui, kb], gtab.r, writes=[gtab.r])
                    for cc in range(L // 512):
                        fbuf = ft[cc % 2]
                        S.dma("sp", fbuf.t[:], FEAT[ui][:, cc * 512:(cc + 1) * 512], fbuf.r, writes=[fbuf.r])
                        mm(PSA[0:64, 0:512], w1.t[:], fbuf.t[:], True, True, [w1.r, fbuf.r], [BK[0]])
                        sin_layer(PSA[0:64, 0:512], BK[0], 3, h1c.t[:], h1c.r)
                        mm(PSA[0:64, 512:1024], w2.t[:], h1c.t[:], True, True, [w2.r, h1c.r], [BK[1]])
                        sin_layer(PSA[0:64, 512:1024], BK[1], 4, h2T.t[:, cc * 512:(cc + 1) * 512], h2T.r)
                    for g in range(NGLIM or NG):
                        for o in range(2):
                            for n2 in range(N2):
                                b = 2 + n2 % 2
                                d_ = dec[n2 % 2]
                                mm(PSA[0:64, b * 512:b * 512 + 256], h2T.t[:, n2:L:N2],
                                   w3sd.t[:, g, o].rearrange("p a c -> p (a c)"), True, True, [h2T.r, w3sd.r], [BK[b]])
                                S.op("act", lambda d_=d_, n2=n2: nc.scalar.activation(
                                    out=d_.t[:], in_=dl.t[:, g * 128:(g + 1) * 128], func=AF.Exp, scale=su.t[:, n2:n2 + 1]),
                                    reads=[dl.r, su.r], writes=[d_.r])
                                for bb in range(nbt):
                                    i = bb * N2 + n2
                                    S.op("dve", lambda b=b, d_=d_, i=i: nc.vector.tensor_tensor(
                                        out=htok.t[:, :, i, :],
                                        in0=PSA[0:64, b * 512:b * 512 + 256].rearrange("p (a c) -> p a c", a=2),
                                        in1=d_.t[:].unsqueeze(1).to_broadcast([64, 2, 128]), op=ALU.mult),
                                        reads=[BK[b], d_.r], writes=[htok.r])
                            for bb in range(nbt):
                                i0 = bb * N2
                                S.op("dve", lambda i0=i0: nc.vector.tensor_tensor(
                                    out=t0f.t[0:1, 0:128], in0=htok.t[0:1, 0, i0, :], in1=htok.t[0:1, 1, i0, :], op=ALU.add),
                                    reads=[htok.r], writes=[t0f.r])
                                S.op("dve", lambda: nc.vector.scalar_tensor_tensor(
                                    out=t0f.t[0:1, 128:256], in0=t0f.t[0:1, 0:128], scalar=0.5,
                                    in1=fb.t[0:1, o * MW + g * 128:o * MW + (g + 1) * 128], op0=ALU.mult, op1=ALU.add),
                                    reads=[t0f.r, fb.r], writes=[t0f.r])
                                for a_ in range(2):
                                    S.op("dve", lambda a_=a_, i0=i0: nc.vector.tensor_copy(
                                        out=htok.t[0:1, a_, i0, :], in_=t0f.t[0:1, 128:256]),
                                        reads=[t0f.r], writes=[htok.r])
                            for c0 in range(0, 128, 4):
                                b = (c0 // 4) % 4
                                for c in range(c0, c0 + 4):
                                    mm(PSA[:, b * 512 + (c - c0) * 128:b * 512 + (c - c0 + 1) * 128],
                                       htok.t[:, :, :, c], f1.t[:], True, True, [htok.r, f1.r], [BK[b]])
                                evac((c0 // 4) % 2, Abuf.t[:, :, :, c0:c0 + 4].rearrange("p k r c -> p (k r) c"),
                                     PSA[:, b * 512:(b + 1) * 512].rearrange("p (c x) -> p x c", c=4),
                                     [BK[b]], [Abuf.r])
                            for kb in range(8):
                                gt = gtab
                                ks = kst[kb % 2]
                                for kp in range(4):
                                    for a_ in range(2):
                                        lo = a_ * 64
                                        b = (4 + kp % 2) if a_ == 0 else (6 if kp % 2 == 0 else 3)
                                        for kk in range(2):
                                            kl = kp * 2 + kk
                                            k1 = kb * 8 + kl
                                            v0, v1 = (0, 2) if a_ == 0 else (1, 0)
                                            out = PSA[0:64, b * 512 + kk * 128:b * 512 + (kk + 1) * 128]
                                            mm(out, gt.t[lo:lo + 64, kb, kl, v0, :], Abuf.t[lo:lo + 64, k1, 0, :], True, False,
                                               [gt.r, Abuf.r], [BK[b]])
                                            mm(out, gt.t[lo:lo + 64, kb, kl, v1, :], Abuf.t[lo:lo + 64, k1, 1, :], False, True,
                                               [gt.r, Abuf.r], [BK[b]])
                                        evac(a_, ks.t[:, kp * 2:kp * 2 + 2, a_, :],
                                             PSA[0:64, b * 512:b * 512 + 256].rearrange("p (k c) -> p k c", k=2),
                                             [BK[b]], [ks.r])
                                S.dma("sp", KF[ui, g, o, kb], ks.t[:], ks.r, reads=[ks.r])
                S.barrier()

        if "FD" in stages:
            with ExitStack() as ph:
                f1 = load_unit_tables(ph)
                dtb = sb(ph, "h_dtb", [64, 3, 64], BF16)
                vtok = sb(ph, "h_vtok", [64, 128, 64], BF16)
                x1tok = sb(ph, "h_x1tok", [64, 128, 64], BF16)
                zview = vtok.t[:].rearrange("p c i -> p (c i)").rearrange("p (i c) -> p i c", c=128)
                AC = sb(ph, "h_AC", [64, 64, 2, 128], BF16)
                Yb = sb(ph, "h_Y", [64, 2, 64, 128], BF16)
                Xs = [sb(ph, "h_xs%d" % i, [64, 8, 2, 128], BF16) for i in range(2)]
                kfs = [sb(ph, "h_kf%d" % i, [64, 8, 2, 128], BF16) for i in range(2)]
                t1 = sb(ph, "h_t1", [64, 8, 128], F32)
                t2 = sb(ph, "h_t2", [64, 8, 128], F32)
                gtab = sb(ph, "h_gtabd", [64, 8, 8, 3, 64], BF16)
                ets = [sb(ph, "h_et%d" % i, [64, 8, 2, 64], BF16) for i in range(2)]
                x2s = sb(ph, "h_x2s", [128, 4096], BF16)
                mixs = sb(ph, "h_mixs", [128, 4096], BF16)
                units = UNITS if LIMIT is None else UNITS[:LIMIT]
                for ui, U in enumerate(units):
                    L, N2, nbt, tok0 = U["L"], U["N2"], U["nb"], U["tok0"]
                    S.dma("pool", dtb.t[:], DTB[ui], dtb.r, writes=[dtb.r])
                    for kb in range(8):
                        S.dma("pool", gtab.t[:, kb], GT[ui, kb], gtab.r, writes=[gtab.r])
                    for g in range(NG if STOP is None else STOP):
                        for bb in range(nbt):
                            lo = tok0 + bb * L
                            S.dma("sp", vtok.t[:, :, bb * N2:(bb + 1) * N2],
                                  PT[g * 128:(g + 1) * 128, lo:lo + L].rearrange("c (n1 n2) -> n1 c n2", n2=N2),
                                  vtok.r, writes=[vtok.r])
                            S.dma("sp", x1tok.t[:, :, bb * N2:(bb + 1) * N2],
                                  PT[MW + g * 128:MW + (g + 1) * 128, lo:lo + L].rearrange("c (n1 n2) -> n1 c n2", n2=N2),
                                  x1tok.r, writes=[x1tok.r])
                        S.dma("sp", x2s.t[:], PT[2 * MW + g * 128:2 * MW + (g + 1) * 128, tok0:tok0 + 4096], x2s.r,
                              writes=[x2s.r])
                        for o in range(2):
                            for c0 in range(0, 128, 4):
                                b = (c0 // 4) % 4
                                for c in range(c0, c0 + 4):
                                    mm(PSA[0:64, b * 512 + (c - c0) * 128:b * 512 + (c - c0 + 1) * 128],
                                       vtok.t[:, c, :] if o == 0 else zview[:, :, c], f1.t[:], True, True,
                                       [vtok.r, f1.r], [BK[b]])
                                evac(0, AC.t[:, :, :, c0:c0 + 4].rearrange("p k r c -> p (k r) c"),
                                     PSA[0:64, b * 512:(b + 1) * 512].rearrange("p (c x) -> p x c", c=4),
                                     [BK[b]], [AC.r])
                            for kb in range(8):
                                gt, kf, xs = gtab, kfs[kb % 2], Xs[kb % 2]
                                S.dma("sp", kf.t[:], KF[ui, g, o, kb], kf.r, writes=[kf.r])
                                for kp in range(4):
                                    b = 4 + (kb * 4 + kp) % 3
                                    for kk in range(2):
                                        kl = kp * 2 + kk
                                        k1 = kb * 8 + kl
                                        ore = PSA[0:64, b * 512 + kk * 256:b * 512 + kk * 256 + 128]
                                        oim = PSA[0:64, b * 512 + kk * 256 + 128:b * 512 + kk * 256 + 256]
                                        mm(ore, gt.t[:, kb, kl, 0, :], AC.t[:, k1, 0, :], True, False, [gt.r, AC.r], [BK[b]])
                                        mm(ore, gt.t[:, kb, kl, 2, :], AC.t[:, k1, 1, :], False, True, [gt.r, AC.r], [BK[b]])
                                        mm(oim, gt.t[:, kb, kl, 1, :], AC.t[:, k1, 0, :], True, False, [gt.r, AC.r], [BK[b]])
                                        mm(oim, gt.t[:, kb, kl, 0, :], AC.t[:, k1, 1, :], False, True, [gt.r, AC.r], [BK[b]])
                                    evac(0, xs.t[:, kp * 2:kp * 2 + 2, :, :].rearrange("p k r c -> p (k r c)"),
                                         PSA[0:64, b * 512:(b + 1) * 512], [BK[b]], [xs.r])
                                xre, xim = xs.t[:, :, 0, :], xs.t[:, :, 1, :]
                                kre, kim = kf.t[:, :, 0, :], kf.t[:, :, 1, :]
                                yre = Yb.t[:, 0, kb * 8:(kb + 1) * 8, :]
                                yim = Yb.t[:, 1, kb * 8:(kb + 1) * 8, :]
                                S.op("dve", lambda: nc.vector.tensor_tensor(out=t1.t[:], in0=xre, in1=kre, op=ALU.mult),
                                     reads=[xs.r, kf.r], writes=[t1.r])
                                S.op("dve", lambda: nc.vector.tensor_tensor(out=t2.t[:], in0=xim, in1=kim, op=ALU.mult),
                                     reads=[xs.r, kf.r], writes=[t2.r])
                                S.op("dve", lambda: nc.vector.tensor_tensor(out=yre, in0=t1.t[:], in1=t2.t[:], op=ALU.subtract),
                                     reads=[t1.r, t2.r], writes=[Yb.r])
                                S.op("dve", lambda: nc.vector.tensor_tensor(out=t1.t[:], in0=xre, in1=kim, op=ALU.mult),
                                     reads=[xs.r, kf.r], writes=[t1.r])
                                S.op("dve", lambda: nc.vector.tensor_tensor(out=t2.t[:], in0=xim, in1=kre, op=ALU.mult),
                                     reads=[xs.r, kf.r], writes=[t2.r])
                                S.op("dve", lambda: nc.vector.tensor_tensor(out=yim, in0=t1.t[:], in1=t2.t[:], op=ALU.add),
                                     reads=[t1.r, t2.r], writes=[Yb.r])
                            for c0 in range(0, 128, 4):
                                b = (c0 // 4) % 4
                                for c in range(c0, c0 + 4):
                                    out = PSA[0:64, b * 512 + (c - c0) * 128:b * 512 + (c - c0 + 1) * 128]
                                    mm(out, Yb.t[:, 0, :, c], dtb.t[:, 1:3, :].rearrange("p v i -> p (v i)"), True, False,
                                       [Yb.r, dtb.r], [BK[b]])
                                    mm(out, Yb.t[:, 1, :, c], dtb.t[:, 0:2, :].rearrange("p v i -> p (v i)"), False, True,
                                       [Yb.r, dtb.r], [BK[b]])
                                evac((c0 // 4) % 2, AC.t[:, :, :, c0:c0 + 4],
                                     PSA[0:64, b * 512:(b + 1) * 512].rearrange("p (c r i) -> p i r c", c=4, r=2),
                                     [BK[b]], [AC.r])
                            for ib in range(8):
                                et = ets[ib % 2]
                                S.dma("pool", et.t[:], ET[ui, ib], et.r, writes=[et.r])
                                if o == 0:
                                    for ip in range(2):
                                        b = 4 + (ib * 2 + ip) % 3
                                        for il4 in range(4):
                                            il = ip * 4 + il4
                                            i = ib * 8 + il
                                            out = PSA[0:64, b * 512 + il4 * 128:b * 512 + (il4 + 1) * 128]
                                            mm(out, et.t[:, il, 0, :], AC.t[:, i, 0, :], True, False, [et.r, AC.r], [BK[b]])
                                            mm(out, et.t[:, il, 1, :], AC.t[:, i, 1, :], False, True, [et.r, AC.r], [BK[b]])
                                        i0 = ib * 8 + ip * 4
                                        S.op("dve", lambda b=b, i0=i0: nc.vector.tensor_tensor(
                                            out=zview[:, i0:i0 + 4, :],
                                            in0=PSA[0:64, b * 512:(b + 1) * 512].rearrange("p (i c) -> p i c", i=4),
                                            in1=x1tok.t[:, :, i0:i0 + 4].rearrange("p c i -> p i c"), op=ALU.mult),
                                            reads=[BK[b], x1tok.r], writes=[vtok.r])
                                else:
                                    b = 4 + ib % 3
                                    for il in range(8):
                                        i = ib * 8 + il
                                        out = PSA[:, b * 512 + il * 64:b * 512 + (il + 1) * 64]
                                        mm(out, AC.t[:, i, 0, :], et.t[:, il, 0, :], True, False, [et.r, AC.r], [BK[b]])
                                        mm(out, AC.t[:, i, 1, :], et.t[:, il, 1, :], False, True, [et.r, AC.r], [BK[b]])
                                    bb = (ib * 8) // N2
                                    n20 = (ib * 8) % N2
                                    view = lambda t: t[:, bb * L:(bb + 1) * L].rearrange("p (t1 n2) -> p n2 t1", n2=N2)[:, n20:n20 + 8, :]
                                    S.op("dve", lambda b=b, view=view: nc.vector.tensor_tensor(
                                        out=view(mixs.t), in0=PSA[:, b * 512:(b + 1) * 512].rearrange("p (i t) -> p i t", i=8),
                                        in1=view(x2s.t), op=ALU.mult),
                                        reads=[BK[b], x2s.r], writes=[mixs.r])
                        S.dma("sp", MIXT[g * 128:(g + 1) * 128, tok0:tok0 + 4096], mixs.t[:], mixs.r, reads=[mixs.r])
                S.barrier()
        if "D" in stages:
            with ExitStack() as ph:
                wout = sb(ph, "d_wout", [128, 8, D], BF16)
                S.dma("pool", wout.t[:], W_OUT[1].rearrange("(j p) n -> p j n", p=128), wout.r, writes=[wout.r])
                gpost = sb(ph, "d_gpost", [128, D], F32)
                S.dma("sp", gpost.t[:], ROWV[:, D:2 * D].partition_broadcast(128), gpost.r, writes=[gpost.r])
                wks = [mk_normwk(ph, "d%d" % i) for i in range(2)]
                eks = [mk_epi(ph, "d%d" % i) for i in range(2)]
                xin = [sb(ph, "dx%d" % i, [128, D], F32) for i in range(3)]
                cat = [sb(ph, "d_cat%d" % i, [128, 8, 128], BF16) for i in range(3)]
                nch = LIMIT or NT // 128

                def chunk_gen(c):
                    xb, cb = xin[c % 3], cat[c % 3]
                    t0 = c * 128
                    S.dma("sp", xb.t[:], X2[t0:t0 + 128, :], xb.r, writes=[xb.r])
                    S.dma("sp", cb.t[:, 0:6, :], MIXT.rearrange("(k p) t -> p k t", p=128)[:, :, t0:t0 + 128], cb.r,
                          writes=[cb.r])
                    S.dma("sp", cb.t[:, 6:8, :], ATT.rearrange("(k p) t -> p k t", p=128)[:, :, t0:t0 + 128], cb.r,
                          writes=[cb.r])
                    yield
                    bk = (3, 4) if c % 2 == 0 else (5, 6)
                    for nb in range(2):
                        for k in range(8):
                            mm(ps(bk[nb]), cb.t[:, k, :], wout.t[:, k, nb * 512:(nb + 1) * 512], k == 0, k == 7,
                               [cb.r, wout.r], [BK[bk[nb]]])
                    yield
                    epilogue(ph, bk, xb.t[:], xb.r, gpost.t[:], gpost.r, t0, X3, CV_FFNPRE[1], HNT3, wks[c % 2], eks[c % 2])

                active = []
                nxt = 0
                while active or nxt < nch:
                    if nxt < nch and len(active) < 3:
                        active.append(chunk_gen(nxt))
                        nxt += 1
                    for gen in list(active):
                        try:
                            next(gen)
                        except StopIteration:
                            active.remove(gen)
                S.barrier()

        if "B1" in stages:
            ffn_phase(1, X3, HNT3, Y, None, None)

        S.barrier()
        print("program: %d instructions, %d waits, %d dma sems" % (S.ninst, S.nwait, len(S.chans)))
    return nc


def _hyena_tables():
    t = {}
    n1 = np.arange(64)[:, None]
    k1 = np.arange(64)[None, :]
    ang = 2 * np.pi * n1 * (k1 + 0.5) / 128.0
    f1 = np.zeros((64, 128), np.float64)
    f1[:, 0::2] = np.cos(ang)
    f1[:, 1::2] = -np.sin(ang)
    t["f1t"] = f1.astype(np.float32)
    gt = np.zeros((2, 8, 64, 8, 3, 64), np.float64)
    dtb = np.zeros((2, 64, 3, 64), np.float64)
    et = np.zeros((2, 8, 64, 8, 2, 64), np.float64)
    su = np.zeros((2, 64, 64), np.float64)
    for ui, (L, N2, nb) in enumerate(((2048, 32, 2), (4096, 64, 1))):
        N = 2 * L
        n2 = np.arange(N2)[:, None]
        k2 = np.arange(N2)[None, :]
        for k1v in range(64):
            G = np.exp(-2j * np.pi * n2 * (k1v + 128 * k2 + 0.5) / N)
            Gf = np.zeros((64, 64), np.complex128)
            for b in range(nb):
                Gf[b * N2:(b + 1) * N2, b * N2:(b + 1) * N2] = G
            kb, kl = divmod(k1v, 8)
            gt[ui, kb, :, kl, 0, :] = Gf.real
            gt[ui, kb, :, kl, 1, :] = Gf.imag
            gt[ui, kb, :, kl, 2, :] = -Gf.imag
        Dm = np.exp(2j * np.pi * np.arange(N2)[:, None] * np.arange(N2)[None, :] / N2)
        Df = np.zeros((64, 64), np.complex128)
        for b in range(nb):
            Df[b * N2:(b + 1) * N2, b * N2:(b + 1) * N2] = Dm
        dtb[ui, :, 0, :] = -Df.imag
        dtb[ui, :, 1, :] = Df.real
        dtb[ui, :, 2, :] = Df.imag
        k1c = np.arange(64)[:, None]
        t1 = np.arange(64)[None, :]
        for i in range(64):
            t2 = i % N2
            E = (2.0 / N) * np.exp(2j * np.pi * (t2 + N2 * t1) * (k1c + 0.5) / N)
            ib, il = divmod(i, 8)
            et[ui, ib, :, il, 0, :] = E.real
            et[ui, ib, :, il, 1, :] = -E.imag
        nn1 = np.arange(64)[:, None]
        nn2 = np.arange(64)[None, :]
        su[ui] = np.where(nn2 < N2, -(N2 * nn1 + nn2) / (L - 1.0), 0.0)
        tt = np.linspace(0.0, 1.0, L)[:, None]
        w = (2.0 * np.pi / L) * np.arange(L)[:, None]
        f = np.linspace(1e-4, 15.0, 16)[None, :]
        feat = np.concatenate([tt, np.cos(f * w), -np.sin(f * w)], axis=-1)
        t["featP" if ui == 0 else "featS"] = np.ascontiguousarray(feat.T).astype(np.float32)
    t["gt"] = gt.astype(np.float32)
    t["dtb"] = dtb.astype(np.float32)
    t["et"] = et.astype(np.float32)
    t["su"] = su.astype(np.float32)
    return t


def _prep_shared(inp):
    f = lambda a: np.ascontiguousarray(np.asarray(a, dtype=np.float32))
    sh = {}
    sh["a_w_in"] = f(inp["a_w_in"][0])
    sh["b_w_in"] = f(inp["b_w_in"][0])
    sh["w_kv"] = f(inp["xattn_w_kv"])
    sh["w_out"] = f(inp["w_out"])
    sh["w_up"] = f(inp["ffn_w_up"])
    sh["w_dn"] = f(inp["ffn_w_down"])
    sh["wsT"] = f(np.transpose(np.asarray(inp["a_w_s"][0]), (2, 0, 1)))
    sh["bsT"] = f(np.transpose(np.asarray(inp["a_b_s"][0]), (1, 0)))

    def cols(v):
        v = np.asarray(v, dtype=np.float32)
        return v.reshape(-1, 128).T

    cl = []
    for nm in ("norm_mix_pre", "norm_ffn_pre"):
        pass
    a = inp
    cl += [cols(a["norm_mix_pre"][0]), cols(a["norm_ffn_pre"][0]), cols(a["norm_mix_pre"][1]), cols(a["norm_ffn_pre"][1])]
    cl += [cols(a["norm_mem"][0]), cols(a["norm_mem"][1])]
    for l in range(2):
        for k in range(3):
            cl.append(cols(a["ffn_conv_w"][l][k]))
        cl.append(cols(a["ffn_conv_b"][l]))
    for k in range(3):
        cl.append(cols(a["b_sconv_w"][0][k]))
    cl.append(cols(a["b_sconv_b"][0]))
    colv = np.concatenate(cl, axis=1)
    assert colv.shape == (128, NCV), colv.shape
    sh["colv"] = f(colv)
    deltas = np.abs(np.linspace(math.log(1e-2) / 1.5, math.log(1e-2) / 0.3, MW, dtype=np.float32))
    rowv = np.concatenate([np.asarray(a["norm_mix_post"][0]), np.asarray(a["norm_mix_post"][1]),
                           np.asarray(a["norm_ffn_post"][0]), np.asarray(a["norm_ffn_post"][1]),
                           np.asarray(a["a_ln_g"][0]), np.asarray(a["a_ln_b"][0]), deltas]).astype(np.float32)
    sh["rowv"] = f(rowv[None, :])
    sh["ident"] = np.eye(128, dtype=np.float32)
    sh.update(_hyena_tables())
    sh["fw1"] = f(a["b_filt_w1"][0])
    sh["fw2"] = f(a["b_filt_w2"][0])
    w3 = np.asarray(a["b_filt_w3"][0], dtype=np.float32).reshape(64, 2, 2, 6, 128)
    sh["fw3"] = f(np.transpose(w3, (0, 3, 2, 1, 4)).reshape(64, 3072))
    sh["fvec"] = f(np.stack([np.asarray(a["b_filt_freq"][0]), np.asarray(a["b_filt_b1"][0]),
                             np.asarray(a["b_filt_b2"][0])], axis=1))
    sh["fbias"] = f(np.asarray(a["b_filt_bias"][0]).reshape(1, 1536))
    return sh


def _core_inputs(inp, i):
    xp = np.asarray(inp["x_prompt"])
    xs = np.asarray(inp["x_sample"])
    mp = np.asarray(inp["mem_prompt"])
    ms = np.asarray(inp["mem_sample"])
    X = np.concatenate([xp[2 * i], xp[2 * i + 1], xs[i]], axis=0)
    M = np.concatenate([mp[2 * i], mp[2 * i + 1], ms[i]], axis=0)
    return {"X": np.ascontiguousarray(X, dtype=np.float32), "MEM": np.ascontiguousarray(M, dtype=np.float32)}


def kernel(**inputs):
    sh = _prep_shared(inputs)
    nc = build()
    in_maps = []
    for i in range(8):
        m = dict(sh)
        m.update(_core_inputs(inputs, i))
        in_maps.append(m)
    res = run_bass_kernel_spmd(nc, in_maps, core_ids=list(range(8)))
    yp = np.zeros((16, 2048, D), np.float32)
    ys = np.zeros((8, 4096, D), np.float32)
    for i in range(8):
        y = res.results[i]["Y"]
        yp[2 * i] = y[0:2048]
        yp[2 * i + 1] = y[2048:4096]
        ys[i] = y[4096:8192]
    return (yp, ys)
```

```python
import math
from contextlib import ExitStack

import numpy as np
import concourse.bass as bass
import concourse.mybir as mybir
from concourse.bass_utils import run_bass_kernel_spmd

F32 = mybir.dt.float32
BF16 = mybir.dt.bfloat16
AF = mybir.ActivationFunctionType
ALU = mybir.AluOpType

NT = 8192
SEQS = [(0, 2048), (2048, 2048), (4096, 4096)]
NORM_EPS = 1e-6
LN_EPS = 1e-5
D = 1024
MW = 768
DFF = 2816
NCH_FF = 22
GELU = AF.Gelu_apprx_tanh
STOP = None
NGLIM = None
LIMIT = None

CV_MIXPRE = [0, 16]
CV_FFNPRE = [8, 24]
CV_MEM = [32, 40]
CV_FFC = [48, 48 + 88]
CV_SC = 48 + 176
NCV = CV_SC + 72


class Res:
    __slots__ = ("name", "w", "r", "chan", "excl")

    def __init__(self, name, excl=False):
        self.name = name
        self.w = None
        self.r = []
        self.chan = None
        self.excl = excl


class Chan:
    __slots__ = ("sem", "cnt", "q")

    def __init__(self, sem):
        self.sem = sem
        self.cnt = 0
        self.q = None


class TB:
    def __init__(self, t, name):
        self.t = t
        self.r = Res(name)


class Sched:
    def __init__(self, nc, stack):
        self.nc = nc
        self.stack = stack
        self.eng = {"pe": nc.tensor, "act": nc.scalar, "dve": nc.vector,
                    "pool": nc.gpsimd, "sp": nc.sync}
        self.esem = {}
        self.ecnt = {}
        for e in ("pe", "act", "dve", "pool"):
            self.esem[e] = stack.enter_context(nc.semaphore("s_" + e))
            self.ecnt[e] = 0
        self.known = {e: {} for e in self.eng}
        self.chans = []
        self.free = {"sp": [], "pool": []}
        self.ninst = 0
        self.nwait = 0

    def _need(self, e, ev, same_ok):
        if ev is None:
            return
        sem, val = ev
        if same_ok and e in self.esem and sem is self.esem[e]:
            return
        k = self.known[e]
        if k.get(id(sem), 0) >= val:
            return
        k[id(sem)] = val
        self.eng[e].wait_ge(sem, val)
        self.nwait += 1

    def _deps(self, e, reads, writes, same_ok):
        for r in reads:
            self._need(e, r.w, same_ok)
        for r in writes:
            self._need(e, r.w, same_ok)
            for ev in r.r:
                self._need(e, ev, same_ok)

    def _post(self, ev, reads, writes):
        for r in reads:
            r.r.append(ev)
            if len(r.r) > 10:
                d = {}
                for s, v in r.r:
                    if id(s) not in d or d[id(s)][1] < v:
                        d[id(s)] = (s, v)
                r.r = list(d.values())
        for r in writes:
            r.w = ev
            r.r = []

    def op(self, e, fn, reads=(), writes=()):
        ex = [r for r in reads if r.excl]
        if ex:
            writes = list(writes) + ex
            reads = [r for r in reads if not r.excl]
        self._deps(e, reads, writes, same_ok=(e == "pe"))
        ins = fn()
        self.ecnt[e] += 1
        ins.then_inc(self.esem[e], 1)
        ev = (self.esem[e], self.ecnt[e])
        self._post(ev, reads, writes)
        self.ninst += 1
        return ev

    def dma(self, q, out, in_, chan, reads=(), writes=()):
        skip = chan.chan.sem if chan.chan is not None else None
        for r in reads:
            self._need(q, r.w, False)
        for r in writes:
            if not (r.w is not None and r.w[0] is skip and r is chan and not r.r):
                self._need(q, r.w, False)
            for ev in r.r:
                self._need(q, ev, False)
        if chan.chan is None:
            if self.free[q]:
                chan.chan = self.free[q].pop()
            else:
                c = Chan(self.stack.enter_context(self.nc.semaphore("d%s%d" % (q, len(self.chans)))))
                c.q = q
                self.chans.append(c)
                chan.chan = c
        c = chan.chan
        assert c.q == q, "a DMA channel semaphore must stay on one queue type"
        ins = self.eng[q].dma_start(out=out, in_=in_)
        c.cnt += 16
        ins.then_inc(c.sem, 16)
        ev = (c.sem, c.cnt)
        self._post(ev, reads, writes)
        self.ninst += 1
        return ev

    def barrier(self):
        for e in self.eng:
            for e2 in self.esem:
                if self.ecnt[e2]:
                    self._need(e, (self.esem[e2], self.ecnt[e2]), False)
            for c in self.chans:
                if c.cnt:
                    self._need(e, (c.sem, c.cnt), False)
        self.free = {"sp": [c for c in self.chans if c.q == "sp"], "pool": [c for c in self.chans if c.q == "pool"]}


def build(stages=("kv", "A", "B0", "C", "FF", "FD", "D", "B1"), dbg=(), ext=()):
    nc = bass.Bass("TRN2", target_bir_lowering=False)

    def din(name, shape, dt=F32):
        return nc.dram_tensor(name, list(shape), dt, kind="ExternalInput").ap()

    def dscr(name, shape, dt):
        kind = "ExternalOutput" if name in dbg else ("ExternalInput" if name in ext else "Internal")
        return nc.dram_tensor(name, list(shape), dt, kind=kind).ap()

    X0 = din("X", [NT, D])
    MEM = din("MEM", [768, D])
    A_W_IN = din("a_w_in", [D, 1792])
    B_W_IN = din("b_w_in", [D, 2560])
    W_KV = din("w_kv", [2, D, 512])
    W_OUT = din("w_out", [2, D, D])
    W_UP = din("w_up", [2, D, 2 * DFF])
    W_DN = din("w_dn", [2, DFF, D])
    WST = din("wsT", [128, 12, 128])
    BST = din("bsT", [128, 12])
    COLV = din("colv", [128, NCV])
    ROWV = din("rowv", [1, 4 * D + 3 * MW])
    IDENT = din("ident", [128, 128])
    F1T = din("f1t", [64, 128])
    GT = din("gt", [2, 8, 64, 8, 3, 64])
    DTB = din("dtb", [2, 64, 3, 64])
    ET = din("et", [2, 8, 64, 8, 2, 64])
    FEAT = [din("featP", [33, 2048]), din("featS", [33, 4096])]
    SU = din("su", [2, 64, 64])
    FW1 = din("fw1", [33, 64])
    FW2 = din("fw2", [64, 64])
    FW3 = din("fw3", [64, 3072])
    FVEC = din("fvec", [64, 3])
    FBIAS = din("fbias", [1, 1536])
    Y = nc.dram_tensor("Y", [NT, D], F32, kind="ExternalOutput").ap()

    X1 = dscr("X1", [NT, D], F32)
    X2 = dscr("X2", [NT, D], F32)
    X3 = dscr("X3", [NT, D], F32)
    HNT1 = dscr("HNT1", [D, NT], BF16)
    HNT2 = dscr("HNT2", [D, NT], BF16)
    HNT3 = dscr("HNT3", [D, NT], BF16)
    PT = dscr("PT", [2304, NT], BF16)
    ATT = dscr("ATT", [256, NT], BF16)
    MIXT = dscr("MIXT", [MW, NT], BF16)
    KF = dscr("KF", [2, 6, 2, 8, 64, 8, 2, 128], BF16)

    with ExitStack() as st:
        S = Sched(nc, st)

        uid = [0]

        def sb(stack, name, shape, dt):
            uid[0] += 1
            return TB(stack.enter_context(nc.sbuf_tensor("sb%d_%s" % (uid[0], name), list(shape), dt)), name)

        PSA = st.enter_context(nc.psum_tensor("PSA", [128, 7 * 512], F32))
        PBt = st.enter_context(nc.psum_tensor("PB", [128, 1024], BF16))
        BK = [Res("bank%d" % i, excl=True) for i in range(7)]
        PBr = Res("pb", excl=True)

        def ps(b, lo=0, hi=512):
            return PSA[:, b * 512 + lo: b * 512 + hi]

        ident = sb(st, "ident", [128, 128], BF16)
        ones = sb(st, "ones", [128, 128], BF16)
        colv = sb(st, "colv", [128, NCV], F32)
        kT = sb(st, "kT", [128, 6, 2, 256], BF16)
        vv = sb(st, "vv", [128, 6, 2, 256], BF16)
        S.dma("pool", ident.t[:], IDENT, ident.r, writes=[ident.r])
        S.dma("sp", colv.t[:], COLV, colv.r, writes=[colv.r])
        S.op("dve", lambda: nc.vector.memset(ones.t[:], 1.0), writes=[ones.r])

        def mm(out, lhsT, rhs, start, stop, reads, writes):
            S.op("pe", lambda: nc.tensor.matmul(out, lhsT=lhsT, rhs=rhs, start=start, stop=stop),
                 reads=reads, writes=writes)

        def evac(which, out_ap, in_ap, reads, writes):
            if which == 0:
                S.op("act", lambda: nc.scalar.copy(out=out_ap, in_=in_ap), reads=reads, writes=writes)
            else:
                S.op("dve", lambda: nc.vector.tensor_copy(out=out_ap, in_=in_ap), reads=reads, writes=writes)

        def norm_T(ph, xt_ap, xt_res, gbase, out_ap, out_res, wk):
            junk, ss, xn = wk["junk"], wk["ss"], wk["xn"]
            S.op("act", lambda: nc.scalar.activation(out=junk.t[:, 0:D], in_=xt_ap, func=AF.Square,
                                                     accum_out=ss.t[:, 0:1]),
                 reads=[xt_res], writes=[junk.r, ss.r])
            S.op("act", lambda: nc.scalar.activation(out=ss.t[:, 1:2], in_=ss.t[:, 0:1], func=AF.Sqrt,
                                                     scale=1.0 / D, bias=wk["eps"].t[:, 0:1]),
                 reads=[ss.r, wk["eps"].r], writes=[ss.r])
            S.op("dve", lambda: nc.vector.reciprocal(out=ss.t[:, 2:3], in_=ss.t[:, 1:2]),
                 reads=[ss.r], writes=[ss.r])
            S.op("dve", lambda: nc.vector.tensor_scalar(out=xn.t[:], in0=xt_ap, scalar1=ss.t[:, 2:3],
                                                        scalar2=None, op0=ALU.mult),
                 reads=[xt_res, ss.r], writes=[xn.r])
            for j in range(8):
                S.op("pe", lambda j=j: nc.tensor.transpose(out=PBt[:, j * 128:(j + 1) * 128],
                                                           in_=xn.t[:, j * 128:(j + 1) * 128],
                                                           identity=ident.t[:]),
                     reads=[xn.r, ident.r], writes=[PBr])
            S.op("dve", lambda: nc.vector.tensor_tensor(
                out=out_ap, in0=PBt[:, 0:1024].rearrange("p (j t) -> p j t", j=8),
                in1=colv.t[:, gbase:gbase + 8].unsqueeze(2).to_broadcast([128, 8, 128]), op=ALU.mult),
                reads=[PBr, colv.r], writes=[out_res])

        def mk_normwk(ph, tag):
            wk = {"junk": sb(ph, "junk" + tag, [128, D], BF16),
                  "ss": sb(ph, "ss" + tag, [128, 4], F32),
                  "xn": sb(ph, "xn" + tag, [128, D], BF16),
                  "eps": sb(ph, "eps" + tag, [128, 2], F32)}
            S.op("dve", lambda: nc.vector.memset(wk["eps"].t[:, 0:1], NORM_EPS), writes=[wk["eps"].r])
            S.op("dve", lambda: nc.vector.memset(wk["eps"].t[:, 1:2], LN_EPS), writes=[wk["eps"].r])
            return wk

        def hnt_view(H):
            return H.rearrange("(j p) t -> p j t", p=128)

        def epilogue(ph, banks, xres_ap, xres_res, gpost_ap, gpost_res, tok0, XOUT, gnext, HOUT, wk, ek, pair=None):
            b0 = banks[0]
            pout = PSA[:, b0 * 512: b0 * 512 + 1024]
            br = [BK[b] for b in banks]
            junk, ss = wk["junk"], ek["ss"]
            S.op("act", lambda: nc.scalar.activation(out=junk.t[:, 0:D], in_=pout, func=AF.Square,
                                                     accum_out=ss.t[:, 0:1]),
                 reads=br, writes=[junk.r, ss.r])
            S.op("act", lambda: nc.scalar.activation(out=ss.t[:, 1:2], in_=ss.t[:, 0:1], func=AF.Sqrt,
                                                     scale=1.0 / D, bias=wk["eps"].t[:, 0:1]),
                 reads=[ss.r, wk["eps"].r], writes=[ss.r])
            S.op("dve", lambda: nc.vector.reciprocal(out=ss.t[:, 2:3], in_=ss.t[:, 1:2]),
                 reads=[ss.r], writes=[ss.r])
            tp = ek["tp"]
            S.op("dve", lambda: nc.vector.tensor_tensor(out=tp.t[:], in0=pout, in1=gpost_ap, op=ALU.mult),
                 reads=br + [gpost_res], writes=[tp.r])
            xnew = ek["xnew"]
            S.op("dve", lambda: nc.vector.scalar_tensor_tensor(out=xnew.t[:], in0=tp.t[:], scalar=ss.t[:, 2:3],
                                                               in1=xres_ap, op0=ALU.mult, op1=ALU.add),
                 reads=[tp.r, ss.r, xres_res], writes=[xnew.r])
            S.dma("pool", XOUT[tok0:tok0 + 128, :], xnew.t[:], ek["xst"], reads=[xnew.r])
            if HOUT is not None and pair is None:
                hno = ek["hno"]
                norm_T(ph, xnew.t[:], xnew.r, gnext, hno.t[:], hno.r, wk)
                S.dma("pool", hnt_view(HOUT)[:, :, tok0:tok0 + 128], hno.t[:], ek["hst"], reads=[hno.r])
            elif HOUT is not None:
                hno2, half, hres = pair
                norm_T(ph, xnew.t[:], xnew.r, gnext, hno2.t[:, :, half * 128:(half + 1) * 128], hno2.r, wk)
                if half == 1:
                    S.dma("pool", hnt_view(HOUT)[:, :, tok0 - 128:tok0 + 128], hno2.t[:], hres, reads=[hno2.r])

        def mk_epi(ph, tag):
            return {"ss": sb(ph, "ess" + tag, [128, 4], F32),
                    "tp": sb(ph, "tp" + tag, [128, D], F32),
                    "xnew": sb(ph, "xnew" + tag, [128, D], F32),
                    "hno": sb(ph, "hno" + tag, [128, 8, 128], BF16),
                    "xst": Res("xst" + tag), "hst": Res("hst" + tag)}

        def attention(hn_ap, hn_res, wq_ap, wq_res, ls, cat, wk, banks):
            for _ in attention_gen(hn_ap, hn_res, wq_ap, wq_res, ls, cat, wk, banks):
                pass

        def attention_gen(hn_ap, hn_res, wq_ap, wq_res, ls, cat, wk, banks):
            bq, bs0, bs1, bav0, bav1 = banks
            qT, pT, rden = wk["qT"], wk["pT"], wk["rden"]
            for hc in range(2):
                for j in range(8):
                    mm(ps(bq, hc * 128, hc * 128 + 128), wq_ap(j, hc), hn_ap(j), j == 0, j == 7,
                       [wq_res, hn_res], [BK[bq]])
            yield
            S.op("act", lambda: nc.scalar.activation(out=qT.t[:].rearrange("p a t -> p (a t)"), in_=ps(bq, 0, 256),
                                                     func=AF.Copy, scale=0.125),
                 reads=[BK[bq]], writes=[qT.r])
            yield
            for h in range(4):
                hc, po = h // 2, (h % 2) * 64
                for mc in range(2):
                    idx = (h % 2) * 4 + (h // 2) * 2 + mc
                    b = bs0 if idx < 4 else bs1
                    col = (idx % 4) * 128
                    mm(PSA[:, b * 512 + col: b * 512 + col + 128],
                       kT.t[po:po + 64, ls, hc, mc * 128:(mc + 1) * 128], qT.t[po:po + 64, hc, :], True, True,
                       [kT.r, qT.r], [BK[b]])
            yield
            for half, b in ((0, bs0), (1, bs1)):
                S.op("act", lambda half=half, b=b: nc.scalar.activation(
                    out=pT.t[:, half * 4:(half + 1) * 4, :].rearrange("p a t -> p (a t)"), in_=ps(b), func=AF.Exp),
                    reads=[BK[b]], writes=[pT.r])
            yield
            for hc in range(2):
                b = bav0 if hc == 0 else bav1
                for part in range(4):
                    h = 2 * hc + (part % 2)
                    for mc in range(2):
                        lhsT = vv.t[:, ls, mc, hc * 128:(hc + 1) * 128] if part < 2 else ones.t[:]
                        mm(ps(b, part * 128, part * 128 + 128), lhsT, pT.t[:, (h % 2) * 4 + (h // 2) * 2 + mc, :], mc == 0, mc == 1,
                           [vv.r, ones.r, pT.r], [BK[b]])
                for hh in range(2):
                    lo = hh * 64
                    S.op("dve", lambda hh=hh, lo=lo, b=b, hc=hc: nc.vector.reciprocal(
                        out=rden.t[lo:lo + 64, hc, :], in_=PSA[lo:lo + 64, b * 512 + (2 + hh) * 128: b * 512 + (3 + hh) * 128]),
                        reads=[BK[b]], writes=[rden.r])
                for hh in range(2):
                    lo = hh * 64
                    S.op("dve", lambda hh=hh, lo=lo, b=b, hc=hc: nc.vector.tensor_tensor(
                        out=cat.t[lo:lo + 64, 6 + hc, :], in0=PSA[lo:lo + 64, b * 512 + hh * 128: b * 512 + (hh + 1) * 128],
                        in1=rden.t[lo:lo + 64, hc, :], op=ALU.mult),
                        reads=[BK[b], rden.r], writes=[cat.r])

        def mk_attwk(ph, tag):
            return {"qT": sb(ph, "qT" + tag, [128, 2, 128], BF16),
                    "pT": sb(ph, "pT" + tag, [128, 8, 128], BF16),
                    "rden": sb(ph, "rden" + tag, [128, 2, 128], F32)}

        def seq_of(tok):
            for si, (t0, L) in enumerate(SEQS):
                if t0 <= tok < t0 + L:
                    return si, t0, L
            raise ValueError

        def load_hnt_tile(HSRC, hb, a, T):
            si, t0, L = seq_of(a)
            lo = a - 1 if a > t0 else a
            hi = a + T + 1 if a + T < t0 + L else a + T
            if lo == a:
                S.op("dve", lambda: nc.vector.memset(hb.t[:, :, 0:1], 0.0), writes=[hb.r])
            if hi == a + T:
                S.op("dve", lambda: nc.vector.memset(hb.t[:, :, T + 1:T + 2], 0.0), writes=[hb.r])
            S.dma("sp", hb.t[:, :, lo - (a - 1): hi - (a - 1)], hnt_view(HSRC)[:, :, lo:hi], hb.r, writes=[hb.r])

        def conv_chunk(hb, T, w_ap, w_res, cv, nchk, ci, bmain, tbuf, fin=None):
            for j in range(8):
                mm(ps(bmain, 0, T + 2), w_ap(j), hb.t[:, j, 0:T + 2], j == 0, j == 7, [w_res, hb.r], [BK[bmain]])
            c0, c1, c2, cb = (colv.t[:, cv + k * nchk + ci: cv + k * nchk + ci + 1] for k in range(4))
            S.op("act", lambda: nc.scalar.activation(out=tbuf.t[:, 0:T], in_=ps(bmain, 1, T + 1), func=AF.Identity,
                                                     scale=c1, bias=cb),
                 reads=[BK[bmain], colv.r], writes=[tbuf.r])
            S.op("dve", lambda: nc.vector.scalar_tensor_tensor(out=tbuf.t[:, 0:T], in0=ps(bmain, 0, T), scalar=c0,
                                                               in1=tbuf.t[:, 0:T], op0=ALU.mult, op1=ALU.add),
                 reads=[BK[bmain], tbuf.r, colv.r], writes=[tbuf.r])
            fo = tbuf if fin is None else fin
            S.op("dve", lambda: nc.vector.scalar_tensor_tensor(out=fo.t[:, 0:T], in0=ps(bmain, 2, T + 2), scalar=c2,
                                                               in1=tbuf.t[:, 0:T], op0=ALU.mult, op1=ALU.add),
                 reads=[BK[bmain], tbuf.r, colv.r], writes=[fo.r])

        if "kv" in stages:
            with ExitStack() as ph:
                wkv = sb(ph, "wkv", [128, 2, 8, 512], BF16)
                for l in range(2):
                    S.dma("pool", wkv.t[:, l], W_KV[l].rearrange("(j p) n -> p j n", p=128), wkv.r, writes=[wkv.r])
                wk = mk_normwk(ph, "kv")
                xin = [sb(ph, "kvx%d" % i, [128, D], F32) for i in range(2)]
                hn = [sb(ph, "kvh%d" % i, [128, 8, 128], BF16) for i in range(2)]
                it = 0
                for l in range(2):
                    for s in range(3):
                        for mt in range(2):
                            xb, hb = xin[it % 2], hn[it % 2]
                            it += 1
                            r0 = s * 256 + mt * 128
                            S.dma("sp", xb.t[:], MEM[r0:r0 + 128, :], xb.r, writes=[xb.r])
                            norm_T(ph, xb.t[:], xb.r, CV_MEM[l], hb.t[:], hb.r, wk)
                            ls = l * 3 + s
                            for hc in range(2):
                                b = hc
                                for j in range(8):
                                    mm(ps(b, 0, 128), wkv.t[:, l, j, hc * 128:(hc + 1) * 128], hb.t[:, j, :], j == 0, j == 7,
                                       [wkv.r, hb.r], [BK[b]])
                                S.op("act", lambda b=b, hc=hc, ls=ls, mt=mt: nc.scalar.copy(
                                    out=kT.t[:, ls, hc, mt * 128:(mt + 1) * 128], in_=ps(b, 0, 128)),
                                    reads=[BK[b]], writes=[kT.r])
                            for j in range(8):
                                mm(ps(2, 0, 256), hb.t[:, j, :], wkv.t[:, l, j, 256:512], j == 0, j == 7,
                                   [wkv.r, hb.r], [BK[2]])
                            S.op("act", lambda ls=ls, mt=mt: nc.scalar.copy(out=vv.t[:, ls, mt, :], in_=ps(2, 0, 256)),
                                 reads=[BK[2]], writes=[vv.r])
                S.barrier()

        if "A" in stages:
            with ExitStack() as ph:
                win = sb(ph, "a_win", [128, 8, 1792], BF16)
                for j in range(8):
                    S.dma("pool", win.t[:, j, :], A_W_IN[j * 128:(j + 1) * 128, :], win.r, writes=[win.r])
                wout = sb(ph, "a_wout", [128, 8, D], BF16)
                S.dma("pool", wout.t[:], W_OUT[0].rearrange("(j p) n -> p j n", p=128), wout.r, writes=[wout.r])
                wst = sb(ph, "a_wst", [128, 12, 128], BF16)
                S.dma("pool", wst.t[:], WST, wst.r, writes=[wst.r])
                bst = sb(ph, "a_bst", [128, 12], F32)
                S.dma("sp", bst.t[:], BST, bst.r, writes=[bst.r])
                gpost = sb(ph, "a_gpost", [128, D], F32)
                S.dma("sp", gpost.t[:], ROWV[:, 0:D].partition_broadcast(128), gpost.r, writes=[gpost.r])
                lng = sb(ph, "a_lng", [128, MW], F32)
                lnb = sb(ph, "a_lnb", [128, MW], F32)
                S.dma("sp", lng.t[:], ROWV[:, 4 * D:4 * D + MW].partition_broadcast(128), lng.r, writes=[lng.r])
                S.dma("sp", lnb.t[:], ROWV[:, 4 * D + MW:4 * D + 2 * MW].partition_broadcast(128), lnb.r, writes=[lnb.r])
                wks = [mk_normwk(ph, "a%d" % i) for i in range(2)]
                eks = [mk_epi(ph, "a%d" % i) for i in range(2)]
                awks = [mk_attwk(ph, "a%d" % i) for i in range(2)]
                xin = [sb(ph, "ax%d" % i, [128, D], F32) for i in range(5)]
                hn = [sb(ph, "ah%d" % i, [128, 8, 128], BF16) for i in range(2)]
                gus = [sb(ph, "a_gu%d" % i, [128, MW], F32) for i in range(2)]
                gvs = [sb(ph, "a_gv%d" % i, [128, MW], F32) for i in range(2)]
                vns = [sb(ph, "a_vn%d" % i, [128, MW], BF16) for i in range(2)]
                tmps = [sb(ph, "a_tmp%d" % i, [128, MW], F32) for i in range(2)]
                mixs_ = [sb(ph, "a_mix%d" % i, [128, MW], BF16) for i in range(2)]
                st6s = [sb(ph, "a_st%d" % i, [128, 2, 6], F32) for i in range(2)]
                mvs = [sb(ph, "a_mv%d" % i, [128, 4], F32) for i in range(2)]
                cat = [sb(ph, "a_cat%d" % i, [128, 8, 128], BF16) for i in range(2)]
                nch = LIMIT or NT // 128

                def load_x(c):
                    xb = xin[c % 5]
                    S.dma("sp", xb.t[:], X0[c * 128:(c + 1) * 128, :], xb.r, writes=[xb.r])

                def chunk_gen(c):
                    if c + 3 < nch:
                        load_x(c + 3)
                    p = c % 2
                    xb, hb, cb = xin[c % 5], hn[p], cat[p]
                    wk, ek, awk = wks[p], eks[p], awks[p]
                    gu, gv, vn, tmp, mix, st6, mv = gus[p], gvs[p], vns[p], tmps[p], mixs_[p], st6s[p], mvs[p]
                    si, _, _ = seq_of(c * 128)
                    norm_T(ph, xb.t[:], xb.r, CV_MIXPRE[0], hb.t[:], hb.r, wk)
                    yield
                    for nb in range(3):
                        for j in range(8):
                            mm(ps(nb), hb.t[:, j, :], win.t[:, j, nb * 512:(nb + 1) * 512], j == 0, j == 7,
                               [hb.r, win.r], [BK[nb]])
                    yield
                    S.op("act", lambda: nc.scalar.activation(out=gv.t[:], in_=PSA[:, MW:2 * MW], func=GELU),
                         reads=[BK[1], BK[2]], writes=[gv.r])
                    S.op("act", lambda: nc.scalar.activation(out=gu.t[:], in_=PSA[:, 0:MW], func=GELU),
                         reads=[BK[0], BK[1]], writes=[gu.r])
                    for k in range(2):
                        S.op("dve", lambda k=k: nc.vector.bn_stats(out=st6.t[:, k, :], in_=gv.t[:, k * 384:(k + 1) * 384]),
                             reads=[gv.r], writes=[st6.r])
                    S.op("dve", lambda: nc.vector.bn_aggr(out=mv.t[:, 0:2], in_=st6.t[:]), reads=[st6.r], writes=[mv.r])
                    S.op("act", lambda: nc.scalar.activation(out=mv.t[:, 2:3], in_=mv.t[:, 1:2], func=AF.Sqrt,
                                                             scale=1.0, bias=wk["eps"].t[:, 1:2]),
                         reads=[mv.r, wk["eps"].r], writes=[mv.r])
                    S.op("dve", lambda: nc.vector.reciprocal(out=mv.t[:, 3:4], in_=mv.t[:, 2:3]), reads=[mv.r], writes=[mv.r])
                    S.op("dve", lambda: nc.vector.tensor_scalar(out=gv.t[:], in0=gv.t[:], scalar1=mv.t[:, 0:1],
                                                                scalar2=mv.t[:, 3:4], op0=ALU.subtract, op1=ALU.mult),
                         reads=[gv.r, mv.r], writes=[gv.r])
                    S.op("pool", lambda: nc.gpsimd.tensor_tensor(out=gv.t[:], in0=gv.t[:], in1=lng.t[:], op=ALU.mult),
                         reads=[gv.r, lng.r], writes=[gv.r])
                    S.op("pool", lambda: nc.gpsimd.tensor_tensor(out=vn.t[:], in0=gv.t[:], in1=lnb.t[:], op=ALU.add),
                         reads=[gv.r, lnb.r], writes=[vn.r])
                    yield
                    for g in range(12):
                        col = 3 * 512 + g * 64
                        mm(PSA[:, col:col + 64], wst.t[:, g, :], vn.t[:, g * 64:(g + 1) * 64], True, True,
                           [wst.r, vn.r], [BK[3 + (g // 8)]])
                    yield
                    S.op("dve", lambda: nc.vector.tensor_tensor(
                        out=tmp.t[:].rearrange("p (g d) -> p g d", g=12),
                        in0=PSA[:, 3 * 512:3 * 512 + MW].rearrange("p (g d) -> p g d", g=12),
                        in1=bst.t[:].unsqueeze(2).to_broadcast([128, 12, 64]), op=ALU.add),
                        reads=[BK[3], BK[4], bst.r], writes=[tmp.r])
                    S.op("pool", lambda: nc.gpsimd.tensor_tensor(out=mix.t[:], in0=tmp.t[:], in1=gu.t[:], op=ALU.mult),
                         reads=[tmp.r, gu.r], writes=[mix.r])
                    yield
                    for k in range(6):
                        S.op("pe", lambda k=k: nc.tensor.transpose(out=PBt[:, k * 128:(k + 1) * 128],
                                                                   in_=mix.t[:, k * 128:(k + 1) * 128], identity=ident.t[:]),
                             reads=[mix.r, ident.r], writes=[PBr])
                    S.op("act", lambda: nc.scalar.copy(out=cb.t[:, 0:6, :].rearrange("p a t -> p (a t)"), in_=PBt[:, 0:768]),
                         reads=[PBr], writes=[cb.r])
                    yield
                    yield from attention_gen(lambda j: hb.t[:, j, :], hb.r,
                                             lambda j, hc: win.t[:, j, 1536 + hc * 128:1536 + (hc + 1) * 128], win.r,
                                             0 * 3 + si, cb, awk, (5, 0, 1, 5, 6))
                    yield
                    for nb in range(2):
                        for k in range(8):
                            mm(ps(3 + nb), cb.t[:, k, :], wout.t[:, k, nb * 512:(nb + 1) * 512], k == 0, k == 7,
                               [cb.r, wout.r], [BK[3 + nb]])
                    yield
                    epilogue(ph, (3, 4), xb.t[:], xb.r, gpost.t[:], gpost.r, c * 128, X1, CV_FFNPRE[0], HNT1, wk, ek)

                for c in range(min(3, nch)):
                    load_x(c)
                active = []
                nxt = 0
                while active or nxt < nch:
                    if nxt < nch and len(active) < 2:
                        active.append(chunk_gen(nxt))
                        nxt += 1
                    for gen in list(active):
                        try:
                            next(gen)
                        except StopIteration:
                            active.remove(gen)
                if "KTD" in dbg:
                    KTD = dscr("KTD", [128, 6 * 2 * 256], BF16)
                    VVD = dscr("VVD", [128, 6 * 2 * 256], BF16)
                    S.dma("sp", KTD, kT.t[:].rearrange("p a b c -> p (a b c)"), Res("ktd"), reads=[kT.r])
                    S.dma("sp", VVD, vv.t[:].rearrange("p a b c -> p (a b c)"), Res("vvd"), reads=[vv.r])
                S.barrier()

        def ffn_phase(layer, XIN, HIN, XOUT, gnext, HOUT):
            T = 256
            with ExitStack() as ph:
                wup = sb(ph, "wup", [128, 8, 2 * DFF], BF16)
                for j in range(8):
                    S.dma("pool", wup.t[:, j, :], W_UP[layer, j * 128:(j + 1) * 128, :], wup.r, writes=[wup.r])
                wdn = sb(ph, "wdn", [128, NCH_FF, D], BF16)
                for ci in range(NCH_FF):
                    S.dma("pool", wdn.t[:, ci, :], W_DN[layer, ci * 128:(ci + 1) * 128, :], wdn.r, writes=[wdn.r])
                gpost = sb(ph, "f_gpost", [128, D], F32)
                S.dma("sp", gpost.t[:], ROWV[:, (2 + layer) * D:(3 + layer) * D].partition_broadcast(128), gpost.r,
                      writes=[gpost.r])
                wk = mk_normwk(ph, "f")
                ek = mk_epi(ph, "f")
                hb = sb(ph, "f_hb", [128, 8, T + 2], BF16)
                xin = [sb(ph, "fx%d" % i, [128, D], F32) for i in range(2)]
                tb = [sb(ph, "ft%d" % i, [128, T], F32) for i in range(2)]
                hact = sb(ph, "hact", [128, NCH_FF, T], BF16)
                cv = CV_FFC[layer]
                ntile = LIMIT or NT // T
                xcnt = 0
                for ti in range(ntile):
                    a = ti * T
                    load_hnt_tile(HIN, hb, a, T)
                    for ci in range(NCH_FF):
                        tbuf = tb[ci % 2]
                        bm, bu = ci % 2, 2 + ci % 2
                        conv_chunk(hb, T, lambda j: wup.t[:, j, ci * 128:(ci + 1) * 128], wup.r, cv, NCH_FF, ci,
                                   bm, tbuf)
                        S.op("act", lambda tbuf=tbuf: nc.scalar.activation(out=tbuf.t[:, 0:T], in_=tbuf.t[:, 0:T], func=GELU),
                             reads=[tbuf.r], writes=[tbuf.r])
                        for j in range(8):
                            mm(ps(bu, 0, T), wup.t[:, j, DFF + ci * 128:DFF + (ci + 1) * 128], hb.t[:, j, 1:T + 1],
                               j == 0, j == 7, [wup.r, hb.r], [BK[bu]])
                        S.op("dve", lambda tbuf=tbuf, bu=bu, ci=ci: nc.vector.tensor_tensor(
                            out=hact.t[:, ci, :], in0=ps(bu, 0, T), in1=tbuf.t[:, 0:T], op=ALU.mult),
                            reads=[BK[bu], tbuf.r], writes=[hact.r])
                    for tt in range(T // 128):
                        xb = xin[xcnt % 2]
                        xcnt += 1
                        tok0 = a + tt * 128
                        S.dma("sp", xb.t[:], XIN[tok0:tok0 + 128, :], xb.r, writes=[xb.r])
                        for nb in range(2):
                            for ci in range(NCH_FF):
                                mm(ps(5 + nb), hact.t[:, ci, tt * 128:(tt + 1) * 128], wdn.t[:, ci, nb * 512:(nb + 1) * 512],
                                   ci == 0, ci == NCH_FF - 1, [hact.r, wdn.r], [BK[5 + nb]])
                        epilogue(ph, (5, 6), xb.t[:], xb.r, gpost.t[:], gpost.r, tok0, XOUT, gnext, HOUT, wk, ek)
                S.barrier()

        if "B0" in stages:
            ffn_phase(0, X1, HNT1, X2, CV_MIXPRE[1], HNT2)

        if "C" in stages:
            T = 256
            with ExitStack() as ph:
                win = sb(ph, "b_win", [128, 8, 2560], BF16)
                for j in range(8):
                    S.dma("pool", win.t[:, j, :], B_W_IN[j * 128:(j + 1) * 128, :], win.r, writes=[win.r])
                awks = [mk_attwk(ph, "c%d" % i) for i in range(2)]
                hbs = [sb(ph, "c_hb%d" % i, [128, 8, T + 2], BF16) for i in range(2)]
                tb = [sb(ph, "c_t%d" % i, [128, T], F32) for i in range(2)]
                ob = [sb(ph, "c_o%d" % i, [128, T], BF16) for i in range(3)]
                cat = [sb(ph, "c_cat%d" % i, [128, 8, 128], BF16) for i in range(2)]
                cst = [Res("cst0"), Res("cst1")]
                ntile = LIMIT or NT // T
                oc = 0
                cc = 0
                for ti in range(ntile):
                    a = ti * T
                    hb = hbs[ti % 2]
                    si, _, _ = seq_of(a)
                    load_hnt_tile(HNT2, hb, a, T)
                    gens = []
                    cbs = []
                    for tt in range(T // 128):
                        cb = cat[cc % 2]
                        cc += 1
                        cbs.append((cb, cst[cc % 2], tt))
                        gens.append(attention_gen(lambda j, tt=tt: hb.t[:, j, 1 + tt * 128:1 + (tt + 1) * 128], hb.r,
                                                  lambda j, hc: win.t[:, j, 2304 + hc * 128:2304 + (hc + 1) * 128], win.r,
                                                  3 + si, cb, awks[tt % 2], (5, 2, 3, 5, 6)))
                    for ci in range(18):
                        tbuf = tb[ci % 2]
                        o = ob[oc % 3]
                        oc += 1
                        conv_chunk(hb, T, lambda j: win.t[:, j, ci * 128:(ci + 1) * 128], win.r, CV_SC, 18, ci,
                                   ci % 2, tbuf, fin=o)
                        S.dma("pool", PT[ci * 128:(ci + 1) * 128, a:a + T], o.t[:], o.r, reads=[o.r])
                        next(gens[0] if ci < 9 else gens[1], None)
                        if ci == 8:
                            for _ in gens[0]:
                                pass
                    for gen in gens:
                        for _ in gen:
                            pass
                    for cb, cres, tt in cbs:
                        S.dma("pool", ATT.rearrange("(k p) t -> p k t", p=128)[:, :, a + tt * 128:a + (tt + 1) * 128],
                              cb.t[:, 6:8, :], cres, reads=[cb.r])
                S.barrier()


        UNITS = [dict(tok0=0, L=2048, N2=32, nb=2), dict(tok0=4096, L=4096, N2=64, nb=1)]
        NG = 6

        def load_unit_tables(ph):
            f1 = sb(ph, "h_f1", [64, 128], BF16)
            S.dma("pool", f1.t[:], F1T, f1.r, writes=[f1.r])
            return f1

        if "FF" in stages:
            with ExitStack() as ph:
                f1 = load_unit_tables(ph)
                w3sd = sb(ph, "h_w3sd", [64, NG, 2, 2, 128], BF16)
                fvec = sb(ph, "h_fvec", [64, 8], F32)
                w1 = sb(ph, "h_w1", [33, 64], F32)
                w2 = sb(ph, "h_w2", [64, 64], F32)
                dl = sb(ph, "h_dl", [64, MW], F32)
                fb = sb(ph, "h_fb", [1, 2 * MW], F32)
                S.dma("sp", fvec.t[:, 0:3], FVEC, fvec.r, writes=[fvec.r])
                S.dma("sp", w1.t[:], FW1, w1.r, writes=[w1.r])
                S.dma("sp", w2.t[:], FW2, w2.r, writes=[w2.r])
                S.dma("sp", dl.t[:], ROWV[:, 4 * D + 2 * MW:4 * D + 3 * MW].partition_broadcast(64), dl.r, writes=[dl.r])
                S.dma("sp", fb.t[:], FBIAS, fb.r, writes=[fb.r])
                S.op("dve", lambda: nc.vector.tensor_tensor(out=fvec.t[:, 3:4], in0=fvec.t[:, 0:1], in1=fvec.t[:, 1:2], op=ALU.mult),
                     reads=[fvec.r], writes=[fvec.r])
                S.op("dve", lambda: nc.vector.tensor_tensor(out=fvec.t[:, 4:5], in0=fvec.t[:, 0:1], in1=fvec.t[:, 2:3], op=ALU.mult),
                     reads=[fvec.r], writes=[fvec.r])
                with ExitStack() as tmp:
                    w3r = sb(tmp, "h_w3r", [64, NG, 2, 2, 128], F32)
                    S.dma("sp", w3r.t[:].rearrange("p g o d c -> p (g o d c)"), FW3, w3r.r, writes=[w3r.r])
                    S.op("dve", lambda: nc.vector.tensor_tensor(out=w3sd.t[:, :, :, 0, :], in0=w3r.t[:, :, :, 0, :],
                                                                in1=w3r.t[:, :, :, 1, :], op=ALU.add),
                         reads=[w3r.r], writes=[w3sd.r])
                    S.op("dve", lambda: nc.vector.tensor_tensor(out=w3sd.t[:, :, :, 1, :], in0=w3r.t[:, :, :, 0, :],
                                                                in1=w3r.t[:, :, :, 1, :], op=ALU.subtract),
                         reads=[w3r.r], writes=[w3sd.r])
                    S.barrier()
                h2T = sb(ph, "h_h2T", [64, 4096], BF16)
                htok = sb(ph, "h_htok", [64, 2, 64, 128], BF16)
                Abuf = sb(ph, "h_A", [128, 64, 2, 128], BF16)
                kst = [sb(ph, "h_kst%d" % i, [64, 8, 2, 128], BF16) for i in range(2)]
                gtab = sb(ph, "h_gtab", [128, 8, 8, 3, 64], BF16)
                dec = [sb(ph, "h_dec%d" % i, [64, 128], F32) for i in range(2)]
                su = sb(ph, "h_su", [64, 64], F32)
                mt_ = [sb(ph, "h_mt%d" % i, [64, 512], F32) for i in range(3)]
                h1c = sb(ph, "h_h1c", [64, 512], F32)
                ft = [sb(ph, "h_ft%d" % i, [33, 512], F32) for i in range(2)]
                t0f = sb(ph, "h_t0f", [1, 256], F32)
                MAGIC = 12582912.0
                TWO_PI = 2.0 * math.pi

                def sin_layer(psum_ap, psum_res, fcol, out_ap, out_res):
                    a, u, k = mt_
                    S.op("dve", lambda: nc.vector.tensor_scalar(out=a.t[:], in0=psum_ap, scalar1=fvec.t[:, 0:1],
                                                                scalar2=fvec.t[:, fcol:fcol + 1], op0=ALU.mult, op1=ALU.add),
                         reads=[psum_res, fvec.r], writes=[a.r])
                    S.op("dve", lambda: nc.vector.tensor_scalar(out=u.t[:], in0=a.t[:], scalar1=1.0 / TWO_PI, scalar2=MAGIC,
                                                                op0=ALU.mult, op1=ALU.add), reads=[a.r], writes=[u.r])
                    S.op("dve", lambda: nc.vector.tensor_scalar(out=k.t[:], in0=u.t[:], scalar1=MAGIC, scalar2=-TWO_PI,
                                                                op0=ALU.subtract, op1=ALU.mult), reads=[u.r], writes=[k.r])
                    S.op("dve", lambda: nc.vector.tensor_tensor(out=a.t[:], in0=a.t[:], in1=k.t[:], op=ALU.add),
                         reads=[a.r, k.r], writes=[a.r])
                    S.op("dve", lambda: nc.vector.tensor_scalar(out=a.t[:], in0=a.t[:], scalar1=3.1415925, scalar2=-3.1415925,
                                                                op0=ALU.min, op1=ALU.max), reads=[a.r], writes=[a.r])
                    S.op("act", lambda: nc.scalar.activation(out=out_ap, in_=a.t[:], func=AF.Sin), reads=[a.r], writes=[out_res])

                for ui, U in enumerate(UNITS):
                    L, N2, nbt = U["L"], U["N2"], U["nb"]
                    S.dma("sp", su.t[:], SU[ui], su.r, writes=[su.r])
                    for kb in range(8):
                        S.dma("pool", gtab.t[0:64, kb], GT[ui, kb], gtab.r, writes=[gtab.r])
                        S.dma("pool", gtab.t[64:128, kb], GT[ui, kb], gtab.r, writes=[gtab.r])
                    for cc in range(L // 512):
                        fbuf = ft[cc % 2]
                        S.dma("sp", fbuf.t[:], FEAT[ui][:, cc * 512:(cc + 1) * 512], fbuf.r, writes=[fbuf.r])
                        mm(PSA[0:64, 0:512], w1.t[:], fbuf.t[:], True, True, [w1.r, fbuf.r], [BK[0]])
                        sin_layer(PSA[0:64, 0:512], BK[0], 3, h1c.t[:], h1c.r)
                        mm(PSA[0:64, 512:1024], w2.t[:], h1c.t[:], True, True, [w2.r, h1c.r], [BK[1]])
                        sin_layer(PSA[0:64, 512:1024], BK[1], 4, h2T.t[:, cc * 512:(cc + 1) * 512], h2T.r)
                    for g in range(NGLIM or NG):
                        for o in range(2):
                            for n2 in range(N2):
                                b = 2 + n2 % 2
                                d_ = dec[n2 % 2]
                                mm(PSA[0:64, b * 512:b * 512 + 256], h2T.t[:, n2:L:N2],
                                   w3sd.t[:, g, o].rearrange("p a c -> p (a c)"), True, True, [h2T.r, w3sd.r], [BK[b]])
                                S.op("act", lambda d_=d_, n2=n2: nc.scalar.activation(
                                    out=d_.t[:], in_=dl.t[:, g * 128:(g + 1) * 128], func=AF.Exp, scale=su.t[:, n2:n2 + 1]),
                                    reads=[dl.r, su.r], writes=[d_.r])
                                for bb in range(nbt):
                                    i = bb * N2 + n2
                                    S.op("dve", lambda b=b, d_=d_, i=i: nc.vector.tensor_tensor(
                                        out=htok.t[:, :, i, :],
                                        in0=PSA[0:64, b * 512:b * 512 + 256].rearrange("p (a c) -> p a c", a=2),
                                        in1=d_.t[:].unsqueeze(1).to_broadcast([64, 2, 128]), op=ALU.mult),
                                        reads=[BK[b], d_.r], writes=[htok.r])
                            for bb in range(nbt):
                                i0 = bb * N2
                                S.op("dve", lambda i0=i0: nc.vector.tensor_tensor(
                                    out=t0f.t[0:1, 0:128], in0=htok.t[0:1, 0, i0, :], in1=htok.t[0:1, 1, i0, :], op=ALU.add),
                                    reads=[htok.r], writes=[t0f.r])
                                S.op("dve", lambda: nc.vector.scalar_tensor_tensor(
                                    out=t0f.t[0:1, 128:256], in0=t0f.t[0:1, 0:128], scalar=0.5,
                                    in1=fb.t[0:1, o * MW + g * 128:o * MW + (g + 1) * 128], op0=ALU.mult, op1=ALU.add),
                                    reads=[t0f.r, fb.r], writes=[t0f.r])
                                for a_ in range(2):
                                    S.op("dve", lambda a_=a_, i0=i0: nc.vector.tensor_copy(
                                        out=htok.t[0:1, a_, i0, :], in_=t0f.t[0:1, 128:256]),
                                        reads=[t0f.r], writes=[htok.r])
                            for c0 in range(0, 128, 4):
                                b = (c0 // 4) % 4
                                for c in range(c0, c0 + 4):
                                    mm(PSA[:, b * 512 + (c - c0) * 128:b * 512 + (c - c0 + 1) * 128],
                                       htok.t[:, :, :, c], f1.t[:], True, True, [htok.r, f1.r], [BK[b]])
                                evac((c0 // 4) % 2, Abuf.t[:, :, :, c0:c0 + 4].rearrange("p k r c -> p (k r) c"),
                                     PSA[:, b * 512:(b + 1) * 512].rearrange("p (c x) -> p x c", c=4),
                                     [BK[b]], [Abuf.r])
                            for kb in range(8):
                                gt = gtab
                                ks = kst[kb % 2]
                                for kp in range(4):
                                    for a_ in range(2):
                                        lo = a_ * 64
                                        b = (4 + kp % 2) if a_ == 0 else (6 if kp % 2 == 0 else 3)
                                        for kk in range(2):
                                            kl = kp * 2 + kk
                                            k1 = kb * 8 + kl
                                            v0, v1 = (0, 2) if a_ == 0 else (1, 0)
                                            out = PSA[0:64, b * 512 + kk * 128:b * 512 + (kk + 1) * 128]
                                            mm(out, gt.t[lo:lo + 64, kb, kl, v0, :], Abuf.t[lo:lo + 64, k1, 0, :], True, False,
                                               [gt.r, Abuf.r], [BK[b]])
                                            mm(out, gt.t[lo:lo + 64, kb, kl, v1, :], Abuf.t[lo:lo + 64, k1, 1, :], False, True,
                                               [gt.r, Abuf.r], [BK[b]])
                                        evac(a_, ks.t[:, kp * 2:kp * 2 + 2, a_, :],
                                             PSA[0:64, b * 512:b * 512 + 256].rearrange("p (k c) -> p k c", k=2),
                                             [BK[b]], [ks.r])
                                S.dma("sp", KF[ui, g, o, kb], ks.t[:], ks.r, reads=[ks.r])
                S.barrier()

        if "FD" in stages:
            with ExitStack() as ph:
                f1 = load_unit_tables(ph)
                dtb = sb(ph, "h_dtb", [64, 3, 64], BF16)
                vtok = sb(ph, "h_vtok", [64, 128, 64], BF16)
                x1tok = sb(ph, "h_x1tok", [64, 128, 64], BF16)
                zview = vtok.t[:].rearrange("p c i -> p (c i)").rearrange("p (i c) -> p i c", c=128)
                AC = sb(ph, "h_AC", [64, 64, 2, 128], BF16)
                Yb = sb(ph, "h_Y", [64, 2, 64, 128], BF16)
                Xs = [sb(ph, "h_xs%d" % i, [64, 8, 2, 128], BF16) for i in range(2)]
                kfs = [sb(ph, "h_kf%d" % i, [64, 8, 2, 128], BF16) for i in range(2)]
                t1 = sb(ph, "h_t1", [64, 8, 128], F32)
                t2 = sb(ph, "h_t2", [64, 8, 128], F32)
                gtab = sb(ph, "h_gtabd", [64, 8, 8, 3, 64], BF16)
                ets = [sb(ph, "h_et%d" % i, [64, 8, 2, 64], BF16) for i in range(2)]
                x2s = sb(ph, "h_x2s", [128, 4096], BF16)
                mixs = sb(ph, "h_mixs", [128, 4096], BF16)
                units = UNITS if LIMIT is None else UNITS[:LIMIT]
                for ui, U in enumerate(units):
                    L, N2, nbt, tok0 = U["L"], U["N2"], U["nb"], U["tok0"]
                    S.dma("pool", dtb.t[:], DTB[ui], dtb.r, writes=[dtb.r])
                    for kb in range(8):
                        S.dma("pool", gtab.t[:, kb], GT[ui, kb], gtab.r, writes=[gtab.r])
                    for g in range(NG if STOP is None else STOP):
                        for bb in range(nbt):
                            lo = tok0 + bb * L
                            S.dma("sp", vtok.t[:, :, bb * N2:(bb + 1) * N2],
                                  PT[g * 128:(g + 1) * 128, lo:lo + L].rearrange("c (n1 n2) -> n1 c n2", n2=N2),
                                  vtok.r, writes=[vtok.r])
                            S.dma("sp", x1tok.t[:, :, bb * N2:(bb + 1) * N2],
                                  PT[MW + g * 128:MW + (g + 1) * 128, lo:lo + L].rearrange("c (n1 n2) -> n1 c n2", n2=N2),
                                  x1tok.r, writes=[x1tok.r])
                        S.dma("sp", x2s.t[:], PT[2 * MW + g * 128:2 * MW + (g + 1) * 128, tok0:tok0 + 4096], x2s.r,
                              writes=[x2s.r])
                        for o in range(2):
                            for c0 in range(0, 128, 4):
                                b = (c0 // 4) % 4
                                for c in range(c0, c0 + 4):
                                    mm(PSA[0:64, b * 512 + (c - c0) * 128:b * 512 + (c - c0 + 1) * 128],
                                       vtok.t[:, c, :] if o == 0 else zview[:, :, c], f1.t[:], True, True,
                                       [vtok.r, f1.r], [BK[b]])
                                evac(0, AC.t[:, :, :, c0:c0 + 4].rearrange("p k r c -> p (k r) c"),
                                     PSA[0:64, b * 512:(b + 1) * 512].rearrange("p (c x) -> p x c", c=4),
                                     [BK[b]], [AC.r])
                            for kb in range(8):
                                gt, kf, xs = gtab, kfs[kb % 2], Xs[kb % 2]
                                S.dma("sp", kf.t[:], KF[ui, g, o, kb], kf.r, writes=[kf.r])
                                for kp in range(4):
                                    b = 4 + (kb * 4 + kp) % 3
                                    for kk in range(2):
                                        kl = kp * 2 + kk
                                        k1 = kb * 8 + kl
                                        ore = PSA[0:64, b * 512 + kk * 256:b * 512 + kk * 256 + 128]
                                        oim = PSA[0:64, b * 512 + kk * 256 + 128:b * 512 + kk * 256 + 256]
                                        mm(ore, gt.t[:, kb, kl, 0, :], AC.t[:, k1, 0, :], True, False, [gt.r, AC.r], [BK[b]])
                                        mm(ore, gt.t[:, kb, kl, 2, :], AC.t[:, k1, 1, :], False, True, [gt.r, AC.r], [BK[b]])
                                        mm(oim, gt.t[:, kb, kl, 1, :], AC.t[:, k1, 0, :], True, False, [gt.r, AC.r], [BK[b]])
                                        mm(oim, gt.t[:, kb, kl, 0, :], AC.t[:, k1, 1, :], False, True, [gt.r, AC.r], [BK[b]])
                                    evac(0, xs.t[:, kp * 2:kp * 2 + 2, :, :].rearrange("p k r c -> p (k r c)"),
                                         PSA[0:64, b * 512:(b + 1) * 512], [BK[b]], [xs.r])
                                xre, xim = xs.t[:, :, 0, :], xs.t[:, :, 1, :]
                                kre, kim = kf.t[:, :, 0, :], kf.t[:, :, 1, :]
                                yre = Yb.t[:, 0, kb * 8:(kb + 1) * 8, :]
                                yim = Yb.t[:, 1, kb * 8:(kb + 1) * 8, :]
                                S.op("dve", lambda: nc.vector.tensor_tensor(out=t1.t[:], in0=xre, in1=kre, op=ALU.mult),
                                     reads=[xs.r, kf.r], writes=[t1.r])
                                S.op("dve", lambda: nc.vector.tensor_tensor(out=t2.t[:], in0=xim, in1=kim, op=ALU.mult),
                                     reads=[xs.r, kf.r], writes=[t2.r])
                                S.op("dve", lambda: nc.vector.tensor_tensor(out=yre, in0=t1.t[:], in1=t2.t[:], op=ALU.subtract),
                                     reads=[t1.r, t2.r], writes=[Yb.r])
                                S.op("dve", lambda: nc.vector.tensor_tensor(out=t1.t[:], in0=xre, in1=kim, op=ALU.mult),
                                     reads=[xs.r, kf.r], writes=[t1.r])
                                S.op("dve", lambda: nc.vector.tensor_tensor(out=t2.t[:], in0=xim, in1=kre, op=ALU.mult),
                                     reads=[xs.r, kf.r], writes=[t2.r])
                                S.op("dve", lambda: nc.vector.tensor_tensor(out=yim, in0=t1.t[:], in1=t2.t[:], op=ALU.add),
                                     reads=[t1.r, t2.r], writes=[Yb.r])
                            for c0 in range(0, 128, 4):
                                b = (c0 // 4) % 4
                                for c in range(c0, c0 + 4):
                                    out = PSA[0:64, b * 512 + (c - c0) * 128:b * 512 + (c - c0 + 1) * 128]
                                    mm(out, Yb.t[:, 0, :, c], dtb.t[:, 1:3, :].rearrange("p v i -> p (v i)"), True, False,
                                       [Yb.r, dtb.r], [BK[b]])
                                    mm(out, Yb.t[:, 1, :, c], dtb.t[:, 0:2, :].rearrange("p v i -> p (v i)"), False, True,
                                       [Yb.r, dtb.r], [BK[b]])
                                evac((c0 // 4) % 2, AC.t[:, :, :, c0:c0 + 4],
                                     PSA[0:64, b * 512:(b + 1) * 512].rearrange("p (c r i) -> p i r c", c=4, r=2),
                                     [BK[b]], [AC.r])
                            for ib in range(8):
                                et = ets[ib % 2]
                                S.dma("pool", et.t[:], ET[ui, ib], et.r, writes=[et.r])
                                if o == 0:
                                    for ip in range(2):
                                        b = 4 + (ib * 2 + ip) % 3
                                        for il4 in range(4):
                                            il = ip * 4 + il4
                                            i = ib * 8 + il
                                            out = PSA[0:64, b * 512 + il4 * 128:b * 512 + (il4 + 1) * 128]
                                            mm(out, et.t[:, il, 0, :], AC.t[:, i, 0, :], True, False, [et.r, AC.r], [BK[b]])
                                            mm(out, et.t[:, il, 1, :], AC.t[:, i, 1, :], False, True, [et.r, AC.r], [BK[b]])
                                        i0 = ib * 8 + ip * 4
                                        S.op("dve", lambda b=b, i0=i0: nc.vector.tensor_tensor(
                                            out=zview[:, i0:i0 + 4, :],
                                            in0=PSA[0:64, b * 512:(b + 1) * 512].rearrange("p (i c) -> p i c", i=4),
                                            in1=x1tok.t[:, :, i0:i0 + 4].rearrange("p c i -> p i c"), op=ALU.mult),
                                            reads=[BK[b], x1tok.r], writes=[vtok.r])
                                else:
                                    b = 4 + ib % 3
                                    for il in range(8):
                                        i = ib * 8 + il
                                        out = PSA[:, b * 512 + il * 64:b * 512 + (il + 1) * 64]
                                        mm(out, AC.t[:, i, 0, :], et.t[:, il, 0, :], True, False, [et.r, AC.r], [BK[b]])
                                        mm(out, AC.t[:, i, 1, :], et.t[:, il, 1, :], False, True, [et.r, AC.r], [BK[b]])
                                    bb = (ib * 8) // N2
                                    n20 = (ib * 8) % N2
                                    view = lambda t: t[:, bb * L:(bb + 1) * L].rearrange("p (t1 n2) -> p n2 t1", n2=N2)[:, n20:n20 + 8, :]
                                    S.op("dve", lambda b=b, view=view: nc.vector.tensor_tensor(
                                        out=view(mixs.t), in0=PSA[:, b * 512:(b + 1) * 512].rearrange("p (i t) -> p i t", i=8),
                                        in1=view(x2s.t), op=ALU.mult),
                                        reads=[BK[b], x2s.r], writes=[mixs.r])
                        S.dma("sp", MIXT[g * 128:(g + 1) * 128, tok0:tok0 + 4096], mixs.t[:], mixs.r, reads=[mixs.r])
                S.barrier()
        if "D" in stages:
            with ExitStack() as ph:
                wout = sb(ph, "d_wout", [128, 8, D], BF16)
                S.dma("pool", wout.t[:], W_OUT[1].rearrange("(j p) n -> p j n", p=128), wout.r, writes=[wout.r])
                gpost = sb(ph, "d_gpost", [128, D], F32)
                S.dma("sp", gpost.t[:], ROWV[:, D:2 * D].partition_broadcast(128), gpost.r, writes=[gpost.r])
                wks = [mk_normwk(ph, "d%d" % i) for i in range(2)]
                eks = [mk_epi(ph, "d%d" % i) for i in range(2)]
                xin = [sb(ph, "dx%d" % i, [128, D], F32) for i in range(4)]
                cat = [sb(ph, "d_cat%d" % i, [128, 8, 256], BF16) for i in range(2)]
                hn2 = [sb(ph, "d_hn2%d" % i, [128, 8, 256], BF16) for i in range(2)]
                hres = [Res("d_hst0"), Res("d_hst1")]
                nch = LIMIT or NT // 256

                def chunk_gen(c):
                    cb, h2 = cat[c % 2], hn2[c % 2]
                    t0 = c * 256
                    xbs = [xin[(2 * c) % 4], xin[(2 * c + 1) % 4]]
                    for tt in range(2):
                        S.dma("sp", xbs[tt].t[:], X2[t0 + tt * 128:t0 + (tt + 1) * 128, :], xbs[tt].r, writes=[xbs[tt].r])
                    S.dma("sp", cb.t[:, 0:6, :], MIXT.rearrange("(k p) t -> p k t", p=128)[:, :, t0:t0 + 256], cb.r,
                          writes=[cb.r])
                    S.dma("sp", cb.t[:, 6:8, :], ATT.rearrange("(k p) t -> p k t", p=128)[:, :, t0:t0 + 256], cb.r,
                          writes=[cb.r])
                    yield
                    for tt in range(2):
                        bk = (3, 4) if tt == 0 else (5, 6)
                        for nb in range(2):
                            for k in range(8):
                                mm(ps(bk[nb]), cb.t[:, k, tt * 128:(tt + 1) * 128], wout.t[:, k, nb * 512:(nb + 1) * 512],
                                   k == 0, k == 7, [cb.r, wout.r], [BK[bk[nb]]])
                        yield
                        epilogue(ph, bk, xbs[tt].t[:], xbs[tt].r, gpost.t[:], gpost.r, t0 + tt * 128, X3, CV_FFNPRE[1], HNT3,
                                 wks[tt], eks[tt], pair=(h2, tt, hres[c % 2]))

                active = []
                nxt = 0
                while active or nxt < nch:
                    if nxt < nch and len(active) < 2:
                        active.append(chunk_gen(nxt))
                        nxt += 1
                    for gen in list(active):
                        try:
                            next(gen)
                        except StopIteration:
                            active.remove(gen)
                S.barrier()

        if "B1" in stages:
            ffn_phase(1, X3, HNT3, Y, None, None)

        S.barrier()
        print("program: %d instructions, %d waits, %d dma sems" % (S.ninst, S.nwait, len(S.chans)))
    return nc


def _hyena_tables():
    t = {}
    n1 = np.arange(64)[:, None]
    k1 = np.arange(64)[None, :]
    ang = 2 * np.pi * n1 * (k1 + 0.5) / 128.0
    f1 = np.zeros((64, 128), np.float64)
    f1[:, 0::2] = np.cos(ang)
    f1[:, 1::2] = -np.sin(ang)
    t["f1t"] = f1.astype(np.float32)
    gt = np.zeros((2, 8, 64, 8, 3, 64), np.float64)
    dtb = np.zeros((2, 64, 3, 64), np.float64)
    et = np.zeros((2, 8, 64, 8, 2, 64), np.float64)
    su = np.zeros((2, 64, 64), np.float64)
    for ui, (L, N2, nb) in enumerate(((2048, 32, 2), (4096, 64, 1))):
        N = 2 * L
        n2 = np.arange(N2)[:, None]
        k2 = np.arange(N2)[None, :]
        for k1v in range(64):
            G = np.exp(-2j * np.pi * n2 * (k1v + 128 * k2 + 0.5) / N)
            Gf = np.zeros((64, 64), np.complex128)
            for b in range(nb):
                Gf[b * N2:(b + 1) * N2, b * N2:(b + 1) * N2] = G
            kb, kl = divmod(k1v, 8)
            gt[ui, kb, :, kl, 0, :] = Gf.real
            gt[ui, kb, :, kl, 1, :] = Gf.imag
            gt[ui, kb, :, kl, 2, :] = -Gf.imag
        Dm = np.exp(2j * np.pi * np.arange(N2)[:, None] * np.arange(N2)[None, :] / N2)
        Df = np.zeros((64, 64), np.complex128)
        for b in range(nb):
            Df[b * N2:(b + 1) * N2, b * N2:(b + 1) * N2] = Dm
        dtb[ui, :, 0, :] = -Df.imag
        dtb[ui, :, 1, :] = Df.real
        dtb[ui, :, 2, :] = Df.imag
        k1c = np.arange(64)[:, None]
        t1 = np.arange(64)[None, :]
        for i in range(64):
            t2 = i % N2
            E = (2.0 / N) * np.exp(2j * np.pi * (t2 + N2 * t1) * (k1c + 0.5) / N)
            ib, il = divmod(i, 8)
            et[ui, ib, :, il, 0, :] = E.real
            et[ui, ib, :, il, 1, :] = -E.imag
        nn1 = np.arange(64)[:, None]
        nn2 = np.arange(64)[None, :]
        su[ui] = np.where(nn2 < N2, -(N2 * nn1 + nn2) / (L - 1.0), 0.0)
        tt = np.linspace(0.0, 1.0, L)[:, None]
        w = (2.0 * np.pi / L) * np.arange(L)[:, None]
        f = np.linspace(1e-4, 15.0, 16)[None, :]
        feat = np.concatenate([tt, np.cos(f * w), -np.sin(f * w)], axis=-1)
        t["featP" if ui == 0 else "featS"] = np.ascontiguousarray(feat.T).astype(np.float32)
    t["gt"] = gt.astype(np.float32)
    t["dtb"] = dtb.astype(np.float32)
    t["et"] = et.astype(np.float32)
    t["su"] = su.astype(np.float32)
    return t


def _prep_shared(inp):
    f = lambda a: np.ascontiguousarray(np.asarray(a, dtype=np.float32))
    sh = {}
    sh["a_w_in"] = f(inp["a_w_in"][0])
    sh["b_w_in"] = f(inp["b_w_in"][0])
    sh["w_kv"] = f(inp["xattn_w_kv"])
    sh["w_out"] = f(inp["w_out"])
    sh["w_up"] = f(inp["ffn_w_up"])
    sh["w_dn"] = f(inp["ffn_w_down"])
    sh["wsT"] = f(np.transpose(np.asarray(inp["a_w_s"][0]), (2, 0, 1)))
    sh["bsT"] = f(np.transpose(np.asarray(inp["a_b_s"][0]), (1, 0)))

    def cols(v):
        v = np.asarray(v, dtype=np.float32)
        return v.reshape(-1, 128).T

    cl = []
    for nm in ("norm_mix_pre", "norm_ffn_pre"):
        pass
    a = inp
    cl += [cols(a["norm_mix_pre"][0]), cols(a["norm_ffn_pre"][0]), cols(a["norm_mix_pre"][1]), cols(a["norm_ffn_pre"][1])]
    cl += [cols(a["norm_mem"][0]), cols(a["norm_mem"][1])]
    for l in range(2):
        for k in range(3):
            cl.append(cols(a["ffn_conv_w"][l][k]))
        cl.append(cols(a["ffn_conv_b"][l]))
    for k in range(3):
        cl.append(cols(a["b_sconv_w"][0][k]))
    cl.append(cols(a["b_sconv_b"][0]))
    colv = np.concatenate(cl, axis=1)
    assert colv.shape == (128, NCV), colv.shape
    sh["colv"] = f(colv)
    deltas = np.abs(np.linspace(math.log(1e-2) / 1.5, math.log(1e-2) / 0.3, MW, dtype=np.float32))
    rowv = np.concatenate([np.asarray(a["norm_mix_post"][0]), np.asarray(a["norm_mix_post"][1]),
                           np.asarray(a["norm_ffn_post"][0]), np.asarray(a["norm_ffn_post"][1]),
                           np.asarray(a["a_ln_g"][0]), np.asarray(a["a_ln_b"][0]), deltas]).astype(np.float32)
    sh["rowv"] = f(rowv[None, :])
    sh["ident"] = np.eye(128, dtype=np.float32)
    sh.update(_hyena_tables())
    sh["fw1"] = f(a["b_filt_w1"][0])
    sh["fw2"] = f(a["b_filt_w2"][0])
    w3 = np.asarray(a["b_filt_w3"][0], dtype=np.float32).reshape(64, 2, 2, 6, 128)
    sh["fw3"] = f(np.transpose(w3, (0, 3, 2, 1, 4)).reshape(64, 3072))
    sh["fvec"] = f(np.stack([np.asarray(a["b_filt_freq"][0]), np.asarray(a["b_filt_b1"][0]),
                             np.asarray(a["b_filt_b2"][0])], axis=1))
    sh["fbias"] = f(np.asarray(a["b_filt_bias"][0]).reshape(1, 1536))
    return sh


def _core_inputs(inp, i):
    xp = np.asarray(inp["x_prompt"])
    xs = np.asarray(inp["x_sample"])
    mp = np.asarray(inp["mem_prompt"])
    ms = np.asarray(inp["mem_sample"])
    X = np.concatenate([xp[2 * i], xp[2 * i + 1], xs[i]], axis=0)
    M = np.concatenate([mp[2 * i], mp[2 * i + 1], ms[i]], axis=0)
    return {"X": np.ascontiguousarray(X, dtype=np.float32), "MEM": np.ascontiguousarray(M, dtype=np.float32)}


def kernel(**inputs):
    sh = _prep_shared(inputs)
    nc = build()
    in_maps = []
    for i in range(8):
        m = dict(sh)
        m.update(_core_inputs(inputs, i))
        in_maps.append(m)
    res = run_bass_kernel_spmd(nc, in_maps, core_ids=list(range(8)))
    yp = np.zeros((16, 2048, D), np.float32)
    ys = np.zeros((8, 4096, D), np.float32)
    for i in range(8):
        y = res.results[i]["Y"]
        yp[2 * i] = y[0:2048]
        yp[2 * i + 1] = y[2048:4096]
        ys[i] = y[4096:8192]
    return (yp, ys)
```
